# Optimizing a Trainium2 kernel written in Bass

```python
import jax, jax.numpy as jnp
from jax import lax
import numpy as np

D_MODEL = 1024
BATCH = 2
SEQ = 8192
DEPTH = 2

N_EVEN = (DEPTH + 1) // 2
N_ODD = DEPTH // 2

GLA_HEADS = 4
GLA_DK = 64
GLA_DV = 128
GLA_GATE_RANK = 16
GLA_GATE_TAU = 16.0
GLA_CHUNK = 64

DSA_HEADS = 4
DSA_HEAD_DIM = 128
IDX_HEADS = 8
IDX_DIM = 64
DSA_TOPK = 256

POOL_WINDOWS = (2, 4, 8, 16)
POOL_GROUPS = 4
POOL_CH = 128

SB_HEADS = 4
SB_HEAD_DIM = 128

BLOCK_Q = 128
ROPE_THETA = 10000.0
NORM_EPS = 1e-6
D_FF = ((8 * D_MODEL + 3 * 256 - 1) // (3 * 256)) * 256

AB_SPLITS = (GLA_HEADS * GLA_DK, GLA_HEADS * GLA_DK, GLA_HEADS * GLA_DV, GLA_GATE_RANK,
             GLA_HEADS * GLA_DV,
             DSA_HEADS * DSA_HEAD_DIM, DSA_HEADS * DSA_HEAD_DIM, DSA_HEADS * DSA_HEAD_DIM,
             IDX_HEADS * IDX_DIM, IDX_DIM, IDX_HEADS)
AB_IN = sum(AB_SPLITS)
D_MIX_AB = GLA_HEADS * GLA_DV + DSA_HEADS * DSA_HEAD_DIM
CD_SPLITS = (POOL_GROUPS * POOL_CH, SB_HEADS * SB_HEAD_DIM, SB_HEADS * SB_HEAD_DIM,
             SB_HEADS * SB_HEAD_DIM)
CD_IN = sum(CD_SPLITS)
D_MIX_CD = POOL_GROUPS * POOL_CH + SB_HEADS * SB_HEAD_DIM

kernel_name = "hybrid_gla_dsa_pool_stickbreak_adaln"


def split_cols(t, sizes):
    offsets = np.cumsum(sizes)[:-1].tolist()
    return jnp.split(t, offsets, axis=-1)


def rms_norm(t, w):
    tf = t.astype(jnp.float32)
    y = tf * lax.rsqrt(jnp.mean(tf * tf, axis=-1, keepdims=True) + NORM_EPS)
    return (y * w.astype(jnp.float32)).astype(t.dtype)


def modulate(h, shift, scale):
    return h * (1.0 + scale[:, None, :]) + shift[:, None, :]


def rope(t, positions):
    half = t.shape[-1] // 2
    inv_freq = ROPE_THETA ** (-jnp.arange(half, dtype=jnp.float32) / half)
    ang = positions.astype(jnp.float32)[:, :, None, None] * inv_freq
    cos, sin = jnp.cos(ang), jnp.sin(ang)
    tf = t.astype(jnp.float32)
    t1, t2 = tf[..., :half], tf[..., half:]
    return jnp.concatenate([t1 * cos - t2 * sin, t2 * cos + t1 * sin], axis=-1).astype(t.dtype)


def swiglu(h, w1, w2):
    gate, up = jnp.split(h @ w1, 2, axis=-1)
    return (jax.nn.silu(gate) * up) @ w2


def gla_chunked(q, k, v, log_a):
    B, S, H, DK = q.shape
    DV = v.shape[-1]
    C = GLA_CHUNK
    N = S // C

    def chunks(t):
        return t.astype(jnp.float32).reshape(B, N, C, H, t.shape[-1]).transpose(0, 3, 1, 2, 4)

    q, k, v, g = chunks(q), chunks(k), chunks(v), chunks(log_a)
    q = q * (DK ** -0.5)
    b = jnp.cumsum(g, axis=3)
    b_mid = b[:, :, :, C // 2 - 1:C // 2, :]
    att = jnp.einsum('bhnid,bhnjd->bhnij', q * jnp.exp(b - b_mid), k * jnp.exp(b_mid - b))
    causal = jnp.tril(jnp.ones((C, C), dtype=bool))
    att = jnp.where(causal, att, 0.0)
    o_intra = jnp.einsum('bhnij,bhnjv->bhniv', att, v)
    b_last = b[:, :, :, -1:, :]
    k_dec = k * jnp.exp(b_last - b)
    q_dec = q * jnp.exp(b)
    decay = jnp.exp(b_last[:, :, :, 0, :])
    kv = jnp.einsum('bhncd,bhncv->bhndv', k_dec, v)

    def step(state, inp):
        dec, kv_n = inp
        return dec[..., None] * state + kv_n, state

    init = jnp.zeros((B, H, DK, DV), jnp.float32)
    _, prev = lax.scan(step, init, (jnp.moveaxis(decay, 2, 0), jnp.moveaxis(kv, 2, 0)))
    prev = jnp.moveaxis(prev, 0, 2)
    o_inter = jnp.einsum('bhncd,bhndv->bhncv', q_dec, prev)
    o = o_intra + o_inter
    return o.transpose(0, 2, 3, 1, 4).reshape(B, S, H, DV)


def dsa_attention(q, k, v, iq, ik, iw, n_sel):
    B, S, H, Dh = q.shape
    bidx = jnp.arange(B)[:, None, None]
    key_pos = jnp.arange(S)
    idx_scale = (IDX_HEADS ** -0.5) * (IDX_DIM ** -0.5)
    att_scale = Dh ** -0.5

    def block(i):
        t0 = i * BLOCK_Q
        tpos = t0 + jnp.arange(BLOCK_Q)
        qb = lax.dynamic_slice_in_dim(q, t0, BLOCK_Q, axis=1)
        iqb = lax.dynamic_slice_in_dim(iq, t0, BLOCK_Q, axis=1)
        iwb = lax.dynamic_slice_in_dim(iw, t0, BLOCK_Q, axis=1)
        rel = jax.nn.relu(jnp.einsum('bqhd,bsd->bqhs', iqb, ik).astype(jnp.float32))
        score = jnp.einsum('bqhs,bqh->bqs', rel, iwb.astype(jnp.float32)) * idx_scale
        causal = key_pos[None, :] <= tpos[:, None]
        score = jnp.where(causal[None], score, -jnp.inf)
        _, sel = lax.top_k(score, n_sel)
        kg = k[bidx, sel]
        vg = v[bidx, sel]
        logits = jnp.einsum('bqhd,bqkhd->bhqk', qb, kg).astype(jnp.float32) * att_scale
        valid = (sel <= tpos[None, :, None])[:, None]
        logits = jnp.where(valid, logits, -jnp.inf)
        p = jax.nn.softmax(logits, axis=-1).astype(v.dtype)
        return jnp.einsum('bhqk,bqkhd->bqhd', p, vg)

    out = lax.map(block, jnp.arange(S // BLOCK_Q))
    return jnp.moveaxis(out, 0, 1).reshape(B, S, H, Dh)


def stick_breaking_attention(q, k, v):
    B, S, H, Dh = q.shape
    key_pos = jnp.arange(S)
    scale = Dh ** -0.5

    def block(i):
        t0 = i * BLOCK_Q
        tpos = t0 + jnp.arange(BLOCK_Q)
        qb = lax.dynamic_slice_in_dim(q, t0, BLOCK_Q, axis=1)
        z = jnp.einsum('bqhd,bshd->bhqs', qb, k).astype(jnp.float32) * scale
        strict = (key_pos[None, :] < tpos[:, None])[None, None]
        log_beta = jax.nn.log_sigmoid(z)
        log_1m = jnp.where(strict, jax.nn.log_sigmoid(-z), 0.0)
        after = lax.cumsum(log_1m, axis=3, reverse=True) - log_1m
        w = jnp.where(strict, jnp.exp(log_beta + after), 0.0)
        return jnp.einsum('bhqs,bshd->bqhd', w.astype(v.dtype), v)

    out = lax.map(block, jnp.arange(S // BLOCK_Q))
    return jnp.moveaxis(out, 0, 1).reshape(B, S, H, Dh)


def multiscale_pool(u, pool_w, pool_scale):
    B, S, _ = u.shape
    ug = u.astype(jnp.float32).reshape(B, S, POOL_GROUPS, POOL_CH)
    cs = jnp.cumsum(ug, axis=1)
    cs0 = jnp.concatenate([jnp.zeros((B, 1, POOL_GROUPS, POOL_CH), jnp.float32), cs], axis=1)
    t = jnp.arange(S, dtype=jnp.float32)
    pooled = []
    for g, w in enumerate(POOL_WINDOWS):
        lo = jnp.pad(cs0[:, :S + 1 - w, g], ((0, 0), (w - 1, 0), (0, 0)))
        count = jnp.minimum(t + 1.0, float(w))[None, :, None]
        pooled.append((cs[:, :, g] - lo) / count - ug[:, :, g])
    pooled = jnp.stack(pooled, axis=2)
    y = jnp.einsum('bsgc,gcd->bsgd', pooled, pool_w.astype(jnp.float32))
    y = y * pool_scale.astype(jnp.float32).reshape(POOL_GROUPS, POOL_CH)
    return y.reshape(B, S, POOL_GROUPS * POOL_CH).astype(u.dtype)


def ab_mixer(h, positions, w_in, gate_up, gate_b, out_norm, q_norm, k_norm, w_out):
    B, S, _ = h.shape
    (gq, gk, gv, glow, gr, dq, dk, dv, iq, ik, iw) = split_cols(h @ w_in, AB_SPLITS)
    log_a = jax.nn.log_sigmoid((glow @ gate_up + gate_b).astype(jnp.float32)) / GLA_GATE_TAU
    o_gla = gla_chunked(gq.reshape(B, S, GLA_HEADS, GLA_DK), gk.reshape(B, S, GLA_HEADS, GLA_DK),
                        gv.reshape(B, S, GLA_HEADS, GLA_DV), log_a.reshape(B, S, GLA_HEADS, GLA_DK))
    o_gla = rms_norm(o_gla.astype(h.dtype), out_norm) * jax.nn.silu(gr.reshape(B, S, GLA_HEADS, GLA_DV))
    q = rope(rms_norm(dq.reshape(B, S, DSA_HEADS, DSA_HEAD_DIM), q_norm), positions)
    k = rope(rms_norm(dk.reshape(B, S, DSA_HEADS, DSA_HEAD_DIM), k_norm), positions)
    v = dv.reshape(B, S, DSA_HEADS, DSA_HEAD_DIM)
    iq = rope(iq.reshape(B, S, IDX_HEADS, IDX_DIM), positions)
    ik = rope(ik[:, :, None, :], positions)[:, :, 0, :]
    n_sel = min(DSA_TOPK, S // 4)
    o_dsa = dsa_attention(q, k, v, iq, ik, iw, n_sel)
    mix = jnp.concatenate([o_gla.reshape(B, S, -1), o_dsa.reshape(B, S, -1)], axis=-1)
    return mix @ w_out


def cd_mixer(h, w_in, pool_w, pool_scale, q_norm, k_norm, w_out):
    B, S, _ = h.shape
    u, sq, sk, sv = split_cols(h @ w_in, CD_SPLITS)
    o_pool = multiscale_pool(u, pool_w, pool_scale)
    q = rms_norm(sq.reshape(B, S, SB_HEADS, SB_HEAD_DIM), q_norm)
    k = rms_norm(sk.reshape(B, S, SB_HEADS, SB_HEAD_DIM), k_norm)
    v = sv.reshape(B, S, SB_HEADS, SB_HEAD_DIM)
    o_sb = stick_breaking_attention(q, k, v)
    mix = jnp.concatenate([o_pool, o_sb.reshape(B, S, -1)], axis=-1)
    return mix @ w_out


def setup_inputs(seed: int = 0) -> dict:
    key = jax.random.key(seed)
    ks = jax.random.split(key, 24)
    f32 = jnp.float32

    def nrm(k, shape, scale):
        return jax.random.normal(k, shape, f32) * scale

    def gain(k, shape):
        return 1.0 + 0.1 * jax.random.normal(k, shape, f32)

    x = nrm(ks[0], (BATCH, SEQ, D_MODEL), 1.0)
    c = nrm(ks[1], (BATCH, D_MODEL), 1.0)
    positions = (jnp.arange(SEQ, dtype=jnp.int32)[None, :]
                 + jax.random.randint(ks[2], (BATCH, 1), 0, 4096, dtype=jnp.int32))
    return {
        "x": x,
        "c": c,
        "positions": positions,
        "ada_w": nrm(ks[3], (DEPTH, D_MODEL, 6 * D_MODEL), 0.5 * D_MODEL ** -0.5),
        "ada_b": nrm(ks[4], (DEPTH, 6 * D_MODEL), 0.01),
        "mix_norm": gain(ks[5], (DEPTH, D_MODEL)),
        "ffn_norm": gain(ks[6], (DEPTH, D_MODEL)),
        "ffn_w1": nrm(ks[7], (DEPTH, D_MODEL, 2 * D_FF), D_MODEL ** -0.5),
        "ffn_w2": nrm(ks[8], (DEPTH, D_FF, D_MODEL), D_FF ** -0.5),
        "ab_w_in": nrm(ks[9], (N_EVEN, D_MODEL, AB_IN), D_MODEL ** -0.5),
        "gla_gate_up": nrm(ks[10], (N_EVEN, GLA_GATE_RANK, GLA_HEADS * GLA_DK), GLA_GATE_RANK ** -0.5),
        "gla_gate_b": nrm(ks[11], (N_EVEN, GLA_HEADS * GLA_DK), 0.01),
        "gla_out_norm": gain(ks[12], (N_EVEN, GLA_DV)),
        "dsa_q_norm": gain(ks[13], (N_EVEN, DSA_HEAD_DIM)),
        "dsa_k_norm": gain(ks[14], (N_EVEN, DSA_HEAD_DIM)),
        "ab_w_out": nrm(ks[15], (N_EVEN, D_MIX_AB, D_MODEL), D_MIX_AB ** -0.5),
        "cd_w_in": nrm(ks[16], (N_ODD, D_MODEL, CD_IN), D_MODEL ** -0.5),
        "pool_w": nrm(ks[17], (N_ODD, POOL_GROUPS, POOL_CH, POOL_CH), POOL_CH ** -0.5),
        "pool_scale": gain(ks[18], (N_ODD, POOL_GROUPS * POOL_CH)),
        "sb_q_norm": gain(ks[19], (N_ODD, SB_HEAD_DIM)),
        "sb_k_norm": gain(ks[20], (N_ODD, SB_HEAD_DIM)),
        "cd_w_out": nrm(ks[21], (N_ODD, D_MIX_CD, D_MODEL), D_MIX_CD ** -0.5),
    }


def reference(x, c, positions, ada_w, ada_b, mix_norm, ffn_norm, ffn_w1, ffn_w2,
              ab_w_in, gla_gate_up, gla_gate_b, gla_out_norm, dsa_q_norm, dsa_k_norm, ab_w_out,
              cd_w_in, pool_w, pool_scale, sb_q_norm, sb_k_norm, cd_w_out):
    cond = jax.nn.silu(c)
    for layer in range(DEPTH):
        mod = cond @ ada_w[layer] + ada_b[layer]
        sh1, sc1, g1, sh2, sc2, g2 = jnp.split(mod, 6, axis=-1)
        h = modulate(rms_norm(x, mix_norm[layer]), sh1, sc1)
        if layer % 2 == 0:
            i = layer // 2
            y = ab_mixer(h, positions, ab_w_in[i], gla_gate_up[i], gla_gate_b[i], gla_out_norm[i],
                         dsa_q_norm[i], dsa_k_norm[i], ab_w_out[i])
        else:
            i = layer // 2
            y = cd_mixer(h, cd_w_in[i], pool_w[i], pool_scale[i], sb_q_norm[i], sb_k_norm[i],
                         cd_w_out[i])
        x = x + g1[:, None, :] * y
        h = modulate(rms_norm(x, ffn_norm[layer]), sh2, sc2)
        x = x + g2[:, None, :] * swiglu(h, ffn_w1[layer], ffn_w2[layer])
    return x
```

```python
import ml_dtypes
import numpy as np
from contextlib import ExitStack
import concourse.bass as bass
import concourse.mybir as mybir
from concourse.bass_utils import run_bass_kernel_spmd

F32 = mybir.dt.float32
BF16 = mybir.dt.bfloat16
I32 = mybir.dt.int32
ALU = mybir.AluOpType
AF = mybir.ActivationFunctionType
AX = mybir.AxisListType

EPOCH = 30000
ND = 32


import types


def bind(fn):
    if getattr(fn, "__closure__", None) is None:
        return fn
    cells = []
    for c in fn.__closure__:
        try:
            cells.append(types.CellType(c.cell_contents))
        except ValueError:
            cells.append(c)
    return types.FunctionType(fn.__code__, fn.__globals__, fn.__name__, fn.__defaults__, tuple(cells))


class Reg:
    __slots__ = ("w", "r", "name")

    def __init__(self, name=""):
        self.w = None
        self.r = []
        self.name = name


class KB:
    def __init__(self, nc, es):
        self.nc = nc
        self.es = es
        self.eng = {"pe": nc.tensor, "act": nc.scalar, "dve": nc.vector, "pool": nc.gpsimd, "sp": nc.sync}
        self.sems = {}
        self.cnt = {k: 0 for k in self.eng}
        self.seen = {k: {} for k in self.eng}
        self.ndma = 0
        self.dsem = [es.enter_context(nc.semaphore(f"d{i}")) for i in range(ND)]
        self.nins = 0
        self.out_toks = []
        self.q = {k: [] for k in self.eng}
        self.rec = None
        self.pes = es
        self.pfx = ""
        self.fused = False
        self.last_phase = True
        self.dma_uses = {}

    def _esem(self, st, epoch):
        key = ("E", st, epoch)
        if key not in self.sems:
            self.sems[key] = self.es.enter_context(self.nc.semaphore(f"e_{st}_{epoch}"))
        return key

    def _semh(self, key):
        if key[0] == "D":
            return self.dsem[key[1]]
        return self.sems[key]

    def _collect(self, st, reads, writes):
        waits = {}

        def need(tok, kind):
            if tok is None:
                return
            tst, key, val = tok
            if tst == st and key[0] == "E":
                if st == "pe":
                    return
                if st in ("act", "dve") and kind != "raw":
                    return
            if self.seen[st].get(key, 0) >= val:
                return
            if waits.get(key, 0) < val:
                waits[key] = val

        for r in reads:
            need(r.w, "raw")
        for w in writes:
            need(w.w, "waw")
            for t in w.r:
                need(t, "war")
        return waits

    def _dowaits(self, st, waits):
        for key, val in waits.items():
            h = self._semh(key)
            self.q[st].append(lambda eng, h=h, val=val: eng.wait_ge(h, val))
            self.seen[st][key] = val
            self.nins += 1

    def _record(self, tok, reads, writes):
        for r in reads:
            if tok[1][0] == "E":
                r.r = [t for t in r.r if not (t[0] == tok[0] and t[1] == tok[1])]
            r.r.append(tok)
        for w in writes:
            w.w = tok
            w.r = []

    def emit_roundrobin(self, chains):
        self.rec = None
        n = max(len(c) for c in chains)
        for k in range(n):
            for c in chains:
                if k < len(c):
                    c[k]()

    def op(self, st, fn, reads=(), writes=()):
        fn = bind(fn)
        if self.rec is not None:
            reads = tuple(reads); writes = tuple(writes); rec = self.rec
            rec.append(lambda: self._norec(rec, self.op, st, fn, reads, writes))
            return None
        waits = self._collect(st, reads, writes)
        self._dowaits(st, waits)
        self.cnt[st] += 1
        c = self.cnt[st]
        epoch, val = divmod(c - 1, EPOCH)
        key = self._esem(st, epoch)
        h = self.sems[key]
        self.q[st].append(lambda eng, fn=fn, h=h: fn(eng).then_inc(h, 1))
        tok = (st, key, val + 1)
        self._record(tok, reads, writes)
        self.nins += 1
        return tok

    def _norec(self, rec, f, *a, **kw):
        saved = self.rec
        self.rec = None
        try:
            return f(*a, **kw)
        finally:
            self.rec = saved

    def group(self, st, fns, reads=(), writes=()):
        if self.rec is not None:
            fns = [bind(f) for f in fns]; reads = tuple(reads); writes = tuple(writes); rec = self.rec
            rec.append(lambda: self._norec(rec, self.group, st, fns, reads, writes))
            return None
        waits = self._collect(st, reads, writes)
        self._dowaits(st, waits)
        fns = [bind(f) for f in fns]
        for fn in fns[:-1]:
            self.q[st].append(fn)
            self.nins += 1
        self.nins += 1
        self.cnt[st] += 1
        c = self.cnt[st]
        epoch, val = divmod(c - 1, EPOCH)
        key = self._esem(st, epoch)
        h = self.sems[key]
        self.q[st].append(lambda eng, fn=fns[-1], h=h: fn(eng).then_inc(h, 1))
        tok = (st, key, val + 1)
        self._record(tok, reads, writes)
        return tok

    def dma(self, st, out, in_, reads=(), writes=(), is_output=False, **kw):
        if self.rec is not None:
            reads = tuple(reads); writes = tuple(writes); rec = self.rec
            rec.append(lambda: self._norec(rec, self.dma, st, out, in_, reads, writes, is_output, **kw))
            return None
        i = self.ndma
        self.ndma += 1
        j = i % ND
        use = i // ND
        key = ("D", j)
        waits = self._collect(st, reads, writes)
        if use > 0 and self.seen[st].get(key, 0) < 16 * use:
            waits[key] = max(waits.get(key, 0), 16 * use)
        self._dowaits(st, waits)
        h = self.dsem[j]
        self.q[st].append(lambda eng, out=out, in_=in_, kw=kw, h=h: eng.dma_start(out=out, in_=in_, **kw).then_inc(h, 16))
        tok = (st, key, 16 * (use + 1))
        self.dma_uses[key] = 16 * (use + 1)
        self._record(tok, reads, writes)
        self.nins += 1
        if is_output:
            self.out_toks.append(tok)
        return tok

    def coll(self, fn, reads=(), writes=()):
        st = "pool"
        fn = bind(fn)
        idx = len([k for k in self.sems if k[0] == "C"])
        key = ("C", idx)
        self.sems[key] = self.es.enter_context(self.nc.semaphore(f"cc{idx}"))
        waits = self._collect(st, reads, writes)
        self._dowaits(st, waits)
        h = self.sems[key]
        self.q[st].append(lambda eng, fn=fn, h=h: fn(eng).then_inc(h, 1))
        tok = (st, key, 1)
        self.dma_uses[key] = 1
        self._record(tok, reads, writes)
        self.nins += 1
        return tok

    def barrier(self, skip=()):
        targets = {k: v for k, v in self.dma_uses.items() if k not in skip}
        for e, c in self.cnt.items():
            if c > 0:
                epoch, val = divmod(c - 1, EPOCH)
                targets[("E", e, epoch)] = val + 1
        for st in self.eng:
            for key, val in targets.items():
                if self.seen[st].get(key, 0) < val:
                    h = self._semh(key)
                    self.q[st].append(lambda eng, h=h, val=val: eng.wait_ge(h, val))
                    self.seen[st][key] = val

    def end_phase(self, skip=()):
        self.barrier(skip)
        self.replay()
        self.q = {k: [] for k in self.eng}
        self.pes.close()

    def finish(self):
        if self.fused and not self.last_phase:
            self.end_phase()
            return
        st = "sp"
        for tok in self.out_toks:
            _, key, val = tok
            if self.seen[st].get(key, 0) < val:
                h = self._semh(key)
                self.q[st].append(lambda eng, h=h, val=val: eng.wait_ge(h, val))
                self.seen[st][key] = val
        self.replay()

    def replay(self):
        q = self.q
        with self.nc.Block() as block:
            @block.sync
            def _(e):
                for f in q["sp"]:
                    f(e)

            @block.tensor
            def _(e):
                for f in q["pe"]:
                    f(e)

            @block.scalar
            def _(e):
                for f in q["act"]:
                    f(e)

            @block.vector
            def _(e):
                for f in q["dve"]:
                    f(e)

            @block.gpsimd
            def _(e):
                for f in q["pool"]:
                    f(e)


class T:
    def __init__(self, t, name=""):
        self.t = t
        self.reg = Reg(name)
        self.sub = {}

    def __getitem__(self, idx):
        return self.t[idx]

    def r(self, key=None):
        if key is None:
            return self.reg
        if key not in self.sub:
            self.sub[key] = Reg()
        return self.sub[key]


def sb(kb, name, shape, dt):
    return T(kb.pes.enter_context(kb.nc.sbuf_tensor("s_" + kb.pfx + name, list(shape), dt)), name)


def ps(kb, name, shape, dt=F32):
    return T(kb.pes.enter_context(kb.nc.psum_tensor("p_" + kb.pfx + name, list(shape), dt)), name)


STAGE = 99.0
NTL = 16

EPS = 1e-6
NT = 16
TWO_PI = float(2 * np.pi)


class FX:
    active = False
    nc = None
    kb = None
    remap = {}
    ext = {}
    n_phase = 0


def fx_ext(name, shape, dt):
    if name not in FX.ext:
        FX.ext[name] = FX.nc.dram_tensor(name, list(shape), dt, kind="ExternalInput").ap()
    return FX.ext[name]


def _get_nc():
    if FX.active:
        return FX.nc
    return bass.Bass("TRN2", target_bir_lowering=False)


class _phase:
    def __init__(self, nc):
        self.nc = nc

    def __enter__(self):
        if FX.active:
            kb = FX.kb
            kb.pes = ExitStack()
            kb.pfx = f"ph{FX.n_phase}_"
            FX.n_phase += 1
            return kb
        self.es = ExitStack()
        self.es.__enter__()
        return KB(self.nc, self.es)

    def __exit__(self, *a):
        if not FX.active:
            self.es.__exit__(*a)
        return False


def dram_in(nc, name, shape, dt):
    if FX.active:
        ap = FX.remap[name]
        assert list(ap.shape) == list(shape), (name, ap.shape, shape)
        return ap
    return nc.dram_tensor(name, list(shape), dt, kind="ExternalInput").ap()


def dram_out(nc, name, shape, dt):
    if FX.active:
        ap = FX.remap[name]
        assert list(ap.shape) == list(shape), (name, ap.shape, shape)
        return ap
    return nc.dram_tensor(name, list(shape), dt, kind="ExternalOutput").ap()


def emit_mod(kb, cT_d, ada_w_d, ada_b_d, modrow, modP, name, pA, pB):
    nc = kb.nc
    cT = sb(kb, name + "cT", [128, 8], F32)
    cond = sb(kb, name + "cond", [128, 8], F32)
    adab = sb(kb, name + "adab", [1, 512], F32)
    one = sb(kb, name + "one", [1, 1], F32)
    wblk = [sb(kb, name + "wblk0", [128, 8, 512], F32)] * 2
    pm = pA; pmp = pB
    kb.dma("sp", cT[:], cT_d, writes=[cT.r()])
    kb.op("dve", lambda e: e.memset(one[:], 1.0), writes=[one.r()])
    kb.op("act", lambda e: e.activation(out=cond[:], in_=cT[:], func=AF.Silu), reads=[cT.r()], writes=[cond.r()])
    wv = ada_w_d.rearrange("(k p) n -> p k n", p=128)
    for cb in range(12):
        w = wblk[cb % 2]
        kb.dma("sp", w[:], wv[:, :, cb * 512:(cb + 1) * 512], writes=[w.r()])
        kb.dma("sp", adab[:], ada_b_d[:, cb * 512:(cb + 1) * 512], writes=[adab.r()])
        kb.group("pe", [(lambda e, k=k, w=w: e.matmul(pm[0:1, :], cond[:, k:k + 1], w[:, k, :], start=(k == 0), stop=(k == 7))) for k in range(8)],
                 reads=[cond.r(), w.r()], writes=[pm.r()])
        kb.op("dve", lambda e, cb=cb: e.tensor_tensor(out=modrow[0:1, cb * 512:(cb + 1) * 512], in0=pm[0:1, :], in1=adab[0:1, :], op=ALU.add),
              reads=[pm.r(), adab.r()], writes=[modrow.r()])
    kb.group("pe", [(lambda e, c=c: e.matmul(pmp[:, c:c + 1], modrow[0:1, c * 128:(c + 1) * 128], one[:], start=True, stop=True)) for c in range(48)],
             reads=[modrow.r(), one.r()], writes=[pmp.r()])
    kb.op("dve", lambda e: e.tensor_copy(modP[:], pmp[:, 0:48]), reads=[pmp.r()], writes=[modP.r()])


def emit_rstd(kb, ss, n, tmp, name=""):
    kb.op("dve", lambda e: e.tensor_scalar(out=tmp[:], in0=ss[:], scalar1=1.0 / n, scalar2=EPS, op0=ALU.mult, op1=ALU.add), reads=[ss.r()], writes=[tmp.r()])
    kb.op("act", lambda e: e.activation(out=tmp[:], in_=tmp[:], func=AF.Sqrt), reads=[tmp.r()], writes=[tmp.r()])
    kb.op("dve", lambda e: e.reciprocal(ss[:], tmp[:]), reads=[tmp.r()], writes=[ss.r()])


def emit_norm_hT(kb, xt, hT, a, sh, ident, junk, xn, pT, st, st2):
    kb.op("act", lambda e: e.activation(out=junk[:], in_=xt[:], func=AF.Square, accum_out=st[:, 0:1]), reads=[xt.r()], writes=[junk.r(), st.r()])
    emit_rstd(kb, st, 1024, st2)
    kb.op("act", lambda e: e.activation(out=xn[:], in_=xt[:], func=AF.Copy, scale=st[:, 0:1]), reads=[xt.r(), st.r()], writes=[xn.r()])
    kb.group("pe", [(lambda e, k=k: e.transpose(pT[:, k, :], xn[:, k * 128:(k + 1) * 128], ident[:])) for k in range(8)],
             reads=[xn.r(), ident.r()], writes=[pT.r()])
    for k in range(8):
        kb.op("act", lambda e, k=k: e.activation(out=hT[:, k, :], in_=pT[:, k, :], func=AF.Identity, scale=a[:, k:k + 1], bias=sh[:, k:k + 1]),
              reads=[pT.r(), a.r(), sh.r()], writes=[hT.r()])


def build_l1():
    nc = _get_nc()
    xs = dram_in(nc, "xs", [NT * 128, 1024], F32)
    pos_d = dram_in(nc, "pos", [128, NT], I32)
    cT_d = dram_in(nc, "cT", [128, 8], F32)
    ada_w = dram_in(nc, "ada_w", [1024, 6144], F32)
    ada_b = dram_in(nc, "ada_b", [1, 6144], F32)
    mixn_d = dram_in(nc, "mixn", [128, 8], F32)
    w_in = dram_in(nc, "w_in", [1024, 3672], F32)
    gate_up = dram_in(nc, "gate_up", [16, 256], F32)
    gate_b = dram_in(nc, "gate_b", [256], F32)
    qn_d = dram_in(nc, "q_norm", [128], F32)
    kn_d = dram_in(nc, "k_norm", [128], F32)
    invf_d = dram_in(nc, "invf", [64], F32)
    identb_d = dram_in(nc, "identb", [128, 128], BF16)
    identf_d = dram_in(nc, "identf", [128, 128], F32)
    tri3_d = dram_in(nc, "tri3", [128, 3, 128], F32)
    csel_d = dram_in(nc, "csel", [128, 2], F32)
    amask_d = dram_in(nc, "amask", [128, 128], F32)

    o_mod = dram_out(nc, "o_mod", [1, 6144], F32)
    o_qT = dram_out(nc, "o_qT", [128, 4, NT * 128], BF16)
    o_kT = dram_out(nc, "o_kT", [128, 4, NT * 128], BF16)
    o_v = dram_out(nc, "o_v", [128, NT, 512], BF16)
    o_iqT = dram_out(nc, "o_iqT", [128, 4, NT * 128], BF16)
    o_ikT = dram_out(nc, "o_ikT", [128, NT * 128], BF16)
    o_iw = dram_out(nc, "o_iw", [128, NT, 8], F32)
    o_oin = dram_out(nc, "o_oin", [128, NT, 512], F32)
    o_qdT = dram_out(nc, "o_qdT", [128, 2, NT * 128], BF16)
    o_kv = dram_out(nc, "o_kv", [128, 2, 2 * NT, 128], BF16)
    o_dec = dram_out(nc, "o_dec", [128, 2, 2 * NT], F32)
    o_gr = dram_out(nc, "o_gr", [128, NT, 512], F32)

    with _phase(nc) as kb:
        S = lambda n, s, d: sb(kb, n, s, d)
        identb = S("identb", [128, 128], BF16); identf = S("identf", [128, 128], F32)
        tri3 = S("tri3", [128, 3, 128], F32); csel = S("csel", [128, 2], F32); amask = S("amask", [128, 128], F32)
        invf = S("invf", [128, 64], F32); qnb = S("qnb", [128, 128], F32); knb = S("knb", [128, 128], F32)
        gbb = S("gbb", [128, 256], F32); gup = S("gup", [16, 256], F32); mixn = S("mixn", [128, 8], F32)
        posi = S("posi", [128, NT], I32)
        for t, d in ((identb, identb_d), (identf, identf_d), (tri3, tri3_d), (csel, csel_d), (amask, amask_d), (gup, gate_up), (mixn, mixn_d), (posi, pos_d)):
            kb.dma("sp", t[:], d, writes=[t.r()])
        for t, d in ((invf, invf_d), (qnb, qn_d), (knb, kn_d), (gbb, gate_b)):
            kb.dma("sp", t[:], d.partition_broadcast(128), writes=[t.r()])
        wb = S("wb", [128, 8, 3672], BF16)
        for k in range(8):
            kb.dma("pool", wb[:, k, :], w_in[k * 128:(k + 1) * 128, :], writes=[wb.r(("k", k))])
        wb_regs = [wb.r(("k", k)) for k in range(8)]
        modrow = S("modrow", [1, 6144], F32); modP = S("modP", [128, 48], F32)
        pT = ps(kb, "pT", [128, 8, 128], BF16)
        pp = [ps(kb, f"pp{i}", [128, 512], F32) for i in range(2)]
        pA = ps(kb, "pA", [128, 512], F32)
        pB = ps(kb, "pB", [128, 512], F32)
        pTb = pT
        pTg = ps(kb, "pTg", [128, 8, 128], BF16)
        pKV = ps(kb, "pKV", [128, 4, 256], F32)
        emit_mod(kb, cT_d, ada_w, ada_b, modrow, modP, "m0", pA, pB)
        kb.dma("sp", o_mod, modrow[:], reads=[modrow.r()], is_output=True)
        a1 = S("a1", [128, 8], F32)
        kb.op("dve", lambda e: e.scalar_tensor_tensor(out=a1[:], in0=modP[:, 8:16], scalar=1.0, in1=mixn[:], op0=ALU.add, op1=ALU.mult),
              reads=[modP.r(), mixn.r()], writes=[a1.r()])
        if STAGE < 2:
            kb.finish(); return nc
        posf = S("posf", [128, NT], F32)
        ang = S("ang", [128, NT * 64], F32); ki = S("ki", [128, NT * 64], I32); kf = S("kf", [128, NT * 64], F32)
        rs = ang; rc = S("rc", [128, NT * 64], F32); m1 = kf
        sinT = S("sinT", [128, NT, 64], F32); cosT = S("cosT", [128, NT, 64], F32)
        kb.op("dve", lambda e: e.tensor_copy(posf[:], posi[:]), reads=[posi.r()], writes=[posf.r()])
        for i in range(NT):
            kb.op("dve", lambda e, i=i: e.tensor_scalar(out=ang[:, i * 64:(i + 1) * 64], in0=invf[:], scalar1=posf[:, i:i + 1], scalar2=None, op0=ALU.mult),
                  reads=[invf.r(), posf.r()], writes=[ang.r()])
        C1 = 6.28125; C2 = TWO_PI - C1
        D = lambda fn, r, w: kb.op("dve", fn, reads=r, writes=w)
        D(lambda e: e.tensor_scalar(out=kf[:], in0=ang[:], scalar1=1.0 / TWO_PI, scalar2=None, op0=ALU.mult), [ang.r()], [kf.r()])
        D(lambda e: e.tensor_copy(ki[:], kf[:]), [kf.r()], [ki.r()])
        D(lambda e: e.tensor_copy(kf[:], ki[:]), [ki.r()], [kf.r()])
        D(lambda e: e.scalar_tensor_tensor(out=rs[:], in0=kf[:], scalar=-C1, in1=ang[:], op0=ALU.mult, op1=ALU.add), [kf.r(), ang.r()], [rs.r()])
        D(lambda e: e.scalar_tensor_tensor(out=rs[:], in0=kf[:], scalar=-C2, in1=rs[:], op0=ALU.mult, op1=ALU.add), [kf.r(), rs.r()], [rs.r()])
        PI = float(np.pi)
        D(lambda e: e.tensor_scalar(out=m1[:], in0=rs[:], scalar1=PI, scalar2=-TWO_PI, op0=ALU.is_gt, op1=ALU.mult), [rs.r()], [m1.r()])
        D(lambda e: e.tensor_tensor(out=rs[:], in0=rs[:], in1=m1[:], op=ALU.add), [rs.r(), m1.r()], [rs.r()])
        D(lambda e: e.tensor_scalar(out=m1[:], in0=rs[:], scalar1=-PI, scalar2=TWO_PI, op0=ALU.is_lt, op1=ALU.mult), [rs.r()], [m1.r()])
        D(lambda e: e.tensor_tensor(out=rs[:], in0=rs[:], in1=m1[:], op=ALU.add), [rs.r(), m1.r()], [rs.r()])
        D(lambda e: e.tensor_scalar(out=rc[:], in0=rs[:], scalar1=PI / 2, scalar2=None, op0=ALU.add), [rs.r()], [rc.r()])
        D(lambda e: e.tensor_scalar(out=m1[:], in0=rc[:], scalar1=PI, scalar2=-TWO_PI, op0=ALU.is_gt, op1=ALU.mult), [rc.r()], [m1.r()])
        D(lambda e: e.tensor_tensor(out=rc[:], in0=rc[:], in1=m1[:], op=ALU.add), [rc.r(), m1.r()], [rc.r()])
        for t in (rs, rc):
            D(lambda e, t=t: e.tensor_scalar(out=t[:], in0=t[:], scalar1=PI, scalar2=-PI, op0=ALU.min, op1=ALU.max), [t.r()], [t.r()])
        kb.op("act", lambda e: e.activation(out=sinT[:].rearrange("p a b -> p (a b)"), in_=rs[:], func=AF.Sin), reads=[rs.r()], writes=[sinT.r()])
        kb.op("act", lambda e: e.activation(out=cosT[:].rearrange("p a b -> p (a b)"), in_=rc[:], func=AF.Sin), reads=[rc.r()], writes=[cosT.r()])

        qT_t = S("qT_t", [128, 4, 128], BF16); kT_t = S("kT_t", [128, 4, 128], BF16)
        iqT_t = S("iqT_t", [128, 4, 128], BF16); ikT_t = S("ikT_t", [128, 128], BF16)
        qdT_t = S("qdT_t", [128, 2, 128], BF16)
        iw_sb = S("iw_sb", [128, NT, 8], F32)
        kv_t = S("kv_t", [128, 2, 2, 128], BF16); dec_sb = S("dec_sb", [128, 2, 2 * NT], F32)

        xt = [S(f"xt{i}", [128, 1024], F32) for i in range(2)]
        junk = S("junk", [128, 1024], BF16); xn = S("xn", [128, 1024], BF16)
        st = S("st", [128, 1], F32); st2 = S("st2", [128, 1], F32)
        hT = [S(f"hT{i}", [128, 8, 128], BF16) for i in range(2)]
        proj = S("proj", [128, 3672], F32)
        sq = S("sq", [128, 512], F32); ssq = S("ssq", [128, 4], F32); ssq2 = S("ssq2", [128, 4], F32)
        qn = S("qn", [128, 4, 128], F32); qr = S("qr", [128, 4, 128], BF16)
        tA = S("tA", [128, 4, 64], F32); tB = S("tB", [128, 4, 64], F32)
        iqr = S("iqr", [128, 8, 64], BF16); iA = S("iA", [128, 8, 32], F32); iB = S("iB", [128, 8, 32], F32)
        ik2 = S("ik2", [128, 128], BF16); kA = S("kA", [128, 32], F32); kB_ = S("kB_", [128, 32], F32)
        vb = S("vb", [128, 512], BF16); gvb = S("gvb", [128, 512], BF16)
        glT = S("glT", [16, 128], F32); pre = S("pre", [128, 256], F32); lg = S("lg", [128, 256], F32)
        bmid = S("bmid", [128, 256], F32); blast = S("blast", [128, 256], F32)
        d1 = S("d1", [128, 256], F32); d3 = S("d3", [128, 256], F32)
        E1 = S("E1", [128, 256], F32); E2 = S("E2", [128, 256], F32); E3 = S("E3", [128, 256], F32); E4 = S("E4", [128, 256], F32)
        qkd = S("qkd", [128, 3, 256], BF16)
        kdec = S("kdec", [128, 256], BF16)
        qkT = S("qkT", [128, 6, 128], BF16)
        attT = S("attT", [128, 4, 128], BF16)
        oin = S("oin", [128, 512], F32); grs = S("grs", [128, 512], F32)
        dect = S("dect", [128, 2, 2], F32)

        def rope(src, dst, nh, half, cos, sin, A, B, cols_per_head):
            sv = src
            t1 = sv[:, :, 0:half]; t2 = sv[:, :, half:2 * half]
            cb_ = cos.unsqueeze(1).to_broadcast([128, nh, half]); sb_ = sin.unsqueeze(1).to_broadcast([128, nh, half])
            return t1, t2, cb_, sb_

        for i in range(NTL if STAGE >= 3 else 0):
            x_ = xt[i % 2]; h_ = hT[i % 2]
            kb.dma("sp", x_[:], xs[i * 128:(i + 1) * 128, :], writes=[x_.r()])
            emit_norm_hT(kb, x_, h_, a1, modP_sh(modP), identb, junk, xn, pT, st, st2)
            for cb in range(8):
                c0 = cb * 512; c1 = min(3672, c0 + 512); p_ = pp[cb % 2]
                kb.group("pe", [(lambda e, k=k, c0=c0, c1=c1, p_=p_, h_=h_: e.matmul(p_[:, 0:c1 - c0], h_[:, k, :], wb[:, k, c0:c1], start=(k == 0), stop=(k == 7))) for k in range(8)],
                         reads=[h_.r()] + wb_regs, writes=[p_.r()])
                if cb % 2 == 0:
                    kb.op("act", lambda e, c0=c0, c1=c1, p_=p_: e.copy(proj[:, c0:c1], p_[:, 0:c1 - c0]), reads=[p_.r()], writes=[proj.r(("c", cb))])
                else:
                    kb.op("dve", lambda e, c0=c0, c1=c1, p_=p_: e.tensor_copy(proj[:, c0:c1], p_[:, 0:c1 - c0]), reads=[p_.r()], writes=[proj.r(("c", cb))])
            PR = [proj.r(("c", cb)) for cb in range(8)]
            cos_i = cosT[:, i, :]; sin_i = sinT[:, i, :]
            cos32 = cosT[:, i, 0:64:2]; sin32 = sinT[:, i, 0:64:2]

            C1, C2, C3 = [], [], []
            kb.rec = C1
            for (c0, gain, dstT, dstD) in ((1552, qnb, qT_t, o_qT), (2064, knb, kT_t, o_kT)):
                src = proj[:, c0:c0 + 512]
                D(lambda e, src=src: e.tensor_tensor(out=sq[:], in0=src, in1=src, op=ALU.mult), PR, [sq.r()])
                D(lambda e: e.tensor_reduce(out=ssq[:], in_=sq[:].rearrange("p (h d) -> p h d", h=4), axis=AX.X, op=ALU.add), [sq.r()], [ssq.r()])
                emit_rstd(kb, ssq, 128, ssq2)
                s3 = src.rearrange("p (h d) -> p h d", h=4)
                D(lambda e, s3=s3: e.tensor_tensor(out=qn[:], in0=s3, in1=ssq[:].unsqueeze(2).to_broadcast([128, 4, 128]), op=ALU.mult), PR + [ssq.r()], [qn.r()])
                D(lambda e, gain=gain: e.tensor_tensor(out=qn[:], in0=qn[:], in1=gain[:].unsqueeze(1).to_broadcast([128, 4, 128]), op=ALU.mult), [qn.r(), gain.r()], [qn.r()])
                t1 = qn[:, :, 0:64]; t2 = qn[:, :, 64:128]
                cb_ = cos_i.unsqueeze(1).to_broadcast([128, 4, 64]); sb_ = sin_i.unsqueeze(1).to_broadcast([128, 4, 64])
                D(lambda e: e.tensor_tensor(out=tA[:], in0=t1, in1=cb_, op=ALU.mult), [qn.r(), cosT.r()], [tA.r()])
                D(lambda e: e.tensor_tensor(out=tB[:], in0=t2, in1=sb_, op=ALU.mult), [qn.r(), sinT.r()], [tB.r()])
                D(lambda e: e.tensor_tensor(out=qr[:, :, 0:64], in0=tA[:], in1=tB[:], op=ALU.subtract), [tA.r(), tB.r()], [qr.r()])
                D(lambda e: e.tensor_tensor(out=tA[:], in0=t2, in1=cb_, op=ALU.mult), [qn.r(), cosT.r()], [tA.r()])
                D(lambda e: e.tensor_tensor(out=tB[:], in0=t1, in1=sb_, op=ALU.mult), [qn.r(), sinT.r()], [tB.r()])
                D(lambda e: e.tensor_tensor(out=qr[:, :, 64:128], in0=tA[:], in1=tB[:], op=ALU.add), [tA.r(), tB.r()], [qr.r()])
                kb.group("pe", [(lambda e, h=h: e.transpose(pTb[:, h, :], qr[:, h, :], identb[:])) for h in range(4)], reads=[qr.r(), identb.r()], writes=[pTb.r()])
                kb.op("act", lambda e, dstT=dstT: e.copy(dstT[:], pTb[:, 0:4, :]), reads=[pTb.r()], writes=[dstT.r()])
                kb.dma("sp", dstD[:, :, i * 128:(i + 1) * 128], dstT[:], reads=[dstT.r()], is_output=True)
            kb.rec = C2
            D(lambda e: e.tensor_copy(vb[:], proj[:, 2576:3088]), PR, [vb.r()])
            kb.dma("sp", o_v[:, i, :], vb[:], reads=[vb.r()], is_output=True)
            kb.rec = C1
            iq3 = proj[:, 3088:3600].rearrange("p (h d) -> p h d", h=8)
            t1 = iq3[:, :, 0:32]; t2 = iq3[:, :, 32:64]
            cb_ = cos32.unsqueeze(1).to_broadcast([128, 8, 32]); sb_ = sin32.unsqueeze(1).to_broadcast([128, 8, 32])
            D(lambda e: e.tensor_tensor(out=iA[:], in0=t1, in1=cb_, op=ALU.mult), PR + [cosT.r()], [iA.r()])
            D(lambda e: e.tensor_tensor(out=iB[:], in0=t2, in1=sb_, op=ALU.mult), PR + [sinT.r()], [iB.r()])
            D(lambda e: e.tensor_tensor(out=iqr[:, :, 0:32], in0=iA[:], in1=iB[:], op=ALU.subtract), [iA.r(), iB.r()], [iqr.r()])
            D(lambda e: e.tensor_tensor(out=iA[:], in0=t2, in1=cb_, op=ALU.mult), PR + [cosT.r()], [iA.r()])
            D(lambda e: e.tensor_tensor(out=iB[:], in0=t1, in1=sb_, op=ALU.mult), PR + [sinT.r()], [iB.r()])
            D(lambda e: e.tensor_tensor(out=iqr[:, :, 32:64], in0=iA[:], in1=iB[:], op=ALU.add), [iA.r(), iB.r()], [iqr.r()])
            iq2 = iqr[:].rearrange("p h d -> p (h d)")
            kb.group("pe", [(lambda e, g=g: e.transpose(pTb[:, g, :], iq2[:, g * 128:(g + 1) * 128], identb[:])) for g in range(4)], reads=[iqr.r(), identb.r()], writes=[pTb.r()])
            kb.op("act", lambda e: e.copy(iqT_t[:], pTb[:, 0:4, :]), reads=[pTb.r()], writes=[iqT_t.r()])
            kb.dma("sp", o_iqT[:, :, i * 128:(i + 1) * 128], iqT_t[:], reads=[iqT_t.r()], is_output=True)
            kb.rec = C2
            k1 = proj[:, 3600:3632]; k2 = proj[:, 3632:3664]
            D(lambda e: e.tensor_tensor(out=kA[:], in0=k1, in1=cos32, op=ALU.mult), PR + [cosT.r()], [kA.r()])
            D(lambda e: e.tensor_tensor(out=kB_[:], in0=k2, in1=sin32, op=ALU.mult), PR + [sinT.r()], [kB_.r()])
            D(lambda e: e.tensor_tensor(out=ik2[:, 0:32], in0=kA[:], in1=kB_[:], op=ALU.subtract), [kA.r(), kB_.r()], [ik2.r()])
            D(lambda e: e.tensor_tensor(out=kA[:], in0=k2, in1=cos32, op=ALU.mult), PR + [cosT.r()], [kA.r()])
            D(lambda e: e.tensor_tensor(out=kB_[:], in0=k1, in1=sin32, op=ALU.mult), PR + [sinT.r()], [kB_.r()])
            D(lambda e: e.tensor_tensor(out=ik2[:, 32:64], in0=kA[:], in1=kB_[:], op=ALU.add), [kA.r(), kB_.r()], [ik2.r()])
            D(lambda e: e.tensor_copy(ik2[:, 64:128], ik2[:, 0:64]), [ik2.r()], [ik2.r()])
            kb.op("pe", lambda e: e.transpose(pTb[:, 4, :], ik2[:], identb[:]), reads=[ik2.r(), identb.r()], writes=[pTb.r()])
            kb.op("act", lambda e: e.copy(ikT_t[:], pTb[:, 4, :]), reads=[pTb.r()], writes=[ikT_t.r()])
            kb.dma("sp", o_ikT[:, i * 128:(i + 1) * 128], ikT_t[:], reads=[ikT_t.r()], is_output=True)
            D(lambda e: e.tensor_copy(iw_sb[:, i, :], proj[:, 3664:3672]), PR, [iw_sb.r()])
            kb.rec = C3
            kb.op("pe", lambda e: e.transpose(pA[0:16, 0:128], proj[:, 1024:1040], identf[:]), reads=PR + [identf.r()], writes=[pA.r()])
            kb.op("act", lambda e: e.copy(glT[:], pA[0:16, 0:128]), reads=[pA.r()], writes=[glT.r()])
            kb.op("pe", lambda e: e.matmul(pA[:, 0:256], glT[:], gup[:], start=True, stop=True), reads=[glT.r(), gup.r()], writes=[pA.r()])
            D(lambda e: e.tensor_tensor(out=pre[:], in0=pA[:, 0:256], in1=gbb[:], op=ALU.add), [pA.r(), gbb.r()], [pre.r()])
            kb.op("act", lambda e: e.activation(out=lg[:], in_=pre[:], func=AF.Exp, scale=-1.0), reads=[pre.r()], writes=[lg.r()])
            kb.op("act", lambda e: e.activation(out=lg[:], in_=lg[:], func=AF.Ln, bias=1.0), reads=[lg.r()], writes=[lg.r()])
            kb.group("pe", [
                lambda e: e.matmul(pA[:, 0:256], tri3[:, 0, :], lg[:], start=True, stop=True),
                lambda e: e.matmul(pA[:, 256:512], tri3[:, 1, :], lg[:], start=True, stop=True),
                lambda e: e.matmul(pB[:, 0:256], tri3[:, 2, :], lg[:], start=True, stop=True),
                lambda e: e.matmul(pB[:, 256:258], lg[:, 0:128], csel[:], start=True, stop=True),
                lambda e: e.matmul(pB[:, 258:260], lg[:, 128:256], csel[:], start=True, stop=True),
            ], reads=[tri3.r(), lg.r(), csel.r()], writes=[pA.r(), pB.r()])
            kb.op("act", lambda e: e.copy(bmid[:], pA[:, 256:512]), reads=[pA.r()], writes=[bmid.r()])
            kb.op("act", lambda e: e.copy(blast[:], pB[:, 0:256]), reads=[pB.r()], writes=[blast.r()])
            kb.op("act", lambda e: e.activation(out=dec_sb[:, :, 2 * i:2 * i + 2], in_=pB[:, 256:260].rearrange("p (a c) -> p a c", a=2), func=AF.Exp), reads=[pB.r()], writes=[dec_sb.r()])
            D(lambda e: e.tensor_tensor(out=d1[:], in0=pA[:, 0:256], in1=bmid[:], op=ALU.subtract), [pA.r(), bmid.r()], [d1.r()])
            D(lambda e: e.tensor_tensor(out=d3[:], in0=blast[:], in1=pA[:, 0:256], op=ALU.subtract), [pA.r(), blast.r()], [d3.r()])
            kb.op("act", lambda e: e.activation(out=E1[:], in_=d1[:], func=AF.Exp), reads=[d1.r()], writes=[E1.r()])
            kb.op("act", lambda e: e.activation(out=E2[:], in_=d1[:], func=AF.Exp, scale=-1.0), reads=[d1.r()], writes=[E2.r()])
            kb.op("act", lambda e: e.activation(out=E3[:], in_=d3[:], func=AF.Exp), reads=[d3.r()], writes=[E3.r()])
            kb.op("act", lambda e: e.activation(out=E4[:], in_=pA[:, 0:256], func=AF.Exp), reads=[pA.r()], writes=[E4.r()])
            gq = proj[:, 0:256]; gk = proj[:, 256:512]
            D(lambda e: e.scalar_tensor_tensor(out=qkd[:, 0, :], in0=gq, scalar=0.125, in1=E1[:], op0=ALU.mult, op1=ALU.mult), PR + [E1.r()], [qkd.r()])
            D(lambda e: e.tensor_tensor(out=qkd[:, 1, :], in0=gk, in1=E2[:], op=ALU.mult), PR + [E2.r()], [qkd.r()])
            D(lambda e: e.scalar_tensor_tensor(out=qkd[:, 2, :], in0=gq, scalar=0.125, in1=E4[:], op0=ALU.mult, op1=ALU.mult), PR + [E4.r()], [qkd.r()])
            D(lambda e: e.tensor_tensor(out=kdec[:], in0=gk, in1=E3[:], op=ALU.mult), PR + [E3.r()], [kdec.r()])
            kb.group("pe", [(lambda e, w=w, p=p: e.transpose(pTg[:, w * 2 + p, :], qkd[:, w, p * 128:(p + 1) * 128], identb[:])) for w in range(3) for p in range(2)],
                     reads=[qkd.r(), identb.r()], writes=[pTg.r()])
            kb.op("act", lambda e: e.copy(qkT[:], pTg[:, 0:6, :]), reads=[pTg.r()], writes=[qkT.r()])
            D(lambda e: e.tensor_copy(qdT_t[:], qkT[:, 4:6, :]), [qkT.r()], [qdT_t.r()])
            kb.dma("sp", o_qdT[:, :, i * 128:(i + 1) * 128], qdT_t[:], reads=[qdT_t.r()], is_output=True)
            D(lambda e: e.tensor_copy(gvb[:], proj[:, 512:1024]), PR, [gvb.r()])
            def attmm(e, h):
                p, hb = divmod(h, 2); hb *= 64
                dst = pp[0] if hb == 0 else pp[1]
                return e.matmul(dst[:, p * 128:(p + 1) * 128], qkT[hb:hb + 64, 2 + p, :], qkT[hb:hb + 64, 0 + p, :], start=True, stop=True)
            kb.group("pe", [(lambda e, h=h: attmm(e, h)) for h in (0, 2, 1, 3)], reads=[qkT.r()], writes=[pp[0].r(), pp[1].r()])
            for hh in range(2):
                D(lambda e, hh=hh: e.tensor_tensor(out=attT[:, hh:4:2, :], in0=pp[hh][:, 0:256].rearrange("p (h i) -> p h i", h=2), in1=amask[:].unsqueeze(1).to_broadcast([128, 2, 128]), op=ALU.mult),
                  [pp[hh].r(), amask.r()], [attT.r()])
            kb.group("pe", [(lambda e, h=h: e.matmul(pB[:, h * 128:(h + 1) * 128], attT[:, h, :], gvb[:, h * 128:(h + 1) * 128], start=True, stop=True)) for h in range(4)],
                     reads=[attT.r(), gvb.r()], writes=[pB.r()])
            kb.op("act", lambda e: e.copy(oin[:], pB[:]), reads=[pB.r()], writes=[oin.r()])
            kb.dma("sp", o_oin[:, i, :], oin[:], reads=[oin.r()], is_output=True)
            kb.group("pe", [(lambda e, p=p, c=c: e.matmul(pKV[:, c * 2 + p, :], kdec[c * 64:(c + 1) * 64, p * 128:(p + 1) * 128], gvb[c * 64:(c + 1) * 64, p * 256:(p + 1) * 256], start=True, stop=True))
                            for p in range(2) for c in range(2)], reads=[kdec.r(), gvb.r()], writes=[pKV.r()])
            pk = pKV[:].rearrange("q (c p) n -> q p c n", p=2)
            kb.op("act", lambda e: e.copy(kv_t[0:64], pk[0:64, :, :, 0:128]), reads=[pKV.r()], writes=[kv_t.r()])
            D(lambda e: e.tensor_copy(kv_t[64:128], pk[64:128, :, :, 128:256]), [pKV.r()], [kv_t.r()])
            kb.dma("sp", o_kv[:, :, 2 * i:2 * i + 2, :], kv_t[:], reads=[kv_t.r()], is_output=True)
            kb.rec = C2
            kb.op("act", lambda e: e.activation(out=grs[:], in_=proj[:, 1040:1552], func=AF.Silu), reads=PR, writes=[grs.r()])
            kb.dma("sp", o_gr[:, i, :], grs[:], reads=[grs.r()], is_output=True)
            kb.emit_roundrobin([C3, C1, C2])
        for t, d in ((iw_sb, o_iw), (dec_sb, o_dec)):
            kb.dma("sp", d, t[:], reads=[t.r()], is_output=True)
        kb.finish()
        print("L1 instructions:", kb.nins, {k: len(v) for k, v in kb.q.items()})
    return nc


class _ShView:
    pass


def modP_sh(modP):
    class V:
        def __getitem__(s, idx):
            return modP[idx]
        def r(s, key=None):
            return modP.r(key)
    return V()


def l1_consts():
    half = 64
    invf = (10000.0 ** (-np.arange(half, dtype=np.float32) / half)).astype(np.float32)
    j = np.arange(128)[:, None]; i = np.arange(128)[None, :]
    same = (j // 64) == (i // 64)
    tri = (same & (j <= i)).astype(np.float32)
    mmid = (same & ((j % 64) <= 31)).astype(np.float32)
    mlast = same.astype(np.float32)
    tri3 = np.stack([tri, mmid, mlast], axis=1) * (-1.0 / 16.0)
    csel = np.stack([(np.arange(128) < 64), (np.arange(128) >= 64)], axis=1).astype(np.float32) * (-1.0 / 16.0)
    amask = tri.copy()
    return dict(invf=invf, identb=np.eye(128, dtype=np.float32).astype(ml_dtypes.bfloat16), identf=np.eye(128, dtype=np.float32),
                tri3=np.ascontiguousarray(tri3.astype(np.float32)), csel=csel, amask=amask)


NIT = 10
C0 = 11.3137085
SCALE = float(128 ** -0.5)


def emit_skewed(its, nst):
    n = len(its)
    for step in range(n + nst - 1):
        for stg in range(nst):
            k = step - stg
            if 0 <= k < n:
                its[k][stg]()


def build_dsa(QT=tuple(range(16))):
    nc = _get_nc()
    qT_d = dram_in(nc, "qT", [128, 4, 2048], BF16)
    iqT_d = dram_in(nc, "iqT", [128, 4, 2048], BF16)
    iw_d = dram_in(nc, "iw", [128, 16, 8], F32)
    kT_d = dram_in(nc, "kT", [2, 4, 64, 8192], BF16)
    v_d = dram_in(nc, "v", [2, 4, 64, 8192], BF16)
    ikT_d = dram_in(nc, "ikT", [1, 4, 128, 2048], BF16)
    mneg_d = dram_in(nc, "mneg", [128, 512], F32)
    mpos_d = dram_in(nc, "mpos", [128, 512], F32)
    identb_d = dram_in(nc, "identb", [128, 128], BF16)
    identf_d = dram_in(nc, "identf", [128, 128], F32)
    pow2_d = dram_in(nc, "pow2", [128, NIT + 1], F32)
    o_dsa = dram_out(nc, "o_dsa", [128, 16, 512], F32)
    with _phase(nc) as kb:
        S = lambda n, s, d: sb(kb, n, s, d)
        D = lambda fn, r, w: kb.op("dve", fn, reads=r, writes=w)
        A = lambda fn, r, w: kb.op("act", fn, reads=r, writes=w)
        v = S("v", [128, 64, 512], BF16); ikT = S("ikT", [128, 8192], BF16)
        kTb = [S(f"kTb{i}", [128, 512], BF16) for i in range(3)]
        for jj in range(4):
            for q in range(2):
                kb.dma("sp", v[q * 64:(q + 1) * 64].rearrange("p (i j) c -> p i j c", j=4)[:, :, jj, :], v_d[q, jj].rearrange("p (i c) -> p i c", c=512), writes=[v.r(("g", jj))])
        for jj in range(4):
            kb.dma("sp", ikT[:].rearrange("p (i j s) -> p i j s", j=4, s=128)[:, :, jj, :], ikT_d[0, jj].rearrange("p (i s) -> p i s", s=128), writes=[ikT.r()])
        VV = [v.r(("g", g)) for g in range(8)]
        iw = S("iw", [128, 16, 8], F32); mneg = S("mneg", [128, 512], F32); mpos = S("mpos", [128, 512], F32)
        identb = S("identb", [128, 128], BF16); identf = S("identf", [128, 128], F32); pow2 = S("pow2", [128, NIT + 1], F32)
        for t, d in ((iw, iw_d), (mneg, mneg_d), (mpos, mpos_d), (identb, identb_d), (identf, identf_d), (pow2, pow2_d)):
            kb.dma("sp", t[:], d, writes=[t.r()])
        B = [ps(kb, f"B{i}", [128, 512], F32) for i in range(4)] + [None, None] + [ps(kb, f"B{i}", [128, 512], F32) for i in (6, 7)]
        X2 = ps(kb, "X2", [128, 2, 512], F32)
        qts = [S(f"qt{i}", [128, 4, 128], BF16) for i in range(2)]; iqts = [S(f"iqt{i}", [128, 4, 128], BF16) for i in range(2)]
        diagw = S("diagw", [128, 8, 128], BF16)
        Rt2 = [S(f"Rt2_{i}", [128, 2, 512], BF16) for i in range(2)]
        scores = [S(f"score{i}", [128, 8192], F32) for i in range(2)]
        masks = [S(f"maskb{i}", [128, 8192], BF16) for i in range(2)]
        tmp = S("tmp", [128, 512], F32)
        sts = [S(f"st{i}", [128, 8], F32) for i in range(2)]
        Hh = S("Hh", [128, NIT + 1], F32)
        lo_t = S("lo_t", [128, 1], F32); mid_t = S("mid_t", [128, 1], F32); cnt_t = S("cnt_t", [128, 1], F32); g_t = S("g_t", [128, 1], F32)
        Eb = [S(f"Eb{i}", [128, 512], BF16) for i in range(2)]
        Pb = [S(f"Pb{i}", [128, 512], BF16) for i in range(2)]
        PT = [S(f"PT{i}", [128, 4, 128], BF16) for i in range(2)]
        rs = S("rs", [128, 4, 16], F32); rsum = S("rsum", [128, 4], F32); rinv = S("rinv", [128, 4], F32)
        osb = S("osb", [128, 512], F32)
        Sbanks = (B[0], B[1], B[7]); Ob = B[2]; pTv = B[3][:].bitcast(BF16); SC = B[6]
        cstate = [0]

        def phaseI(i, par):
            iqt = iqts[par]; score = scores[par]; st = sts[par]
            kb.dma("sp", iqt[:], iqT_d[:, :, i * 128:(i + 1) * 128], writes=[iqt.r()])
            for h in range(8):
                A(lambda e, h=h: e.activation(out=diagw[:, h, :], in_=identf[:], func=AF.Copy, scale=iw[:, i, h:h + 1]), [identf.r(), iw.r()], [diagw.r()])
            yield
            for m in range(i + 1):
                ks = slice(m * 512, (m + 1) * 512)
                for p in range(4):
                    kb.group("pe", [lambda e, p=p: e.matmul(X2[:, 0, :], iqt[0:64, p, :], ikT[0:64, ks], start=True, stop=True),
                                    lambda e, p=p: e.matmul(X2[:, 1, :], iqt[64:128, p, :], ikT[64:128, ks], start=True, stop=True)],
                             reads=[iqt.r(), ikT.r()], writes=[X2.r()])
                    R_ = Rt2[p % 2]
                    A(lambda e, R_=R_: e.activation(out=R_[:].rearrange("p a b -> p (a b)"), in_=X2[:].rearrange("p a b -> p (a b)"), func=AF.Relu), [X2.r()], [R_.r(("h", 0)), R_.r(("h", 1))])
                    kb.group("pe", [lambda e, p=p, R_=R_: e.matmul(SC[:], diagw[:, 2 * p, :], R_[:, 0, :], start=(p == 0), stop=False),
                                    lambda e, p=p, R_=R_: e.matmul(SC[:], diagw[:, 2 * p + 1, :], R_[:, 1, :], start=False, stop=(p == 3))],
                             reads=[diagw.r(), R_.r(("h", 0)), R_.r(("h", 1))], writes=[SC.r()])
                    yield
                if m < i:
                    A(lambda e: e.copy(score[:, ks], SC[:]), [SC.r()], [score.r(("m", m))])
                else:
                    D(lambda e: e.tensor_tensor(out=score[:, ks], in0=SC[:], in1=mneg[:], op=ALU.add), [SC.r(), mneg.r()], [score.r(("m", m))])
                    D(lambda e: e.tensor_tensor(out=tmp[:], in0=SC[:], in1=mpos[:], op=ALU.add), [SC.r(), mpos.r()], [tmp.r()])
                    D(lambda e: e.tensor_reduce(out=st[:, 1:2], in_=tmp[:], axis=AX.X, op=ALU.min), [tmp.r()], [st.r()])
                yield

        def phaseII(i, par):
            score = scores[par]; st = sts[par]; maskb = masks[par]
            SCR = [score.r(("m", m)) for m in range(i + 1)]
            W = (i + 1) * 512
            if i > 0:
                D(lambda e: e.tensor_reduce(out=st[:, 0:1], in_=score[:, 0:i * 512], axis=AX.X, op=ALU.min), SCR, [st.r()])
                D(lambda e: e.tensor_tensor(out=st[:, 2:3], in0=st[:, 0:1], in1=st[:, 1:2], op=ALU.min), [st.r()], [st.r()])
            else:
                D(lambda e: e.tensor_copy(st[:, 2:3], st[:, 1:2]), [st.r()], [st.r()])
            yield
            D(lambda e: e.tensor_reduce(out=st[:, 3:4], in_=score[:, 0:W], axis=AX.X, op=ALU.max), SCR, [st.r()])
            D(lambda e: e.tensor_tensor(out=st[:, 4:5], in0=st[:, 3:4], in1=st[:, 2:3], op=ALU.subtract), [st.r()], [st.r()])
            yield
            D(lambda e: e.tensor_scalar(out=Hh[:], in0=pow2[:], scalar1=st[:, 4:5], scalar2=None, op0=ALU.mult), [pow2.r(), st.r()], [Hh.r()])
            D(lambda e: e.tensor_copy(lo_t[:], st[:, 2:3]), [st.r()], [lo_t.r()])
            D(lambda e: e.tensor_tensor(out=mid_t[:], in0=st[:, 2:3], in1=Hh[:, 0:1], op=ALU.add), [st.r(), Hh.r()], [mid_t.r()])
            yield
            for k in range(NIT):
                D(lambda e: e.tensor_scalar(out=maskb[:, 0:W], in0=score[:, 0:W], scalar1=mid_t[:, 0:1], scalar2=None, op0=ALU.is_ge, op1=ALU.add, accum_out=cnt_t[:, 0:1]),
                  SCR + [mid_t.r()], [maskb.r(), cnt_t.r()])
                yield
                D(lambda e, k=k: e.tensor_scalar(out=g_t[:], in0=cnt_t[:], scalar1=255.5, scalar2=Hh[:, k:k + 1], op0=ALU.is_ge, op1=ALU.mult), [cnt_t.r(), Hh.r()], [g_t.r()])
                yield
                D(lambda e, k=k: e.scalar_tensor_tensor(out=mid_t[:], in0=g_t[:], scalar=lo_t[:, 0:1], in1=Hh[:, k + 1:k + 2], op0=ALU.add, op1=ALU.add), [g_t.r(), lo_t.r(), Hh.r()], [mid_t.r()])
                D(lambda e: e.tensor_tensor(out=lo_t[:], in0=lo_t[:], in1=g_t[:], op=ALU.add), [lo_t.r(), g_t.r()], [lo_t.r()])
                yield
            D(lambda e: e.tensor_scalar(out=maskb[:, 0:W], in0=score[:, 0:W], scalar1=lo_t[:, 0:1], scalar2=None, op0=ALU.is_ge), SCR + [lo_t.r()], [maskb.r()])
            yield

        def phaseIII(i, par):
            qt = qts[par]; maskb = masks[par]
            kb.dma("sp", qt[:], qT_d[:, :, i * 128:(i + 1) * 128], writes=[qt.r()])

            def make_it3(m, h, c3):
                ks = slice(m * 512, (m + 1) * 512)
                Sb = Sbanks[c3 % 3]; E_ = Eb[c3 % 2]; P_ = Pb[c3 % 2]; PT_ = PT[c3 % 2]; kt_ = kTb[c3 % 3]
                first = (m == 0 and h == 0)
                hc = slice(h * 128, (h + 1) * 128)

                def S1():
                    for q in range(2):
                        kb.dma("sp", kt_[q * 64:(q + 1) * 64].rearrange("p (j s) -> p j s", j=4), kT_d[q].rearrange("j p (h i s) -> p j h i s", h=4, s=128)[:, :, h, m, :], writes=[kt_.r()])
                    kb.op("pe", lambda e: e.matmul(Sb[:], qt[:, h, :], kt_[:], start=True, stop=True), reads=[qt.r(), kt_.r()], writes=[Sb.r()])
                    A(lambda e: e.activation(out=E_[:], in_=Sb[:], func=AF.Exp, scale=SCALE, bias=-C0), [Sb.r()], [E_.r()])

                def S2():
                    D(lambda e: e.scalar_tensor_tensor(out=P_[:], in0=maskb[:, ks], scalar=1.0, in1=E_[:], op0=ALU.mult, op1=ALU.mult, accum_out=rs[:, h, m:m + 1]),
                      [maskb.r(), E_.r()], [P_.r(), rs.r()])
                    kb.group("pe", [(lambda e, x=x: e.transpose(pTv[:, x * 128:(x + 1) * 128], P_[:, x * 128:(x + 1) * 128], identb[:])) for x in range(4)],
                             reads=[P_.r(), identb.r()], writes=[B[3].r()])
                    A(lambda e: e.copy(PT_[:].rearrange("p a b -> p (a b)"), pTv[:, 0:512]), [B[3].r()], [PT_.r()])

                def S3():
                    kb.group("pe", [(lambda e, x=x: e.matmul(Ob[:, hc], PT_[:, x, :], v[:, m * 4 + x, h * 128:(h + 1) * 128], start=(first and x == 0), stop=(m == i and h == 3 and x == 3))) for x in range(4)],
                             reads=[PT_.r()] + VV, writes=[Ob.r(("h", h))] + ([Ob.r(("h", hh)) for hh in range(4)] if first else []))
                return (S1, S2, S3)

            its3 = []
            for m in range(i + 1):
                for h in range(4):
                    its3.append(make_it3(m, h, cstate[0]))
                    cstate[0] += 1
            n = len(its3)
            for step in range(n + 2):
                for stg in range(3):
                    k = step - stg
                    if 0 <= k < n:
                        its3[k][stg]()
                yield
            D(lambda e: e.tensor_reduce(out=rsum[:], in_=rs[:, :, 0:i + 1], axis=AX.X, op=ALU.add), [rs.r()], [rsum.r()])
            D(lambda e: e.reciprocal(rinv[:], rsum[:]), [rsum.r()], [rinv.r()])
            D(lambda e: e.tensor_tensor(out=osb[:].rearrange("p (h d) -> p h d", h=4), in0=Ob[:].rearrange("p (h d) -> p h d", h=4), in1=rinv[:].unsqueeze(2).to_broadcast([128, 4, 128]), op=ALU.mult),
              [Ob.r(("h", hh)) for hh in range(4)] + [rinv.r()], [osb.r()])
            kb.dma("sp", o_dsa[:, i, :], osb[:], reads=[osb.r()], is_output=True)
            yield

        def run_interleaved(gens):
            lists = []
            for g in gens:
                lists.append(g)
            active = [[g, w, 0.0] for g, w in lists]
            total = max(w for _, w in lists)
            for stepi in range(total):
                for a in active:
                    g, w, acc = a
                    a[2] += w / total
                    while a[2] >= 1.0:
                        a[2] -= 1.0
                        try:
                            next(g)
                        except StopIteration:
                            a[2] = -1e9
            for a in active:
                for _ in a[0]:
                    pass

        def est_I(i):
            return 1 + (i + 1) * 5

        def est_II(i):
            return 3 + NIT * 3 + 1

        def est_III(i):
            return 4 * (i + 1) + 3

        QL = list(QT)
        nq = len(QL)

        class Stream:
            def __init__(self, mk, est, can_start):
                self.mk = mk; self.est = est; self.can_start = can_start
                self.t = 0; self.gen = None; self.done = 0; self.acc = 0.0

            def finished(self):
                return self.t >= nq and self.gen is None

            def step(self):
                if self.gen is None:
                    if self.t >= nq or not self.can_start(self.t):
                        return False
                    self.gen = self.mk(QL[self.t], self.t % 2)
                try:
                    next(self.gen)
                except StopIteration:
                    self.gen = None
                    self.t += 1
                    self.done += 1
                return True

        sI = Stream(phaseI, est_I, lambda t: t < 2 or sII.done >= t - 1)
        sII = Stream(phaseII, est_II, lambda t: sI.done >= t + 1 and (t < 2 or sIII.done >= t - 1))
        sIII = Stream(phaseIII, est_III, lambda t: sII.done >= t + 1)
        streams = [sII, sIII, sI]
        guard = 0
        while not all(st_.finished() for st_ in streams):
            progressed = False
            ref = float(est_II(0))
            for st_ in streams:
                if st_.finished():
                    continue
                tt = min(st_.t, nq - 1)
                st_.acc += st_.est(QL[tt]) / ref
                while st_.acc >= 1.0:
                    st_.acc -= 1.0
                    if st_.step():
                        progressed = True
                    else:
                        st_.acc = 0.0
                        break
            if not progressed:
                for st_ in streams:
                    if not st_.finished() and st_.step():
                        progressed = True
                        break
            guard += 1
            assert progressed and guard < 200000, "DSA stream scheduler stuck"
        kb.finish()
        print("DSA instructions:", kb.nins, {k: len(v) for k, v in kb.q.items()})
    return nc


def dsa_masks(j):
    q = np.arange(128)[:, None]
    col = np.arange(512)[None, :]
    jj = col // 128; s = col % 128
    vis = (jj < j) | ((jj == j) & (s <= q))
    mneg = np.where(vis, 0.0, -1e30).astype(np.float32)
    mpos = np.where(vis, 0.0, 1e30).astype(np.float32)
    return mneg, mpos


def gather_global(per_core, axis_tok_tiles):
    st = np.stack(per_core, axis=axis_tok_tiles + 1)
    sh = list(st.shape)
    sh[axis_tok_tiles:axis_tok_tiles + 2] = [64]
    return st.reshape(sh)


def build_gla(NB=16):
    nc = _get_nc()
    kv_d = dram_in(nc, "kv", [2, 4, 64, 8192], BF16)
    dec_d = dram_in(nc, "dec", [1, 4, 128, 64], F32)
    qd_d = dram_in(nc, "qdT", [128, 2, 2048], BF16)
    oin_d = dram_in(nc, "oin", [128, 16, 512], F32)
    grs_d = dram_in(nc, "grs", [128, 16, 512], F32)
    gsel_d = dram_in(nc, "gsel", [128, 2, 8], F32)
    gn_d = dram_in(nc, "gnorm", [128], F32)
    o_gla = dram_out(nc, "o_gla", [128, 16, 512], F32)
    with _phase(nc) as kb:
        S_ = lambda n, s, d: sb(kb, n, s, d)
        D = lambda fn, r, w: kb.op("dve", fn, reads=r, writes=w)
        A = lambda fn, r, w: kb.op("act", fn, reads=r, writes=w)
        dec = S_("dec", [128, 2, 128], F32); gsel = S_("gsel", [128, 2, 8], F32); gnb = S_("gnb", [128, 128], F32)
        for jj in range(4):
            kb.dma("sp", dec[:].rearrange("p a (i j c) -> p a i j c", j=4, c=2)[:, :, :, jj, :], dec_d[0, jj].rearrange("p (a i c) -> p a i c", a=2, c=2), writes=[dec.r()])
        kb.dma("sp", gsel[:], gsel_d, writes=[gsel.r()])
        kb.dma("sp", gnb[:], gn_d.partition_broadcast(128), writes=[gnb.r()])
        St = S_("St", [128, 2, 128], F32); Ssels = [S_(f"Ssel{i}", [128, 2, 256], F32) for i in range(2)]; Sselb = S_("Sselb", [128, 2, 2, 128], BF16)
        kvb = [S_(f"kvb{i}", [128, 4, 2, 2, 128], BF16) for i in range(2)]
        qd = S_("qd", [128, 2, 128], BF16); oin = S_("oin", [128, 512], F32); grs = S_("grs", [128, 512], F32)
        og = S_("og", [128, 512], F32); sq = S_("sq", [128, 512], F32); ssq = S_("ssq", [128, 4], F32); ssq2 = S_("ssq2", [128, 4], F32)
        BA = ps(kb, "BA", [128, 512], F32); BB = ps(kb, "BB", [128, 512], F32)
        D(lambda e: e.memset(St[:], 0.0), [], [St.r(("p", 0)), St.r(("p", 1))])
        Sflat = St[:].rearrange("p a b -> p (a b)")

        def scan(i):
            Ssel = Ssels[i % 2]
            D(lambda e: e.memset(Ssel[:], 0.0), [], [Ssel.r(("c", 0)), Ssel.r(("c", 1))])
            kvb_ = kvb[i % 2]
            for jj in range(4):
                for q in range(2):
                    kb.dma("sp", kvb_[q * 64:(q + 1) * 64, jj], kv_d[q, jj].rearrange("p (a n d) -> p a n d", a=2, d=128)[:, :, 2 * i:2 * i + 2, :], writes=[kvb_.r()])
            for k in range(8):
                n = 8 * i + k
                for c in range(2):
                    D(lambda e, c=c, k=k: e.scalar_tensor_tensor(out=Ssel[:, c, :], in0=Sflat, scalar=gsel[:, c, k:k + 1], in1=Ssel[:, c, :], op0=ALU.mult, op1=ALU.add),
                      [St.r(("p", 0)), St.r(("p", 1)), gsel.r(), Ssel.r(("c", c))], [Ssel.r(("c", c))])
                for p in range(2):
                    D(lambda e, p=p, n=n, k=k: e.scalar_tensor_tensor(out=St[:, p, :], in0=St[:, p, :], scalar=dec[:, p, n:n + 1], in1=kvb_[:, k // 2, p, k % 2, :], op0=ALU.mult, op1=ALU.add),
                      [St.r(("p", p)), dec.r(), kvb_.r()], [St.r(("p", p))])
                yield

        def epilogue(i):
            Ssel = Ssels[i % 2]
            A(lambda e: e.copy(Sselb[:].rearrange("p c a b -> p (c a b)"), Ssel[:].rearrange("p c x -> p (c x)")), [Ssel.r(("c", 0)), Ssel.r(("c", 1))], [Sselb.r()])
            kb.dma("sp", qd[:], qd_d[:, :, i * 128:(i + 1) * 128], writes=[qd.r()])
            kb.dma("sp", oin[:], oin_d[:, i, :], writes=[oin.r()])
            kb.dma("sp", grs[:], grs_d[:, i, :], writes=[grs.r()])
            yield
            fns = []
            for c in range(2):
                for p in range(2):
                    for half in range(2):
                        bank = BA if half == 0 else BB
                        fns.append(lambda e, c=c, p=p, half=half, bank=bank: e.matmul(bank[:, (c * 2 + p) * 128:(c * 2 + p + 1) * 128], qd[half * 64:(half + 1) * 64, p, :], Sselb[half * 64:(half + 1) * 64, c, p, :], start=True, stop=True))
            kb.group("pe", fns, reads=[qd.r(), Sselb.r()], writes=[BA.r(), BB.r()])
            yield
            for h in range(4):
                p, half = divmod(h, 2)
                bank = BA if half == 0 else BB
                for c in range(2):
                    rows = slice(c * 64, (c + 1) * 64)
                    D(lambda e, h=h, c=c, p=p, bank=bank, rows=rows: e.tensor_tensor(out=og[rows, h * 128:(h + 1) * 128], in0=bank[rows, (c * 2 + p) * 128:(c * 2 + p + 1) * 128], in1=oin[rows, h * 128:(h + 1) * 128], op=ALU.add),
                      [bank.r(), oin.r()], [og.r(("h", h))])
                yield
            OG = [og.r(("h", h)) for h in range(4)]
            D(lambda e: e.tensor_tensor(out=sq[:], in0=og[:], in1=og[:], op=ALU.mult), OG, [sq.r()])
            yield
            D(lambda e: e.tensor_reduce(out=ssq[:], in_=sq[:].rearrange("p (h d) -> p h d", h=4), axis=AX.X, op=ALU.add), [sq.r()], [ssq.r()])
            yield
            kb.op("dve", lambda e: e.tensor_scalar(out=ssq2[:], in0=ssq[:], scalar1=1.0 / 128, scalar2=EPS, op0=ALU.mult, op1=ALU.add), reads=[ssq.r()], writes=[ssq2.r()])
            yield
            kb.op("act", lambda e: e.activation(out=ssq2[:], in_=ssq2[:], func=AF.Sqrt), reads=[ssq2.r()], writes=[ssq2.r()])
            yield
            kb.op("dve", lambda e: e.reciprocal(ssq[:], ssq2[:]), reads=[ssq2.r()], writes=[ssq.r()])
            yield
            og3 = og[:].rearrange("p (h d) -> p h d", h=4)
            D(lambda e: e.tensor_tensor(out=og3, in0=og3, in1=ssq[:].unsqueeze(2).to_broadcast([128, 4, 128]), op=ALU.mult), OG + [ssq.r()], OG)
            yield
            D(lambda e: e.tensor_tensor(out=og3, in0=og3, in1=gnb[:].unsqueeze(1).to_broadcast([128, 4, 128]), op=ALU.mult), OG + [gnb.r()], OG)
            yield
            D(lambda e: e.tensor_tensor(out=og[:], in0=og[:], in1=grs[:], op=ALU.mult), OG + [grs.r()], OG)
            kb.dma("sp", o_gla[:, i, :], og[:], reads=OG, is_output=True)
            yield

        for _ in scan(0):
            pass
        for i in range(NB):
            ep = epilogue(i)
            if i + 1 < NB:
                for _ in scan(i + 1):
                    for _n in range(2):
                        try:
                            next(ep)
                        except StopIteration:
                            break
            for _ in ep:
                pass
        kb.finish()
        print("GLA instructions:", kb.nins, {k: len(v) for k, v in kb.q.items()})
    return nc


def gla_sel(j):
    g = np.zeros((128, 2, 8), np.float32)
    for c in range(2):
        g[:, c, 2 * j + c] = 1.0
    return g


NT = 16
POOL_W = (2, 4, 8, 16)


def build_l1b():
    nc = _get_nc()
    xs = dram_in(nc, "xs", [NT * 128, 1024], F32)
    cT_d = dram_in(nc, "cT", [128, 8], F32)
    ada_w = dram_in(nc, "ada_w", [1024, 6144], F32)
    ada_b = dram_in(nc, "ada_b", [1, 6144], F32)
    mixn_d = dram_in(nc, "mixn", [128, 8], F32)
    w_in = dram_in(nc, "w_in", [1024, 2048], F32)
    qn_d = dram_in(nc, "q_norm", [128], F32)
    kn_d = dram_in(nc, "k_norm", [128], F32)
    identb_d = dram_in(nc, "identb", [128, 128], BF16)
    o_mod = dram_out(nc, "o_mod", [1, 6144], F32)
    o_qT = dram_out(nc, "o_qT", [128, 4, NT * 128], BF16)
    o_kT = dram_out(nc, "o_kT", [128, 4, NT * 128], BF16)
    o_v = dram_out(nc, "o_v", [128, NT, 512], BF16)
    o_u = dram_out(nc, "o_u", [128, NT, 512], F32)
    o_uh = dram_out(nc, "o_uh", [256, 512], F32)
    with _phase(nc) as kb:
        S = lambda n, s, d: sb(kb, n, s, d)
        D = lambda fn, r, w: kb.op("dve", fn, reads=r, writes=w)
        A = lambda fn, r, w: kb.op("act", fn, reads=r, writes=w)
        identb = S("identb", [128, 128], BF16); mixn = S("mixn", [128, 8], F32)
        qnb = S("qnb", [128, 128], F32); knb = S("knb", [128, 128], F32)
        for t, d in ((identb, identb_d), (mixn, mixn_d)):
            kb.dma("sp", t[:], d, writes=[t.r()])
        for t, d in ((qnb, qn_d), (knb, kn_d)):
            kb.dma("sp", t[:], d.partition_broadcast(128), writes=[t.r()])
        wb = S("wb", [128, 8, 2048], BF16)
        for k in range(8):
            kb.dma("pool", wb[:, k, :], w_in[k * 128:(k + 1) * 128, :], writes=[wb.r(("k", k))])
        WB = [wb.r(("k", k)) for k in range(8)]
        modrow = S("modrow", [1, 6144], F32); modP = S("modP", [128, 48], F32)
        pT = ps(kb, "pT", [128, 8, 128], BF16)
        pp = [ps(kb, f"pp{i}", [128, 512], F32) for i in range(2)]
        pA = ps(kb, "pA", [128, 512], F32); pB = ps(kb, "pB", [128, 512], F32)
        emit_mod(kb, cT_d, ada_w, ada_b, modrow, modP, "m1", pA, pB)
        kb.dma("sp", o_mod, modrow[:], reads=[modrow.r()], is_output=True)
        a1 = S("a1", [128, 8], F32)
        D(lambda e: e.scalar_tensor_tensor(out=a1[:], in0=modP[:, 8:16], scalar=1.0, in1=mixn[:], op0=ALU.add, op1=ALU.mult), [modP.r(), mixn.r()], [a1.r()])
        xt = [S(f"xt{i}", [128, 1024], F32) for i in range(2)]
        junk = S("junk", [128, 1024], BF16); xn = S("xn", [128, 1024], BF16)
        st = S("st", [128, 1], F32); st2 = S("st2", [128, 1], F32)
        hT = [S(f"hT{i}", [128, 8, 128], BF16) for i in range(2)]
        proj = S("proj", [128, 2048], F32)
        sq = S("sq", [128, 512], F32); ssq = S("ssq", [128, 4], F32); ssq2 = S("ssq2", [128, 4], F32)
        qn = S("qn", [128, 4, 128], F32); qr = S("qr", [128, 4, 128], BF16)
        sqb = S("sqb", [128, 512], F32); ssqb = S("ssqb", [128, 4], F32); ssq2b = S("ssq2b", [128, 4], F32)
        qnb2 = S("qnb2", [128, 4, 128], F32); qrb = S("qrb", [128, 4, 128], BF16)
        TMP = [(sq, ssq, ssq2, qn, qr), (sqb, ssqb, ssq2b, qnb2, qrb)]
        qT_t = S("qT_t", [128, 4, 128], BF16); kT_t = S("kT_t", [128, 4, 128], BF16)
        vb = S("vb", [128, 512], BF16)
        for i in range(NT):
            x_ = xt[i % 2]; h_ = hT[i % 2]
            kb.dma("sp", x_[:], xs[i * 128:(i + 1) * 128, :], writes=[x_.r()])
            emit_norm_hT(kb, x_, h_, a1, modP_sh(modP), identb, junk, xn, pT, st, st2)
            for cb in range(4):
                p_ = pp[cb % 2]
                kb.group("pe", [(lambda e, k=k, cb=cb, p_=p_, h_=h_: e.matmul(p_[:], h_[:, k, :], wb[:, k, cb * 512:(cb + 1) * 512], start=(k == 0), stop=(k == 7))) for k in range(8)],
                         reads=[h_.r()] + WB, writes=[p_.r()])
                if cb % 2 == 0:
                    A(lambda e, cb=cb, p_=p_: e.copy(proj[:, cb * 512:(cb + 1) * 512], p_[:]), [p_.r()], [proj.r(("c", cb))])
                else:
                    D(lambda e, cb=cb, p_=p_: e.tensor_copy(proj[:, cb * 512:(cb + 1) * 512], p_[:]), [p_.r()], [proj.r(("c", cb))])
            PR = [proj.r(("c", cb)) for cb in range(4)]
            CH = [[], [], []]
            for ci, (c0, gain, dstT, dstD) in enumerate(((512, qnb, qT_t, o_qT), (1024, knb, kT_t, o_kT))):
                kb.rec = CH[ci]
                sq_, ssq_, ssq2_, qn_, qr_ = TMP[ci]
                src = proj[:, c0:c0 + 512]
                D(lambda e, src=src, sq_=sq_: e.tensor_tensor(out=sq_[:], in0=src, in1=src, op=ALU.mult), PR, [sq_.r()])
                D(lambda e, sq_=sq_, ssq_=ssq_: e.tensor_reduce(out=ssq_[:], in_=sq_[:].rearrange("p (h d) -> p h d", h=4), axis=AX.X, op=ALU.add), [sq_.r()], [ssq_.r()])
                emit_rstd(kb, ssq_, 128, ssq2_)
                s3 = src.rearrange("p (h d) -> p h d", h=4)
                D(lambda e, s3=s3, qn_=qn_, ssq_=ssq_: e.tensor_tensor(out=qn_[:], in0=s3, in1=ssq_[:].unsqueeze(2).to_broadcast([128, 4, 128]), op=ALU.mult), PR + [ssq_.r()], [qn_.r()])
                D(lambda e, gain=gain, qn_=qn_, qr_=qr_: e.tensor_tensor(out=qr_[:], in0=qn_[:], in1=gain[:].unsqueeze(1).to_broadcast([128, 4, 128]), op=ALU.mult), [qn_.r(), gain.r()], [qr_.r()])
                kb.group("pe", [(lambda e, h=h, qr_=qr_, ci=ci: e.transpose(pT[:, 4 * ci + h, :], qr_[:, h, :], identb[:])) for h in range(4)], reads=[qr_.r(), identb.r()], writes=[pT.r()])
                A(lambda e, dstT=dstT, ci=ci: e.copy(dstT[:], pT[:, 4 * ci:4 * ci + 4, :]), [pT.r()], [dstT.r()])
                kb.dma("sp", dstD[:, :, i * 128:(i + 1) * 128], dstT[:], reads=[dstT.r()], is_output=True)
            kb.rec = CH[2]
            D(lambda e: e.tensor_copy(vb[:], proj[:, 1536:2048]), PR, [vb.r()])
            kb.dma("sp", o_v[:, i, :], vb[:], reads=[vb.r()], is_output=True)
            kb.dma("sp", o_u[:, i, :], proj[:, 0:512], reads=PR, is_output=True)
            kb.dma("sp", o_uh[i * 16:(i + 1) * 16, :], proj[112:128, 0:512], reads=PR, is_output=True)
            kb.emit_roundrobin(CH)
        kb.finish()
        print("L1b instructions:", kb.nins, {k: len(v) for k, v in kb.q.items()})
    return nc


def build_pool():
    nc = _get_nc()
    u_d = dram_in(nc, "u", [128, NT, 512], F32)
    guh_d = dram_in(nc, "guh", [1, 4, 256, 512], F32)
    hsel_d = dram_in(nc, "hsel", [128, 4], F32)
    band_d = dram_in(nc, "band", [128, 4, 128], F32)
    band0_d = dram_in(nc, "band0", [128, 4, 128], F32)
    bandhc_d = dram_in(nc, "bandhc", [128, 2, NT, 4, 128], F32)
    pw_d = dram_in(nc, "pool_w", [4, 128, 128], F32)
    psc_d = dram_in(nc, "pool_scale", [512], F32)
    o_pool = dram_out(nc, "o_pool", [128, NT, 512], F32)
    with _phase(nc) as kb:
        S = lambda n, s, d: sb(kb, n, s, d)
        D = lambda fn, r, w: kb.op("dve", fn, reads=r, writes=w)
        A = lambda fn, r, w: kb.op("act", fn, reads=r, writes=w)
        hsel = S("hsel", [128, 4], F32); band = S("band", [128, 4, 128], F32); band0 = S("band0", [128, 4, 128], F32)
        pscb = S("pscb", [128, 512], F32); pw = S("pw", [128, 4, 128], BF16)
        for t, d in ((hsel, hsel_d), (band, band_d), (band0, band0_d)):
            kb.dma("sp", t[:], d, writes=[t.r()])
        kb.dma("sp", pscb[:], psc_d.partition_broadcast(128), writes=[pscb.r()])
        kb.dma("pool", pw[:], pw_d.rearrange("g c d -> c g d"), writes=[pw.r()])
        uhc = S("uhc", [128, 4, 2, 512], F32); uhs = S("uhs", [128, 2, 512], F32)
        for jj in range(4):
            for hh in range(2):
                kb.dma("sp", uhc[:, jj, hh, :], guh_d[0, jj, hh * 128:(hh + 1) * 128, :], writes=[uhc.r()])
        uc = uhc[:].rearrange("p j h c -> p j (h c)"); us = uhs[:].rearrange("p h c -> p (h c)")
        D(lambda e: e.tensor_scalar(out=us, in0=uc[:, 0, :], scalar1=hsel[:, 0:1], scalar2=None, op0=ALU.mult), [uhc.r(), hsel.r()], [uhs.r()])
        for jj in range(1, 4):
            D(lambda e, jj=jj: e.scalar_tensor_tensor(out=us, in0=uc[:, jj, :], scalar=hsel[:, jj:jj + 1], in1=us, op0=ALU.mult, op1=ALU.add), [uhc.r(), hsel.r(), uhs.r()], [uhs.r()])
        pA = ps(kb, "pA", [128, 512], F32); pB = ps(kb, "pB", [128, 512], F32)
        ut = [S(f"ut{i}", [128, 512], F32) for i in range(2)]
        bh = [S(f"bh{i}", [128, 2, 4, 128], F32) for i in range(2)]
        plT = S("plT", [128, 4, 128], BF16); opl = S("opl", [128, 512], F32)
        for i in range(NT):
            u_ = ut[i % 2]; bh_ = bh[i % 2]
            kb.dma("sp", u_[:], u_d[:, i, :], writes=[u_.r()])
            kb.dma("sp", bh_[:], bandhc_d[:, :, i, :, :], writes=[bh_.r()])
            fns = []
            for g in range(4):
                bo = band0[:, g, :] if i == 0 else band[:, g, :]
                fns.append(lambda e, g=g, bo=bo, u_=u_: e.matmul(pA[:, g * 128:(g + 1) * 128], u_[:, g * 128:(g + 1) * 128], bo, start=True, stop=False))
                fns.append(lambda e, g=g, bh_=bh_: e.matmul(pA[:, g * 128:(g + 1) * 128], uhs[:, 0, g * 128:(g + 1) * 128], bh_[:, 0, g, :], start=False, stop=False))
                fns.append(lambda e, g=g, bh_=bh_: e.matmul(pA[:, g * 128:(g + 1) * 128], uhs[:, 1, g * 128:(g + 1) * 128], bh_[:, 1, g, :], start=False, stop=True))
            kb.group("pe", fns, reads=[u_.r(), bh_.r(), uhs.r(), band.r(), band0.r()], writes=[pA.r()])
            A(lambda e: e.copy(plT[:].rearrange("p g t -> p (g t)"), pA[:]), [pA.r()], [plT.r()])
            kb.group("pe", [(lambda e, g=g: e.matmul(pB[:, g * 128:(g + 1) * 128], plT[:, g, :], pw[:, g, :], start=True, stop=True)) for g in range(4)], reads=[plT.r(), pw.r()], writes=[pB.r()])
            D(lambda e: e.tensor_tensor(out=opl[:], in0=pB[:], in1=pscb[:], op=ALU.mult), [pB.r(), pscb.r()], [opl.r()])
            kb.dma("sp", o_pool[:, i, :], opl[:], reads=[opl.r()], is_output=True)
        kb.finish()
    return nc


def pool_consts_core(j):
    s_ = np.arange(128)[:, None]; t_ = np.arange(128)[None, :]
    band = np.zeros((128, 4, 128), np.float32); band_first = np.zeros((128, 4, 128), np.float32)
    bandhc = np.zeros((128, 2, 16, 4, 128), np.float32)
    for g, w in enumerate(POOL_W):
        inwin = ((t_ - s_) >= 0) & ((t_ - s_) <= w - 1)
        band[:, g, :] = inwin / float(w) - (s_ == t_)
        cnt = np.minimum(t_ + 1.0, float(w))
        band_first[:, g, :] = inwin / cnt - (s_ == t_)
        for i in range(16):
            isrc = i if j > 0 else i - 1
            if isrc < 0:
                continue
            half, slot = divmod(isrc, 8)
            for r in range(16):
                srel = r - 16
                row = (((np.arange(128) - srel) >= 0) & ((np.arange(128) - srel) <= w - 1)) / float(w)
                bandhc[slot * 16 + r, half, i, g, :] = row
    hsel = np.zeros((128, 4), np.float32)
    hsel[:, (j - 1) % 4] = 1.0
    return band, (band_first if j == 0 else band), bandhc, hsel


def pool_consts():
    s = np.arange(128)[:, None]; t = np.arange(128)[None, :]
    band = np.zeros((128, 4, 128), np.float32); band_first = np.zeros((128, 4, 128), np.float32)
    bandh = np.zeros((128, 8, 4, 128), np.float32)
    for g, w in enumerate(POOL_W):
        inwin = ((t - s) >= 0) & ((t - s) <= w - 1)
        band[:, g, :] = inwin / float(w) - (s == t)
        cnt = np.minimum(t + 1.0, float(w))
        band_first[:, g, :] = inwin / cnt - (s == t)
        for r in range(16):
            srel = r - 16
            row = (((np.arange(128) - srel) >= 0) & ((np.arange(128) - srel) <= w - 1)) / float(w)
            for slot in range(8):
                bandh[slot * 16 + r, slot, g, :] = row
    return band, bandh, band_first


SCALE = float(128 ** -0.5)


def build_sb(QT=tuple(range(16))):
    nc = _get_nc()
    qT_d = dram_in(nc, "qT", [128, 4, 2048], BF16)
    kT_d = dram_in(nc, "kT", [2, 4, 64, 8192], BF16)
    v_d = dram_in(nc, "v", [2, 4, 64, 8192], BF16)
    mask_d = dram_in(nc, "sbmask", [128, 512], F32)
    U_d = dram_in(nc, "U", [128, 128], BF16)
    ones_d = dram_in(nc, "ones", [128, 128], BF16)
    o_sb = dram_out(nc, "o_sb", [128, 16, 512], F32)
    with _phase(nc) as kb:
        S = lambda n, s, d: sb(kb, n, s, d)
        D = lambda fn, r, w: kb.op("dve", fn, reads=r, writes=w)
        A = lambda fn, r, w: kb.op("act", fn, reads=r, writes=w)
        kT = S("kT", [128, 4, 8192], BF16); v = S("v", [128, 64, 512], BF16)
        for h in range(4):
            for jj in range(4):
                for q in range(2):
                    kb.dma("sp", kT[q * 64:(q + 1) * 64, h, :].rearrange("p (i j s) -> p i j s", j=4, s=128)[:, :, jj, :],
                           kT_d[q, jj].rearrange("p (h i s) -> p h i s", h=4, s=128)[:, h, :, :], writes=[kT.r(("h", h))])
        for jj in range(4):
            for q in range(2):
                kb.dma("sp", v[q * 64:(q + 1) * 64].rearrange("p (i j) c -> p i j c", j=4)[:, :, jj, :], v_d[q, jj].rearrange("p (i c) -> p i c", c=512), writes=[v.r(("g", jj))])
        KT = [kT.r(("h", h)) for h in range(4)]; VV = [v.r(("g", g)) for g in range(8)]
        mask = S("mask", [128, 512], F32); U = S("U", [128, 128], BF16); ones = S("ones", [128, 128], BF16)
        for t, d in ((mask, mask_d), (U, U_d), (ones, ones_d)):
            kb.dma("sp", t[:], d, writes=[t.r()])
        B = [ps(kb, f"B{i}", [128, 512], F32) for i in range(8)]
        qt = S("qt", [128, 4, 128], BF16)
        NBUF = 5
        eb = [S(f"eb{i}", [128, 512], F32) for i in range(NBUF)]
        spb = [S(f"spb{i}", [128, 512], F32) for i in range(NBUF)]
        Lbb = [S(f"Lb{i}", [128, 512], BF16) for i in range(NBUF)]
        tb = [S(f"tb{i}", [128, 512], F32) for i in range(NBUF)]
        wbb = [S(f"wb{i}", [128, 512], BF16) for i in range(NBUF)]
        Csb = S("Csb", [128, 4, 128], F32)
        osb = S("osb", [128, 512], F32)
        Zs = (B[0], B[1], B[2]); As = (B[3], B[4], B[5], B[6]); Ob = B[7]
        qts = [qt, S("qt2", [128, 4, 128], BF16)]

        def make_it(i, m, h, ctr, qt_):
            Z = Zs[ctr % 3]; Aa = As[ctr % 4]
            e_ = eb[ctr % NBUF]; sp_ = spb[ctr % NBUF]; L_ = Lbb[ctr % NBUF]; t_ = tb[ctr % NBUF]; w_ = wbb[ctr % NBUF]
            diag = (m == i)
            first = diag and h == 0
            last = (m == 0 and h == 3)
            hc = slice(h * 128, (h + 1) * 128)

            def S1():
                if first:
                    kb.dma("sp", qt_[:], qT_d[:, :, i * 128:(i + 1) * 128], writes=[qt_.r()])
                kb.group("pe", [(lambda e, x=x: e.matmul(Z[:, x * 128:(x + 1) * 128], kT[:, h, m * 512 + x * 128: m * 512 + (x + 1) * 128], qt_[:, h, :], start=True, stop=True)) for x in range(4)],
                         reads=[KT[h], qt_.r()], writes=[Z.r()])
                A(lambda e: e.activation(out=e_[:], in_=Z[:], func=AF.Exp, scale=-SCALE), [Z.r()], [e_.r()])

            def S2():
                A(lambda e: e.activation(out=sp_[:], in_=e_[:], func=AF.Ln, bias=1.0), [e_.r()], [sp_.r()])
                D(lambda e: e.scalar_tensor_tensor(out=L_[:], in0=Z[:], scalar=-SCALE, in1=sp_[:], op0=ALU.mult, op1=ALU.subtract), [Z.r(), sp_.r()], [L_.r()])
                if diag:
                    D(lambda e: e.tensor_tensor(out=L_[:], in0=L_[:], in1=mask[:], op=ALU.mult), [L_.r(), mask.r()], [L_.r()])

            def S3():
                fns = []
                for x in range(4):
                    fns.append(lambda e, x=x: e.matmul(Aa[:, x * 128:(x + 1) * 128], U[:], L_[:, x * 128:(x + 1) * 128], start=True, stop=(x == 3)))
                    for x2 in range(x + 1, 4):
                        fns.append(lambda e, x=x, x2=x2: e.matmul(Aa[:, x * 128:(x + 1) * 128], ones[:], L_[:, x2 * 128:(x2 + 1) * 128], start=False, stop=(x2 == 3)))
                kb.group("pe", fns, reads=[U.r(), ones.r(), L_.r()], writes=[Aa.r()])
                if diag:
                    D(lambda e: e.tensor_tensor(out=t_[:], in0=Aa[:], in1=sp_[:], op=ALU.subtract), [Aa.r(), sp_.r()], [t_.r()])
                else:
                    D(lambda e: e.tensor_tensor(out=t_[:].rearrange("p (x q) -> p x q", x=4), in0=Aa[:].rearrange("p (x q) -> p x q", x=4), in1=Csb[:, h, :].unsqueeze(1).to_broadcast([128, 4, 128]), op=ALU.add),
                      [Aa.r(), Csb.r(("h", h))], [t_.r()])
                    D(lambda e: e.tensor_tensor(out=t_[:], in0=t_[:], in1=sp_[:], op=ALU.subtract), [t_.r(), sp_.r()], [t_.r()])

            def S4():
                A(lambda e: e.activation(out=w_[:], in_=t_[:], func=AF.Exp), [t_.r()], [w_.r()])
                if diag:
                    D(lambda e: e.tensor_tensor(out=w_[:], in0=w_[:], in1=mask[:], op=ALU.mult), [w_.r(), mask.r()], [w_.r()])

            def S5():
                if m > 0:
                    kb.group("pe", [(lambda e, x=x: e.matmul(Aa[:, 0:128], ones[:], L_[:, x * 128:(x + 1) * 128], start=(x == 0), stop=(x == 3))) for x in range(4)],
                             reads=[ones.r(), L_.r()], writes=[Aa.r()])
                    if diag:
                        D(lambda e: e.tensor_copy(Csb[:, h, :], Aa[:, 0:128]), [Aa.r()], [Csb.r(("h", h))])
                    else:
                        D(lambda e: e.tensor_tensor(out=Csb[:, h, :], in0=Aa[:, 0:128], in1=Csb[:, h, :], op=ALU.add), [Aa.r(), Csb.r(("h", h))], [Csb.r(("h", h))])
                kb.group("pe", [(lambda e, x=x: e.matmul(Ob[:, hc], w_[:, x * 128:(x + 1) * 128], v[:, m * 4 + x, h * 128:(h + 1) * 128], start=(first and x == 0), stop=(last and x == 3))) for x in range(4)],
                         reads=[w_.r()] + VV, writes=[Ob.r(("h", h))] + ([Ob.r(("h", hh)) for hh in range(4)] if first else []))
                if last:
                    A(lambda e: e.copy(osb[:], Ob[:]), [Ob.r(("h", hh)) for hh in range(4)], [osb.r()])
                    kb.dma("sp", o_sb[:, i, :], osb[:], reads=[osb.r()], is_output=True)
            return (S1, S2, S3, S4, S5)

        its = []
        ctr = 0
        for qi, i in enumerate(QT):
            for m in range(i, -1, -1):
                for h in range(4):
                    its.append(make_it(i, m, h, ctr, qts[qi % 2]))
                    ctr += 1
        emit_skewed(its, 5)
        kb.finish()
        print("SB instructions:", kb.nins, {k: len(v) for k, v in kb.q.items()})
    return nc


def sb_mask(j):
    s = np.arange(128)[:, None]
    col = np.arange(512)[None, :]
    jj = col // 128; t = col % 128
    vis = (jj < j) | ((jj == j) & (s < t))
    return vis.astype(np.float32)


def sb_consts():
    jx = np.arange(128)[:, None]; sx = np.arange(128)[None, :]
    U = (jx > sx).astype(np.float32).astype(ml_dtypes.bfloat16)
    ones = np.ones((128, 128), np.float32).astype(ml_dtypes.bfloat16)
    return U, ones


NT = 16
GT = 2


def build_tail():
    nc = _get_nc()
    xs = dram_in(nc, "xs", [NT * 128, 1024], F32)
    mixa_d = dram_in(nc, "mixa", [128, NT, 512], F32)
    mixb_d = dram_in(nc, "mixb", [128, NT, 512], F32)
    modrow_d = dram_in(nc, "modrow", [1, 6144], F32)
    fnorm_d = dram_in(nc, "fnorm", [128, 8], F32)
    w_out = dram_in(nc, "w_out", [1024, 1024], F32)
    w1 = dram_in(nc, "w1", [1024, 5632], F32)
    w2 = dram_in(nc, "w2", [2816, 1024], F32)
    identb_d = dram_in(nc, "identb", [128, 128], BF16)
    identf_d = dram_in(nc, "identf", [128, 128], F32)
    o_x = dram_out(nc, "o_x", [NT * 128, 1024], F32)
    with _phase(nc) as kb:
        S = lambda n, s, d: sb(kb, n, s, d)
        D = lambda fn, r, w: kb.op("dve", fn, reads=r, writes=w)
        identb = S("identb", [128, 128], BF16); fnorm = S("fnorm", [128, 8], F32)
        mod48 = S("mod48", [48, 128], F32); identf = S("identf", [128, 128], F32)
        kb.dma("sp", identb[:], identb_d, writes=[identb.r()])
        kb.dma("sp", fnorm[:], fnorm_d, writes=[fnorm.r()])
        kb.dma("sp", mod48[:], modrow_d.rearrange("o (c p) -> (o c) p", p=128), writes=[mod48.r()])
        kb.dma("sp", identf[:], identf_d, writes=[identf.r()])
        woutb = S("woutb", [128, 8, 1024], BF16); w1b = S("w1b", [128, 8, 5632], BF16); w2b = S("w2b", [128, 22, 1024], BF16)
        for k in range(8):
            kb.dma("pool", woutb[:, k, :], w_out[k * 128:(k + 1) * 128, :], writes=[woutb.r(("k", k))])
        for k in range(8):
            kb.dma("pool", w1b[:, k, :], w1[k * 128:(k + 1) * 128, :], writes=[w1b.r(("k", k))])
        for f in range(22):
            kb.dma("pool", w2b[:, f, :], w2[f * 128:(f + 1) * 128, :], writes=[w2b.r(("k", f))])
        WO = [woutb.r(("k", k)) for k in range(8)]; W1 = [w1b.r(("k", k)) for k in range(8)]; W1f = [W1] * 22; W1u = [W1] * 22; W2 = [w2b.r(("k", f)) for f in range(22)]
        pT = ps(kb, "pT", [128, 8, 128], BF16)
        pp = [ps(kb, f"pp{i}", [128, 512], F32) for i in range(2)]
        pg = ps(kb, "pg", [128, 512], F32); pu = ps(kb, "pu", [128, 512], F32)
        modP = S("modP", [128, 48], F32); G1b = S("G1b", [128, 1024], F32); G2b = S("G2b", [128, 1024], F32)
        kb.op("pe", lambda e: e.transpose(pg[:, 0:48], mod48[:], identf[0:48, 0:48]), reads=[mod48.r(), identf.r()], writes=[pg.r()])
        D(lambda e: e.tensor_copy(modP[:], pg[:, 0:48]), [pg.r()], [modP.r()])
        kb.dma("sp", G1b[:], modrow_d[0, 2048:3072].partition_broadcast(128), writes=[G1b.r()])
        kb.dma("sp", G2b[:], modrow_d[0, 5120:6144].partition_broadcast(128), writes=[G2b.r()])
        a2 = S("a2", [128, 8], F32)
        D(lambda e: e.scalar_tensor_tensor(out=a2[:], in0=modP[:, 32:40], scalar=1.0, in1=fnorm[:], op0=ALU.add, op1=ALU.mult), [modP.r(), fnorm.r()], [a2.r()])

        class SH:
            def __getitem__(s, idx):
                p, sl = idx
                return modP[p, slice(sl.start + 24, sl.stop + 24)]
            def r(s, key=None):
                return modP.r(key)
        sh2 = SH()
        xt = [S("xt0", [128, 1024], F32)] * 2
        mixt = [S("mixt0", [128, 8, 128], BF16)] * 2
        x1s = [S(f"x1_{i}", [128, GT, 1024], F32) for i in range(2)]
        xn = S("xn", [128, 1024], BF16); junk = xn
        st = S("st", [128, 1], F32); st2 = S("st2", [128, 1], F32)
        hT4s = [S(f"hT4_{i}", [128, 8, GT * 128], BF16) for i in range(2)]
        sg = S("sg", [128, GT * 128], F32); actT = S("actT", [128, 22, GT * 128], BF16)
        yt = S("yt", [128, 1024], F32)
        mst = yt

        def front(g):
            x1 = x1s[g % 2]; hT4 = hT4s[g % 2]
            for t in range(GT):
                i = g * GT + t
                x_ = xt[i % 2]; m_ = mixt[i % 2]
                kb.dma("sp", x_[:], xs[i * 128:(i + 1) * 128, :], writes=[x_.r()])
                kb.dma("sp", mst[:, 0:512], mixa_d[:, i, :], writes=[yt.r(("c", 0))])
                kb.dma("sp", mst[:, 512:1024], mixb_d[:, i, :], writes=[yt.r(("c", 1))])
                D(lambda e: e.tensor_copy(xn[:], mst[:]), [yt.r(("c", 0)), yt.r(("c", 1))], [xn.r()])
                yield
                kb.group("pe", [(lambda e, k=k: e.transpose(pT[:, k, :], xn[:, k * 128:(k + 1) * 128], identb[:])) for k in range(8)], reads=[xn.r(), identb.r()], writes=[pT.r()])
                kb.op("act", lambda e, m_=m_: e.copy(m_[:], pT[:]), reads=[pT.r()], writes=[m_.r()])
                yield
                for cb in range(2):
                    p_ = pp[cb]
                    kb.group("pe", [(lambda e, k=k, cb=cb, p_=p_, m_=m_: e.matmul(p_[:], m_[:, k, :], woutb[:, k, cb * 512:(cb + 1) * 512], start=(k == 0), stop=(k == 7))) for k in range(8)],
                             reads=[m_.r()] + WO, writes=[p_.r()])
                    D(lambda e, cb=cb, p_=p_: e.tensor_tensor(out=mst[:, cb * 512:(cb + 1) * 512], in0=p_[:], in1=G1b[:, cb * 512:(cb + 1) * 512], op=ALU.mult), [p_.r(), G1b.r()], [yt.r(("c", cb))])
                    yield
                    D(lambda e, cb=cb, x_=x_, t=t: e.tensor_tensor(out=x1[:, t, cb * 512:(cb + 1) * 512], in0=mst[:, cb * 512:(cb + 1) * 512], in1=x_[:, cb * 512:(cb + 1) * 512], op=ALU.add),
                      [yt.r(("c", cb)), x_.r()], [x1.r(("t", t, cb))])
                    yield
                kb.op("act", lambda e, t=t: e.activation(out=junk[:], in_=x1[:, t, :], func=AF.Square, accum_out=st[:, 0:1]),
                      reads=[x1.r(("t", t, 0)), x1.r(("t", t, 1))], writes=[junk.r(), st.r()])
                yield
                kb.op("dve", lambda e: e.tensor_scalar(out=st2[:], in0=st[:], scalar1=1.0 / 1024, scalar2=EPS, op0=ALU.mult, op1=ALU.add), reads=[st.r()], writes=[st2.r()])
                yield
                kb.op("act", lambda e: e.activation(out=st2[:], in_=st2[:], func=AF.Sqrt), reads=[st2.r()], writes=[st2.r()])
                yield
                kb.op("dve", lambda e: e.reciprocal(st[:], st2[:]), reads=[st2.r()], writes=[st.r()])
                yield
                kb.op("act", lambda e, t=t: e.activation(out=xn[:], in_=x1[:, t, :], func=AF.Copy, scale=st[:, 0:1]), reads=[x1.r(("t", t, 0)), x1.r(("t", t, 1)), st.r()], writes=[xn.r()])
                yield
                kb.group("pe", [(lambda e, k=k: e.transpose(pT[:, k, :], xn[:, k * 128:(k + 1) * 128], identb[:])) for k in range(8)], reads=[xn.r(), identb.r()], writes=[pT.r()])
                yield
                for k in range(8):
                    kb.op("act", lambda e, k=k, t=t: e.activation(out=hT4[:, k, t * 128:(t + 1) * 128], in_=pT[:, k, :], func=AF.Identity, scale=a2[:, k:k + 1], bias=modP[:, 24 + k:25 + k]),
                          reads=[pT.r(), a2.r(), modP.r()], writes=[hT4.r(("t", t))])
                    if k % 4 == 3:
                        yield

        def drain(gen, n=None):
            cnt = 0
            for _ in gen:
                cnt += 1
                if n is not None and cnt >= n:
                    return

        NG = NT // GT
        gens = [front(g) for g in range(NG)]
        drain(gens[0])
        for g in range(NG):
            x1 = x1s[g % 2]; hT4 = hT4s[g % 2]
            HT = [hT4.r(("t", t)) for t in range(GT)]
            for f in range(22):
                kb.group("pe", [(lambda e, k=k, f=f: e.matmul(pg[:, 0:GT * 128], w1b[:, k, f * 128:(f + 1) * 128], hT4[:, k, :], start=(k == 0), stop=(k == 7))) for k in range(8)],
                         reads=HT + W1f[f], writes=[pg.r()])
                kb.group("pe", [(lambda e, k=k, f=f: e.matmul(pu[:, 0:GT * 128], w1b[:, k, 2816 + f * 128:2816 + (f + 1) * 128], hT4[:, k, :], start=(k == 0), stop=(k == 7))) for k in range(8)],
                         reads=HT + W1u[f], writes=[pu.r()])
                kb.op("act", lambda e: e.activation(out=sg[:], in_=pg[:, 0:GT * 128], func=AF.Silu), reads=[pg.r()], writes=[sg.r()])
                D(lambda e, f=f: e.tensor_tensor(out=actT[:, f, :], in0=sg[:], in1=pu[:, 0:GT * 128], op=ALU.mult), [sg.r(), pu.r()], [actT.r(("f", f))])
                if g + 1 < NG:
                    drain(gens[g + 1], 2)
            if g + 1 < NG:
                drain(gens[g + 1])
            AT = [actT.r(("f", f)) for f in range(22)]
            for t in range(GT):
                i = g * GT + t
                for cb in range(2):
                    p_ = pp[cb]
                    kb.group("pe", [(lambda e, f=f, cb=cb, p_=p_, t=t: e.matmul(p_[:], actT[:, f, t * 128:(t + 1) * 128], w2b[:, f, cb * 512:(cb + 1) * 512], start=(f == 0), stop=(f == 21))) for f in range(22)],
                             reads=AT + W2, writes=[p_.r()])
                    D(lambda e, cb=cb, p_=p_: e.tensor_tensor(out=yt[:, cb * 512:(cb + 1) * 512], in0=p_[:], in1=G2b[:, cb * 512:(cb + 1) * 512], op=ALU.mult), [p_.r(), G2b.r()], [yt.r(("c", cb))])
                    D(lambda e, cb=cb, t=t: e.tensor_tensor(out=yt[:, cb * 512:(cb + 1) * 512], in0=yt[:, cb * 512:(cb + 1) * 512], in1=x1[:, t, cb * 512:(cb + 1) * 512], op=ALU.add),
                      [yt.r(("c", cb)), x1.r(("t", t, cb))], [yt.r(("c", cb))])
                kb.dma("sp", o_x[i * 128:(i + 1) * 128, :], yt[:], reads=[yt.r(("c", 0)), yt.r(("c", 1))], is_output=True)
        kb.finish()
        print("tail instructions:", kb.nins, {k: len(v) for k, v in kb.q.items()})
    return nc


DBG = False


def build_fused():
    FX.active = True
    FX.nc = bass.Bass("TRN2", target_bir_lowering=False)
    FX.ext = {}
    FX.n_phase = 0
    nc = FX.nc
    es = ExitStack()
    es.__enter__()
    kb = KB(nc, es)
    kb.fused = True
    kb.last_phase = False
    FX.kb = kb
    E = fx_ext
    I = lambda name, shape, dt: nc.dram_tensor(name, list(shape), dt).ap()
    RG = [[0, 1, 2, 3], [4, 5, 6, 7]]

    def allgather(pairs, n_wait=None):
        kb.pes = ExitStack()
        skip = []
        for pi, (src, dst) in enumerate(pairs):
            nq = dst.shape[0]
            rp = src.shape[0] // nq
            for q in range(nq):
                si = src[q * rp:(q + 1) * rp, :]
                do = dst[q].rearrange("j p c -> (j p) c")
                tok = kb.coll(lambda e, si=si, do=do: e.collective_compute("AllGather", ALU.bypass, replica_groups=RG, ins=[si], outs=[do]))
                if n_wait is not None and pi >= n_wait:
                    skip.append(tok[1])
        kb.end_phase(skip=tuple(skip))

    common = dict(identb=E("identb", [128, 128], BF16), identf=E("identf", [128, 128], F32), cT=E("cT", [128, 8], F32))
    xs = E("xs", [2048, 1024], F32)
    i1 = dict(o_mod=I("i_mod0", [1, 6144], F32), o_qT=I("i_qT0", [128, 4, 2048], BF16), o_kT=I("i_kT0", [128, 4, 2048], BF16), o_v=I("i_v0", [128, 16, 512], BF16),
              o_iqT=I("i_iqT", [128, 4, 2048], BF16), o_ikT=I("i_ikT", [128, 2048], BF16), o_iw=I("i_iw", [128, 16, 8], F32), o_oin=I("i_oin", [128, 16, 512], F32),
              o_qdT=I("i_qdT", [128, 2, 2048], BF16), o_kv=I("i_kv", [128, 2, 32, 128], BF16), o_dec=I("i_dec", [128, 2, 32], F32), o_gr=I("i_gr", [128, 16, 512], F32))
    FX.remap = dict(common, xs=xs, pos=E("pos", [128, 16], I32), ada_w=E("ada_w0", [1024, 6144], F32), ada_b=E("ada_b0", [1, 6144], F32),
                    mixn=E("mixn0", [128, 8], F32), w_in=E("ab_w_in", [1024, 3672], F32), gate_up=E("gate_up", [16, 256], F32), gate_b=E("gate_b", [256], F32),
                    q_norm=E("dsa_qn", [128], F32), k_norm=E("dsa_kn", [128], F32), invf=E("invf", [64], F32), tri3=E("tri3", [128, 3, 128], F32),
                    csel=E("csel", [128, 2], F32), amask=E("amask", [128, 128], F32), **i1)
    build_l1()
    G_kT0 = I("g_kT0", [2, 4, 64, 8192], BF16); G_v0 = I("g_v0", [2, 4, 64, 8192], BF16); G_ik = I("g_ik", [1, 4, 128, 2048], BF16)
    G_kv = I("g_kv", [2, 4, 64, 8192], BF16); G_dec = I("g_dec", [1, 4, 128, 64], F32)
    allgather([(i1["o_kv"].rearrange("p a n d -> p (a n d)"), G_kv), (i1["o_dec"].rearrange("p a n -> p (a n)"), G_dec),
               (i1["o_kT"].rearrange("p h t -> p (h t)"), G_kT0), (i1["o_v"].rearrange("p i c -> p (i c)"), G_v0), (i1["o_ikT"], G_ik)], n_wait=2)
    i_gla = I("i_gla", [128, 16, 512], F32)
    FX.remap = dict(common, kv=G_kv, dec=G_dec, qdT=i1["o_qdT"], oin=i1["o_oin"], grs=i1["o_gr"], gsel=E("gsel", [128, 2, 8], F32), gnorm=E("gnorm", [128], F32), o_gla=i_gla)
    build_gla()
    i_dsa = I("i_dsa", [128, 16, 512], F32)
    FX.remap = dict(common, qT=i1["o_qT"], iqT=i1["o_iqT"], iw=i1["o_iw"], kT=G_kT0, v=G_v0, ikT=G_ik, mneg=E("mneg", [128, 512], F32), mpos=E("mpos", [128, 512], F32),
                    pow2=E("pow2", [128, NIT + 1], F32), o_dsa=i_dsa)
    build_dsa()
    i_x1 = I("i_x1", [2048, 1024], F32)
    FX.remap = dict(common, xs=xs, mixa=i_gla, mixb=i_dsa, modrow=i1["o_mod"], fnorm=E("fnorm0", [128, 8], F32), w_out=E("ab_w_out", [1024, 1024], F32),
                    w1=E("w1_0", [1024, 5632], F32), w2=E("w2_0", [2816, 1024], F32), o_x=i_x1)
    build_tail()
    i5 = dict(o_mod=I("i_mod1", [1, 6144], F32), o_qT=I("i_qT1", [128, 4, 2048], BF16), o_kT=I("i_kT1", [128, 4, 2048], BF16), o_v=I("i_v1", [128, 16, 512], BF16),
              o_u=I("i_u", [128, 16, 512], F32), o_uh=I("i_uh", [256, 512], F32))
    FX.remap = dict(common, xs=i_x1, ada_w=E("ada_w1", [1024, 6144], F32), ada_b=E("ada_b1", [1, 6144], F32), mixn=E("mixn1", [128, 8], F32),
                    w_in=E("cd_w_in", [1024, 2048], F32), q_norm=E("sb_qn", [128], F32), k_norm=E("sb_kn", [128], F32), **i5)
    build_l1b()
    G_kT1 = I("g_kT1", [2, 4, 64, 8192], BF16); G_v1 = I("g_v1", [2, 4, 64, 8192], BF16); G_uh = I("g_uh", [1, 4, 256, 512], F32)
    allgather([(i5["o_uh"], G_uh), (i5["o_kT"].rearrange("p h t -> p (h t)"), G_kT1), (i5["o_v"].rearrange("p i c -> p (i c)"), G_v1)], n_wait=1)
    i_pool = I("i_pool", [128, 16, 512], F32)
    FX.remap = dict(common, u=i5["o_u"], guh=G_uh, hsel=E("hsel", [128, 4], F32), band=E("band", [128, 4, 128], F32), band0=E("band0", [128, 4, 128], F32),
                    bandhc=E("bandhc", [128, 2, 16, 4, 128], F32), pool_w=E("pool_w", [4, 128, 128], F32), pool_scale=E("pool_scale", [512], F32), o_pool=i_pool)
    build_pool()
    i_sb = I("i_sb", [128, 16, 512], F32)
    FX.remap = dict(common, qT=i5["o_qT"], kT=G_kT1, v=G_v1, sbmask=E("sbmask", [128, 512], F32), U=E("U", [128, 128], BF16), ones=E("ones", [128, 128], BF16), o_sb=i_sb)
    build_sb()
    if DBG:
        kb.pes = ExitStack()
        for nm, ap in (("d_dsa", i_dsa), ("d_gla", i_gla), ("d_pool", i_pool), ("d_sb", i_sb)):
            o = nc.dram_tensor(nm, [128, 16, 512], F32, kind="ExternalOutput").ap()
            kb.dma("sp", o, ap, is_output=True)
        o = nc.dram_tensor("d_x1", [2048, 1024], F32, kind="ExternalOutput").ap()
        kb.dma("sp", o, i_x1, is_output=True)
        o = nc.dram_tensor("d_mod1", [1, 6144], F32, kind="ExternalOutput").ap()
        kb.dma("sp", o, i5["o_mod"], is_output=True)
        kb.end_phase()
    out = nc.dram_tensor("out", [2048, 1024], F32, kind="ExternalOutput").ap()
    FX.remap = dict(common, xs=i_x1, mixa=i_pool, mixb=i_sb, modrow=i5["o_mod"], fnorm=E("fnorm1", [128, 8], F32), w_out=E("cd_w_out", [1024, 1024], F32),
                    w1=E("w1_1", [1024, 5632], F32), w2=E("w2_1", [2816, 1024], F32), o_x=out)
    kb.last_phase = True
    build_tail()
    es.close()
    FX.active = False
    return nc


def fused_maps(inp):
    identb = np.eye(128, dtype=np.float32).astype(ml_dtypes.bfloat16)
    identf = np.eye(128, dtype=np.float32)
    cs1 = l1_consts()
    pow2 = np.tile((2.0 ** -(np.arange(NIT + 1) + 1)).astype(np.float32)[None, :], (128, 1))
    U, ones = sb_consts()
    L = lambda a: np.ascontiguousarray(a.reshape(8, 128).T)
    maps = []
    for c in range(8):
        b, j = divmod(c, 4)
        mneg, mpos = dsa_masks(j)
        band, band0, bandhc, hsel = pool_consts_core(j)
        maps.append(dict(
            identb=identb, identf=identf, cT=L(inp["c"][b]),
            xs=np.ascontiguousarray(inp["x"][b].reshape(64, 128, 1024)[j::4].reshape(2048, 1024)),
            pos=np.ascontiguousarray(inp["positions"][b].reshape(64, 128)[j::4].T.astype(np.int32)),
            ada_w0=inp["ada_w"][0], ada_b0=inp["ada_b"][0][None, :], ada_w1=inp["ada_w"][1], ada_b1=inp["ada_b"][1][None, :],
            mixn0=L(inp["mix_norm"][0]), mixn1=L(inp["mix_norm"][1]), fnorm0=L(inp["ffn_norm"][0]), fnorm1=L(inp["ffn_norm"][1]),
            ab_w_in=inp["ab_w_in"][0], gate_up=inp["gla_gate_up"][0], gate_b=inp["gla_gate_b"][0], dsa_qn=inp["dsa_q_norm"][0], dsa_kn=inp["dsa_k_norm"][0],
            invf=cs1["invf"], tri3=cs1["tri3"], csel=cs1["csel"], amask=cs1["amask"],
            mneg=mneg, mpos=mpos, pow2=pow2, gsel=gla_sel(j), gnorm=inp["gla_out_norm"][0],
            ab_w_out=inp["ab_w_out"][0], w1_0=inp["ffn_w1"][0], w2_0=inp["ffn_w2"][0],
            cd_w_in=inp["cd_w_in"][0], sb_qn=inp["sb_q_norm"][0], sb_kn=inp["sb_k_norm"][0],
            hsel=hsel, band=band, band0=band0, bandhc=bandhc, pool_w=inp["pool_w"][0], pool_scale=inp["pool_scale"][0],
            sbmask=sb_mask(j), U=U, ones=ones,
            cd_w_out=inp["cd_w_out"][0], w1_1=inp["ffn_w1"][1], w2_1=inp["ffn_w2"][1]))
    return maps


def kernel(**inputs):
    inp = {k: np.asarray(v) for k, v in inputs.items()}
    nc = build_fused()
    res = run_bass_kernel_spmd(nc, fused_maps(inp), core_ids=list(range(8)))
    out = np.zeros((2, 64, 128, 1024), np.float32)
    for c in range(8):
        b, j = divmod(c, 4)
        out[b, j::4] = res.results[c]["out"].reshape(16, 128, 1024)
    kernel.last_results = res.results
    return out.reshape(2, 8192, 1024)
```

```python
import ml_dtypes
import numpy as np
from contextlib import ExitStack
import concourse.bass as bass
import concourse.mybir as mybir
from concourse.bass_utils import run_bass_kernel_spmd

F32 = mybir.dt.float32
BF16 = mybir.dt.bfloat16
I32 = mybir.dt.int32
ALU = mybir.AluOpType
AF = mybir.ActivationFunctionType
AX = mybir.AxisListType

EPOCH = 30000
ND = 32


import types


def bind(fn):
    if getattr(fn, "__closure__", None) is None:
        return fn
    cells = []
    for c in fn.__closure__:
        try:
            cells.append(types.CellType(c.cell_contents))
        except ValueError:
            cells.append(c)
    return types.FunctionType(fn.__code__, fn.__globals__, fn.__name__, fn.__defaults__, tuple(cells))


class Reg:
    __slots__ = ("w", "r", "name")

    def __init__(self, name=""):
        self.w = None
        self.r = []
        self.name = name


class KB:
    def __init__(self, nc, es):
        self.nc = nc
        self.es = es
        self.eng = {"pe": nc.tensor, "act": nc.scalar, "dve": nc.vector, "pool": nc.gpsimd, "sp": nc.sync}
        self.sems = {}
        self.cnt = {k: 0 for k in self.eng}
        self.seen = {k: {} for k in self.eng}
        self.ndma = 0
        self.dsem = [es.enter_context(nc.semaphore(f"d{i}")) for i in range(ND)]
        self.nins = 0
        self.out_toks = []
        self.q = {k: [] for k in self.eng}
        self.rec = None
        self.pes = es
        self.pfx = ""
        self.fused = False
        self.last_phase = True
        self.dma_uses = {}

    def _esem(self, st, epoch):
        key = ("E", st, epoch)
        if key not in self.sems:
            self.sems[key] = self.es.enter_context(self.nc.semaphore(f"e_{st}_{epoch}"))
        return key

    def _semh(self, key):
        if key[0] == "D":
            return self.dsem[key[1]]
        return self.sems[key]

    def _collect(self, st, reads, writes):
        waits = {}

        def need(tok, kind):
            if tok is None:
                return
            tst, key, val = tok
            if tst == st and key[0] == "E":
                if st == "pe":
                    return
                if st in ("act", "dve") and kind != "raw":
                    return
            if self.seen[st].get(key, 0) >= val:
                return
            if waits.get(key, 0) < val:
                waits[key] = val

        for r in reads:
            need(r.w, "raw")
        for w in writes:
            need(w.w, "waw")
            for t in w.r:
                need(t, "war")
        return waits

    def _dowaits(self, st, waits):
        for key, val in waits.items():
            h = self._semh(key)
            self.q[st].append(lambda eng, h=h, val=val: eng.wait_ge(h, val))
            self.seen[st][key] = val
            self.nins += 1

    def _record(self, tok, reads, writes):
        for r in reads:
            if tok[1][0] == "E":
                r.r = [t for t in r.r if not (t[0] == tok[0] and t[1] == tok[1])]
            r.r.append(tok)
        for w in writes:
            w.w = tok
            w.r = []

    def emit_roundrobin(self, chains):
        self.rec = None
        n = max(len(c) for c in chains)
        for k in range(n):
            for c in chains:
                if k < len(c):
                    c[k]()

    def op(self, st, fn, reads=(), writes=()):
        fn = bind(fn)
        if self.rec is not None:
            reads = tuple(reads); writes = tuple(writes); rec = self.rec
            rec.append(lambda: self._norec(rec, self.op, st, fn, reads, writes))
            return None
        waits = self._collect(st, reads, writes)
        self._dowaits(st, waits)
        self.cnt[st] += 1
        c = self.cnt[st]
        epoch, val = divmod(c - 1, EPOCH)
        key = self._esem(st, epoch)
        h = self.sems[key]
        self.q[st].append(lambda eng, fn=fn, h=h: fn(eng).then_inc(h, 1))
        tok = (st, key, val + 1)
        self._record(tok, reads, writes)
        self.nins += 1
        return tok

    def _norec(self, rec, f, *a, **kw):
        saved = self.rec
        self.rec = None
        try:
            return f(*a, **kw)
        finally:
            self.rec = saved

    def group(self, st, fns, reads=(), writes=()):
        if self.rec is not None:
            fns = [bind(f) for f in fns]; reads = tuple(reads); writes = tuple(writes); rec = self.rec
            rec.append(lambda: self._norec(rec, self.group, st, fns, reads, writes))
            return None
        waits = self._collect(st, reads, writes)
        self._dowaits(st, waits)
        fns = [bind(f) for f in fns]
        for fn in fns[:-1]:
            self.q[st].append(fn)
            self.nins += 1
        self.nins += 1
        self.cnt[st] += 1
        c = self.cnt[st]
        epoch, val = divmod(c - 1, EPOCH)
        key = self._esem(st, epoch)
        h = self.sems[key]
        self.q[st].append(lambda eng, fn=fns[-1], h=h: fn(eng).then_inc(h, 1))
        tok = (st, key, val + 1)
        self._record(tok, reads, writes)
        return tok

    def dma(self, st, out, in_, reads=(), writes=(), is_output=False, **kw):
        if self.rec is not None:
            reads = tuple(reads); writes = tuple(writes); rec = self.rec
            rec.append(lambda: self._norec(rec, self.dma, st, out, in_, reads, writes, is_output, **kw))
            return None
        i = self.ndma
        self.ndma += 1
        j = i % ND
        use = i // ND
        key = ("D", j)
        waits = self._collect(st, reads, writes)
        if use > 0 and self.seen[st].get(key, 0) < 16 * use:
            waits[key] = max(waits.get(key, 0), 16 * use)
        self._dowaits(st, waits)
        h = self.dsem[j]
        self.q[st].append(lambda eng, out=out, in_=in_, kw=kw, h=h: eng.dma_start(out=out, in_=in_, **kw).then_inc(h, 16))
        tok = (st, key, 16 * (use + 1))
        self.dma_uses[key] = 16 * (use + 1)
        self._record(tok, reads, writes)
        self.nins += 1
        if is_output:
            self.out_toks.append(tok)
        return tok

    def coll(self, fn, reads=(), writes=()):
        st = "pool"
        fn = bind(fn)
        idx = len([k for k in self.sems if k[0] == "C"])
        key = ("C", idx)
        self.sems[key] = self.es.enter_context(self.nc.semaphore(f"cc{idx}"))
        waits = self._collect(st, reads, writes)
        self._dowaits(st, waits)
        h = self.sems[key]
        self.q[st].append(lambda eng, fn=fn, h=h: fn(eng).then_inc(h, 1))
        tok = (st, key, 1)
        self.dma_uses[key] = 1
        self._record(tok, reads, writes)
        self.nins += 1
        return tok

    def barrier(self, skip=()):
        targets = {k: v for k, v in self.dma_uses.items() if k not in skip}
        for e, c in self.cnt.items():
            if c > 0:
                epoch, val = divmod(c - 1, EPOCH)
                targets[("E", e, epoch)] = val + 1
        for st in self.eng:
            for key, val in targets.items():
                if self.seen[st].get(key, 0) < val:
                    h = self._semh(key)
                    self.q[st].append(lambda eng, h=h, val=val: eng.wait_ge(h, val))
                    self.seen[st][key] = val

    def end_phase(self, skip=()):
        self.barrier(skip)
        self.replay()
        self.q = {k: [] for k in self.eng}
        self.pes.close()

    def finish(self):
        if self.fused and not self.last_phase:
            self.end_phase()
            return
        st = "sp"
        for tok in self.out_toks:
            _, key, val = tok
            if self.seen[st].get(key, 0) < val:
                h = self._semh(key)
                self.q[st].append(lambda eng, h=h, val=val: eng.wait_ge(h, val))
                self.seen[st][key] = val
        self.replay()

    def replay(self):
        q = self.q
        with self.nc.Block() as block:
            @block.sync
            def _(e):
                for f in q["sp"]:
                    f(e)

            @block.tensor
            def _(e):
                for f in q["pe"]:
                    f(e)

            @block.scalar
            def _(e):
                for f in q["act"]:
                    f(e)

            @block.vector
            def _(e):
                for f in q["dve"]:
                    f(e)

            @block.gpsimd
            def _(e):
                for f in q["pool"]:
                    f(e)


class T:
    def __init__(self, t, name=""):
        self.t = t
        self.reg = Reg(name)
        self.sub = {}

    def __getitem__(self, idx):
        return self.t[idx]

    def r(self, key=None):
        if key is None:
            return self.reg
        if key not in self.sub:
            self.sub[key] = Reg()
        return self.sub[key]


def sb(kb, name, shape, dt):
    return T(kb.pes.enter_context(kb.nc.sbuf_tensor("s_" + kb.pfx + name, list(shape), dt)), name)


def ps(kb, name, shape, dt=F32):
    return T(kb.pes.enter_context(kb.nc.psum_tensor("p_" + kb.pfx + name, list(shape), dt)), name)


STAGE = 99.0
NTL = 16

EPS = 1e-6
NT = 16
TWO_PI = float(2 * np.pi)


class FX:
    active = False
    nc = None
    kb = None
    remap = {}
    ext = {}
    n_phase = 0


def fx_ext(name, shape, dt):
    if name not in FX.ext:
        FX.ext[name] = FX.nc.dram_tensor(name, list(shape), dt, kind="ExternalInput").ap()
    return FX.ext[name]


def _get_nc():
    if FX.active:
        return FX.nc
    return bass.Bass("TRN2", target_bir_lowering=False)


class _phase:
    def __init__(self, nc):
        self.nc = nc

    def __enter__(self):
        if FX.active:
            kb = FX.kb
            kb.pes = ExitStack()
            kb.pfx = f"ph{FX.n_phase}_"
            FX.n_phase += 1
            return kb
        self.es = ExitStack()
        self.es.__enter__()
        return KB(self.nc, self.es)

    def __exit__(self, *a):
        if not FX.active:
            self.es.__exit__(*a)
        return False


def dram_in(nc, name, shape, dt):
    if FX.active:
        ap = FX.remap[name]
        assert list(ap.shape) == list(shape), (name, ap.shape, shape)
        return ap
    return nc.dram_tensor(name, list(shape), dt, kind="ExternalInput").ap()


def dram_out(nc, name, shape, dt):
    if FX.active:
        ap = FX.remap[name]
        assert list(ap.shape) == list(shape), (name, ap.shape, shape)
        return ap
    return nc.dram_tensor(name, list(shape), dt, kind="ExternalOutput").ap()


def emit_mod(kb, cT_d, ada_w_d, ada_b_d, modrow, modP, name, pA, pB):
    nc = kb.nc
    cT = sb(kb, name + "cT", [128, 8], F32)
    cond = sb(kb, name + "cond", [128, 8], F32)
    adab = sb(kb, name + "adab", [1, 512], F32)
    one = sb(kb, name + "one", [1, 1], F32)
    wblk = [sb(kb, name + "wblk0", [128, 8, 512], F32)] * 2
    pm = pA; pmp = pB
    kb.dma("sp", cT[:], cT_d, writes=[cT.r()])
    kb.op("dve", lambda e: e.memset(one[:], 1.0), writes=[one.r()])
    kb.op("act", lambda e: e.activation(out=cond[:], in_=cT[:], func=AF.Silu), reads=[cT.r()], writes=[cond.r()])
    wv = ada_w_d.rearrange("(k p) n -> p k n", p=128)
    for cb in range(12):
        w = wblk[cb % 2]
        kb.dma("sp", w[:], wv[:, :, cb * 512:(cb + 1) * 512], writes=[w.r()])
        kb.dma("sp", adab[:], ada_b_d[:, cb * 512:(cb + 1) * 512], writes=[adab.r()])
        kb.group("pe", [(lambda e, k=k, w=w: e.matmul(pm[0:1, :], cond[:, k:k + 1], w[:, k, :], start=(k == 0), stop=(k == 7))) for k in range(8)],
                 reads=[cond.r(), w.r()], writes=[pm.r()])
        kb.op("dve", lambda e, cb=cb: e.tensor_tensor(out=modrow[0:1, cb * 512:(cb + 1) * 512], in0=pm[0:1, :], in1=adab[0:1, :], op=ALU.add),
              reads=[pm.r(), adab.r()], writes=[modrow.r()])
    kb.group("pe", [(lambda e, c=c: e.matmul(pmp[:, c:c + 1], modrow[0:1, c * 128:(c + 1) * 128], one[:], start=True, stop=True)) for c in range(48)],
             reads=[modrow.r(), one.r()], writes=[pmp.r()])
    kb.op("dve", lambda e: e.tensor_copy(modP[:], pmp[:, 0:48]), reads=[pmp.r()], writes=[modP.r()])


def emit_rstd(kb, ss, n, tmp, name=""):
    kb.op("dve", lambda e: e.tensor_scalar(out=tmp[:], in0=ss[:], scalar1=1.0 / n, scalar2=EPS, op0=ALU.mult, op1=ALU.add), reads=[ss.r()], writes=[tmp.r()])
    kb.op("act", lambda e: e.activation(out=tmp[:], in_=tmp[:], func=AF.Sqrt), reads=[tmp.r()], writes=[tmp.r()])
    kb.op("dve", lambda e: e.reciprocal(ss[:], tmp[:]), reads=[tmp.r()], writes=[ss.r()])


def emit_norm_hT(kb, xt, hT, a, sh, ident, junk, xn, pT, st, st2):
    kb.op("act", lambda e: e.activation(out=junk[:], in_=xt[:], func=AF.Square, accum_out=st[:, 0:1]), reads=[xt.r()], writes=[junk.r(), st.r()])
    emit_rstd(kb, st, 1024, st2)
    kb.op("act", lambda e: e.activation(out=xn[:], in_=xt[:], func=AF.Copy, scale=st[:, 0:1]), reads=[xt.r(), st.r()], writes=[xn.r()])
    kb.group("pe", [(lambda e, k=k: e.transpose(pT[:, k, :], xn[:, k * 128:(k + 1) * 128], ident[:])) for k in range(8)],
             reads=[xn.r(), ident.r()], writes=[pT.r()])
    for k in range(8):
        kb.op("act", lambda e, k=k: e.activation(out=hT[:, k, :], in_=pT[:, k, :], func=AF.Identity, scale=a[:, k:k + 1], bias=sh[:, k:k + 1]),
              reads=[pT.r(), a.r(), sh.r()], writes=[hT.r()])


def build_l1():
    nc = _get_nc()
    xs = dram_in(nc, "xs", [NT * 128, 1024], F32)
    pos_d = dram_in(nc, "pos", [128, NT], I32)
    cT_d = dram_in(nc, "cT", [128, 8], F32)
    ada_w = dram_in(nc, "ada_w", [1024, 6144], F32)
    ada_b = dram_in(nc, "ada_b", [1, 6144], F32)
    mixn_d = dram_in(nc, "mixn", [128, 8], F32)
    w_in = dram_in(nc, "w_in", [1024, 3672], F32)
    gate_up = dram_in(nc, "gate_up", [16, 256], F32)
    gate_b = dram_in(nc, "gate_b", [256], F32)
    qn_d = dram_in(nc, "q_norm", [128], F32)
    kn_d = dram_in(nc, "k_norm", [128], F32)
    invf_d = dram_in(nc, "invf", [64], F32)
    identb_d = dram_in(nc, "identb", [128, 128], BF16)
    identf_d = dram_in(nc, "identf", [128, 128], F32)
    tri3_d = dram_in(nc, "tri3", [128, 3, 128], F32)
    csel_d = dram_in(nc, "csel", [128, 2], F32)
    amask_d = dram_in(nc, "amask", [128, 128], F32)

    o_mod = dram_out(nc, "o_mod", [1, 6144], F32)
    o_qT = dram_out(nc, "o_qT", [128, 4, NT * 128], BF16)
    o_kT = dram_out(nc, "o_kT", [128, 4, NT * 128], BF16)
    o_v = dram_out(nc, "o_v", [128, NT, 512], BF16)
    o_iqT = dram_out(nc, "o_iqT", [128, 4, NT * 128], BF16)
    o_ikT = dram_out(nc, "o_ikT", [128, NT * 128], BF16)
    o_iw = dram_out(nc, "o_iw", [128, NT, 8], F32)
    o_oin = dram_out(nc, "o_oin", [128, NT, 512], F32)
    o_qdT = dram_out(nc, "o_qdT", [128, 2, NT * 128], BF16)
    o_kv = dram_out(nc, "o_kv", [128, 2, 2 * NT, 128], BF16)
    o_dec = dram_out(nc, "o_dec", [128, 2, 2 * NT], F32)
    o_gr = dram_out(nc, "o_gr", [128, NT, 512], F32)

    with _phase(nc) as kb:
        S = lambda n, s, d: sb(kb, n, s, d)
        identb = S("identb", [128, 128], BF16); identf = S("identf", [128, 128], F32)
        tri3 = S("tri3", [128, 3, 128], F32); csel = S("csel", [128, 2], F32); amask = S("amask", [128, 128], F32)
        invf = S("invf", [128, 64], F32); qnb = S("qnb", [128, 128], F32); knb = S("knb", [128, 128], F32)
        gbb = S("gbb", [128, 256], F32); gup = S("gup", [16, 256], F32); mixn = S("mixn", [128, 8], F32)
        posi = S("posi", [128, NT], I32)
        for t, d in ((identb, identb_d), (identf, identf_d), (tri3, tri3_d), (csel, csel_d), (amask, amask_d), (gup, gate_up), (mixn, mixn_d), (posi, pos_d)):
            kb.dma("sp", t[:], d, writes=[t.r()])
        for t, d in ((invf, invf_d), (qnb, qn_d), (knb, kn_d), (gbb, gate_b)):
            kb.dma("sp", t[:], d.partition_broadcast(128), writes=[t.r()])
        wb = S("wb", [128, 8, 3672], BF16)
        for k in range(8):
            kb.dma("pool", wb[:, k, :], w_in[k * 128:(k + 1) * 128, :], writes=[wb.r(("k", k))])
        wb_regs = [wb.r(("k", k)) for k in range(8)]
        modrow = S("modrow", [1, 6144], F32); modP = S("modP", [128, 48], F32)
        pT = ps(kb, "pT", [128, 8, 128], BF16)
        pp = [ps(kb, f"pp{i}", [128, 512], F32) for i in range(2)]
        pA = ps(kb, "pA", [128, 512], F32)
        pB = ps(kb, "pB", [128, 512], F32)
        pTb = pT
        pTg = ps(kb, "pTg", [128, 8, 128], BF16)
        pKV = ps(kb, "pKV", [128, 4, 256], F32)
        emit_mod(kb, cT_d, ada_w, ada_b, modrow, modP, "m0", pA, pB)
        kb.dma("sp", o_mod, modrow[:], reads=[modrow.r()], is_output=True)
        a1 = S("a1", [128, 8], F32)
        kb.op("dve", lambda e: e.scalar_tensor_tensor(out=a1[:], in0=modP[:, 8:16], scalar=1.0, in1=mixn[:], op0=ALU.add, op1=ALU.mult),
              reads=[modP.r(), mixn.r()], writes=[a1.r()])
        if STAGE < 2:
            kb.finish(); return nc
        posf = S("posf", [128, NT], F32)
        ang = S("ang", [128, NT * 64], F32); ki = S("ki", [128, NT * 64], I32); kf = S("kf", [128, NT * 64], F32)
        rs = ang; rc = S("rc", [128, NT * 64], F32); m1 = kf
        sinT = S("sinT", [128, NT, 64], F32); cosT = S("cosT", [128, NT, 64], F32)
        kb.op("dve", lambda e: e.tensor_copy(posf[:], posi[:]), reads=[posi.r()], writes=[posf.r()])
        for i in range(NT):
            kb.op("dve", lambda e, i=i: e.tensor_scalar(out=ang[:, i * 64:(i + 1) * 64], in0=invf[:], scalar1=posf[:, i:i + 1], scalar2=None, op0=ALU.mult),
                  reads=[invf.r(), posf.r()], writes=[ang.r()])
        C1 = 6.28125; C2 = TWO_PI - C1
        D = lambda fn, r, w: kb.op("dve", fn, reads=r, writes=w)
        D(lambda e: e.tensor_scalar(out=kf[:], in0=ang[:], scalar1=1.0 / TWO_PI, scalar2=None, op0=ALU.mult), [ang.r()], [kf.r()])
        D(lambda e: e.tensor_copy(ki[:], kf[:]), [kf.r()], [ki.r()])
        D(lambda e: e.tensor_copy(kf[:], ki[:]), [ki.r()], [kf.r()])
        D(lambda e: e.scalar_tensor_tensor(out=rs[:], in0=kf[:], scalar=-C1, in1=ang[:], op0=ALU.mult, op1=ALU.add), [kf.r(), ang.r()], [rs.r()])
        D(lambda e: e.scalar_tensor_tensor(out=rs[:], in0=kf[:], scalar=-C2, in1=rs[:], op0=ALU.mult, op1=ALU.add), [kf.r(), rs.r()], [rs.r()])
        PI = float(np.pi)
        D(lambda e: e.tensor_scalar(out=m1[:], in0=rs[:], scalar1=PI, scalar2=-TWO_PI, op0=ALU.is_gt, op1=ALU.mult), [rs.r()], [m1.r()])
        D(lambda e: e.tensor_tensor(out=rs[:], in0=rs[:], in1=m1[:], op=ALU.add), [rs.r(), m1.r()], [rs.r()])
        D(lambda e: e.tensor_scalar(out=m1[:], in0=rs[:], scalar1=-PI, scalar2=TWO_PI, op0=ALU.is_lt, op1=ALU.mult), [rs.r()], [m1.r()])
        D(lambda e: e.tensor_tensor(out=rs[:], in0=rs[:], in1=m1[:], op=ALU.add), [rs.r(), m1.r()], [rs.r()])
        D(lambda e: e.tensor_scalar(out=rc[:], in0=rs[:], scalar1=PI / 2, scalar2=None, op0=ALU.add), [rs.r()], [rc.r()])
        D(lambda e: e.tensor_scalar(out=m1[:], in0=rc[:], scalar1=PI, scalar2=-TWO_PI, op0=ALU.is_gt, op1=ALU.mult), [rc.r()], [m1.r()])
        D(lambda e: e.tensor_tensor(out=rc[:], in0=rc[:], in1=m1[:], op=ALU.add), [rc.r(), m1.r()], [rc.r()])
        for t in (rs, rc):
            D(lambda e, t=t: e.tensor_scalar(out=t[:], in0=t[:], scalar1=PI, scalar2=-PI, op0=ALU.min, op1=ALU.max), [t.r()], [t.r()])
        kb.op("act", lambda e: e.activation(out=sinT[:].rearrange("p a b -> p (a b)"), in_=rs[:], func=AF.Sin), reads=[rs.r()], writes=[sinT.r()])
        kb.op("act", lambda e: e.activation(out=cosT[:].rearrange("p a b -> p (a b)"), in_=rc[:], func=AF.Sin), reads=[rc.r()], writes=[cosT.r()])

        qT_t = S("qT_t", [128, 4, 128], BF16); kT_t = S("kT_t", [128, 4, 128], BF16)
        iqT_t = S("iqT_t", [128, 4, 128], BF16); ikT_t = S("ikT_t", [128, 128], BF16)
        qdT_t = S("qdT_t", [128, 2, 128], BF16)
        iw_sb = S("iw_sb", [128, NT, 8], F32)
        kv_t = S("kv_t", [128, 2, 2, 128], BF16); dec_sb = S("dec_sb", [128, 2, 2 * NT], F32)

        xt = [S(f"xt{i}", [128, 1024], F32) for i in range(2)]
        junk = S("junk", [128, 1024], BF16); xn = S("xn", [128, 1024], BF16)
        st = S("st", [128, 1], F32); st2 = S("st2", [128, 1], F32)
        hT = [S(f"hT{i}", [128, 8, 128], BF16) for i in range(2)]
        proj = S("proj", [128, 3672], F32)
        sq = S("sq", [128, 512], F32); ssq = S("ssq", [128, 4], F32); ssq2 = S("ssq2", [128, 4], F32)
        qn = S("qn", [128, 4, 128], F32); qr = S("qr", [128, 4, 128], BF16)
        tA = S("tA", [128, 4, 64], F32); tB = S("tB", [128, 4, 64], F32)
        iqr = S("iqr", [128, 8, 64], BF16); iA = S("iA", [128, 8, 32], F32); iB = S("iB", [128, 8, 32], F32)
        ik2 = S("ik2", [128, 128], BF16); kA = S("kA", [128, 32], F32); kB_ = S("kB_", [128, 32], F32)
        vb = S("vb", [128, 512], BF16); gvb = S("gvb", [128, 512], BF16)
        glT = S("glT", [16, 128], F32); pre = S("pre", [128, 256], F32); lg = S("lg", [128, 256], F32)
        bmid = S("bmid", [128, 256], F32); blast = S("blast", [128, 256], F32)
        d1 = S("d1", [128, 256], F32); d3 = S("d3", [128, 256], F32)
        E1 = S("E1", [128, 256], F32); E2 = S("E2", [128, 256], F32); E3 = S("E3", [128, 256], F32); E4 = S("E4", [128, 256], F32)
        qkd = S("qkd", [128, 3, 256], BF16)
        kdec = S("kdec", [128, 256], BF16)
        qkT = S("qkT", [128, 6, 128], BF16)
        attT = S("attT", [128, 4, 128], BF16)
        oin = S("oin", [128, 512], F32); grs = S("grs", [128, 512], F32)
        dect = S("dect", [128, 2, 2], F32)

        def rope(src, dst, nh, half, cos, sin, A, B, cols_per_head):
            sv = src
            t1 = sv[:, :, 0:half]; t2 = sv[:, :, half:2 * half]
            cb_ = cos.unsqueeze(1).to_broadcast([128, nh, half]); sb_ = sin.unsqueeze(1).to_broadcast([128, nh, half])
            return t1, t2, cb_, sb_

        for i in range(NTL if STAGE >= 3 else 0):
            x_ = xt[i % 2]; h_ = hT[i % 2]
            kb.dma("sp", x_[:], xs[i * 128:(i + 1) * 128, :], writes=[x_.r()])
            emit_norm_hT(kb, x_, h_, a1, modP_sh(modP), identb, junk, xn, pT, st, st2)
            for cb in range(8):
                c0 = cb * 512; c1 = min(3672, c0 + 512); p_ = pp[cb % 2]
                kb.group("pe", [(lambda e, k=k, c0=c0, c1=c1, p_=p_, h_=h_: e.matmul(p_[:, 0:c1 - c0], h_[:, k, :], wb[:, k, c0:c1], start=(k == 0), stop=(k == 7))) for k in range(8)],
                         reads=[h_.r()] + wb_regs, writes=[p_.r()])
                if cb % 2 == 0:
                    kb.op("act", lambda e, c0=c0, c1=c1, p_=p_: e.copy(proj[:, c0:c1], p_[:, 0:c1 - c0]), reads=[p_.r()], writes=[proj.r(("c", cb))])
                else:
                    kb.op("dve", lambda e, c0=c0, c1=c1, p_=p_: e.tensor_copy(proj[:, c0:c1], p_[:, 0:c1 - c0]), reads=[p_.r()], writes=[proj.r(("c", cb))])
            PR = [proj.r(("c", cb)) for cb in range(8)]
            cos_i = cosT[:, i, :]; sin_i = sinT[:, i, :]
            cos32 = cosT[:, i, 0:64:2]; sin32 = sinT[:, i, 0:64:2]

            C1, C2, C3 = [], [], []
            kb.rec = C1
            for (c0, gain, dstT, dstD) in ((1552, qnb, qT_t, o_qT), (2064, knb, kT_t, o_kT)):
                src = proj[:, c0:c0 + 512]
                D(lambda e, src=src: e.tensor_tensor(out=sq[:], in0=src, in1=src, op=ALU.mult), PR, [sq.r()])
                D(lambda e: e.tensor_reduce(out=ssq[:], in_=sq[:].rearrange("p (h d) -> p h d", h=4), axis=AX.X, op=ALU.add), [sq.r()], [ssq.r()])
                emit_rstd(kb, ssq, 128, ssq2)
                s3 = src.rearrange("p (h d) -> p h d", h=4)
                D(lambda e, s3=s3: e.tensor_tensor(out=qn[:], in0=s3, in1=ssq[:].unsqueeze(2).to_broadcast([128, 4, 128]), op=ALU.mult), PR + [ssq.r()], [qn.r()])
                D(lambda e, gain=gain: e.tensor_tensor(out=qn[:], in0=qn[:], in1=gain[:].unsqueeze(1).to_broadcast([128, 4, 128]), op=ALU.mult), [qn.r(), gain.r()], [qn.r()])
                t1 = qn[:, :, 0:64]; t2 = qn[:, :, 64:128]
                cb_ = cos_i.unsqueeze(1).to_broadcast([128, 4, 64]); sb_ = sin_i.unsqueeze(1).to_broadcast([128, 4, 64])
                D(lambda e: e.tensor_tensor(out=tA[:], in0=t1, in1=cb_, op=ALU.mult), [qn.r(), cosT.r()], [tA.r()])
                D(lambda e: e.tensor_tensor(out=tB[:], in0=t2, in1=sb_, op=ALU.mult), [qn.r(), sinT.r()], [tB.r()])
                D(lambda e: e.tensor_tensor(out=qr[:, :, 0:64], in0=tA[:], in1=tB[:], op=ALU.subtract), [tA.r(), tB.r()], [qr.r()])
                D(lambda e: e.tensor_tensor(out=tA[:], in0=t2, in1=cb_, op=ALU.mult), [qn.r(), cosT.r()], [tA.r()])
                D(lambda e: e.tensor_tensor(out=tB[:], in0=t1, in1=sb_, op=ALU.mult), [qn.r(), sinT.r()], [tB.r()])
                D(lambda e: e.tensor_tensor(out=qr[:, :, 64:128], in0=tA[:], in1=tB[:], op=ALU.add), [tA.r(), tB.r()], [qr.r()])
                kb.group("pe", [(lambda e, h=h: e.transpose(pTb[:, h, :], qr[:, h, :], identb[:])) for h in range(4)], reads=[qr.r(), identb.r()], writes=[pTb.r()])
                kb.op("act", lambda e, dstT=dstT: e.copy(dstT[:], pTb[:, 0:4, :]), reads=[pTb.r()], writes=[dstT.r()])
                kb.dma("sp", dstD[:, :, i * 128:(i + 1) * 128], dstT[:], reads=[dstT.r()], is_output=True)
            kb.rec = C2
            D(lambda e: e.tensor_copy(vb[:], proj[:, 2576:3088]), PR, [vb.r()])
            kb.dma("sp", o_v[:, i, :], vb[:], reads=[vb.r()], is_output=True)
            kb.rec = C1
            iq3 = proj[:, 3088:3600].rearrange("p (h d) -> p h d", h=8)
            t1 = iq3[:, :, 0:32]; t2 = iq3[:, :, 32:64]
            cb_ = cos32.unsqueeze(1).to_broadcast([128, 8, 32]); sb_ = sin32.unsqueeze(1).to_broadcast([128, 8, 32])
            D(lambda e: e.tensor_tensor(out=iA[:], in0=t1, in1=cb_, op=ALU.mult), PR + [cosT.r()], [iA.r()])
            D(lambda e: e.tensor_tensor(out=iB[:], in0=t2, in1=sb_, op=ALU.mult), PR + [sinT.r()], [iB.r()])
            D(lambda e: e.tensor_tensor(out=iqr[:, :, 0:32], in0=iA[:], in1=iB[:], op=ALU.subtract), [iA.r(), iB.r()], [iqr.r()])
            D(lambda e: e.tensor_tensor(out=iA[:], in0=t2, in1=cb_, op=ALU.mult), PR + [cosT.r()], [iA.r()])
            D(lambda e: e.tensor_tensor(out=iB[:], in0=t1, in1=sb_, op=ALU.mult), PR + [sinT.r()], [iB.r()])
            D(lambda e: e.tensor_tensor(out=iqr[:, :, 32:64], in0=iA[:], in1=iB[:], op=ALU.add), [iA.r(), iB.r()], [iqr.r()])
            iq2 = iqr[:].rearrange("p h d -> p (h d)")
            kb.group("pe", [(lambda e, g=g: e.transpose(pTb[:, g, :], iq2[:, g * 128:(g + 1) * 128], identb[:])) for g in range(4)], reads=[iqr.r(), identb.r()], writes=[pTb.r()])
            kb.op("act", lambda e: e.copy(iqT_t[:], pTb[:, 0:4, :]), reads=[pTb.r()], writes=[iqT_t.r()])
            kb.dma("sp", o_iqT[:, :, i * 128:(i + 1) * 128], iqT_t[:], reads=[iqT_t.r()], is_output=True)
            kb.rec = C2
            k1 = proj[:, 3600:3632]; k2 = proj[:, 3632:3664]
            D(lambda e: e.tensor_tensor(out=kA[:], in0=k1, in1=cos32, op=ALU.mult), PR + [cosT.r()], [kA.r()])
            D(lambda e: e.tensor_tensor(out=kB_[:], in0=k2, in1=sin32, op=ALU.mult), PR + [sinT.r()], [kB_.r()])
            D(lambda e: e.tensor_tensor(out=ik2[:, 0:32], in0=kA[:], in1=kB_[:], op=ALU.subtract), [kA.r(), kB_.r()], [ik2.r()])
            D(lambda e: e.tensor_tensor(out=kA[:], in0=k2, in1=cos32, op=ALU.mult), PR + [cosT.r()], [kA.r()])
            D(lambda e: e.tensor_tensor(out=kB_[:], in0=k1, in1=sin32, op=ALU.mult), PR + [sinT.r()], [kB_.r()])
            D(lambda e: e.tensor_tensor(out=ik2[:, 32:64], in0=kA[:], in1=kB_[:], op=ALU.add), [kA.r(), kB_.r()], [ik2.r()])
            D(lambda e: e.tensor_copy(ik2[:, 64:128], ik2[:, 0:64]), [ik2.r()], [ik2.r()])
            kb.op("pe", lambda e: e.transpose(pTb[:, 4, :], ik2[:], identb[:]), reads=[ik2.r(), identb.r()], writes=[pTb.r()])
            kb.op("act", lambda e: e.copy(ikT_t[:], pTb[:, 4, :]), reads=[pTb.r()], writes=[ikT_t.r()])
            kb.dma("sp", o_ikT[:, i * 128:(i + 1) * 128], ikT_t[:], reads=[ikT_t.r()], is_output=True)
            D(lambda e: e.tensor_copy(iw_sb[:, i, :], proj[:, 3664:3672]), PR, [iw_sb.r()])
            kb.rec = C3
            kb.op("pe", lambda e: e.transpose(pA[0:16, 0:128], proj[:, 1024:1040], identf[:]), reads=PR + [identf.r()], writes=[pA.r()])
            kb.op("act", lambda e: e.copy(glT[:], pA[0:16, 0:128]), reads=[pA.r()], writes=[glT.r()])
            kb.op("pe", lambda e: e.matmul(pA[:, 0:256], glT[:], gup[:], start=True, stop=True), reads=[glT.r(), gup.r()], writes=[pA.r()])
            D(lambda e: e.tensor_tensor(out=pre[:], in0=pA[:, 0:256], in1=gbb[:], op=ALU.add), [pA.r(), gbb.r()], [pre.r()])
            kb.op("act", lambda e: e.activation(out=lg[:], in_=pre[:], func=AF.Exp, scale=-1.0), reads=[pre.r()], writes=[lg.r()])
            kb.op("act", lambda e: e.activation(out=lg[:], in_=lg[:], func=AF.Ln, bias=1.0), reads=[lg.r()], writes=[lg.r()])
            kb.group("pe", [
                lambda e: e.matmul(pA[:, 0:256], tri3[:, 0, :], lg[:], start=True, stop=True),
                lambda e: e.matmul(pA[:, 256:512], tri3[:, 1, :], lg[:], start=True, stop=True),
                lambda e: e.matmul(pB[:, 0:256], tri3[:, 2, :], lg[:], start=True, stop=True),
                lambda e: e.matmul(pB[:, 256:258], lg[:, 0:128], csel[:], start=True, stop=True),
                lambda e: e.matmul(pB[:, 258:260], lg[:, 128:256], csel[:], start=True, stop=True),
            ], reads=[tri3.r(), lg.r(), csel.r()], writes=[pA.r(), pB.r()])
            kb.op("act", lambda e: e.copy(bmid[:], pA[:, 256:512]), reads=[pA.r()], writes=[bmid.r()])
            kb.op("act", lambda e: e.copy(blast[:], pB[:, 0:256]), reads=[pB.r()], writes=[blast.r()])
            kb.op("act", lambda e: e.activation(out=dec_sb[:, :, 2 * i:2 * i + 2], in_=pB[:, 256:260].rearrange("p (a c) -> p a c", a=2), func=AF.Exp), reads=[pB.r()], writes=[dec_sb.r()])
            D(lambda e: e.tensor_tensor(out=d1[:], in0=pA[:, 0:256], in1=bmid[:], op=ALU.subtract), [pA.r(), bmid.r()], [d1.r()])
            D(lambda e: e.tensor_tensor(out=d3[:], in0=blast[:], in1=pA[:, 0:256], op=ALU.subtract), [pA.r(), blast.r()], [d3.r()])
            kb.op("act", lambda e: e.activation(out=E1[:], in_=d1[:], func=AF.Exp), reads=[d1.r()], writes=[E1.r()])
            kb.op("act", lambda e: e.activation(out=E2[:], in_=d1[:], func=AF.Exp, scale=-1.0), reads=[d1.r()], writes=[E2.r()])
            kb.op("act", lambda e: e.activation(out=E3[:], in_=d3[:], func=AF.Exp), reads=[d3.r()], writes=[E3.r()])
            kb.op("act", lambda e: e.activation(out=E4[:], in_=pA[:, 0:256], func=AF.Exp), reads=[pA.r()], writes=[E4.r()])
            gq = proj[:, 0:256]; gk = proj[:, 256:512]
            D(lambda e: e.scalar_tensor_tensor(out=qkd[:, 0, :], in0=gq, scalar=0.125, in1=E1[:], op0=ALU.mult, op1=ALU.mult), PR + [E1.r()], [qkd.r()])
            D(lambda e: e.tensor_tensor(out=qkd[:, 1, :], in0=gk, in1=E2[:], op=ALU.mult), PR + [E2.r()], [qkd.r()])
            D(lambda e: e.scalar_tensor_tensor(out=qkd[:, 2, :], in0=gq, scalar=0.125, in1=E4[:], op0=ALU.mult, op1=ALU.mult), PR + [E4.r()], [qkd.r()])
            D(lambda e: e.tensor_tensor(out=kdec[:], in0=gk, in1=E3[:], op=ALU.mult), PR + [E3.r()], [kdec.r()])
            kb.group("pe", [(lambda e, w=w, p=p: e.transpose(pTg[:, w * 2 + p, :], qkd[:, w, p * 128:(p + 1) * 128], identb[:])) for w in range(3) for p in range(2)],
                     reads=[qkd.r(), identb.r()], writes=[pTg.r()])
            kb.op("act", lambda e: e.copy(qkT[:], pTg[:, 0:6, :]), reads=[pTg.r()], writes=[qkT.r()])
            D(lambda e: e.tensor_copy(qdT_t[:], qkT[:, 4:6, :]), [qkT.r()], [qdT_t.r()])
            kb.dma("sp", o_qdT[:, :, i * 128:(i + 1) * 128], qdT_t[:], reads=[qdT_t.r()], is_output=True)
            D(lambda e: e.tensor_copy(gvb[:], proj[:, 512:1024]), PR, [gvb.r()])
            def attmm(e, h):
                p, hb = divmod(h, 2); hb *= 64
                dst = pp[0] if hb == 0 else pp[1]
                return e.matmul(dst[:, p * 128:(p + 1) * 128], qkT[hb:hb + 64, 2 + p, :], qkT[hb:hb + 64, 0 + p, :], start=True, stop=True)
            kb.group("pe", [(lambda e, h=h: attmm(e, h)) for h in (0, 2, 1, 3)], reads=[qkT.r()], writes=[pp[0].r(), pp[1].r()])
            for hh in range(2):
                D(lambda e, hh=hh: e.tensor_tensor(out=attT[:, hh:4:2, :], in0=pp[hh][:, 0:256].rearrange("p (h i) -> p h i", h=2), in1=amask[:].unsqueeze(1).to_broadcast([128, 2, 128]), op=ALU.mult),
                  [pp[hh].r(), amask.r()], [attT.r()])
            kb.group("pe", [(lambda e, h=h: e.matmul(pB[:, h * 128:(h + 1) * 128], attT[:, h, :], gvb[:, h * 128:(h + 1) * 128], start=True, stop=True)) for h in range(4)],
                     reads=[attT.r(), gvb.r()], writes=[pB.r()])
            kb.op("act", lambda e: e.copy(oin[:], pB[:]), reads=[pB.r()], writes=[oin.r()])
            kb.dma("sp", o_oin[:, i, :], oin[:], reads=[oin.r()], is_output=True)
            kb.group("pe", [(lambda e, p=p, c=c: e.matmul(pKV[:, c * 2 + p, :], kdec[c * 64:(c + 1) * 64, p * 128:(p + 1) * 128], gvb[c * 64:(c + 1) * 64, p * 256:(p + 1) * 256], start=True, stop=True))
                            for p in range(2) for c in range(2)], reads=[kdec.r(), gvb.r()], writes=[pKV.r()])
            pk = pKV[:].rearrange("q (c p) n -> q p c n", p=2)
            kb.op("act", lambda e: e.copy(kv_t[0:64], pk[0:64, :, :, 0:128]), reads=[pKV.r()], writes=[kv_t.r()])
            D(lambda e: e.tensor_copy(kv_t[64:128], pk[64:128, :, :, 128:256]), [pKV.r()], [kv_t.r()])
            kb.dma("sp", o_kv[:, :, 2 * i:2 * i + 2, :], kv_t[:], reads=[kv_t.r()], is_output=True)
            kb.rec = C2
            kb.op("act", lambda e: e.activation(out=grs[:], in_=proj[:, 1040:1552], func=AF.Silu), reads=PR, writes=[grs.r()])
            kb.dma("sp", o_gr[:, i, :], grs[:], reads=[grs.r()], is_output=True)
            kb.emit_roundrobin([C3, C1, C2])
        for t, d in ((iw_sb, o_iw), (dec_sb, o_dec)):
            kb.dma("sp", d, t[:], reads=[t.r()], is_output=True)
        kb.finish()
        print("L1 instructions:", kb.nins, {k: len(v) for k, v in kb.q.items()})
    return nc


class _ShView:
    pass


def modP_sh(modP):
    class V:
        def __getitem__(s, idx):
            return modP[idx]
        def r(s, key=None):
            return modP.r(key)
    return V()


def l1_consts():
    half = 64
    invf = (10000.0 ** (-np.arange(half, dtype=np.float32) / half)).astype(np.float32)
    j = np.arange(128)[:, None]; i = np.arange(128)[None, :]
    same = (j // 64) == (i // 64)
    tri = (same & (j <= i)).astype(np.float32)
    mmid = (same & ((j % 64) <= 31)).astype(np.float32)
    mlast = same.astype(np.float32)
    tri3 = np.stack([tri, mmid, mlast], axis=1) * (-1.0 / 16.0)
    csel = np.stack([(np.arange(128) < 64), (np.arange(128) >= 64)], axis=1).astype(np.float32) * (-1.0 / 16.0)
    amask = tri.copy()
    return dict(invf=invf, identb=np.eye(128, dtype=np.float32).astype(ml_dtypes.bfloat16), identf=np.eye(128, dtype=np.float32),
                tri3=np.ascontiguousarray(tri3.astype(np.float32)), csel=csel, amask=amask)


NIT = 10
C0 = 11.3137085
SCALE = float(128 ** -0.5)


def emit_skewed(its, nst):
    n = len(its)
    for step in range(n + nst - 1):
        for stg in range(nst):
            k = step - stg
            if 0 <= k < n:
                its[k][stg]()


def build_dsa(QT=tuple(range(16))):
    nc = _get_nc()
    qT_d = dram_in(nc, "qT", [128, 4, 2048], BF16)
    iqT_d = dram_in(nc, "iqT", [128, 4, 2048], BF16)
    iw_d = dram_in(nc, "iw", [128, 16, 8], F32)
    kT_d = dram_in(nc, "kT", [2, 4, 64, 8192], BF16)
    v_d = dram_in(nc, "v", [2, 4, 64, 8192], BF16)
    ikT_d = dram_in(nc, "ikT", [1, 4, 128, 2048], BF16)
    mneg_d = dram_in(nc, "mneg", [128, 512], F32)
    mpos_d = dram_in(nc, "mpos", [128, 512], F32)
    identb_d = dram_in(nc, "identb", [128, 128], BF16)
    identf_d = dram_in(nc, "identf", [128, 128], F32)
    pow2_d = dram_in(nc, "pow2", [128, NIT + 1], F32)
    o_dsa = dram_out(nc, "o_dsa", [128, 16, 512], F32)
    with _phase(nc) as kb:
        S = lambda n, s, d: sb(kb, n, s, d)
        D = lambda fn, r, w: kb.op("dve", fn, reads=r, writes=w)
        A = lambda fn, r, w: kb.op("act", fn, reads=r, writes=w)
        v = S("v", [128, 64, 512], BF16); ikT = S("ikT", [128, 8192], BF16)
        kTb = [S(f"kTb{i}", [128, 512], BF16) for i in range(3)]
        for jj in range(4):
            for q in range(2):
                kb.dma("sp", v[q * 64:(q + 1) * 64].rearrange("p (i j) c -> p i j c", j=4)[:, :, jj, :], v_d[q, jj].rearrange("p (i c) -> p i c", c=512), writes=[v.r(("g", jj))])
        for jj in range(4):
            kb.dma("sp", ikT[:].rearrange("p (i j s) -> p i j s", j=4, s=128)[:, :, jj, :], ikT_d[0, jj].rearrange("p (i s) -> p i s", s=128), writes=[ikT.r()])
        VV = [v.r(("g", g)) for g in range(8)]
        iw = S("iw", [128, 16, 8], F32); mneg = S("mneg", [128, 512], F32); mpos = S("mpos", [128, 512], F32)
        identb = S("identb", [128, 128], BF16); identf = S("identf", [128, 128], F32); pow2 = S("pow2", [128, NIT + 1], F32)
        for t, d in ((iw, iw_d), (mneg, mneg_d), (mpos, mpos_d), (identb, identb_d), (identf, identf_d), (pow2, pow2_d)):
            kb.dma("sp", t[:], d, writes=[t.r()])
        B = [ps(kb, f"B{i}", [128, 512], F32) for i in range(4)] + [None, None] + [ps(kb, f"B{i}", [128, 512], F32) for i in (6, 7)]
        X2 = ps(kb, "X2", [128, 2, 512], F32)
        qts = [S(f"qt{i}", [128, 4, 128], BF16) for i in range(2)]; iqts = [S(f"iqt{i}", [128, 4, 128], BF16) for i in range(2)]
        diagw = S("diagw", [128, 8, 128], BF16)
        Rt2 = [S(f"Rt2_{i}", [128, 2, 512], BF16) for i in range(2)]
        scores = [S(f"score{i}", [128, 8192], F32) for i in range(2)]
        masks = [S(f"maskb{i}", [128, 8192], BF16) for i in range(2)]
        tmp = S("tmp", [128, 512], F32)
        sts = [S(f"st{i}", [128, 8], F32) for i in range(2)]
        Hh = S("Hh", [128, NIT + 1], F32)
        lo_t = S("lo_t", [128, 1], F32); mid_t = S("mid_t", [128, 1], F32); cnt_t = S("cnt_t", [128, 1], F32); g_t = S("g_t", [128, 1], F32)
        Eb = [S(f"Eb{i}", [128, 512], BF16) for i in range(2)]
        Pb = [S(f"Pb{i}", [128, 512], BF16) for i in range(2)]
        PT = [S(f"PT{i}", [128, 4, 128], BF16) for i in range(2)]
        rs = S("rs", [128, 4, 16], F32); rsum = S("rsum", [128, 4], F32); rinv = S("rinv", [128, 4], F32)
        osb = S("osb", [128, 512], F32)
        Sbanks = (B[0], B[1], B[7]); Ob = B[2]; pTv = B[3][:].bitcast(BF16); SC = B[6]
        cstate = [0]

        def phaseI(i, par):
            iqt = iqts[par]; score = scores[par]; st = sts[par]
            kb.dma("sp", iqt[:], iqT_d[:, :, i * 128:(i + 1) * 128], writes=[iqt.r()])
            for h in range(8):
                A(lambda e, h=h: e.activation(out=diagw[:, h, :], in_=identf[:], func=AF.Copy, scale=iw[:, i, h:h + 1]), [identf.r(), iw.r()], [diagw.r()])
            yield
            for m in range(i + 1):
                ks = slice(m * 512, (m + 1) * 512)
                for p in range(4):
                    kb.group("pe", [lambda e, p=p: e.matmul(X2[:, 0, :], iqt[0:64, p, :], ikT[0:64, ks], start=True, stop=True),
                                    lambda e, p=p: e.matmul(X2[:, 1, :], iqt[64:128, p, :], ikT[64:128, ks], start=True, stop=True)],
                             reads=[iqt.r(), ikT.r()], writes=[X2.r()])
                    R_ = Rt2[p % 2]
                    A(lambda e, R_=R_: e.activation(out=R_[:].rearrange("p a b -> p (a b)"), in_=X2[:].rearrange("p a b -> p (a b)"), func=AF.Relu), [X2.r()], [R_.r(("h", 0)), R_.r(("h", 1))])
                    kb.group("pe", [lambda e, p=p, R_=R_: e.matmul(SC[:], diagw[:, 2 * p, :], R_[:, 0, :], start=(p == 0), stop=False),
                                    lambda e, p=p, R_=R_: e.matmul(SC[:], diagw[:, 2 * p + 1, :], R_[:, 1, :], start=False, stop=(p == 3))],
                             reads=[diagw.r(), R_.r(("h", 0)), R_.r(("h", 1))], writes=[SC.r()])
                    yield
                if m < i:
                    A(lambda e: e.copy(score[:, ks], SC[:]), [SC.r()], [score.r(("m", m))])
                else:
                    D(lambda e: e.tensor_tensor(out=score[:, ks], in0=SC[:], in1=mneg[:], op=ALU.add), [SC.r(), mneg.r()], [score.r(("m", m))])
                    D(lambda e: e.tensor_tensor(out=tmp[:], in0=SC[:], in1=mpos[:], op=ALU.add), [SC.r(), mpos.r()], [tmp.r()])
                    D(lambda e: e.tensor_reduce(out=st[:, 1:2], in_=tmp[:], axis=AX.X, op=ALU.min), [tmp.r()], [st.r()])
                yield

        def phaseII(i, par):
            score = scores[par]; st = sts[par]; maskb = masks[par]
            SCR = [score.r(("m", m)) for m in range(i + 1)]
            W = (i + 1) * 512
            if i > 0:
                D(lambda e: e.tensor_reduce(out=st[:, 0:1], in_=score[:, 0:i * 512], axis=AX.X, op=ALU.min), SCR, [st.r()])
                D(lambda e: e.tensor_tensor(out=st[:, 2:3], in0=st[:, 0:1], in1=st[:, 1:2], op=ALU.min), [st.r()], [st.r()])
            else:
                D(lambda e: e.tensor_copy(st[:, 2:3], st[:, 1:2]), [st.r()], [st.r()])
            yield
            D(lambda e: e.tensor_reduce(out=st[:, 3:4], in_=score[:, 0:W], axis=AX.X, op=ALU.max), SCR, [st.r()])
            D(lambda e: e.tensor_tensor(out=st[:, 4:5], in0=st[:, 3:4], in1=st[:, 2:3], op=ALU.subtract), [st.r()], [st.r()])
            yield
            D(lambda e: e.tensor_scalar(out=Hh[:], in0=pow2[:], scalar1=st[:, 4:5], scalar2=None, op0=ALU.mult), [pow2.r(), st.r()], [Hh.r()])
            D(lambda e: e.tensor_copy(lo_t[:], st[:, 2:3]), [st.r()], [lo_t.r()])
            D(lambda e: e.tensor_tensor(out=mid_t[:], in0=st[:, 2:3], in1=Hh[:, 0:1], op=ALU.add), [st.r(), Hh.r()], [mid_t.r()])
            yield
            for k in range(NIT):
                D(lambda e: e.tensor_scalar(out=maskb[:, 0:W], in0=score[:, 0:W], scalar1=mid_t[:, 0:1], scalar2=None, op0=ALU.is_ge, op1=ALU.add, accum_out=cnt_t[:, 0:1]),
                  SCR + [mid_t.r()], [maskb.r(), cnt_t.r()])
                yield
                D(lambda e, k=k: e.tensor_scalar(out=g_t[:], in0=cnt_t[:], scalar1=255.5, scalar2=Hh[:, k:k + 1], op0=ALU.is_ge, op1=ALU.mult), [cnt_t.r(), Hh.r()], [g_t.r()])
                yield
                D(lambda e, k=k: e.scalar_tensor_tensor(out=mid_t[:], in0=g_t[:], scalar=lo_t[:, 0:1], in1=Hh[:, k + 1:k + 2], op0=ALU.add, op1=ALU.add), [g_t.r(), lo_t.r(), Hh.r()], [mid_t.r()])
                D(lambda e: e.tensor_tensor(out=lo_t[:], in0=lo_t[:], in1=g_t[:], op=ALU.add), [lo_t.r(), g_t.r()], [lo_t.r()])
                yield
            D(lambda e: e.tensor_scalar(out=maskb[:, 0:W], in0=score[:, 0:W], scalar1=lo_t[:, 0:1], scalar2=None, op0=ALU.is_ge), SCR + [lo_t.r()], [maskb.r()])
            yield

        def phaseIII(i, par):
            qt = qts[par]; maskb = masks[par]
            kb.dma("sp", qt[:], qT_d[:, :, i * 128:(i + 1) * 128], writes=[qt.r()])

            def make_it3(m, h, c3):
                ks = slice(m * 512, (m + 1) * 512)
                Sb = Sbanks[c3 % 3]; E_ = Eb[c3 % 2]; P_ = Pb[c3 % 2]; PT_ = PT[c3 % 2]; kt_ = kTb[c3 % 3]
                first = (m == 0 and h == 0)
                hc = slice(h * 128, (h + 1) * 128)

                def S1():
                    for q in range(2):
                        kb.dma("sp", kt_[q * 64:(q + 1) * 64].rearrange("p (j s) -> p j s", j=4), kT_d[q].rearrange("j p (h i s) -> p j h i s", h=4, s=128)[:, :, h, m, :], writes=[kt_.r()])
                    kb.op("pe", lambda e: e.matmul(Sb[:], qt[:, h, :], kt_[:], start=True, stop=True), reads=[qt.r(), kt_.r()], writes=[Sb.r()])
                    A(lambda e: e.activation(out=E_[:], in_=Sb[:], func=AF.Exp, scale=SCALE, bias=-C0), [Sb.r()], [E_.r()])

                def S2():
                    D(lambda e: e.scalar_tensor_tensor(out=P_[:], in0=maskb[:, ks], scalar=1.0, in1=E_[:], op0=ALU.mult, op1=ALU.mult, accum_out=rs[:, h, m:m + 1]),
                      [maskb.r(), E_.r()], [P_.r(), rs.r()])
                    kb.group("pe", [(lambda e, x=x: e.transpose(pTv[:, x * 128:(x + 1) * 128], P_[:, x * 128:(x + 1) * 128], identb[:])) for x in range(4)],
                             reads=[P_.r(), identb.r()], writes=[B[3].r()])
                    A(lambda e: e.copy(PT_[:].rearrange("p a b -> p (a b)"), pTv[:, 0:512]), [B[3].r()], [PT_.r()])

                def S3():
                    kb.group("pe", [(lambda e, x=x: e.matmul(Ob[:, hc], PT_[:, x, :], v[:, m * 4 + x, h * 128:(h + 1) * 128], start=(first and x == 0), stop=(m == i and h == 3 and x == 3))) for x in range(4)],
                             reads=[PT_.r()] + VV, writes=[Ob.r(("h", h))] + ([Ob.r(("h", hh)) for hh in range(4)] if first else []))
                return (S1, S2, S3)

            its3 = []
            for m in range(i + 1):
                for h in range(4):
                    its3.append(make_it3(m, h, cstate[0]))
                    cstate[0] += 1
            n = len(its3)
            for step in range(n + 2):
                for stg in range(3):
                    k = step - stg
                    if 0 <= k < n:
                        its3[k][stg]()
                yield
            D(lambda e: e.tensor_reduce(out=rsum[:], in_=rs[:, :, 0:i + 1], axis=AX.X, op=ALU.add), [rs.r()], [rsum.r()])
            D(lambda e: e.reciprocal(rinv[:], rsum[:]), [rsum.r()], [rinv.r()])
            D(lambda e: e.tensor_tensor(out=osb[:].rearrange("p (h d) -> p h d", h=4), in0=Ob[:].rearrange("p (h d) -> p h d", h=4), in1=rinv[:].unsqueeze(2).to_broadcast([128, 4, 128]), op=ALU.mult),
              [Ob.r(("h", hh)) for hh in range(4)] + [rinv.r()], [osb.r()])
            kb.dma("sp", o_dsa[:, i, :], osb[:], reads=[osb.r()], is_output=True)
            yield

        def run_interleaved(gens):
            lists = []
            for g in gens:
                lists.append(g)
            active = [[g, w, 0.0] for g, w in lists]
            total = max(w for _, w in lists)
            for stepi in range(total):
                for a in active:
                    g, w, acc = a
                    a[2] += w / total
                    while a[2] >= 1.0:
                        a[2] -= 1.0
                        try:
                            next(g)
                        except StopIteration:
                            a[2] = -1e9
            for a in active:
                for _ in a[0]:
                    pass

        def est_I(i):
            return 1 + (i + 1) * 5

        def est_II(i):
            return 3 + NIT * 3 + 1

        def est_III(i):
            return 4 * (i + 1) + 3

        QL = list(QT)
        for _ in phaseI(QL[0], 0):
            pass
        nq = len(QL)
        for t in range(nq + 1):
            gens = []
            if t < nq:
                gens.append((phaseII(QL[t], t % 2), est_II(QL[t])))
            if t >= 1:
                gens.append((phaseIII(QL[t - 1], (t - 1) % 2), est_III(QL[t - 1])))
            if t + 1 < nq:
                gens.append((phaseI(QL[t + 1], (t + 1) % 2), est_I(QL[t + 1])))
            run_interleaved(gens)
        kb.finish()
        print("DSA instructions:", kb.nins, {k: len(v) for k, v in kb.q.items()})
    return nc


def dsa_masks(j):
    q = np.arange(128)[:, None]
    col = np.arange(512)[None, :]
    jj = col // 128; s = col % 128
    vis = (jj < j) | ((jj == j) & (s <= q))
    mneg = np.where(vis, 0.0, -1e30).astype(np.float32)
    mpos = np.where(vis, 0.0, 1e30).astype(np.float32)
    return mneg, mpos


def gather_global(per_core, axis_tok_tiles):
    st = np.stack(per_core, axis=axis_tok_tiles + 1)
    sh = list(st.shape)
    sh[axis_tok_tiles:axis_tok_tiles + 2] = [64]
    return st.reshape(sh)


def build_gla(NB=16):
    nc = _get_nc()
    kv_d = dram_in(nc, "kv", [2, 4, 64, 8192], BF16)
    dec_d = dram_in(nc, "dec", [1, 4, 128, 64], F32)
    qd_d = dram_in(nc, "qdT", [128, 2, 2048], BF16)
    oin_d = dram_in(nc, "oin", [128, 16, 512], F32)
    grs_d = dram_in(nc, "grs", [128, 16, 512], F32)
    gsel_d = dram_in(nc, "gsel", [128, 2, 8], F32)
    gn_d = dram_in(nc, "gnorm", [128], F32)
    o_gla = dram_out(nc, "o_gla", [128, 16, 512], F32)
    with _phase(nc) as kb:
        S_ = lambda n, s, d: sb(kb, n, s, d)
        D = lambda fn, r, w: kb.op("dve", fn, reads=r, writes=w)
        A = lambda fn, r, w: kb.op("act", fn, reads=r, writes=w)
        dec = S_("dec", [128, 2, 128], F32); gsel = S_("gsel", [128, 2, 8], F32); gnb = S_("gnb", [128, 128], F32)
        for jj in range(4):
            kb.dma("sp", dec[:].rearrange("p a (i j c) -> p a i j c", j=4, c=2)[:, :, :, jj, :], dec_d[0, jj].rearrange("p (a i c) -> p a i c", a=2, c=2), writes=[dec.r()])
        kb.dma("sp", gsel[:], gsel_d, writes=[gsel.r()])
        kb.dma("sp", gnb[:], gn_d.partition_broadcast(128), writes=[gnb.r()])
        St = S_("St", [128, 2, 128], F32); Ssels = [S_(f"Ssel{i}", [128, 2, 256], F32) for i in range(2)]; Sselb = S_("Sselb", [128, 2, 2, 128], BF16)
        kvb = [S_(f"kvb{i}", [128, 4, 2, 2, 128], BF16) for i in range(2)]
        qd = S_("qd", [128, 2, 128], BF16); oin = S_("oin", [128, 512], F32); grs = S_("grs", [128, 512], F32)
        og = S_("og", [128, 512], F32); sq = S_("sq", [128, 512], F32); ssq = S_("ssq", [128, 4], F32); ssq2 = S_("ssq2", [128, 4], F32)
        BA = ps(kb, "BA", [128, 512], F32); BB = ps(kb, "BB", [128, 512], F32)
        D(lambda e: e.memset(St[:], 0.0), [], [St.r(("p", 0)), St.r(("p", 1))])
        Sflat = St[:].rearrange("p a b -> p (a b)")

        def scan(i):
            Ssel = Ssels[i % 2]
            D(lambda e: e.memset(Ssel[:], 0.0), [], [Ssel.r(("c", 0)), Ssel.r(("c", 1))])
            kvb_ = kvb[i % 2]
            for jj in range(4):
                for q in range(2):
                    kb.dma("sp", kvb_[q * 64:(q + 1) * 64, jj], kv_d[q, jj].rearrange("p (a n d) -> p a n d", a=2, d=128)[:, :, 2 * i:2 * i + 2, :], writes=[kvb_.r()])
            for k in range(8):
                n = 8 * i + k
                for c in range(2):
                    D(lambda e, c=c, k=k: e.scalar_tensor_tensor(out=Ssel[:, c, :], in0=Sflat, scalar=gsel[:, c, k:k + 1], in1=Ssel[:, c, :], op0=ALU.mult, op1=ALU.add),
                      [St.r(("p", 0)), St.r(("p", 1)), gsel.r(), Ssel.r(("c", c))], [Ssel.r(("c", c))])
                for p in range(2):
                    D(lambda e, p=p, n=n, k=k: e.scalar_tensor_tensor(out=St[:, p, :], in0=St[:, p, :], scalar=dec[:, p, n:n + 1], in1=kvb_[:, k // 2, p, k % 2, :], op0=ALU.mult, op1=ALU.add),
                      [St.r(("p", p)), dec.r(), kvb_.r()], [St.r(("p", p))])
                yield

        def epilogue(i):
            Ssel = Ssels[i % 2]
            A(lambda e: e.copy(Sselb[:].rearrange("p c a b -> p (c a b)"), Ssel[:].rearrange("p c x -> p (c x)")), [Ssel.r(("c", 0)), Ssel.r(("c", 1))], [Sselb.r()])
            kb.dma("sp", qd[:], qd_d[:, :, i * 128:(i + 1) * 128], writes=[qd.r()])
            kb.dma("sp", oin[:], oin_d[:, i, :], writes=[oin.r()])
            kb.dma("sp", grs[:], grs_d[:, i, :], writes=[grs.r()])
            yield
            fns = []
            for c in range(2):
                for p in range(2):
                    for half in range(2):
                        bank = BA if half == 0 else BB
                        fns.append(lambda e, c=c, p=p, half=half, bank=bank: e.matmul(bank[:, (c * 2 + p) * 128:(c * 2 + p + 1) * 128], qd[half * 64:(half + 1) * 64, p, :], Sselb[half * 64:(half + 1) * 64, c, p, :], start=True, stop=True))
            kb.group("pe", fns, reads=[qd.r(), Sselb.r()], writes=[BA.r(), BB.r()])
            yield
            for h in range(4):
                p, half = divmod(h, 2)
                bank = BA if half == 0 else BB
                for c in range(2):
                    rows = slice(c * 64, (c + 1) * 64)
                    D(lambda e, h=h, c=c, p=p, bank=bank, rows=rows: e.tensor_tensor(out=og[rows, h * 128:(h + 1) * 128], in0=bank[rows, (c * 2 + p) * 128:(c * 2 + p + 1) * 128], in1=oin[rows, h * 128:(h + 1) * 128], op=ALU.add),
                      [bank.r(), oin.r()], [og.r(("h", h))])
                yield
            OG = [og.r(("h", h)) for h in range(4)]
            D(lambda e: e.tensor_tensor(out=sq[:], in0=og[:], in1=og[:], op=ALU.mult), OG, [sq.r()])
            yield
            D(lambda e: e.tensor_reduce(out=ssq[:], in_=sq[:].rearrange("p (h d) -> p h d", h=4), axis=AX.X, op=ALU.add), [sq.r()], [ssq.r()])
            yield
            kb.op("dve", lambda e: e.tensor_scalar(out=ssq2[:], in0=ssq[:], scalar1=1.0 / 128, scalar2=EPS, op0=ALU.mult, op1=ALU.add), reads=[ssq.r()], writes=[ssq2.r()])
            yield
            kb.op("act", lambda e: e.activation(out=ssq2[:], in_=ssq2[:], func=AF.Sqrt), reads=[ssq2.r()], writes=[ssq2.r()])
            yield
            kb.op("dve", lambda e: e.reciprocal(ssq[:], ssq2[:]), reads=[ssq2.r()], writes=[ssq.r()])
            yield
            og3 = og[:].rearrange("p (h d) -> p h d", h=4)
            D(lambda e: e.tensor_tensor(out=og3, in0=og3, in1=ssq[:].unsqueeze(2).to_broadcast([128, 4, 128]), op=ALU.mult), OG + [ssq.r()], OG)
            yield
            D(lambda e: e.tensor_tensor(out=og3, in0=og3, in1=gnb[:].unsqueeze(1).to_broadcast([128, 4, 128]), op=ALU.mult), OG + [gnb.r()], OG)
            yield
            D(lambda e: e.tensor_tensor(out=og[:], in0=og[:], in1=grs[:], op=ALU.mult), OG + [grs.r()], OG)
            kb.dma("sp", o_gla[:, i, :], og[:], reads=OG, is_output=True)
            yield

        for _ in scan(0):
            pass
        for i in range(NB):
            ep = epilogue(i)
            if i + 1 < NB:
                for _ in scan(i + 1):
                    for _n in range(2):
                        try:
                            next(ep)
                        except StopIteration:
                            break
            for _ in ep:
                pass
        kb.finish()
        print("GLA instructions:", kb.nins, {k: len(v) for k, v in kb.q.items()})
    return nc


def gla_sel(j):
    g = np.zeros((128, 2, 8), np.float32)
    for c in range(2):
        g[:, c, 2 * j + c] = 1.0
    return g


NT = 16
POOL_W = (2, 4, 8, 16)


def build_l1b():
    nc = _get_nc()
    xs = dram_in(nc, "xs", [NT * 128, 1024], F32)
    cT_d = dram_in(nc, "cT", [128, 8], F32)
    ada_w = dram_in(nc, "ada_w", [1024, 6144], F32)
    ada_b = dram_in(nc, "ada_b", [1, 6144], F32)
    mixn_d = dram_in(nc, "mixn", [128, 8], F32)
    w_in = dram_in(nc, "w_in", [1024, 2048], F32)
    qn_d = dram_in(nc, "q_norm", [128], F32)
    kn_d = dram_in(nc, "k_norm", [128], F32)
    identb_d = dram_in(nc, "identb", [128, 128], BF16)
    o_mod = dram_out(nc, "o_mod", [1, 6144], F32)
    o_qT = dram_out(nc, "o_qT", [128, 4, NT * 128], BF16)
    o_kT = dram_out(nc, "o_kT", [128, 4, NT * 128], BF16)
    o_v = dram_out(nc, "o_v", [128, NT, 512], BF16)
    o_u = dram_out(nc, "o_u", [128, NT, 512], F32)
    o_uh = dram_out(nc, "o_uh", [256, 512], F32)
    with _phase(nc) as kb:
        S = lambda n, s, d: sb(kb, n, s, d)
        D = lambda fn, r, w: kb.op("dve", fn, reads=r, writes=w)
        A = lambda fn, r, w: kb.op("act", fn, reads=r, writes=w)
        identb = S("identb", [128, 128], BF16); mixn = S("mixn", [128, 8], F32)
        qnb = S("qnb", [128, 128], F32); knb = S("knb", [128, 128], F32)
        for t, d in ((identb, identb_d), (mixn, mixn_d)):
            kb.dma("sp", t[:], d, writes=[t.r()])
        for t, d in ((qnb, qn_d), (knb, kn_d)):
            kb.dma("sp", t[:], d.partition_broadcast(128), writes=[t.r()])
        wb = S("wb", [128, 8, 2048], BF16)
        for k in range(8):
            kb.dma("pool", wb[:, k, :], w_in[k * 128:(k + 1) * 128, :], writes=[wb.r(("k", k))])
        WB = [wb.r(("k", k)) for k in range(8)]
        modrow = S("modrow", [1, 6144], F32); modP = S("modP", [128, 48], F32)
        pT = ps(kb, "pT", [128, 8, 128], BF16)
        pp = [ps(kb, f"pp{i}", [128, 512], F32) for i in range(2)]
        pA = ps(kb, "pA", [128, 512], F32); pB = ps(kb, "pB", [128, 512], F32)
        emit_mod(kb, cT_d, ada_w, ada_b, modrow, modP, "m1", pA, pB)
        kb.dma("sp", o_mod, modrow[:], reads=[modrow.r()], is_output=True)
        a1 = S("a1", [128, 8], F32)
        D(lambda e: e.scalar_tensor_tensor(out=a1[:], in0=modP[:, 8:16], scalar=1.0, in1=mixn[:], op0=ALU.add, op1=ALU.mult), [modP.r(), mixn.r()], [a1.r()])
        xt = [S(f"xt{i}", [128, 1024], F32) for i in range(2)]
        junk = S("junk", [128, 1024], BF16); xn = S("xn", [128, 1024], BF16)
        st = S("st", [128, 1], F32); st2 = S("st2", [128, 1], F32)
        hT = [S(f"hT{i}", [128, 8, 128], BF16) for i in range(2)]
        proj = S("proj", [128, 2048], F32)
        sq = S("sq", [128, 512], F32); ssq = S("ssq", [128, 4], F32); ssq2 = S("ssq2", [128, 4], F32)
        qn = S("qn", [128, 4, 128], F32); qr = S("qr", [128, 4, 128], BF16)
        sqb = S("sqb", [128, 512], F32); ssqb = S("ssqb", [128, 4], F32); ssq2b = S("ssq2b", [128, 4], F32)
        qnb2 = S("qnb2", [128, 4, 128], F32); qrb = S("qrb", [128, 4, 128], BF16)
        TMP = [(sq, ssq, ssq2, qn, qr), (sqb, ssqb, ssq2b, qnb2, qrb)]
        qT_t = S("qT_t", [128, 4, 128], BF16); kT_t = S("kT_t", [128, 4, 128], BF16)
        vb = S("vb", [128, 512], BF16)
        for i in range(NT):
            x_ = xt[i % 2]; h_ = hT[i % 2]
            kb.dma("sp", x_[:], xs[i * 128:(i + 1) * 128, :], writes=[x_.r()])
            emit_norm_hT(kb, x_, h_, a1, modP_sh(modP), identb, junk, xn, pT, st, st2)
            for cb in range(4):
                p_ = pp[cb % 2]
                kb.group("pe", [(lambda e, k=k, cb=cb, p_=p_, h_=h_: e.matmul(p_[:], h_[:, k, :], wb[:, k, cb * 512:(cb + 1) * 512], start=(k == 0), stop=(k == 7))) for k in range(8)],
                         reads=[h_.r()] + WB, writes=[p_.r()])
                if cb % 2 == 0:
                    A(lambda e, cb=cb, p_=p_: e.copy(proj[:, cb * 512:(cb + 1) * 512], p_[:]), [p_.r()], [proj.r(("c", cb))])
                else:
                    D(lambda e, cb=cb, p_=p_: e.tensor_copy(proj[:, cb * 512:(cb + 1) * 512], p_[:]), [p_.r()], [proj.r(("c", cb))])
            PR = [proj.r(("c", cb)) for cb in range(4)]
            CH = [[], [], []]
            for ci, (c0, gain, dstT, dstD) in enumerate(((512, qnb, qT_t, o_qT), (1024, knb, kT_t, o_kT))):
                kb.rec = CH[ci]
                sq_, ssq_, ssq2_, qn_, qr_ = TMP[ci]
                src = proj[:, c0:c0 + 512]
                D(lambda e, src=src, sq_=sq_: e.tensor_tensor(out=sq_[:], in0=src, in1=src, op=ALU.mult), PR, [sq_.r()])
                D(lambda e, sq_=sq_, ssq_=ssq_: e.tensor_reduce(out=ssq_[:], in_=sq_[:].rearrange("p (h d) -> p h d", h=4), axis=AX.X, op=ALU.add), [sq_.r()], [ssq_.r()])
                emit_rstd(kb, ssq_, 128, ssq2_)
                s3 = src.rearrange("p (h d) -> p h d", h=4)
                D(lambda e, s3=s3, qn_=qn_, ssq_=ssq_: e.tensor_tensor(out=qn_[:], in0=s3, in1=ssq_[:].unsqueeze(2).to_broadcast([128, 4, 128]), op=ALU.mult), PR + [ssq_.r()], [qn_.r()])
                D(lambda e, gain=gain, qn_=qn_, qr_=qr_: e.tensor_tensor(out=qr_[:], in0=qn_[:], in1=gain[:].unsqueeze(1).to_broadcast([128, 4, 128]), op=ALU.mult), [qn_.r(), gain.r()], [qr_.r()])
                kb.group("pe", [(lambda e, h=h, qr_=qr_, ci=ci: e.transpose(pT[:, 4 * ci + h, :], qr_[:, h, :], identb[:])) for h in range(4)], reads=[qr_.r(), identb.r()], writes=[pT.r()])
                A(lambda e, dstT=dstT, ci=ci: e.copy(dstT[:], pT[:, 4 * ci:4 * ci + 4, :]), [pT.r()], [dstT.r()])
                kb.dma("sp", dstD[:, :, i * 128:(i + 1) * 128], dstT[:], reads=[dstT.r()], is_output=True)
            kb.rec = CH[2]
            D(lambda e: e.tensor_copy(vb[:], proj[:, 1536:2048]), PR, [vb.r()])
            kb.dma("sp", o_v[:, i, :], vb[:], reads=[vb.r()], is_output=True)
            kb.dma("sp", o_u[:, i, :], proj[:, 0:512], reads=PR, is_output=True)
            kb.dma("sp", o_uh[i * 16:(i + 1) * 16, :], proj[112:128, 0:512], reads=PR, is_output=True)
            kb.emit_roundrobin(CH)
        kb.finish()
        print("L1b instructions:", kb.nins, {k: len(v) for k, v in kb.q.items()})
    return nc


def build_pool():
    nc = _get_nc()
    u_d = dram_in(nc, "u", [128, NT, 512], F32)
    guh_d = dram_in(nc, "guh", [1, 4, 256, 512], F32)
    hsel_d = dram_in(nc, "hsel", [128, 4], F32)
    band_d = dram_in(nc, "band", [128, 4, 128], F32)
    band0_d = dram_in(nc, "band0", [128, 4, 128], F32)
    bandhc_d = dram_in(nc, "bandhc", [128, 2, NT, 4, 128], F32)
    pw_d = dram_in(nc, "pool_w", [4, 128, 128], F32)
    psc_d = dram_in(nc, "pool_scale", [512], F32)
    o_pool = dram_out(nc, "o_pool", [128, NT, 512], F32)
    with _phase(nc) as kb:
        S = lambda n, s, d: sb(kb, n, s, d)
        D = lambda fn, r, w: kb.op("dve", fn, reads=r, writes=w)
        A = lambda fn, r, w: kb.op("act", fn, reads=r, writes=w)
        hsel = S("hsel", [128, 4], F32); band = S("band", [128, 4, 128], F32); band0 = S("band0", [128, 4, 128], F32)
        pscb = S("pscb", [128, 512], F32); pw = S("pw", [128, 4, 128], BF16)
        for t, d in ((hsel, hsel_d), (band, band_d), (band0, band0_d)):
            kb.dma("sp", t[:], d, writes=[t.r()])
        kb.dma("sp", pscb[:], psc_d.partition_broadcast(128), writes=[pscb.r()])
        kb.dma("pool", pw[:], pw_d.rearrange("g c d -> c g d"), writes=[pw.r()])
        uhc = S("uhc", [128, 4, 2, 512], F32); uhs = S("uhs", [128, 2, 512], F32)
        for jj in range(4):
            for hh in range(2):
                kb.dma("sp", uhc[:, jj, hh, :], guh_d[0, jj, hh * 128:(hh + 1) * 128, :], writes=[uhc.r()])
        uc = uhc[:].rearrange("p j h c -> p j (h c)"); us = uhs[:].rearrange("p h c -> p (h c)")
        D(lambda e: e.tensor_scalar(out=us, in0=uc[:, 0, :], scalar1=hsel[:, 0:1], scalar2=None, op0=ALU.mult), [uhc.r(), hsel.r()], [uhs.r()])
        for jj in range(1, 4):
            D(lambda e, jj=jj: e.scalar_tensor_tensor(out=us, in0=uc[:, jj, :], scalar=hsel[:, jj:jj + 1], in1=us, op0=ALU.mult, op1=ALU.add), [uhc.r(), hsel.r(), uhs.r()], [uhs.r()])
        pA = ps(kb, "pA", [128, 512], F32); pB = ps(kb, "pB", [128, 512], F32)
        ut = [S(f"ut{i}", [128, 512], F32) for i in range(2)]
        bh = [S(f"bh{i}", [128, 2, 4, 128], F32) for i in range(2)]
        plT = S("plT", [128, 4, 128], BF16); opl = S("opl", [128, 512], F32)
        for i in range(NT):
            u_ = ut[i % 2]; bh_ = bh[i % 2]
            kb.dma("sp", u_[:], u_d[:, i, :], writes=[u_.r()])
            kb.dma("sp", bh_[:], bandhc_d[:, :, i, :, :], writes=[bh_.r()])
            fns = []
            for g in range(4):
                bo = band0[:, g, :] if i == 0 else band[:, g, :]
                fns.append(lambda e, g=g, bo=bo, u_=u_: e.matmul(pA[:, g * 128:(g + 1) * 128], u_[:, g * 128:(g + 1) * 128], bo, start=True, stop=False))
                fns.append(lambda e, g=g, bh_=bh_: e.matmul(pA[:, g * 128:(g + 1) * 128], uhs[:, 0, g * 128:(g + 1) * 128], bh_[:, 0, g, :], start=False, stop=False))
                fns.append(lambda e, g=g, bh_=bh_: e.matmul(pA[:, g * 128:(g + 1) * 128], uhs[:, 1, g * 128:(g + 1) * 128], bh_[:, 1, g, :], start=False, stop=True))
            kb.group("pe", fns, reads=[u_.r(), bh_.r(), uhs.r(), band.r(), band0.r()], writes=[pA.r()])
            A(lambda e: e.copy(plT[:].rearrange("p g t -> p (g t)"), pA[:]), [pA.r()], [plT.r()])
            kb.group("pe", [(lambda e, g=g: e.matmul(pB[:, g * 128:(g + 1) * 128], plT[:, g, :], pw[:, g, :], start=True, stop=True)) for g in range(4)], reads=[plT.r(), pw.r()], writes=[pB.r()])
            D(lambda e: e.tensor_tensor(out=opl[:], in0=pB[:], in1=pscb[:], op=ALU.mult), [pB.r(), pscb.r()], [opl.r()])
            kb.dma("sp", o_pool[:, i, :], opl[:], reads=[opl.r()], is_output=True)
        kb.finish()
    return nc


def pool_consts_core(j):
    s_ = np.arange(128)[:, None]; t_ = np.arange(128)[None, :]
    band = np.zeros((128, 4, 128), np.float32); band_first = np.zeros((128, 4, 128), np.float32)
    bandhc = np.zeros((128, 2, 16, 4, 128), np.float32)
    for g, w in enumerate(POOL_W):
        inwin = ((t_ - s_) >= 0) & ((t_ - s_) <= w - 1)
        band[:, g, :] = inwin / float(w) - (s_ == t_)
        cnt = np.minimum(t_ + 1.0, float(w))
        band_first[:, g, :] = inwin / cnt - (s_ == t_)
        for i in range(16):
            isrc = i if j > 0 else i - 1
            if isrc < 0:
                continue
            half, slot = divmod(isrc, 8)
            for r in range(16):
                srel = r - 16
                row = (((np.arange(128) - srel) >= 0) & ((np.arange(128) - srel) <= w - 1)) / float(w)
                bandhc[slot * 16 + r, half, i, g, :] = row
    hsel = np.zeros((128, 4), np.float32)
    hsel[:, (j - 1) % 4] = 1.0
    return band, (band_first if j == 0 else band), bandhc, hsel


def pool_consts():
    s = np.arange(128)[:, None]; t = np.arange(128)[None, :]
    band = np.zeros((128, 4, 128), np.float32); band_first = np.zeros((128, 4, 128), np.float32)
    bandh = np.zeros((128, 8, 4, 128), np.float32)
    for g, w in enumerate(POOL_W):
        inwin = ((t - s) >= 0) & ((t - s) <= w - 1)
        band[:, g, :] = inwin / float(w) - (s == t)
        cnt = np.minimum(t + 1.0, float(w))
        band_first[:, g, :] = inwin / cnt - (s == t)
        for r in range(16):
            srel = r - 16
            row = (((np.arange(128) - srel) >= 0) & ((np.arange(128) - srel) <= w - 1)) / float(w)
            for slot in range(8):
                bandh[slot * 16 + r, slot, g, :] = row
    return band, bandh, band_first


SCALE = float(128 ** -0.5)


def build_sb(QT=tuple(range(16))):
    nc = _get_nc()
    qT_d = dram_in(nc, "qT", [128, 4, 2048], BF16)
    kT_d = dram_in(nc, "kT", [2, 4, 64, 8192], BF16)
    v_d = dram_in(nc, "v", [2, 4, 64, 8192], BF16)
    mask_d = dram_in(nc, "sbmask", [128, 512], F32)
    U_d = dram_in(nc, "U", [128, 128], BF16)
    ones_d = dram_in(nc, "ones", [128, 128], BF16)
    o_sb = dram_out(nc, "o_sb", [128, 16, 512], F32)
    with _phase(nc) as kb:
        S = lambda n, s, d: sb(kb, n, s, d)
        D = lambda fn, r, w: kb.op("dve", fn, reads=r, writes=w)
        A = lambda fn, r, w: kb.op("act", fn, reads=r, writes=w)
        kT = S("kT", [128, 4, 8192], BF16); v = S("v", [128, 64, 512], BF16)
        for h in range(4):
            for jj in range(4):
                for q in range(2):
                    kb.dma("sp", kT[q * 64:(q + 1) * 64, h, :].rearrange("p (i j s) -> p i j s", j=4, s=128)[:, :, jj, :],
                           kT_d[q, jj].rearrange("p (h i s) -> p h i s", h=4, s=128)[:, h, :, :], writes=[kT.r(("h", h))])
        for jj in range(4):
            for q in range(2):
                kb.dma("sp", v[q * 64:(q + 1) * 64].rearrange("p (i j) c -> p i j c", j=4)[:, :, jj, :], v_d[q, jj].rearrange("p (i c) -> p i c", c=512), writes=[v.r(("g", jj))])
        KT = [kT.r(("h", h)) for h in range(4)]; VV = [v.r(("g", g)) for g in range(8)]
        mask = S("mask", [128, 512], F32); U = S("U", [128, 128], BF16); ones = S("ones", [128, 128], BF16)
        for t, d in ((mask, mask_d), (U, U_d), (ones, ones_d)):
            kb.dma("sp", t[:], d, writes=[t.r()])
        B = [ps(kb, f"B{i}", [128, 512], F32) for i in range(8)]
        qt = S("qt", [128, 4, 128], BF16)
        NBUF = 5
        eb = [S(f"eb{i}", [128, 512], F32) for i in range(NBUF)]
        spb = [S(f"spb{i}", [128, 512], F32) for i in range(NBUF)]
        Lbb = [S(f"Lb{i}", [128, 512], BF16) for i in range(NBUF)]
        tb = [S(f"tb{i}", [128, 512], F32) for i in range(NBUF)]
        wbb = [S(f"wb{i}", [128, 512], BF16) for i in range(NBUF)]
        Csb = S("Csb", [128, 4, 128], F32)
        osb = S("osb", [128, 512], F32)
        Zs = (B[0], B[1], B[2]); As = (B[3], B[4], B[5], B[6]); Ob = B[7]
        qts = [qt, S("qt2", [128, 4, 128], BF16)]
        qss = [S("qs0", [128, 4, 128], BF16), S("qs1", [128, 4, 128], BF16)]

        def make_it(i, m, h, ctr, qt_, qs_):
            Z = Zs[ctr % 3]; Aa = As[ctr % 4]
            e_ = eb[ctr % NBUF]; sp_ = spb[ctr % NBUF]; L_ = Lbb[ctr % NBUF]; t_ = tb[ctr % NBUF]; w_ = wbb[ctr % NBUF]
            diag = (m == i)
            first = diag and h == 0
            last = (m == 0 and h == 3)
            hc = slice(h * 128, (h + 1) * 128)

            def S1():
                if first:
                    kb.dma("sp", qt_[:], qT_d[:, :, i * 128:(i + 1) * 128], writes=[qt_.r()])
                    D(lambda e: e.tensor_scalar(out=qs_[:], in0=qt_[:], scalar1=SCALE, scalar2=None, op0=ALU.mult), [qt_.r()], [qs_.r()])
                kb.group("pe", [(lambda e, x=x: e.matmul(Z[:, x * 128:(x + 1) * 128], kT[:, h, m * 512 + x * 128: m * 512 + (x + 1) * 128], qt_[:, h, :], start=True, stop=True)) for x in range(4)],
                         reads=[KT[h], qt_.r()], writes=[Z.r()])
                A(lambda e: e.activation(out=e_[:], in_=Z[:], func=AF.Exp, scale=-SCALE), [Z.r()], [e_.r()])

            def S2():
                A(lambda e: e.activation(out=sp_[:], in_=e_[:], func=AF.Ln, bias=1.0), [e_.r()], [sp_.r()])
                D(lambda e: e.scalar_tensor_tensor(out=L_[:], in0=Z[:], scalar=-SCALE, in1=sp_[:], op0=ALU.mult, op1=ALU.subtract), [Z.r(), sp_.r()], [L_.r()])
                if diag:
                    D(lambda e: e.tensor_tensor(out=L_[:], in0=L_[:], in1=mask[:], op=ALU.mult), [L_.r(), mask.r()], [L_.r()])

            def S3():
                fns = []
                for x in range(4):
                    fns.append(lambda e, x=x: e.matmul(Aa[:, x * 128:(x + 1) * 128], U[:], L_[:, x * 128:(x + 1) * 128], start=True, stop=False))
                    for x2 in range(x + 1, 4):
                        fns.append(lambda e, x=x, x2=x2: e.matmul(Aa[:, x * 128:(x + 1) * 128], ones[:], L_[:, x2 * 128:(x2 + 1) * 128], start=False, stop=False))
                    fns.append(lambda e, x=x: e.matmul(Aa[:, x * 128:(x + 1) * 128], kT[:, h, m * 512 + x * 128: m * 512 + (x + 1) * 128], qs_[:, h, :], start=False, stop=(x == 3)))
                kb.group("pe", fns, reads=[U.r(), ones.r(), L_.r(), KT[h], qs_.r()], writes=[Aa.r()])
                if not diag:
                    D(lambda e: e.tensor_tensor(out=t_[:].rearrange("p (x q) -> p x q", x=4), in0=Aa[:].rearrange("p (x q) -> p x q", x=4), in1=Csb[:, h, :].unsqueeze(1).to_broadcast([128, 4, 128]), op=ALU.add),
                      [Aa.r(), Csb.r(("h", h))], [t_.r()])

            def S4():
                if diag:
                    A(lambda e: e.activation(out=w_[:], in_=Aa[:], func=AF.Exp), [Aa.r()], [w_.r()])
                else:
                    A(lambda e: e.activation(out=w_[:], in_=t_[:], func=AF.Exp), [t_.r()], [w_.r()])
                if diag:
                    D(lambda e: e.tensor_tensor(out=w_[:], in0=w_[:], in1=mask[:], op=ALU.mult), [w_.r(), mask.r()], [w_.r()])

            def S5():
                if m > 0:
                    kb.group("pe", [(lambda e, x=x: e.matmul(Aa[:, 0:128], ones[:], L_[:, x * 128:(x + 1) * 128], start=(x == 0), stop=(x == 3))) for x in range(4)],
                             reads=[ones.r(), L_.r()], writes=[Aa.r()])
                    if diag:
                        D(lambda e: e.tensor_copy(Csb[:, h, :], Aa[:, 0:128]), [Aa.r()], [Csb.r(("h", h))])
                    else:
                        D(lambda e: e.tensor_tensor(out=Csb[:, h, :], in0=Aa[:, 0:128], in1=Csb[:, h, :], op=ALU.add), [Aa.r(), Csb.r(("h", h))], [Csb.r(("h", h))])
                kb.group("pe", [(lambda e, x=x: e.matmul(Ob[:, hc], w_[:, x * 128:(x + 1) * 128], v[:, m * 4 + x, h * 128:(h + 1) * 128], start=(first and x == 0), stop=(last and x == 3))) for x in range(4)],
                         reads=[w_.r()] + VV, writes=[Ob.r(("h", h))] + ([Ob.r(("h", hh)) for hh in range(4)] if first else []))
                if last:
                    A(lambda e: e.copy(osb[:], Ob[:]), [Ob.r(("h", hh)) for hh in range(4)], [osb.r()])
                    kb.dma("sp", o_sb[:, i, :], osb[:], reads=[osb.r()], is_output=True)
            return (S1, S2, S3, S4, S5)

        its = []
        ctr = 0
        for qi, i in enumerate(QT):
            for m in range(i, -1, -1):
                for h in range(4):
                    its.append(make_it(i, m, h, ctr, qts[qi % 2], qss[qi % 2]))
                    ctr += 1
        emit_skewed(its, 5)
        kb.finish()
        print("SB instructions:", kb.nins, {k: len(v) for k, v in kb.q.items()})
    return nc


def sb_mask(j):
    s = np.arange(128)[:, None]
    col = np.arange(512)[None, :]
    jj = col // 128; t = col % 128
    vis = (jj < j) | ((jj == j) & (s < t))
    return vis.astype(np.float32)


def sb_consts():
    jx = np.arange(128)[:, None]; sx = np.arange(128)[None, :]
    U = (jx >= sx).astype(np.float32).astype(ml_dtypes.bfloat16)
    ones = np.ones((128, 128), np.float32).astype(ml_dtypes.bfloat16)
    return U, ones


NT = 16
GT = 2


def build_tail():
    nc = _get_nc()
    xs = dram_in(nc, "xs", [NT * 128, 1024], F32)
    mixa_d = dram_in(nc, "mixa", [128, NT, 512], F32)
    mixb_d = dram_in(nc, "mixb", [128, NT, 512], F32)
    modrow_d = dram_in(nc, "modrow", [1, 6144], F32)
    fnorm_d = dram_in(nc, "fnorm", [128, 8], F32)
    w_out = dram_in(nc, "w_out", [1024, 1024], F32)
    w1 = dram_in(nc, "w1", [1024, 5632], F32)
    w2 = dram_in(nc, "w2", [2816, 1024], F32)
    identb_d = dram_in(nc, "identb", [128, 128], BF16)
    identf_d = dram_in(nc, "identf", [128, 128], F32)
    o_x = dram_out(nc, "o_x", [NT * 128, 1024], F32)
    with _phase(nc) as kb:
        S = lambda n, s, d: sb(kb, n, s, d)
        D = lambda fn, r, w: kb.op("dve", fn, reads=r, writes=w)
        identb = S("identb", [128, 128], BF16); fnorm = S("fnorm", [128, 8], F32)
        mod48 = S("mod48", [48, 128], F32); identf = S("identf", [128, 128], F32)
        kb.dma("sp", identb[:], identb_d, writes=[identb.r()])
        kb.dma("sp", fnorm[:], fnorm_d, writes=[fnorm.r()])
        kb.dma("sp", mod48[:], modrow_d.rearrange("o (c p) -> (o c) p", p=128), writes=[mod48.r()])
        kb.dma("sp", identf[:], identf_d, writes=[identf.r()])
        woutb = S("woutb", [128, 8, 1024], BF16); w1b = S("w1b", [128, 8, 5632], BF16); w2b = S("w2b", [128, 22, 1024], BF16)
        for k in range(8):
            kb.dma("pool", woutb[:, k, :], w_out[k * 128:(k + 1) * 128, :], writes=[woutb.r(("k", k))])
        for k in range(8):
            kb.dma("pool", w1b[:, k, :], w1[k * 128:(k + 1) * 128, :], writes=[w1b.r(("k", k))])
        for f in range(22):
            kb.dma("pool", w2b[:, f, :], w2[f * 128:(f + 1) * 128, :], writes=[w2b.r(("k", f))])
        WO = [woutb.r(("k", k)) for k in range(8)]; W1 = [w1b.r(("k", k)) for k in range(8)]; W1f = [W1] * 22; W1u = [W1] * 22; W2 = [w2b.r(("k", f)) for f in range(22)]
        pT = ps(kb, "pT", [128, 8, 128], BF16)
        pp = [ps(kb, f"pp{i}", [128, 512], F32) for i in range(2)]
        pg = ps(kb, "pg", [128, 512], F32); pu = ps(kb, "pu", [128, 512], F32)
        modP = S("modP", [128, 48], F32); G1b = S("G1b", [128, 1024], F32); G2b = S("G2b", [128, 1024], F32)
        kb.op("pe", lambda e: e.transpose(pg[:, 0:48], mod48[:], identf[0:48, 0:48]), reads=[mod48.r(), identf.r()], writes=[pg.r()])
        D(lambda e: e.tensor_copy(modP[:], pg[:, 0:48]), [pg.r()], [modP.r()])
        kb.dma("sp", G1b[:], modrow_d[0, 2048:3072].partition_broadcast(128), writes=[G1b.r()])
        kb.dma("sp", G2b[:], modrow_d[0, 5120:6144].partition_broadcast(128), writes=[G2b.r()])
        a2 = S("a2", [128, 8], F32)
        D(lambda e: e.scalar_tensor_tensor(out=a2[:], in0=modP[:, 32:40], scalar=1.0, in1=fnorm[:], op0=ALU.add, op1=ALU.mult), [modP.r(), fnorm.r()], [a2.r()])

        class SH:
            def __getitem__(s, idx):
                p, sl = idx
                return modP[p, slice(sl.start + 24, sl.stop + 24)]
            def r(s, key=None):
                return modP.r(key)
        sh2 = SH()
        xt = [S("xt0", [128, 1024], F32)] * 2
        mixt = [S("mixt0", [128, 8, 128], BF16)] * 2
        x1s = [S(f"x1_{i}", [128, GT, 1024], F32) for i in range(2)]
        xn = S("xn", [128, 1024], BF16); junk = xn
        st = S("st", [128, 1], F32); st2 = S("st2", [128, 1], F32)
        hT4s = [S(f"hT4_{i}", [128, 8, GT * 128], BF16) for i in range(2)]
        sg = S("sg", [128, GT * 128], F32); actT = S("actT", [128, 22, GT * 128], BF16)
        yt = S("yt", [128, 1024], F32)
        mst = yt

        def front(g):
            x1 = x1s[g % 2]; hT4 = hT4s[g % 2]
            for t in range(GT):
                i = g * GT + t
                x_ = xt[i % 2]; m_ = mixt[i % 2]
                kb.dma("sp", x_[:], xs[i * 128:(i + 1) * 128, :], writes=[x_.r()])
                kb.dma("sp", mst[:, 0:512], mixa_d[:, i, :], writes=[yt.r(("c", 0))])
                kb.dma("sp", mst[:, 512:1024], mixb_d[:, i, :], writes=[yt.r(("c", 1))])
                D(lambda e: e.tensor_copy(xn[:], mst[:]), [yt.r(("c", 0)), yt.r(("c", 1))], [xn.r()])
                yield
                kb.group("pe", [(lambda e, k=k: e.transpose(pT[:, k, :], xn[:, k * 128:(k + 1) * 128], identb[:])) for k in range(8)], reads=[xn.r(), identb.r()], writes=[pT.r()])
                kb.op("act", lambda e, m_=m_: e.copy(m_[:], pT[:]), reads=[pT.r()], writes=[m_.r()])
                yield
                for cb in range(2):
                    p_ = pp[cb]
                    kb.group("pe", [(lambda e, k=k, cb=cb, p_=p_, m_=m_: e.matmul(p_[:], m_[:, k, :], woutb[:, k, cb * 512:(cb + 1) * 512], start=(k == 0), stop=(k == 7))) for k in range(8)],
                             reads=[m_.r()] + WO, writes=[p_.r()])
                    D(lambda e, cb=cb, p_=p_: e.tensor_tensor(out=mst[:, cb * 512:(cb + 1) * 512], in0=p_[:], in1=G1b[:, cb * 512:(cb + 1) * 512], op=ALU.mult), [p_.r(), G1b.r()], [yt.r(("c", cb))])
                    yield
                    D(lambda e, cb=cb, x_=x_, t=t: e.tensor_tensor(out=x1[:, t, cb * 512:(cb + 1) * 512], in0=mst[:, cb * 512:(cb + 1) * 512], in1=x_[:, cb * 512:(cb + 1) * 512], op=ALU.add),
                      [yt.r(("c", cb)), x_.r()], [x1.r(("t", t, cb))])
                    yield
                kb.op("act", lambda e, t=t: e.activation(out=junk[:], in_=x1[:, t, :], func=AF.Square, accum_out=st[:, 0:1]),
                      reads=[x1.r(("t", t, 0)), x1.r(("t", t, 1))], writes=[junk.r(), st.r()])
                yield
                kb.op("dve", lambda e: e.tensor_scalar(out=st2[:], in0=st[:], scalar1=1.0 / 1024, scalar2=EPS, op0=ALU.mult, op1=ALU.add), reads=[st.r()], writes=[st2.r()])
                yield
                kb.op("act", lambda e: e.activation(out=st2[:], in_=st2[:], func=AF.Sqrt), reads=[st2.r()], writes=[st2.r()])
                yield
                kb.op("dve", lambda e: e.reciprocal(st[:], st2[:]), reads=[st2.r()], writes=[st.r()])
                yield
                kb.op("act", lambda e, t=t: e.activation(out=xn[:], in_=x1[:, t, :], func=AF.Copy, scale=st[:, 0:1]), reads=[x1.r(("t", t, 0)), x1.r(("t", t, 1)), st.r()], writes=[xn.r()])
                yield
                kb.group("pe", [(lambda e, k=k: e.transpose(pT[:, k, :], xn[:, k * 128:(k + 1) * 128], identb[:])) for k in range(8)], reads=[xn.r(), identb.r()], writes=[pT.r()])
                yield
                for k in range(8):
                    kb.op("act", lambda e, k=k, t=t: e.activation(out=hT4[:, k, t * 128:(t + 1) * 128], in_=pT[:, k, :], func=AF.Identity, scale=a2[:, k:k + 1], bias=modP[:, 24 + k:25 + k]),
                          reads=[pT.r(), a2.r(), modP.r()], writes=[hT4.r(("t", t))])
                    if k % 4 == 3:
                        yield

        def drain(gen, n=None):
            cnt = 0
            for _ in gen:
                cnt += 1
                if n is not None and cnt >= n:
                    return

        NG = NT // GT
        gens = [front(g) for g in range(NG)]
        drain(gens[0])
        for g in range(NG):
            x1 = x1s[g % 2]; hT4 = hT4s[g % 2]
            HT = [hT4.r(("t", t)) for t in range(GT)]
            for f in range(22):
                kb.group("pe", [(lambda e, k=k, f=f: e.matmul(pg[:, 0:GT * 128], w1b[:, k, f * 128:(f + 1) * 128], hT4[:, k, :], start=(k == 0), stop=(k == 7))) for k in range(8)],
                         reads=HT + W1f[f], writes=[pg.r()])
                kb.group("pe", [(lambda e, k=k, f=f: e.matmul(pu[:, 0:GT * 128], w1b[:, k, 2816 + f * 128:2816 + (f + 1) * 128], hT4[:, k, :], start=(k == 0), stop=(k == 7))) for k in range(8)],
                         reads=HT + W1u[f], writes=[pu.r()])
                kb.op("act", lambda e: e.activation(out=sg[:], in_=pg[:, 0:GT * 128], func=AF.Silu), reads=[pg.r()], writes=[sg.r()])
                D(lambda e, f=f: e.tensor_tensor(out=actT[:, f, :], in0=sg[:], in1=pu[:, 0:GT * 128], op=ALU.mult), [sg.r(), pu.r()], [actT.r(("f", f))])
                if g + 1 < NG:
                    drain(gens[g + 1], 2)
            if g + 1 < NG:
                drain(gens[g + 1])
            AT = [actT.r(("f", f)) for f in range(22)]
            for t in range(GT):
                i = g * GT + t
                for cb in range(2):
                    p_ = pp[cb]
                    kb.group("pe", [(lambda e, f=f, cb=cb, p_=p_, t=t: e.matmul(p_[:], actT[:, f, t * 128:(t + 1) * 128], w2b[:, f, cb * 512:(cb + 1) * 512], start=(f == 0), stop=(f == 21))) for f in range(22)],
                             reads=AT + W2, writes=[p_.r()])
                    D(lambda e, cb=cb, p_=p_: e.tensor_tensor(out=yt[:, cb * 512:(cb + 1) * 512], in0=p_[:], in1=G2b[:, cb * 512:(cb + 1) * 512], op=ALU.mult), [p_.r(), G2b.r()], [yt.r(("c", cb))])
                    D(lambda e, cb=cb, t=t: e.tensor_tensor(out=yt[:, cb * 512:(cb + 1) * 512], in0=yt[:, cb * 512:(cb + 1) * 512], in1=x1[:, t, cb * 512:(cb + 1) * 512], op=ALU.add),
                      [yt.r(("c", cb)), x1.r(("t", t, cb))], [yt.r(("c", cb))])
                kb.dma("sp", o_x[i * 128:(i + 1) * 128, :], yt[:], reads=[yt.r(("c", 0)), yt.r(("c", 1))], is_output=True)
        kb.finish()
        print("tail instructions:", kb.nins, {k: len(v) for k, v in kb.q.items()})
    return nc


DBG = False


def build_fused():
    FX.active = True
    FX.nc = bass.Bass("TRN2", target_bir_lowering=False)
    FX.ext = {}
    FX.n_phase = 0
    nc = FX.nc
    es = ExitStack()
    es.__enter__()
    kb = KB(nc, es)
    kb.fused = True
    kb.last_phase = False
    FX.kb = kb
    E = fx_ext
    I = lambda name, shape, dt: nc.dram_tensor(name, list(shape), dt).ap()
    RG = [[0, 1, 2, 3], [4, 5, 6, 7]]

    def allgather(pairs, n_wait=None):
        kb.pes = ExitStack()
        skip = []
        for pi, (src, dst) in enumerate(pairs):
            nq = dst.shape[0]
            rp = src.shape[0] // nq
            for q in range(nq):
                si = src[q * rp:(q + 1) * rp, :]
                do = dst[q].rearrange("j p c -> (j p) c")
                tok = kb.coll(lambda e, si=si, do=do: e.collective_compute("AllGather", ALU.bypass, replica_groups=RG, ins=[si], outs=[do]))
                if n_wait is not None and pi >= n_wait:
                    skip.append(tok[1])
        kb.end_phase(skip=tuple(skip))

    common = dict(identb=E("identb", [128, 128], BF16), identf=E("identf", [128, 128], F32), cT=E("cT", [128, 8], F32))
    xs = E("xs", [2048, 1024], F32)
    i1 = dict(o_mod=I("i_mod0", [1, 6144], F32), o_qT=I("i_qT0", [128, 4, 2048], BF16), o_kT=I("i_kT0", [128, 4, 2048], BF16), o_v=I("i_v0", [128, 16, 512], BF16),
              o_iqT=I("i_iqT", [128, 4, 2048], BF16), o_ikT=I("i_ikT", [128, 2048], BF16), o_iw=I("i_iw", [128, 16, 8], F32), o_oin=I("i_oin", [128, 16, 512], F32),
              o_qdT=I("i_qdT", [128, 2, 2048], BF16), o_kv=I("i_kv", [128, 2, 32, 128], BF16), o_dec=I("i_dec", [128, 2, 32], F32), o_gr=I("i_gr", [128, 16, 512], F32))
    FX.remap = dict(common, xs=xs, pos=E("pos", [128, 16], I32), ada_w=E("ada_w0", [1024, 6144], F32), ada_b=E("ada_b0", [1, 6144], F32),
                    mixn=E("mixn0", [128, 8], F32), w_in=E("ab_w_in", [1024, 3672], F32), gate_up=E("gate_up", [16, 256], F32), gate_b=E("gate_b", [256], F32),
                    q_norm=E("dsa_qn", [128], F32), k_norm=E("dsa_kn", [128], F32), invf=E("invf", [64], F32), tri3=E("tri3", [128, 3, 128], F32),
                    csel=E("csel", [128, 2], F32), amask=E("amask", [128, 128], F32), **i1)
    build_l1()
    G_kT0 = I("g_kT0", [2, 4, 64, 8192], BF16); G_v0 = I("g_v0", [2, 4, 64, 8192], BF16); G_ik = I("g_ik", [1, 4, 128, 2048], BF16)
    G_kv = I("g_kv", [2, 4, 64, 8192], BF16); G_dec = I("g_dec", [1, 4, 128, 64], F32)
    allgather([(i1["o_kv"].rearrange("p a n d -> p (a n d)"), G_kv), (i1["o_dec"].rearrange("p a n -> p (a n)"), G_dec),
               (i1["o_kT"].rearrange("p h t -> p (h t)"), G_kT0), (i1["o_v"].rearrange("p i c -> p (i c)"), G_v0), (i1["o_ikT"], G_ik)], n_wait=2)
    i_gla = I("i_gla", [128, 16, 512], F32)
    FX.remap = dict(common, kv=G_kv, dec=G_dec, qdT=i1["o_qdT"], oin=i1["o_oin"], grs=i1["o_gr"], gsel=E("gsel", [128, 2, 8], F32), gnorm=E("gnorm", [128], F32), o_gla=i_gla)
    build_gla()
    i_dsa = I("i_dsa", [128, 16, 512], F32)
    FX.remap = dict(common, qT=i1["o_qT"], iqT=i1["o_iqT"], iw=i1["o_iw"], kT=G_kT0, v=G_v0, ikT=G_ik, mneg=E("mneg", [128, 512], F32), mpos=E("mpos", [128, 512], F32),
                    pow2=E("pow2", [128, NIT + 1], F32), o_dsa=i_dsa)
    build_dsa()
    i_x1 = I("i_x1", [2048, 1024], F32)
    FX.remap = dict(common, xs=xs, mixa=i_gla, mixb=i_dsa, modrow=i1["o_mod"], fnorm=E("fnorm0", [128, 8], F32), w_out=E("ab_w_out", [1024, 1024], F32),
                    w1=E("w1_0", [1024, 5632], F32), w2=E("w2_0", [2816, 1024], F32), o_x=i_x1)
    build_tail()
    i5 = dict(o_mod=I("i_mod1", [1, 6144], F32), o_qT=I("i_qT1", [128, 4, 2048], BF16), o_kT=I("i_kT1", [128, 4, 2048], BF16), o_v=I("i_v1", [128, 16, 512], BF16),
              o_u=I("i_u", [128, 16, 512], F32), o_uh=I("i_uh", [256, 512], F32))
    FX.remap = dict(common, xs=i_x1, ada_w=E("ada_w1", [1024, 6144], F32), ada_b=E("ada_b1", [1, 6144], F32), mixn=E("mixn1", [128, 8], F32),
                    w_in=E("cd_w_in", [1024, 2048], F32), q_norm=E("sb_qn", [128], F32), k_norm=E("sb_kn", [128], F32), **i5)
    build_l1b()
    G_kT1 = I("g_kT1", [2, 4, 64, 8192], BF16); G_v1 = I("g_v1", [2, 4, 64, 8192], BF16); G_uh = I("g_uh", [1, 4, 256, 512], F32)
    allgather([(i5["o_uh"], G_uh), (i5["o_kT"].rearrange("p h t -> p (h t)"), G_kT1), (i5["o_v"].rearrange("p i c -> p (i c)"), G_v1)], n_wait=1)
    i_pool = I("i_pool", [128, 16, 512], F32)
    FX.remap = dict(common, u=i5["o_u"], guh=G_uh, hsel=E("hsel", [128, 4], F32), band=E("band", [128, 4, 128], F32), band0=E("band0", [128, 4, 128], F32),
                    bandhc=E("bandhc", [128, 2, 16, 4, 128], F32), pool_w=E("pool_w", [4, 128, 128], F32), pool_scale=E("pool_scale", [512], F32), o_pool=i_pool)
    build_pool()
    i_sb = I("i_sb", [128, 16, 512], F32)
    FX.remap = dict(common, qT=i5["o_qT"], kT=G_kT1, v=G_v1, sbmask=E("sbmask", [128, 512], F32), U=E("U", [128, 128], BF16), ones=E("ones", [128, 128], BF16), o_sb=i_sb)
    build_sb()
    if DBG:
        kb.pes = ExitStack()
        for nm, ap in (("d_dsa", i_dsa), ("d_gla", i_gla), ("d_pool", i_pool), ("d_sb", i_sb)):
            o = nc.dram_tensor(nm, [128, 16, 512], F32, kind="ExternalOutput").ap()
            kb.dma("sp", o, ap, is_output=True)
        o = nc.dram_tensor("d_x1", [2048, 1024], F32, kind="ExternalOutput").ap()
        kb.dma("sp", o, i_x1, is_output=True)
        o = nc.dram_tensor("d_mod1", [1, 6144], F32, kind="ExternalOutput").ap()
        kb.dma("sp", o, i5["o_mod"], is_output=True)
        kb.end_phase()
    out = nc.dram_tensor("out", [2048, 1024], F32, kind="ExternalOutput").ap()
    FX.remap = dict(common, xs=i_x1, mixa=i_pool, mixb=i_sb, modrow=i5["o_mod"], fnorm=E("fnorm1", [128, 8], F32), w_out=E("cd_w_out", [1024, 1024], F32),
                    w1=E("w1_1", [1024, 5632], F32), w2=E("w2_1", [2816, 1024], F32), o_x=out)
    kb.last_phase = True
    build_tail()
    es.close()
    FX.active = False
    return nc


def fused_maps(inp):
    identb = np.eye(128, dtype=np.float32).astype(ml_dtypes.bfloat16)
    identf = np.eye(128, dtype=np.float32)
    cs1 = l1_consts()
    pow2 = np.tile((2.0 ** -(np.arange(NIT + 1) + 1)).astype(np.float32)[None, :], (128, 1))
    U, ones = sb_consts()
    L = lambda a: np.ascontiguousarray(a.reshape(8, 128).T)
    maps = []
    for c in range(8):
        b, j = divmod(c, 4)
        mneg, mpos = dsa_masks(j)
        band, band0, bandhc, hsel = pool_consts_core(j)
        maps.append(dict(
            identb=identb, identf=identf, cT=L(inp["c"][b]),
            xs=np.ascontiguousarray(inp["x"][b].reshape(64, 128, 1024)[j::4].reshape(2048, 1024)),
            pos=np.ascontiguousarray(inp["positions"][b].reshape(64, 128)[j::4].T.astype(np.int32)),
            ada_w0=inp["ada_w"][0], ada_b0=inp["ada_b"][0][None, :], ada_w1=inp["ada_w"][1], ada_b1=inp["ada_b"][1][None, :],
            mixn0=L(inp["mix_norm"][0]), mixn1=L(inp["mix_norm"][1]), fnorm0=L(inp["ffn_norm"][0]), fnorm1=L(inp["ffn_norm"][1]),
            ab_w_in=inp["ab_w_in"][0], gate_up=inp["gla_gate_up"][0], gate_b=inp["gla_gate_b"][0], dsa_qn=inp["dsa_q_norm"][0], dsa_kn=inp["dsa_k_norm"][0],
            invf=cs1["invf"], tri3=cs1["tri3"], csel=cs1["csel"], amask=cs1["amask"],
            mneg=mneg, mpos=mpos, pow2=pow2, gsel=gla_sel(j), gnorm=inp["gla_out_norm"][0],
            ab_w_out=inp["ab_w_out"][0], w1_0=inp["ffn_w1"][0], w2_0=inp["ffn_w2"][0],
            cd_w_in=inp["cd_w_in"][0], sb_qn=inp["sb_q_norm"][0], sb_kn=inp["sb_k_norm"][0],
            hsel=hsel, band=band, band0=band0, bandhc=bandhc, pool_w=inp["pool_w"][0], pool_scale=inp["pool_scale"][0],
            sbmask=sb_mask(j), U=U, ones=ones,
            cd_w_out=inp["cd_w_out"][0], w1_1=inp["ffn_w1"][1], w2_1=inp["ffn_w2"][1]))
    return maps


def kernel(**inputs):
    inp = {k: np.asarray(v) for k, v in inputs.items()}
    nc = build_fused()
    res = run_bass_kernel_spmd(nc, fused_maps(inp), core_ids=list(range(8)))
    out = np.zeros((2, 64, 128, 1024), np.float32)
    for c in range(8):
        b, j = divmod(c, 4)
        out[b, j::4] = res.results[c]["out"].reshape(16, 128, 1024)
    kernel.last_results = res.results
    return out.reshape(2, 8192, 1024)
```

```python
import ml_dtypes
import numpy as np
from contextlib import ExitStack
import concourse.bass as bass
import concourse.mybir as mybir
from concourse.bass_utils import run_bass_kernel_spmd

F32 = mybir.dt.float32
BF16 = mybir.dt.bfloat16
I32 = mybir.dt.int32
ALU = mybir.AluOpType
AF = mybir.ActivationFunctionType
AX = mybir.AxisListType

EPOCH = 30000
ND = 32


import types


def bind(fn):
    if getattr(fn, "__closure__", None) is None:
        return fn
    cells = []
    for c in fn.__closure__:
        try:
            cells.append(types.CellType(c.cell_contents))
        except ValueError:
            cells.append(c)
    return types.FunctionType(fn.__code__, fn.__globals__, fn.__name__, fn.__defaults__, tuple(cells))


class Reg:
    __slots__ = ("w", "r", "name")

    def __init__(self, name=""):
        self.w = None
        self.r = []
        self.name = name


class KB:
    def __init__(self, nc, es):
        self.nc = nc
        self.es = es
        self.eng = {"pe": nc.tensor, "act": nc.scalar, "dve": nc.vector, "pool": nc.gpsimd, "sp": nc.sync}
        self.sems = {}
        self.cnt = {k: 0 for k in self.eng}
        self.seen = {k: {} for k in self.eng}
        self.ndma = 0
        self.dsem = [es.enter_context(nc.semaphore(f"d{i}")) for i in range(ND)]
        self.nins = 0
        self.out_toks = []
        self.q = {k: [] for k in self.eng}
        self.rec = None
        self.pes = es
        self.pfx = ""
        self.fused = False
        self.last_phase = True
        self.dma_uses = {}

    def _esem(self, st, epoch):
        key = ("E", st, epoch)
        if key not in self.sems:
            self.sems[key] = self.es.enter_context(self.nc.semaphore(f"e_{st}_{epoch}"))
        return key

    def _semh(self, key):
        if key[0] == "D":
            return self.dsem[key[1]]
        return self.sems[key]

    def _collect(self, st, reads, writes):
        waits = {}

        def need(tok, kind):
            if tok is None:
                return
            tst, key, val = tok
            if tst == st and key[0] == "E":
                if st == "pe":
                    return
                if st in ("act", "dve") and kind != "raw":
                    return
            if self.seen[st].get(key, 0) >= val:
                return
            if waits.get(key, 0) < val:
                waits[key] = val

        for r in reads:
            need(r.w, "raw")
        for w in writes:
            need(w.w, "waw")
            for t in w.r:
                need(t, "war")
        return waits

    def _dowaits(self, st, waits):
        for key, val in waits.items():
            h = self._semh(key)
            self.q[st].append(lambda eng, h=h, val=val: eng.wait_ge(h, val))
            self.seen[st][key] = val
            self.nins += 1

    def _record(self, tok, reads, writes):
        for r in reads:
            if tok[1][0] == "E":
                r.r = [t for t in r.r if not (t[0] == tok[0] and t[1] == tok[1])]
            r.r.append(tok)
        for w in writes:
            w.w = tok
            w.r = []

    def emit_roundrobin(self, chains):
        self.rec = None
        n = max(len(c) for c in chains)
        for k in range(n):
            for c in chains:
                if k < len(c):
                    c[k]()

    def op(self, st, fn, reads=(), writes=()):
        fn = bind(fn)
        if self.rec is not None:
            reads = tuple(reads); writes = tuple(writes); rec = self.rec
            rec.append(lambda: self._norec(rec, self.op, st, fn, reads, writes))
            return None
        waits = self._collect(st, reads, writes)
        self._dowaits(st, waits)
        self.cnt[st] += 1
        c = self.cnt[st]
        epoch, val = divmod(c - 1, EPOCH)
        key = self._esem(st, epoch)
        h = self.sems[key]
        self.q[st].append(lambda eng, fn=fn, h=h: fn(eng).then_inc(h, 1))
        tok = (st, key, val + 1)
        self._record(tok, reads, writes)
        self.nins += 1
        return tok

    def _norec(self, rec, f, *a, **kw):
        saved = self.rec
        self.rec = None
        try:
            return f(*a, **kw)
        finally:
            self.rec = saved

    def group(self, st, fns, reads=(), writes=()):
        if self.rec is not None:
            fns = [bind(f) for f in fns]; reads = tuple(reads); writes = tuple(writes); rec = self.rec
            rec.append(lambda: self._norec(rec, self.group, st, fns, reads, writes))
            return None
        waits = self._collect(st, reads, writes)
        self._dowaits(st, waits)
        fns = [bind(f) for f in fns]
        for fn in fns[:-1]:
            self.q[st].append(fn)
            self.nins += 1
        self.nins += 1
        self.cnt[st] += 1
        c = self.cnt[st]
        epoch, val = divmod(c - 1, EPOCH)
        key = self._esem(st, epoch)
        h = self.sems[key]
        self.q[st].append(lambda eng, fn=fns[-1], h=h: fn(eng).then_inc(h, 1))
        tok = (st, key, val + 1)
        self._record(tok, reads, writes)
        return tok

    def dma(self, st, out, in_, reads=(), writes=(), is_output=False, **kw):
        if self.rec is not None:
            reads = tuple(reads); writes = tuple(writes); rec = self.rec
            rec.append(lambda: self._norec(rec, self.dma, st, out, in_, reads, writes, is_output, **kw))
            return None
        i = self.ndma
        self.ndma += 1
        j = i % ND
        use = i // ND
        key = ("D", j)
        waits = self._collect(st, reads, writes)
        if use > 0 and self.seen[st].get(key, 0) < 16 * use:
            waits[key] = max(waits.get(key, 0), 16 * use)
        self._dowaits(st, waits)
        h = self.dsem[j]
        self.q[st].append(lambda eng, out=out, in_=in_, kw=kw, h=h: eng.dma_start(out=out, in_=in_, **kw).then_inc(h, 16))
        tok = (st, key, 16 * (use + 1))
        self.dma_uses[key] = 16 * (use + 1)
        self._record(tok, reads, writes)
        self.nins += 1
        if is_output:
            self.out_toks.append(tok)
        return tok

    def coll(self, fn, reads=(), writes=()):
        st = "pool"
        fn = bind(fn)
        idx = len([k for k in self.sems if k[0] == "C"])
        key = ("C", idx)
        self.sems[key] = self.es.enter_context(self.nc.semaphore(f"cc{idx}"))
        waits = self._collect(st, reads, writes)
        self._dowaits(st, waits)
        h = self.sems[key]
        self.q[st].append(lambda eng, fn=fn, h=h: fn(eng).then_inc(h, 1))
        tok = (st, key, 1)
        self.dma_uses[key] = 1
        self._record(tok, reads, writes)
        self.nins += 1
        return tok

    def barrier(self, skip=()):
        targets = {k: v for k, v in self.dma_uses.items() if k not in skip}
        for e, c in self.cnt.items():
            if c > 0:
                epoch, val = divmod(c - 1, EPOCH)
                targets[("E", e, epoch)] = val + 1
        for st in self.eng:
            for key, val in targets.items():
                if self.seen[st].get(key, 0) < val:
                    h = self._semh(key)
                    self.q[st].append(lambda eng, h=h, val=val: eng.wait_ge(h, val))
                    self.seen[st][key] = val

    def end_phase(self, skip=()):
        self.barrier(skip)
        self.replay()
        self.q = {k: [] for k in self.eng}
        self.pes.close()

    def finish(self):
        if self.fused and not self.last_phase:
            self.end_phase()
            return
        st = "sp"
        for tok in self.out_toks:
            _, key, val = tok
            if self.seen[st].get(key, 0) < val:
                h = self._semh(key)
                self.q[st].append(lambda eng, h=h, val=val: eng.wait_ge(h, val))
                self.seen[st][key] = val
        self.replay()

    def replay(self):
        q = self.q
        with self.nc.Block() as block:
            @block.sync
            def _(e):
                for f in q["sp"]:
                    f(e)

            @block.tensor
            def _(e):
                for f in q["pe"]:
                    f(e)

            @block.scalar
            def _(e):
                for f in q["act"]:
                    f(e)

            @block.vector
            def _(e):
                for f in q["dve"]:
                    f(e)

            @block.gpsimd
            def _(e):
                for f in q["pool"]:
                    f(e)


class T:
    def __init__(self, t, name=""):
        self.t = t
        self.reg = Reg(name)
        self.sub = {}

    def __getitem__(self, idx):
        return self.t[idx]

    def r(self, key=None):
        if key is None:
            return self.reg
        if key not in self.sub:
            self.sub[key] = Reg()
        return self.sub[key]


def sb(kb, name, shape, dt):
    return T(kb.pes.enter_context(kb.nc.sbuf_tensor("s_" + kb.pfx + name, list(shape), dt)), name)


def ps(kb, name, shape, dt=F32):
    return T(kb.pes.enter_context(kb.nc.psum_tensor("p_" + kb.pfx + name, list(shape), dt)), name)


STAGE = 99.0
NTL = 16

EPS = 1e-6
NT = 16
TWO_PI = float(2 * np.pi)


class FX:
    active = False
    nc = None
    kb = None
    remap = {}
    ext = {}
    n_phase = 0


def fx_ext(name, shape, dt):
    if name not in FX.ext:
        FX.ext[name] = FX.nc.dram_tensor(name, list(shape), dt, kind="ExternalInput").ap()
    return FX.ext[name]


def _get_nc():
    if FX.active:
        return FX.nc
    return bass.Bass("TRN2", target_bir_lowering=False)


class _phase:
    def __init__(self, nc):
        self.nc = nc

    def __enter__(self):
        if FX.active:
            kb = FX.kb
            kb.pes = ExitStack()
            kb.pfx = f"ph{FX.n_phase}_"
            FX.n_phase += 1
            return kb
        self.es = ExitStack()
        self.es.__enter__()
        return KB(self.nc, self.es)

    def __exit__(self, *a):
        if not FX.active:
            self.es.__exit__(*a)
        return False


def dram_in(nc, name, shape, dt):
    if FX.active:
        ap = FX.remap[name]
        assert list(ap.shape) == list(shape), (name, ap.shape, shape)
        return ap
    return nc.dram_tensor(name, list(shape), dt, kind="ExternalInput").ap()


def dram_out(nc, name, shape, dt):
    if FX.active:
        ap = FX.remap[name]
        assert list(ap.shape) == list(shape), (name, ap.shape, shape)
        return ap
    return nc.dram_tensor(name, list(shape), dt, kind="ExternalOutput").ap()


def emit_mod(kb, cT_d, ada_w_d, ada_b_d, modrow, modP, name, pA, pB):
    nc = kb.nc
    cT = sb(kb, name + "cT", [128, 8], F32)
    cond = sb(kb, name + "cond", [128, 8], F32)
    adab = sb(kb, name + "adab", [1, 512], F32)
    one = sb(kb, name + "one", [1, 1], F32)
    wblk = [sb(kb, name + "wblk0", [128, 8, 512], F32)] * 2
    pm = pA; pmp = pB
    kb.dma("sp", cT[:], cT_d, writes=[cT.r()])
    kb.op("dve", lambda e: e.memset(one[:], 1.0), writes=[one.r()])
    kb.op("act", lambda e: e.activation(out=cond[:], in_=cT[:], func=AF.Silu), reads=[cT.r()], writes=[cond.r()])
    wv = ada_w_d.rearrange("(k p) n -> p k n", p=128)
    for cb in range(12):
        w = wblk[cb % 2]
        kb.dma("sp", w[:], wv[:, :, cb * 512:(cb + 1) * 512], writes=[w.r()])
        kb.dma("sp", adab[:], ada_b_d[:, cb * 512:(cb + 1) * 512], writes=[adab.r()])
        kb.group("pe", [(lambda e, k=k, w=w: e.matmul(pm[0:1, :], cond[:, k:k + 1], w[:, k, :], start=(k == 0), stop=(k == 7))) for k in range(8)],
                 reads=[cond.r(), w.r()], writes=[pm.r()])
        kb.op("dve", lambda e, cb=cb: e.tensor_tensor(out=modrow[0:1, cb * 512:(cb + 1) * 512], in0=pm[0:1, :], in1=adab[0:1, :], op=ALU.add),
              reads=[pm.r(), adab.r()], writes=[modrow.r()])
    kb.group("pe", [(lambda e, c=c: e.matmul(pmp[:, c:c + 1], modrow[0:1, c * 128:(c + 1) * 128], one[:], start=True, stop=True)) for c in range(48)],
             reads=[modrow.r(), one.r()], writes=[pmp.r()])
    kb.op("dve", lambda e: e.tensor_copy(modP[:], pmp[:, 0:48]), reads=[pmp.r()], writes=[modP.r()])


def emit_rstd(kb, ss, n, tmp, name=""):
    kb.op("dve", lambda e: e.tensor_scalar(out=tmp[:], in0=ss[:], scalar1=1.0 / n, scalar2=EPS, op0=ALU.mult, op1=ALU.add), reads=[ss.r()], writes=[tmp.r()])
    kb.op("act", lambda e: e.activation(out=tmp[:], in_=tmp[:], func=AF.Sqrt), reads=[tmp.r()], writes=[tmp.r()])
    kb.op("dve", lambda e: e.reciprocal(ss[:], tmp[:]), reads=[tmp.r()], writes=[ss.r()])


def emit_norm_hT(kb, xt, hT, a, sh, ident, junk, xn, pT, st, st2):
    kb.op("act", lambda e: e.activation(out=junk[:], in_=xt[:], func=AF.Square, accum_out=st[:, 0:1]), reads=[xt.r()], writes=[junk.r(), st.r()])
    emit_rstd(kb, st, 1024, st2)
    kb.op("act", lambda e: e.activation(out=xn[:], in_=xt[:], func=AF.Copy, scale=st[:, 0:1]), reads=[xt.r(), st.r()], writes=[xn.r()])
    kb.group("pe", [(lambda e, k=k: e.transpose(pT[:, k, :], xn[:, k * 128:(k + 1) * 128], ident[:])) for k in range(8)],
             reads=[xn.r(), ident.r()], writes=[pT.r()])
    for k in range(8):
        kb.op("act", lambda e, k=k: e.activation(out=hT[:, k, :], in_=pT[:, k, :], func=AF.Identity, scale=a[:, k:k + 1], bias=sh[:, k:k + 1]),
              reads=[pT.r(), a.r(), sh.r()], writes=[hT.r()])


def build_l1():
    nc = _get_nc()
    xs = dram_in(nc, "xs", [NT * 128, 1024], F32)
    pos_d = dram_in(nc, "pos", [128, NT], I32)
    cT_d = dram_in(nc, "cT", [128, 8], F32)
    ada_w = dram_in(nc, "ada_w", [1024, 6144], F32)
    ada_b = dram_in(nc, "ada_b", [1, 6144], F32)
    mixn_d = dram_in(nc, "mixn", [128, 8], F32)
    w_in = dram_in(nc, "w_in", [1024, 3672], F32)
    gate_up = dram_in(nc, "gate_up", [16, 256], F32)
    gate_b = dram_in(nc, "gate_b", [256], F32)
    qn_d = dram_in(nc, "q_norm", [128], F32)
    kn_d = dram_in(nc, "k_norm", [128], F32)
    invf_d = dram_in(nc, "invf", [64], F32)
    identb_d = dram_in(nc, "identb", [128, 128], BF16)
    identf_d = dram_in(nc, "identf", [128, 128], F32)
    tri3_d = dram_in(nc, "tri3", [128, 3, 128], F32)
    csel_d = dram_in(nc, "csel", [128, 2], F32)
    amask_d = dram_in(nc, "amask", [128, 128], F32)

    o_mod = dram_out(nc, "o_mod", [1, 6144], F32)
    o_qT = dram_out(nc, "o_qT", [128, 4, NT * 128], BF16)
    o_kT = dram_out(nc, "o_kT", [128, 4, NT * 128], BF16)
    o_v = dram_out(nc, "o_v", [128, NT, 512], BF16)
    o_iqT = dram_out(nc, "o_iqT", [128, 4, NT * 128], BF16)
    o_ikT = dram_out(nc, "o_ikT", [128, NT * 128], BF16)
    o_iw = dram_out(nc, "o_iw", [128, NT, 8], F32)
    o_oin = dram_out(nc, "o_oin", [128, NT, 512], F32)
    o_qdT = dram_out(nc, "o_qdT", [128, 2, NT * 128], BF16)
    o_kv = dram_out(nc, "o_kv", [128, 2, 2 * NT, 128], BF16)
    o_dec = dram_out(nc, "o_dec", [128, 2, 2 * NT], F32)
    o_gr = dram_out(nc, "o_gr", [128, NT, 512], F32)

    with _phase(nc) as kb:
        S = lambda n, s, d: sb(kb, n, s, d)
        identb = S("identb", [128, 128], BF16); identf = S("identf", [128, 128], F32)
        tri3 = S("tri3", [128, 3, 128], F32); csel = S("csel", [128, 2], F32); amask = S("amask", [128, 128], F32)
        invf = S("invf", [128, 64], F32); qnb = S("qnb", [128, 128], F32); knb = S("knb", [128, 128], F32)
        gbb = S("gbb", [128, 256], F32); gup = S("gup", [16, 256], F32); mixn = S("mixn", [128, 8], F32)
        posi = S("posi", [128, NT], I32)
        for t, d in ((identb, identb_d), (identf, identf_d), (tri3, tri3_d), (csel, csel_d), (amask, amask_d), (gup, gate_up), (mixn, mixn_d), (posi, pos_d)):
            kb.dma("sp", t[:], d, writes=[t.r()])
        for t, d in ((invf, invf_d), (qnb, qn_d), (knb, kn_d), (gbb, gate_b)):
            kb.dma("sp", t[:], d.partition_broadcast(128), writes=[t.r()])
        wb = S("wb", [128, 8, 3672], BF16)
        for k in range(8):
            kb.dma("pool", wb[:, k, :], w_in[k * 128:(k + 1) * 128, :], writes=[wb.r(("k", k))])
        wb_regs = [wb.r(("k", k)) for k in range(8)]
        modrow = S("modrow", [1, 6144], F32); modP = S("modP", [128, 48], F32)
        pT = ps(kb, "pT", [128, 8, 128], BF16)
        pp = [ps(kb, f"pp{i}", [128, 512], F32) for i in range(2)]
        pA = ps(kb, "pA", [128, 512], F32)
        pB = ps(kb, "pB", [128, 512], F32)
        pTb = pT
        pTg = ps(kb, "pTg", [128, 8, 128], BF16)
        pKV = ps(kb, "pKV", [128, 4, 256], F32)
        emit_mod(kb, cT_d, ada_w, ada_b, modrow, modP, "m0", pA, pB)
        kb.dma("sp", o_mod, modrow[:], reads=[modrow.r()], is_output=True)
        a1 = S("a1", [128, 8], F32)
        kb.op("dve", lambda e: e.scalar_tensor_tensor(out=a1[:], in0=modP[:, 8:16], scalar=1.0, in1=mixn[:], op0=ALU.add, op1=ALU.mult),
              reads=[modP.r(), mixn.r()], writes=[a1.r()])
        if STAGE < 2:
            kb.finish(); return nc
        posf = S("posf", [128, NT], F32)
        ang = S("ang", [128, NT * 64], F32); ki = S("ki", [128, NT * 64], I32); kf = S("kf", [128, NT * 64], F32)
        rs = ang; rc = S("rc", [128, NT * 64], F32); m1 = kf
        sinT = S("sinT", [128, NT, 64], F32); cosT = S("cosT", [128, NT, 64], F32)
        kb.op("dve", lambda e: e.tensor_copy(posf[:], posi[:]), reads=[posi.r()], writes=[posf.r()])
        for i in range(NT):
            kb.op("dve", lambda e, i=i: e.tensor_scalar(out=ang[:, i * 64:(i + 1) * 64], in0=invf[:], scalar1=posf[:, i:i + 1], scalar2=None, op0=ALU.mult),
                  reads=[invf.r(), posf.r()], writes=[ang.r()])
        C1 = 6.28125; C2 = TWO_PI - C1
        D = lambda fn, r, w: kb.op("dve", fn, reads=r, writes=w)
        D(lambda e: e.tensor_scalar(out=kf[:], in0=ang[:], scalar1=1.0 / TWO_PI, scalar2=None, op0=ALU.mult), [ang.r()], [kf.r()])
        D(lambda e: e.tensor_copy(ki[:], kf[:]), [kf.r()], [ki.r()])
        D(lambda e: e.tensor_copy(kf[:], ki[:]), [ki.r()], [kf.r()])
        D(lambda e: e.scalar_tensor_tensor(out=rs[:], in0=kf[:], scalar=-C1, in1=ang[:], op0=ALU.mult, op1=ALU.add), [kf.r(), ang.r()], [rs.r()])
        D(lambda e: e.scalar_tensor_tensor(out=rs[:], in0=kf[:], scalar=-C2, in1=rs[:], op0=ALU.mult, op1=ALU.add), [kf.r(), rs.r()], [rs.r()])
        PI = float(np.pi)
        D(lambda e: e.tensor_scalar(out=m1[:], in0=rs[:], scalar1=PI, scalar2=-TWO_PI, op0=ALU.is_gt, op1=ALU.mult), [rs.r()], [m1.r()])
        D(lambda e: e.tensor_tensor(out=rs[:], in0=rs[:], in1=m1[:], op=ALU.add), [rs.r(), m1.r()], [rs.r()])
        D(lambda e: e.tensor_scalar(out=m1[:], in0=rs[:], scalar1=-PI, scalar2=TWO_PI, op0=ALU.is_lt, op1=ALU.mult), [rs.r()], [m1.r()])
        D(lambda e: e.tensor_tensor(out=rs[:], in0=rs[:], in1=m1[:], op=ALU.add), [rs.r(), m1.r()], [rs.r()])
        D(lambda e: e.tensor_scalar(out=rc[:], in0=rs[:], scalar1=PI / 2, scalar2=None, op0=ALU.add), [rs.r()], [rc.r()])
        D(lambda e: e.tensor_scalar(out=m1[:], in0=rc[:], scalar1=PI, scalar2=-TWO_PI, op0=ALU.is_gt, op1=ALU.mult), [rc.r()], [m1.r()])
        D(lambda e: e.tensor_tensor(out=rc[:], in0=rc[:], in1=m1[:], op=ALU.add), [rc.r(), m1.r()], [rc.r()])
        for t in (rs, rc):
            D(lambda e, t=t: e.tensor_scalar(out=t[:], in0=t[:], scalar1=PI, scalar2=-PI, op0=ALU.min, op1=ALU.max), [t.r()], [t.r()])
        kb.op("act", lambda e: e.activation(out=sinT[:].rearrange("p a b -> p (a b)"), in_=rs[:], func=AF.Sin), reads=[rs.r()], writes=[sinT.r()])
        kb.op("act", lambda e: e.activation(out=cosT[:].rearrange("p a b -> p (a b)"), in_=rc[:], func=AF.Sin), reads=[rc.r()], writes=[cosT.r()])

        qT_t = S("qT_t", [128, 4, 128], BF16); kT_t = S("kT_t", [128, 4, 128], BF16)
        iqT_t = S("iqT_t", [128, 4, 128], BF16); ikT_t = S("ikT_t", [128, 128], BF16)
        qdT_t = S("qdT_t", [128, 2, 128], BF16)
        iw_sb = S("iw_sb", [128, NT, 8], F32)
        kv_t = S("kv_t", [128, 2, 2, 128], BF16); dec_sb = S("dec_sb", [128, 2, 2 * NT], F32)

        xt = [S(f"xt{i}", [128, 1024], F32) for i in range(2)]
        junk = S("junk", [128, 1024], BF16); xn = S("xn", [128, 1024], BF16)
        st = S("st", [128, 1], F32); st2 = S("st2", [128, 1], F32)
        hT = [S(f"hT{i}", [128, 8, 128], BF16) for i in range(2)]
        proj = S("proj", [128, 3672], F32)
        sq = S("sq", [128, 512], F32); ssq = S("ssq", [128, 4], F32); ssq2 = S("ssq2", [128, 4], F32)
        qn = S("qn", [128, 4, 128], F32); qr = S("qr", [128, 4, 128], BF16)
        tA = S("tA", [128, 4, 64], F32); tB = S("tB", [128, 4, 64], F32)
        iqr = S("iqr", [128, 8, 64], BF16); iA = S("iA", [128, 8, 32], F32); iB = S("iB", [128, 8, 32], F32)
        ik2 = S("ik2", [128, 128], BF16); kA = S("kA", [128, 32], F32); kB_ = S("kB_", [128, 32], F32)
        vb = S("vb", [128, 512], BF16); gvb = S("gvb", [128, 512], BF16)
        glT = S("glT", [16, 128], F32); pre = S("pre", [128, 256], F32); lg = S("lg", [128, 256], F32)
        bmid = S("bmid", [128, 256], F32); blast = S("blast", [128, 256], F32)
        d1 = S("d1", [128, 256], F32); d3 = S("d3", [128, 256], F32)
        E1 = S("E1", [128, 256], F32); E2 = S("E2", [128, 256], F32); E3 = S("E3", [128, 256], F32); E4 = S("E4", [128, 256], F32)
        qkd = S("qkd", [128, 3, 256], BF16)
        kdec = S("kdec", [128, 256], BF16)
        qkT = S("qkT", [128, 6, 128], BF16)
        attT = S("attT", [128, 4, 128], BF16)
        oin = S("oin", [128, 512], F32); grs = S("grs", [128, 512], F32)
        dect = S("dect", [128, 2, 2], F32)

        def rope(src, dst, nh, half, cos, sin, A, B, cols_per_head):
            sv = src
            t1 = sv[:, :, 0:half]; t2 = sv[:, :, half:2 * half]
            cb_ = cos.unsqueeze(1).to_broadcast([128, nh, half]); sb_ = sin.unsqueeze(1).to_broadcast([128, nh, half])
            return t1, t2, cb_, sb_

        for i in range(NTL if STAGE >= 3 else 0):
            x_ = xt[i % 2]; h_ = hT[i % 2]
            kb.dma("sp", x_[:], xs[i * 128:(i + 1) * 128, :], writes=[x_.r()])
            emit_norm_hT(kb, x_, h_, a1, modP_sh(modP), identb, junk, xn, pT, st, st2)
            for cb in range(8):
                c0 = cb * 512; c1 = min(3672, c0 + 512); p_ = pp[cb % 2]
                kb.group("pe", [(lambda e, k=k, c0=c0, c1=c1, p_=p_, h_=h_: e.matmul(p_[:, 0:c1 - c0], h_[:, k, :], wb[:, k, c0:c1], start=(k == 0), stop=(k == 7))) for k in range(8)],
                         reads=[h_.r()] + wb_regs, writes=[p_.r()])
                if cb % 2 == 0:
                    kb.op("act", lambda e, c0=c0, c1=c1, p_=p_: e.copy(proj[:, c0:c1], p_[:, 0:c1 - c0]), reads=[p_.r()], writes=[proj.r(("c", cb))])
                else:
                    kb.op("dve", lambda e, c0=c0, c1=c1, p_=p_: e.tensor_copy(proj[:, c0:c1], p_[:, 0:c1 - c0]), reads=[p_.r()], writes=[proj.r(("c", cb))])
            PR = [proj.r(("c", cb)) for cb in range(8)]
            cos_i = cosT[:, i, :]; sin_i = sinT[:, i, :]
            cos32 = cosT[:, i, 0:64:2]; sin32 = sinT[:, i, 0:64:2]

            C1, C2, C3 = [], [], []
            kb.rec = C1
            for (c0, gain, dstT, dstD) in ((1552, qnb, qT_t, o_qT), (2064, knb, kT_t, o_kT)):
                src = proj[:, c0:c0 + 512]
                D(lambda e, src=src: e.tensor_tensor(out=sq[:], in0=src, in1=src, op=ALU.mult), PR, [sq.r()])
                D(lambda e: e.tensor_reduce(out=ssq[:], in_=sq[:].rearrange("p (h d) -> p h d", h=4), axis=AX.X, op=ALU.add), [sq.r()], [ssq.r()])
                emit_rstd(kb, ssq, 128, ssq2)
                s3 = src.rearrange("p (h d) -> p h d", h=4)
                D(lambda e, s3=s3: e.tensor_tensor(out=qn[:], in0=s3, in1=ssq[:].unsqueeze(2).to_broadcast([128, 4, 128]), op=ALU.mult), PR + [ssq.r()], [qn.r()])
                D(lambda e, gain=gain: e.tensor_tensor(out=qn[:], in0=qn[:], in1=gain[:].unsqueeze(1).to_broadcast([128, 4, 128]), op=ALU.mult), [qn.r(), gain.r()], [qn.r()])
                t1 = qn[:, :, 0:64]; t2 = qn[:, :, 64:128]
                cb_ = cos_i.unsqueeze(1).to_broadcast([128, 4, 64]); sb_ = sin_i.unsqueeze(1).to_broadcast([128, 4, 64])
                D(lambda e: e.tensor_tensor(out=tA[:], in0=t1, in1=cb_, op=ALU.mult), [qn.r(), cosT.r()], [tA.r()])
                D(lambda e: e.tensor_tensor(out=tB[:], in0=t2, in1=sb_, op=ALU.mult), [qn.r(), sinT.r()], [tB.r()])
                D(lambda e: e.tensor_tensor(out=qr[:, :, 0:64], in0=tA[:], in1=tB[:], op=ALU.subtract), [tA.r(), tB.r()], [qr.r()])
                D(lambda e: e.tensor_tensor(out=tA[:], in0=t2, in1=cb_, op=ALU.mult), [qn.r(), cosT.r()], [tA.r()])
                D(lambda e: e.tensor_tensor(out=tB[:], in0=t1, in1=sb_, op=ALU.mult), [qn.r(), sinT.r()], [tB.r()])
                D(lambda e: e.tensor_tensor(out=qr[:, :, 64:128], in0=tA[:], in1=tB[:], op=ALU.add), [tA.r(), tB.r()], [qr.r()])
                kb.group("pe", [(lambda e, h=h: e.transpose(pTb[:, h, :], qr[:, h, :], identb[:])) for h in range(4)], reads=[qr.r(), identb.r()], writes=[pTb.r()])
                kb.op("act", lambda e, dstT=dstT: e.copy(dstT[:], pTb[:, 0:4, :]), reads=[pTb.r()], writes=[dstT.r()])
                kb.dma("sp", dstD[:, :, i * 128:(i + 1) * 128], dstT[:], reads=[dstT.r()], is_output=True)
            kb.rec = C2
            D(lambda e: e.tensor_copy(vb[:], proj[:, 2576:3088]), PR, [vb.r()])
            kb.dma("sp", o_v[:, i, :], vb[:], reads=[vb.r()], is_output=True)
            kb.rec = C1
            iq3 = proj[:, 3088:3600].rearrange("p (h d) -> p h d", h=8)
            t1 = iq3[:, :, 0:32]; t2 = iq3[:, :, 32:64]
            cb_ = cos32.unsqueeze(1).to_broadcast([128, 8, 32]); sb_ = sin32.unsqueeze(1).to_broadcast([128, 8, 32])
            D(lambda e: e.tensor_tensor(out=iA[:], in0=t1, in1=cb_, op=ALU.mult), PR + [cosT.r()], [iA.r()])
            D(lambda e: e.tensor_tensor(out=iB[:], in0=t2, in1=sb_, op=ALU.mult), PR + [sinT.r()], [iB.r()])
            D(lambda e: e.tensor_tensor(out=iqr[:, :, 0:32], in0=iA[:], in1=iB[:], op=ALU.subtract), [iA.r(), iB.r()], [iqr.r()])
            D(lambda e: e.tensor_tensor(out=iA[:], in0=t2, in1=cb_, op=ALU.mult), PR + [cosT.r()], [iA.r()])
            D(lambda e: e.tensor_tensor(out=iB[:], in0=t1, in1=sb_, op=ALU.mult), PR + [sinT.r()], [iB.r()])
            D(lambda e: e.tensor_tensor(out=iqr[:, :, 32:64], in0=iA[:], in1=iB[:], op=ALU.add), [iA.r(), iB.r()], [iqr.r()])
            iq2 = iqr[:].rearrange("p h d -> p (h d)")
            kb.group("pe", [(lambda e, g=g: e.transpose(pTb[:, g, :], iq2[:, g * 128:(g + 1) * 128], identb[:])) for g in range(4)], reads=[iqr.r(), identb.r()], writes=[pTb.r()])
            kb.op("act", lambda e: e.copy(iqT_t[:], pTb[:, 0:4, :]), reads=[pTb.r()], writes=[iqT_t.r()])
            kb.dma("sp", o_iqT[:, :, i * 128:(i + 1) * 128], iqT_t[:], reads=[iqT_t.r()], is_output=True)
            kb.rec = C2
            k1 = proj[:, 3600:3632]; k2 = proj[:, 3632:3664]
            D(lambda e: e.tensor_tensor(out=kA[:], in0=k1, in1=cos32, op=ALU.mult), PR + [cosT.r()], [kA.r()])
            D(lambda e: e.tensor_tensor(out=kB_[:], in0=k2, in1=sin32, op=ALU.mult), PR + [sinT.r()], [kB_.r()])
            D(lambda e: e.tensor_tensor(out=ik2[:, 0:32], in0=kA[:], in1=kB_[:], op=ALU.subtract), [kA.r(), kB_.r()], [ik2.r()])
            D(lambda e: e.tensor_tensor(out=kA[:], in0=k2, in1=cos32, op=ALU.mult), PR + [cosT.r()], [kA.r()])
            D(lambda e: e.tensor_tensor(out=kB_[:], in0=k1, in1=sin32, op=ALU.mult), PR + [sinT.r()], [kB_.r()])
            D(lambda e: e.tensor_tensor(out=ik2[:, 32:64], in0=kA[:], in1=kB_[:], op=ALU.add), [kA.r(), kB_.r()], [ik2.r()])
            D(lambda e: e.tensor_copy(ik2[:, 64:128], ik2[:, 0:64]), [ik2.r()], [ik2.r()])
            kb.op("pe", lambda e: e.transpose(pTb[:, 4, :], ik2[:], identb[:]), reads=[ik2.r(), identb.r()], writes=[pTb.r()])
            kb.op("act", lambda e: e.copy(ikT_t[:], pTb[:, 4, :]), reads=[pTb.r()], writes=[ikT_t.r()])
            kb.dma("sp", o_ikT[:, i * 128:(i + 1) * 128], ikT_t[:], reads=[ikT_t.r()], is_output=True)
            D(lambda e: e.tensor_copy(iw_sb[:, i, :], proj[:, 3664:3672]), PR, [iw_sb.r()])
            kb.rec = C3
            kb.op("pe", lambda e: e.transpose(pA[0:16, 0:128], proj[:, 1024:1040], identf[:]), reads=PR + [identf.r()], writes=[pA.r()])
            kb.op("act", lambda e: e.copy(glT[:], pA[0:16, 0:128]), reads=[pA.r()], writes=[glT.r()])
            kb.op("pe", lambda e: e.matmul(pA[:, 0:256], glT[:], gup[:], start=True, stop=True), reads=[glT.r(), gup.r()], writes=[pA.r()])
            D(lambda e: e.tensor_tensor(out=pre[:], in0=pA[:, 0:256], in1=gbb[:], op=ALU.add), [pA.r(), gbb.r()], [pre.r()])
            kb.op("act", lambda e: e.activation(out=lg[:], in_=pre[:], func=AF.Exp, scale=-1.0), reads=[pre.r()], writes=[lg.r()])
            kb.op("act", lambda e: e.activation(out=lg[:], in_=lg[:], func=AF.Ln, bias=1.0), reads=[lg.r()], writes=[lg.r()])
            kb.group("pe", [
                lambda e: e.matmul(pA[:, 0:256], tri3[:, 0, :], lg[:], start=True, stop=True),
                lambda e: e.matmul(pA[:, 256:512], tri3[:, 1, :], lg[:], start=True, stop=True),
                lambda e: e.matmul(pB[:, 0:256], tri3[:, 2, :], lg[:], start=True, stop=True),
                lambda e: e.matmul(pB[:, 256:258], lg[:, 0:128], csel[:], start=True, stop=True),
                lambda e: e.matmul(pB[:, 258:260], lg[:, 128:256], csel[:], start=True, stop=True),
            ], reads=[tri3.r(), lg.r(), csel.r()], writes=[pA.r(), pB.r()])
            kb.op("act", lambda e: e.copy(bmid[:], pA[:, 256:512]), reads=[pA.r()], writes=[bmid.r()])
            kb.op("act", lambda e: e.copy(blast[:], pB[:, 0:256]), reads=[pB.r()], writes=[blast.r()])
            kb.op("act", lambda e: e.activation(out=dec_sb[:, :, 2 * i:2 * i + 2], in_=pB[:, 256:260].rearrange("p (a c) -> p a c", a=2), func=AF.Exp), reads=[pB.r()], writes=[dec_sb.r()])
            D(lambda e: e.tensor_tensor(out=d1[:], in0=pA[:, 0:256], in1=bmid[:], op=ALU.subtract), [pA.r(), bmid.r()], [d1.r()])
            D(lambda e: e.tensor_tensor(out=d3[:], in0=blast[:], in1=pA[:, 0:256], op=ALU.subtract), [pA.r(), blast.r()], [d3.r()])
            kb.op("act", lambda e: e.activation(out=E1[:], in_=d1[:], func=AF.Exp), reads=[d1.r()], writes=[E1.r()])
            kb.op("act", lambda e: e.activation(out=E2[:], in_=d1[:], func=AF.Exp, scale=-1.0), reads=[d1.r()], writes=[E2.r()])
            kb.op("act", lambda e: e.activation(out=E3[:], in_=d3[:], func=AF.Exp), reads=[d3.r()], writes=[E3.r()])
            kb.op("act", lambda e: e.activation(out=E4[:], in_=pA[:, 0:256], func=AF.Exp), reads=[pA.r()], writes=[E4.r()])
            gq = proj[:, 0:256]; gk = proj[:, 256:512]
            D(lambda e: e.scalar_tensor_tensor(out=qkd[:, 0, :], in0=gq, scalar=0.125, in1=E1[:], op0=ALU.mult, op1=ALU.mult), PR + [E1.r()], [qkd.r()])
            D(lambda e: e.tensor_tensor(out=qkd[:, 1, :], in0=gk, in1=E2[:], op=ALU.mult), PR + [E2.r()], [qkd.r()])
            D(lambda e: e.scalar_tensor_tensor(out=qkd[:, 2, :], in0=gq, scalar=0.125, in1=E4[:], op0=ALU.mult, op1=ALU.mult), PR + [E4.r()], [qkd.r()])
            D(lambda e: e.tensor_tensor(out=kdec[:], in0=gk, in1=E3[:], op=ALU.mult), PR + [E3.r()], [kdec.r()])
            kb.group("pe", [(lambda e, w=w, p=p: e.transpose(pTg[:, w * 2 + p, :], qkd[:, w, p * 128:(p + 1) * 128], identb[:])) for w in range(3) for p in range(2)],
                     reads=[qkd.r(), identb.r()], writes=[pTg.r()])
            kb.op("act", lambda e: e.copy(qkT[:], pTg[:, 0:6, :]), reads=[pTg.r()], writes=[qkT.r()])
            D(lambda e: e.tensor_copy(qdT_t[:], qkT[:, 4:6, :]), [qkT.r()], [qdT_t.r()])
            kb.dma("sp", o_qdT[:, :, i * 128:(i + 1) * 128], qdT_t[:], reads=[qdT_t.r()], is_output=True)
            D(lambda e: e.tensor_copy(gvb[:], proj[:, 512:1024]), PR, [gvb.r()])
            def attmm(e, h):
                p, hb = divmod(h, 2); hb *= 64
                dst = pp[0] if hb == 0 else pp[1]
                return e.matmul(dst[:, p * 128:(p + 1) * 128], qkT[hb:hb + 64, 2 + p, :], qkT[hb:hb + 64, 0 + p, :], start=True, stop=True)
            kb.group("pe", [(lambda e, h=h: attmm(e, h)) for h in (0, 2, 1, 3)], reads=[qkT.r()], writes=[pp[0].r(), pp[1].r()])
            for hh in range(2):
                D(lambda e, hh=hh: e.tensor_tensor(out=attT[:, hh:4:2, :], in0=pp[hh][:, 0:256].rearrange("p (h i) -> p h i", h=2), in1=amask[:].unsqueeze(1).to_broadcast([128, 2, 128]), op=ALU.mult),
                  [pp[hh].r(), amask.r()], [attT.r()])
            kb.group("pe", [(lambda e, h=h: e.matmul(pB[:, h * 128:(h + 1) * 128], attT[:, h, :], gvb[:, h * 128:(h + 1) * 128], start=True, stop=True)) for h in range(4)],
                     reads=[attT.r(), gvb.r()], writes=[pB.r()])
            kb.op("act", lambda e: e.copy(oin[:], pB[:]), reads=[pB.r()], writes=[oin.r()])
            kb.dma("sp", o_oin[:, i, :], oin[:], reads=[oin.r()], is_output=True)
            kb.group("pe", [(lambda e, p=p, c=c: e.matmul(pKV[:, c * 2 + p, :], kdec[c * 64:(c + 1) * 64, p * 128:(p + 1) * 128], gvb[c * 64:(c + 1) * 64, p * 256:(p + 1) * 256], start=True, stop=True))
                            for p in range(2) for c in range(2)], reads=[kdec.r(), gvb.r()], writes=[pKV.r()])
            pk = pKV[:].rearrange("q (c p) n -> q p c n", p=2)
            kb.op("act", lambda e: e.copy(kv_t[0:64], pk[0:64, :, :, 0:128]), reads=[pKV.r()], writes=[kv_t.r()])
            D(lambda e: e.tensor_copy(kv_t[64:128], pk[64:128, :, :, 128:256]), [pKV.r()], [kv_t.r()])
            kb.dma("sp", o_kv[:, :, 2 * i:2 * i + 2, :], kv_t[:], reads=[kv_t.r()], is_output=True)
            kb.rec = C2
            kb.op("act", lambda e: e.activation(out=grs[:], in_=proj[:, 1040:1552], func=AF.Silu), reads=PR, writes=[grs.r()])
            kb.dma("sp", o_gr[:, i, :], grs[:], reads=[grs.r()], is_output=True)
            kb.emit_roundrobin([C3, C1, C2])
        for t, d in ((iw_sb, o_iw), (dec_sb, o_dec)):
            kb.dma("sp", d, t[:], reads=[t.r()], is_output=True)
        kb.finish()
        print("L1 instructions:", kb.nins, {k: len(v) for k, v in kb.q.items()})
    return nc


class _ShView:
    pass


def modP_sh(modP):
    class V:
        def __getitem__(s, idx):
            return modP[idx]
        def r(s, key=None):
            return modP.r(key)
    return V()


def l1_consts():
    half = 64
    invf = (10000.0 ** (-np.arange(half, dtype=np.float32) / half)).astype(np.float32)
    j = np.arange(128)[:, None]; i = np.arange(128)[None, :]
    same = (j // 64) == (i // 64)
    tri = (same & (j <= i)).astype(np.float32)
    mmid = (same & ((j % 64) <= 31)).astype(np.float32)
    mlast = same.astype(np.float32)
    tri3 = np.stack([tri, mmid, mlast], axis=1) * (-1.0 / 16.0)
    csel = np.stack([(np.arange(128) < 64), (np.arange(128) >= 64)], axis=1).astype(np.float32) * (-1.0 / 16.0)
    amask = tri.copy()
    return dict(invf=invf, identb=np.eye(128, dtype=np.float32).astype(ml_dtypes.bfloat16), identf=np.eye(128, dtype=np.float32),
                tri3=np.ascontiguousarray(tri3.astype(np.float32)), csel=csel, amask=amask)


NIT = 10
C0 = 11.3137085
SCALE = float(128 ** -0.5)


def emit_skewed(its, nst):
    n = len(its)
    for step in range(n + nst - 1):
        for stg in range(nst):
            k = step - stg
            if 0 <= k < n:
                its[k][stg]()


def build_dsa(QT=tuple(range(16))):
    nc = _get_nc()
    qT_d = dram_in(nc, "qT", [128, 4, 2048], BF16)
    iqT_d = dram_in(nc, "iqT", [128, 4, 2048], BF16)
    iw_d = dram_in(nc, "iw", [128, 16, 8], F32)
    kT_d = dram_in(nc, "kT", [2, 4, 64, 8192], BF16)
    v_d = dram_in(nc, "v", [2, 4, 64, 8192], BF16)
    ikT_d = dram_in(nc, "ikT", [1, 4, 128, 2048], BF16)
    mneg_d = dram_in(nc, "mneg", [128, 512], F32)
    mpos_d = dram_in(nc, "mpos", [128, 512], F32)
    identb_d = dram_in(nc, "identb", [128, 128], BF16)
    identf_d = dram_in(nc, "identf", [128, 128], F32)
    pow2_d = dram_in(nc, "pow2", [128, NIT + 1], F32)
    o_dsa = dram_out(nc, "o_dsa", [128, 16, 512], F32)
    with _phase(nc) as kb:
        S = lambda n, s, d: sb(kb, n, s, d)
        D = lambda fn, r, w: kb.op("dve", fn, reads=r, writes=w)
        A = lambda fn, r, w: kb.op("act", fn, reads=r, writes=w)
        v = S("v", [128, 64, 512], BF16); ikT = S("ikT", [128, 8192], BF16)
        kTb = [S(f"kTb{i}", [128, 512], BF16) for i in range(3)]
        for jj in range(4):
            for q in range(2):
                kb.dma("sp", v[q * 64:(q + 1) * 64].rearrange("p (i j) c -> p i j c", j=4)[:, :, jj, :], v_d[q, jj].rearrange("p (i c) -> p i c", c=512), writes=[v.r(("g", jj))])
        for jj in range(4):
            kb.dma("sp", ikT[:].rearrange("p (i j s) -> p i j s", j=4, s=128)[:, :, jj, :], ikT_d[0, jj].rearrange("p (i s) -> p i s", s=128), writes=[ikT.r()])
        VV = [v.r(("g", g)) for g in range(8)]
        iw = S("iw", [128, 16, 8], F32); mneg = S("mneg", [128, 512], F32); mpos = S("mpos", [128, 512], F32)
        identb = S("identb", [128, 128], BF16); identf = S("identf", [128, 128], F32); pow2 = S("pow2", [128, NIT + 1], F32)
        for t, d in ((iw, iw_d), (mneg, mneg_d), (mpos, mpos_d), (identb, identb_d), (identf, identf_d), (pow2, pow2_d)):
            kb.dma("sp", t[:], d, writes=[t.r()])
        B = [ps(kb, f"B{i}", [128, 512], F32) for i in range(4)] + [None, None] + [ps(kb, f"B{i}", [128, 512], F32) for i in (6, 7)]
        X2 = ps(kb, "X2", [128, 2, 512], F32)
        qts = [S(f"qt{i}", [128, 4, 128], BF16) for i in range(2)]; iqts = [S(f"iqt{i}", [128, 4, 128], BF16) for i in range(2)]
        diagw = S("diagw", [128, 8, 128], BF16)
        Rt2 = [S(f"Rt2_{i}", [128, 2, 512], BF16) for i in range(2)]
        scores = [S(f"score{i}", [128, 8192], F32) for i in range(2)]
        masks = [S(f"maskb{i}", [128, 8192], BF16) for i in range(2)]
        tmp = S("tmp", [128, 512], F32)
        sts = [S(f"st{i}", [128, 8], F32) for i in range(2)]
        Hh = S("Hh", [128, NIT + 1], F32)
        lo_t = S("lo_t", [128, 1], F32); mid_t = S("mid_t", [128, 1], F32); cnt_t = S("cnt_t", [128, 1], F32); g_t = S("g_t", [128, 1], F32)
        Eb = [S(f"Eb{i}", [128, 512], BF16) for i in range(2)]
        Pb = [S(f"Pb{i}", [128, 512], BF16) for i in range(2)]
        PT = [S(f"PT{i}", [128, 4, 128], BF16) for i in range(2)]
        rs = S("rs", [128, 4, 16], F32); rsum = S("rsum", [128, 4], F32); rinv = S("rinv", [128, 4], F32)
        osb = S("osb", [128, 512], F32)
        Sbanks = (B[0], B[1], B[7]); Ob = B[2]; pTv = B[3][:].bitcast(BF16); SC = B[6]
        cstate = [0]

        def phaseI(i, par):
            iqt = iqts[par]; score = scores[par]; st = sts[par]
            kb.dma("sp", iqt[:], iqT_d[:, :, i * 128:(i + 1) * 128], writes=[iqt.r()])
            for h in range(8):
                A(lambda e, h=h: e.activation(out=diagw[:, h, :], in_=identf[:], func=AF.Copy, scale=iw[:, i, h:h + 1]), [identf.r(), iw.r()], [diagw.r()])
            yield
            for m in range(i + 1):
                ks = slice(m * 512, (m + 1) * 512)
                for p in range(4):
                    kb.group("pe", [lambda e, p=p: e.matmul(X2[:, 0, :], iqt[0:64, p, :], ikT[0:64, ks], start=True, stop=True),
                                    lambda e, p=p: e.matmul(X2[:, 1, :], iqt[64:128, p, :], ikT[64:128, ks], start=True, stop=True)],
                             reads=[iqt.r(), ikT.r()], writes=[X2.r()])
                    R_ = Rt2[p % 2]
                    A(lambda e, R_=R_: e.activation(out=R_[:].rearrange("p a b -> p (a b)"), in_=X2[:].rearrange("p a b -> p (a b)"), func=AF.Relu), [X2.r()], [R_.r(("h", 0)), R_.r(("h", 1))])
                    kb.group("pe", [lambda e, p=p, R_=R_: e.matmul(SC[:], diagw[:, 2 * p, :], R_[:, 0, :], start=(p == 0), stop=False),
                                    lambda e, p=p, R_=R_: e.matmul(SC[:], diagw[:, 2 * p + 1, :], R_[:, 1, :], start=False, stop=(p == 3))],
                             reads=[diagw.r(), R_.r(("h", 0)), R_.r(("h", 1))], writes=[SC.r()])
                    yield
                if m < i:
                    A(lambda e: e.copy(score[:, ks], SC[:]), [SC.r()], [score.r(("m", m))])
                else:
                    D(lambda e: e.tensor_tensor(out=score[:, ks], in0=SC[:], in1=mneg[:], op=ALU.add), [SC.r(), mneg.r()], [score.r(("m", m))])
                    D(lambda e: e.tensor_tensor(out=tmp[:], in0=SC[:], in1=mpos[:], op=ALU.add), [SC.r(), mpos.r()], [tmp.r()])
                    D(lambda e: e.tensor_reduce(out=st[:, 1:2], in_=tmp[:], axis=AX.X, op=ALU.min), [tmp.r()], [st.r()])
                yield

        def phaseII(i, par):
            score = scores[par]; st = sts[par]; maskb = masks[par]
            SCR = [score.r(("m", m)) for m in range(i + 1)]
            W = (i + 1) * 512
            if i > 0:
                D(lambda e: e.tensor_reduce(out=st[:, 0:1], in_=score[:, 0:i * 512], axis=AX.X, op=ALU.min), SCR, [st.r()])
                D(lambda e: e.tensor_tensor(out=st[:, 2:3], in0=st[:, 0:1], in1=st[:, 1:2], op=ALU.min), [st.r()], [st.r()])
            else:
                D(lambda e: e.tensor_copy(st[:, 2:3], st[:, 1:2]), [st.r()], [st.r()])
            yield
            D(lambda e: e.tensor_reduce(out=st[:, 3:4], in_=score[:, 0:W], axis=AX.X, op=ALU.max), SCR, [st.r()])
            D(lambda e: e.tensor_tensor(out=st[:, 4:5], in0=st[:, 3:4], in1=st[:, 2:3], op=ALU.subtract), [st.r()], [st.r()])
            yield
            D(lambda e: e.tensor_scalar(out=Hh[:], in0=pow2[:], scalar1=st[:, 4:5], scalar2=None, op0=ALU.mult), [pow2.r(), st.r()], [Hh.r()])
            D(lambda e: e.tensor_copy(lo_t[:], st[:, 2:3]), [st.r()], [lo_t.r()])
            D(lambda e: e.tensor_tensor(out=mid_t[:], in0=st[:, 2:3], in1=Hh[:, 0:1], op=ALU.add), [st.r(), Hh.r()], [mid_t.r()])
            yield
            for k in range(NIT):
                D(lambda e: e.tensor_scalar(out=maskb[:, 0:W], in0=score[:, 0:W], scalar1=mid_t[:, 0:1], scalar2=None, op0=ALU.is_ge, op1=ALU.add, accum_out=cnt_t[:, 0:1]),
                  SCR + [mid_t.r()], [maskb.r(), cnt_t.r()])
                yield
                D(lambda e, k=k: e.tensor_scalar(out=g_t[:], in0=cnt_t[:], scalar1=255.5, scalar2=Hh[:, k:k + 1], op0=ALU.is_ge, op1=ALU.mult), [cnt_t.r(), Hh.r()], [g_t.r()])
                yield
                D(lambda e, k=k: e.scalar_tensor_tensor(out=mid_t[:], in0=g_t[:], scalar=lo_t[:, 0:1], in1=Hh[:, k + 1:k + 2], op0=ALU.add, op1=ALU.add), [g_t.r(), lo_t.r(), Hh.r()], [mid_t.r()])
                D(lambda e: e.tensor_tensor(out=lo_t[:], in0=lo_t[:], in1=g_t[:], op=ALU.add), [lo_t.r(), g_t.r()], [lo_t.r()])
                yield
            D(lambda e: e.tensor_scalar(out=maskb[:, 0:W], in0=score[:, 0:W], scalar1=lo_t[:, 0:1], scalar2=-30000.0, op0=ALU.is_lt, op1=ALU.mult), SCR + [lo_t.r()], [maskb.r()])
            yield

        def phaseIII(i, par):
            qt = qts[par]; maskb = masks[par]
            kb.dma("sp", qt[:], qT_d[:, :, i * 128:(i + 1) * 128], writes=[qt.r()])

            def make_it3(m, h, c3):
                ks = slice(m * 512, (m + 1) * 512)
                Sb = Sbanks[c3 % 3]; E_ = Eb[c3 % 2]; P_ = Pb[c3 % 2]; PT_ = PT[c3 % 2]; kt_ = kTb[c3 % 3]
                first = (m == 0 and h == 0)
                hc = slice(h * 128, (h + 1) * 128)

                def S1():
                    for q in range(2):
                        kb.dma("sp", kt_[q * 64:(q + 1) * 64].rearrange("p (j s) -> p j s", j=4), kT_d[q].rearrange("j p (h i s) -> p j h i s", h=4, s=128)[:, :, h, m, :], writes=[kt_.r()])
                    kb.group("pe", [lambda e: e.matmul(Sb[:], qt[:, h, :], kt_[:], start=True, stop=False),
                                    lambda e: e.matmul(Sb[:], identb[:], maskb[:, ks], start=False, stop=True)],
                             reads=[qt.r(), kt_.r(), identb.r(), maskb.r()], writes=[Sb.r()])
                    A(lambda e: e.activation(out=P_[:], in_=Sb[:], func=AF.Exp, scale=SCALE, bias=-C0, accum_out=rs[:, h, m:m + 1]), [Sb.r()], [P_.r(), rs.r(("c", h, m))])

                def S2():
                    kb.group("pe", [(lambda e, x=x: e.transpose(pTv[:, x * 128:(x + 1) * 128], P_[:, x * 128:(x + 1) * 128], identb[:])) for x in range(4)],
                             reads=[P_.r(), identb.r()], writes=[B[3].r()])
                    A(lambda e: e.copy(PT_[:].rearrange("p a b -> p (a b)"), pTv[:, 0:512]), [B[3].r()], [PT_.r()])

                def S3():
                    kb.group("pe", [(lambda e, x=x: e.matmul(Ob[:, hc], PT_[:, x, :], v[:, m * 4 + x, h * 128:(h + 1) * 128], start=(first and x == 0), stop=(m == i and h == 3 and x == 3))) for x in range(4)],
                             reads=[PT_.r()] + VV, writes=[Ob.r(("h", h))] + ([Ob.r(("h", hh)) for hh in range(4)] if first else []))
                return (S1, S2, S3)

            its3 = []
            for m in range(i + 1):
                for h in range(4):
                    its3.append(make_it3(m, h, cstate[0]))
                    cstate[0] += 1
            n = len(its3)
            for step in range(n + 2):
                for stg in range(3):
                    k = step - stg
                    if 0 <= k < n:
                        its3[k][stg]()
                yield
            D(lambda e: e.tensor_reduce(out=rsum[:], in_=rs[:, :, 0:i + 1], axis=AX.X, op=ALU.add), [rs.r(("c", hh, mm)) for hh in range(4) for mm in range(i + 1)], [rsum.r()])
            D(lambda e: e.reciprocal(rinv[:], rsum[:]), [rsum.r()], [rinv.r()])
            D(lambda e: e.tensor_tensor(out=osb[:].rearrange("p (h d) -> p h d", h=4), in0=Ob[:].rearrange("p (h d) -> p h d", h=4), in1=rinv[:].unsqueeze(2).to_broadcast([128, 4, 128]), op=ALU.mult),
              [Ob.r(("h", hh)) for hh in range(4)] + [rinv.r()], [osb.r()])
            kb.dma("sp", o_dsa[:, i, :], osb[:], reads=[osb.r()], is_output=True)
            yield

        def run_interleaved(gens):
            lists = []
            for g in gens:
                lists.append(g)
            active = [[g, w, 0.0] for g, w in lists]
            total = max(w for _, w in lists)
            for stepi in range(total):
                for a in active:
                    g, w, acc = a
                    a[2] += w / total
                    while a[2] >= 1.0:
                        a[2] -= 1.0
                        try:
                            next(g)
                        except StopIteration:
                            a[2] = -1e9
            for a in active:
                for _ in a[0]:
                    pass

        def est_I(i):
            return 1 + (i + 1) * 5

        def est_II(i):
            return 3 + NIT * 3 + 1

        def est_III(i):
            return 4 * (i + 1) + 3

        QL = list(QT)
        for _ in phaseI(QL[0], 0):
            pass
        nq = len(QL)
        for t in range(nq + 1):
            gens = []
            if t < nq:
                gens.append((phaseII(QL[t], t % 2), est_II(QL[t])))
            if t >= 1:
                gens.append((phaseIII(QL[t - 1], (t - 1) % 2), est_III(QL[t - 1])))
            if t + 1 < nq:
                gens.append((phaseI(QL[t + 1], (t + 1) % 2), est_I(QL[t + 1])))
            run_interleaved(gens)
        kb.finish()
        print("DSA instructions:", kb.nins, {k: len(v) for k, v in kb.q.items()})
    return nc


def dsa_masks(j):
    q = np.arange(128)[:, None]
    col = np.arange(512)[None, :]
    jj = col // 128; s = col % 128
    vis = (jj < j) | ((jj == j) & (s <= q))
    mneg = np.where(vis, 0.0, -1e30).astype(np.float32)
    mpos = np.where(vis, 0.0, 1e30).astype(np.float32)
    return mneg, mpos


def gather_global(per_core, axis_tok_tiles):
    st = np.stack(per_core, axis=axis_tok_tiles + 1)
    sh = list(st.shape)
    sh[axis_tok_tiles:axis_tok_tiles + 2] = [64]
    return st.reshape(sh)


def build_gla(NB=16):
    nc = _get_nc()
    kv_d = dram_in(nc, "kv", [2, 4, 64, 8192], BF16)
    dec_d = dram_in(nc, "dec", [1, 4, 128, 64], F32)
    qd_d = dram_in(nc, "qdT", [128, 2, 2048], BF16)
    oin_d = dram_in(nc, "oin", [128, 16, 512], F32)
    grs_d = dram_in(nc, "grs", [128, 16, 512], F32)
    gsel_d = dram_in(nc, "gsel", [128, 2, 8], F32)
    gn_d = dram_in(nc, "gnorm", [128], F32)
    o_gla = dram_out(nc, "o_gla", [128, 16, 512], F32)
    with _phase(nc) as kb:
        S_ = lambda n, s, d: sb(kb, n, s, d)
        D = lambda fn, r, w: kb.op("dve", fn, reads=r, writes=w)
        A = lambda fn, r, w: kb.op("act", fn, reads=r, writes=w)
        dec = S_("dec", [128, 2, 128], F32); gsel = S_("gsel", [128, 2, 8], F32); gnb = S_("gnb", [128, 128], F32)
        for jj in range(4):
            kb.dma("sp", dec[:].rearrange("p a (i j c) -> p a i j c", j=4, c=2)[:, :, :, jj, :], dec_d[0, jj].rearrange("p (a i c) -> p a i c", a=2, c=2), writes=[dec.r()])
        kb.dma("sp", gsel[:], gsel_d, writes=[gsel.r()])
        kb.dma("sp", gnb[:], gn_d.partition_broadcast(128), writes=[gnb.r()])
        St = S_("St", [128, 2, 128], F32); Ssels = [S_(f"Ssel{i}", [128, 2, 256], F32) for i in range(2)]; Sselb = S_("Sselb", [128, 2, 2, 128], BF16)
        kvb = [S_(f"kvb{i}", [128, 4, 2, 2, 128], BF16) for i in range(2)]
        qd = S_("qd", [128, 2, 128], BF16); oin = S_("oin", [128, 512], F32); grs = S_("grs", [128, 512], F32)
        og = S_("og", [128, 512], F32); sq = S_("sq", [128, 512], F32); ssq = S_("ssq", [128, 4], F32); ssq2 = S_("ssq2", [128, 4], F32)
        BA = ps(kb, "BA", [128, 512], F32); BB = ps(kb, "BB", [128, 512], F32)
        D(lambda e: e.memset(St[:], 0.0), [], [St.r(("p", 0)), St.r(("p", 1))])
        Sflat = St[:].rearrange("p a b -> p (a b)")

        def scan(i):
            Ssel = Ssels[i % 2]
            D(lambda e: e.memset(Ssel[:], 0.0), [], [Ssel.r(("c", 0)), Ssel.r(("c", 1))])
            kvb_ = kvb[i % 2]
            for jj in range(4):
                for q in range(2):
                    kb.dma("sp", kvb_[q * 64:(q + 1) * 64, jj], kv_d[q, jj].rearrange("p (a n d) -> p a n d", a=2, d=128)[:, :, 2 * i:2 * i + 2, :], writes=[kvb_.r()])
            for k in range(8):
                n = 8 * i + k
                for c in range(2):
                    D(lambda e, c=c, k=k: e.scalar_tensor_tensor(out=Ssel[:, c, :], in0=Sflat, scalar=gsel[:, c, k:k + 1], in1=Ssel[:, c, :], op0=ALU.mult, op1=ALU.add),
                      [St.r(("p", 0)), St.r(("p", 1)), gsel.r(), Ssel.r(("c", c))], [Ssel.r(("c", c))])
                for p in range(2):
                    D(lambda e, p=p, n=n, k=k: e.scalar_tensor_tensor(out=St[:, p, :], in0=St[:, p, :], scalar=dec[:, p, n:n + 1], in1=kvb_[:, k // 2, p, k % 2, :], op0=ALU.mult, op1=ALU.add),
                      [St.r(("p", p)), dec.r(), kvb_.r()], [St.r(("p", p))])
                yield

        def epilogue(i):
            Ssel = Ssels[i % 2]
            A(lambda e: e.copy(Sselb[:].rearrange("p c a b -> p (c a b)"), Ssel[:].rearrange("p c x -> p (c x)")), [Ssel.r(("c", 0)), Ssel.r(("c", 1))], [Sselb.r()])
            kb.dma("sp", qd[:], qd_d[:, :, i * 128:(i + 1) * 128], writes=[qd.r()])
            kb.dma("sp", oin[:], oin_d[:, i, :], writes=[oin.r()])
            kb.dma("sp", grs[:], grs_d[:, i, :], writes=[grs.r()])
            yield
            fns = []
            for c in range(2):
                for p in range(2):
                    for half in range(2):
                        bank = BA if half == 0 else BB
                        fns.append(lambda e, c=c, p=p, half=half, bank=bank: e.matmul(bank[:, (c * 2 + p) * 128:(c * 2 + p + 1) * 128], qd[half * 64:(half + 1) * 64, p, :], Sselb[half * 64:(half + 1) * 64, c, p, :], start=True, stop=True))
            kb.group("pe", fns, reads=[qd.r(), Sselb.r()], writes=[BA.r(), BB.r()])
            yield
            for h in range(4):
                p, half = divmod(h, 2)
                bank = BA if half == 0 else BB
                for c in range(2):
                    rows = slice(c * 64, (c + 1) * 64)
                    D(lambda e, h=h, c=c, p=p, bank=bank, rows=rows: e.tensor_tensor(out=og[rows, h * 128:(h + 1) * 128], in0=bank[rows, (c * 2 + p) * 128:(c * 2 + p + 1) * 128], in1=oin[rows, h * 128:(h + 1) * 128], op=ALU.add),
                      [bank.r(), oin.r()], [og.r(("h", h))])
                yield
            OG = [og.r(("h", h)) for h in range(4)]
            D(lambda e: e.tensor_tensor(out=sq[:], in0=og[:], in1=og[:], op=ALU.mult), OG, [sq.r()])
            yield
            D(lambda e: e.tensor_reduce(out=ssq[:], in_=sq[:].rearrange("p (h d) -> p h d", h=4), axis=AX.X, op=ALU.add), [sq.r()], [ssq.r()])
            yield
            kb.op("dve", lambda e: e.tensor_scalar(out=ssq2[:], in0=ssq[:], scalar1=1.0 / 128, scalar2=EPS, op0=ALU.mult, op1=ALU.add), reads=[ssq.r()], writes=[ssq2.r()])
            yield
            kb.op("act", lambda e: e.activation(out=ssq2[:], in_=ssq2[:], func=AF.Sqrt), reads=[ssq2.r()], writes=[ssq2.r()])
            yield
            kb.op("dve", lambda e: e.reciprocal(ssq[:], ssq2[:]), reads=[ssq2.r()], writes=[ssq.r()])
            yield
            og3 = og[:].rearrange("p (h d) -> p h d", h=4)
            D(lambda e: e.tensor_tensor(out=og3, in0=og3, in1=ssq[:].unsqueeze(2).to_broadcast([128, 4, 128]), op=ALU.mult), OG + [ssq.r()], OG)
            yield
            D(lambda e: e.tensor_tensor(out=og3, in0=og3, in1=gnb[:].unsqueeze(1).to_broadcast([128, 4, 128]), op=ALU.mult), OG + [gnb.r()], OG)
            yield
            D(lambda e: e.tensor_tensor(out=og[:], in0=og[:], in1=grs[:], op=ALU.mult), OG + [grs.r()], OG)
            kb.dma("sp", o_gla[:, i, :], og[:], reads=OG, is_output=True)
            yield

        for _ in scan(0):
            pass
        for i in range(NB):
            ep = epilogue(i)
            if i + 1 < NB:
                for _ in scan(i + 1):
                    for _n in range(2):
                        try:
                            next(ep)
                        except StopIteration:
                            break
            for _ in ep:
                pass
        kb.finish()
        print("GLA instructions:", kb.nins, {k: len(v) for k, v in kb.q.items()})
    return nc


def gla_sel(j):
    g = np.zeros((128, 2, 8), np.float32)
    for c in range(2):
        g[:, c, 2 * j + c] = 1.0
    return g


NT = 16
POOL_W = (2, 4, 8, 16)


def build_l1b():
    nc = _get_nc()
    xs = dram_in(nc, "xs", [NT * 128, 1024], F32)
    cT_d = dram_in(nc, "cT", [128, 8], F32)
    ada_w = dram_in(nc, "ada_w", [1024, 6144], F32)
    ada_b = dram_in(nc, "ada_b", [1, 6144], F32)
    mixn_d = dram_in(nc, "mixn", [128, 8], F32)
    w_in = dram_in(nc, "w_in", [1024, 2048], F32)
    qn_d = dram_in(nc, "q_norm", [128], F32)
    kn_d = dram_in(nc, "k_norm", [128], F32)
    identb_d = dram_in(nc, "identb", [128, 128], BF16)
    o_mod = dram_out(nc, "o_mod", [1, 6144], F32)
    o_qT = dram_out(nc, "o_qT", [128, 4, NT * 128], BF16)
    o_kT = dram_out(nc, "o_kT", [128, 4, NT * 128], BF16)
    o_v = dram_out(nc, "o_v", [128, NT, 512], BF16)
    o_u = dram_out(nc, "o_u", [128, NT, 512], F32)
    o_uh = dram_out(nc, "o_uh", [256, 512], F32)
    with _phase(nc) as kb:
        S = lambda n, s, d: sb(kb, n, s, d)
        D = lambda fn, r, w: kb.op("dve", fn, reads=r, writes=w)
        A = lambda fn, r, w: kb.op("act", fn, reads=r, writes=w)
        identb = S("identb", [128, 128], BF16); mixn = S("mixn", [128, 8], F32)
        qnb = S("qnb", [128, 128], F32); knb = S("knb", [128, 128], F32)
        for t, d in ((identb, identb_d), (mixn, mixn_d)):
            kb.dma("sp", t[:], d, writes=[t.r()])
        for t, d in ((qnb, qn_d), (knb, kn_d)):
            kb.dma("sp", t[:], d.partition_broadcast(128), writes=[t.r()])
        wb = S("wb", [128, 8, 2048], BF16)
        for k in range(8):
            kb.dma("pool", wb[:, k, :], w_in[k * 128:(k + 1) * 128, :], writes=[wb.r(("k", k))])
        WB = [wb.r(("k", k)) for k in range(8)]
        modrow = S("modrow", [1, 6144], F32); modP = S("modP", [128, 48], F32)
        pT = ps(kb, "pT", [128, 8, 128], BF16)
        pp = [ps(kb, f"pp{i}", [128, 512], F32) for i in range(2)]
        pA = ps(kb, "pA", [128, 512], F32); pB = ps(kb, "pB", [128, 512], F32)
        emit_mod(kb, cT_d, ada_w, ada_b, modrow, modP, "m1", pA, pB)
        kb.dma("sp", o_mod, modrow[:], reads=[modrow.r()], is_output=True)
        a1 = S("a1", [128, 8], F32)
        D(lambda e: e.scalar_tensor_tensor(out=a1[:], in0=modP[:, 8:16], scalar=1.0, in1=mixn[:], op0=ALU.add, op1=ALU.mult), [modP.r(), mixn.r()], [a1.r()])
        xt = [S(f"xt{i}", [128, 1024], F32) for i in range(2)]
        junk = S("junk", [128, 1024], BF16); xn = S("xn", [128, 1024], BF16)
        st = S("st", [128, 1], F32); st2 = S("st2", [128, 1], F32)
        hT = [S(f"hT{i}", [128, 8, 128], BF16) for i in range(2)]
        proj = S("proj", [128, 2048], F32)
        sq = S("sq", [128, 512], F32); ssq = S("ssq", [128, 4], F32); ssq2 = S("ssq2", [128, 4], F32)
        qn = S("qn", [128, 4, 128], F32); qr = S("qr", [128, 4, 128], BF16)
        sqb = S("sqb", [128, 512], F32); ssqb = S("ssqb", [128, 4], F32); ssq2b = S("ssq2b", [128, 4], F32)
        qnb2 = S("qnb2", [128, 4, 128], F32); qrb = S("qrb", [128, 4, 128], BF16)
        TMP = [(sq, ssq, ssq2, qn, qr), (sqb, ssqb, ssq2b, qnb2, qrb)]
        qT_t = S("qT_t", [128, 4, 128], BF16); kT_t = S("kT_t", [128, 4, 128], BF16)
        vb = S("vb", [128, 512], BF16)
        for i in range(NT):
            x_ = xt[i % 2]; h_ = hT[i % 2]
            kb.dma("sp", x_[:], xs[i * 128:(i + 1) * 128, :], writes=[x_.r()])
            emit_norm_hT(kb, x_, h_, a1, modP_sh(modP), identb, junk, xn, pT, st, st2)
            for cb in range(4):
                p_ = pp[cb % 2]
                kb.group("pe", [(lambda e, k=k, cb=cb, p_=p_, h_=h_: e.matmul(p_[:], h_[:, k, :], wb[:, k, cb * 512:(cb + 1) * 512], start=(k == 0), stop=(k == 7))) for k in range(8)],
                         reads=[h_.r()] + WB, writes=[p_.r()])
                if cb % 2 == 0:
                    A(lambda e, cb=cb, p_=p_: e.copy(proj[:, cb * 512:(cb + 1) * 512], p_[:]), [p_.r()], [proj.r(("c", cb))])
                else:
                    D(lambda e, cb=cb, p_=p_: e.tensor_copy(proj[:, cb * 512:(cb + 1) * 512], p_[:]), [p_.r()], [proj.r(("c", cb))])
            PR = [proj.r(("c", cb)) for cb in range(4)]
            CH = [[], [], []]
            for ci, (c0, gain, dstT, dstD) in enumerate(((512, qnb, qT_t, o_qT), (1024, knb, kT_t, o_kT))):
                kb.rec = CH[ci]
                sq_, ssq_, ssq2_, qn_, qr_ = TMP[ci]
                src = proj[:, c0:c0 + 512]
                D(lambda e, src=src, sq_=sq_: e.tensor_tensor(out=sq_[:], in0=src, in1=src, op=ALU.mult), PR, [sq_.r()])
                D(lambda e, sq_=sq_, ssq_=ssq_: e.tensor_reduce(out=ssq_[:], in_=sq_[:].rearrange("p (h d) -> p h d", h=4), axis=AX.X, op=ALU.add), [sq_.r()], [ssq_.r()])
                emit_rstd(kb, ssq_, 128, ssq2_)
                s3 = src.rearrange("p (h d) -> p h d", h=4)
                D(lambda e, s3=s3, qn_=qn_, ssq_=ssq_: e.tensor_tensor(out=qn_[:], in0=s3, in1=ssq_[:].unsqueeze(2).to_broadcast([128, 4, 128]), op=ALU.mult), PR + [ssq_.r()], [qn_.r()])
                D(lambda e, gain=gain, qn_=qn_, qr_=qr_: e.tensor_tensor(out=qr_[:], in0=qn_[:], in1=gain[:].unsqueeze(1).to_broadcast([128, 4, 128]), op=ALU.mult), [qn_.r(), gain.r()], [qr_.r()])
                kb.group("pe", [(lambda e, h=h, qr_=qr_, ci=ci: e.transpose(pT[:, 4 * ci + h, :], qr_[:, h, :], identb[:])) for h in range(4)], reads=[qr_.r(), identb.r()], writes=[pT.r()])
                A(lambda e, dstT=dstT, ci=ci: e.copy(dstT[:], pT[:, 4 * ci:4 * ci + 4, :]), [pT.r()], [dstT.r()])
                kb.dma("sp", dstD[:, :, i * 128:(i + 1) * 128], dstT[:], reads=[dstT.r()], is_output=True)
            kb.rec = CH[2]
            D(lambda e: e.tensor_copy(vb[:], proj[:, 1536:2048]), PR, [vb.r()])
            kb.dma("sp", o_v[:, i, :], vb[:], reads=[vb.r()], is_output=True)
            kb.dma("sp", o_u[:, i, :], proj[:, 0:512], reads=PR, is_output=True)
            kb.dma("sp", o_uh[i * 16:(i + 1) * 16, :], proj[112:128, 0:512], reads=PR, is_output=True)
            kb.emit_roundrobin(CH)
        kb.finish()
        print("L1b instructions:", kb.nins, {k: len(v) for k, v in kb.q.items()})
    return nc


def build_pool():
    nc = _get_nc()
    u_d = dram_in(nc, "u", [128, NT, 512], F32)
    guh_d = dram_in(nc, "guh", [1, 4, 256, 512], F32)
    hsel_d = dram_in(nc, "hsel", [128, 4], F32)
    band_d = dram_in(nc, "band", [128, 4, 128], F32)
    band0_d = dram_in(nc, "band0", [128, 4, 128], F32)
    bandhc_d = dram_in(nc, "bandhc", [128, 2, NT, 4, 128], F32)
    pw_d = dram_in(nc, "pool_w", [4, 128, 128], F32)
    psc_d = dram_in(nc, "pool_scale", [512], F32)
    o_pool = dram_out(nc, "o_pool", [128, NT, 512], F32)
    with _phase(nc) as kb:
        S = lambda n, s, d: sb(kb, n, s, d)
        D = lambda fn, r, w: kb.op("dve", fn, reads=r, writes=w)
        A = lambda fn, r, w: kb.op("act", fn, reads=r, writes=w)
        hsel = S("hsel", [128, 4], F32); band = S("band", [128, 4, 128], F32); band0 = S("band0", [128, 4, 128], F32)
        pscb = S("pscb", [128, 512], F32); pw = S("pw", [128, 4, 128], BF16)
        for t, d in ((hsel, hsel_d), (band, band_d), (band0, band0_d)):
            kb.dma("sp", t[:], d, writes=[t.r()])
        kb.dma("sp", pscb[:], psc_d.partition_broadcast(128), writes=[pscb.r()])
        kb.dma("pool", pw[:], pw_d.rearrange("g c d -> c g d"), writes=[pw.r()])
        uhc = S("uhc", [128, 4, 2, 512], F32); uhs = S("uhs", [128, 2, 512], F32)
        for jj in range(4):
            for hh in range(2):
                kb.dma("sp", uhc[:, jj, hh, :], guh_d[0, jj, hh * 128:(hh + 1) * 128, :], writes=[uhc.r()])
        uc = uhc[:].rearrange("p j h c -> p j (h c)"); us = uhs[:].rearrange("p h c -> p (h c)")
        D(lambda e: e.tensor_scalar(out=us, in0=uc[:, 0, :], scalar1=hsel[:, 0:1], scalar2=None, op0=ALU.mult), [uhc.r(), hsel.r()], [uhs.r()])
        for jj in range(1, 4):
            D(lambda e, jj=jj: e.scalar_tensor_tensor(out=us, in0=uc[:, jj, :], scalar=hsel[:, jj:jj + 1], in1=us, op0=ALU.mult, op1=ALU.add), [uhc.r(), hsel.r(), uhs.r()], [uhs.r()])
        pA = ps(kb, "pA", [128, 512], F32); pB = ps(kb, "pB", [128, 512], F32)
        ut = [S(f"ut{i}", [128, 512], F32) for i in range(2)]
        bh = [S(f"bh{i}", [128, 2, 4, 128], F32) for i in range(2)]
        plT = S("plT", [128, 4, 128], BF16); opl = S("opl", [128, 512], F32)
        for i in range(NT):
            u_ = ut[i % 2]; bh_ = bh[i % 2]
            kb.dma("sp", u_[:], u_d[:, i, :], writes=[u_.r()])
            kb.dma("sp", bh_[:], bandhc_d[:, :, i, :, :], writes=[bh_.r()])
            fns = []
            for g in range(4):
                bo = band0[:, g, :] if i == 0 else band[:, g, :]
                fns.append(lambda e, g=g, bo=bo, u_=u_: e.matmul(pA[:, g * 128:(g + 1) * 128], u_[:, g * 128:(g + 1) * 128], bo, start=True, stop=False))
                fns.append(lambda e, g=g, bh_=bh_: e.matmul(pA[:, g * 128:(g + 1) * 128], uhs[:, 0, g * 128:(g + 1) * 128], bh_[:, 0, g, :], start=False, stop=False))
                fns.append(lambda e, g=g, bh_=bh_: e.matmul(pA[:, g * 128:(g + 1) * 128], uhs[:, 1, g * 128:(g + 1) * 128], bh_[:, 1, g, :], start=False, stop=True))
            kb.group("pe", fns, reads=[u_.r(), bh_.r(), uhs.r(), band.r(), band0.r()], writes=[pA.r()])
            A(lambda e: e.copy(plT[:].rearrange("p g t -> p (g t)"), pA[:]), [pA.r()], [plT.r()])
            kb.group("pe", [(lambda e, g=g: e.matmul(pB[:, g * 128:(g + 1) * 128], plT[:, g, :], pw[:, g, :], start=True, stop=True)) for g in range(4)], reads=[plT.r(), pw.r()], writes=[pB.r()])
            D(lambda e: e.tensor_tensor(out=opl[:], in0=pB[:], in1=pscb[:], op=ALU.mult), [pB.r(), pscb.r()], [opl.r()])
            kb.dma("sp", o_pool[:, i, :], opl[:], reads=[opl.r()], is_output=True)
        kb.finish()
    return nc


def pool_consts_core(j):
    s_ = np.arange(128)[:, None]; t_ = np.arange(128)[None, :]
    band = np.zeros((128, 4, 128), np.float32); band_first = np.zeros((128, 4, 128), np.float32)
    bandhc = np.zeros((128, 2, 16, 4, 128), np.float32)
    for g, w in enumerate(POOL_W):
        inwin = ((t_ - s_) >= 0) & ((t_ - s_) <= w - 1)
        band[:, g, :] = inwin / float(w) - (s_ == t_)
        cnt = np.minimum(t_ + 1.0, float(w))
        band_first[:, g, :] = inwin / cnt - (s_ == t_)
        for i in range(16):
            isrc = i if j > 0 else i - 1
            if isrc < 0:
                continue
            half, slot = divmod(isrc, 8)
            for r in range(16):
                srel = r - 16
                row = (((np.arange(128) - srel) >= 0) & ((np.arange(128) - srel) <= w - 1)) / float(w)
                bandhc[slot * 16 + r, half, i, g, :] = row
    hsel = np.zeros((128, 4), np.float32)
    hsel[:, (j - 1) % 4] = 1.0
    return band, (band_first if j == 0 else band), bandhc, hsel


def pool_consts():
    s = np.arange(128)[:, None]; t = np.arange(128)[None, :]
    band = np.zeros((128, 4, 128), np.float32); band_first = np.zeros((128, 4, 128), np.float32)
    bandh = np.zeros((128, 8, 4, 128), np.float32)
    for g, w in enumerate(POOL_W):
        inwin = ((t - s) >= 0) & ((t - s) <= w - 1)
        band[:, g, :] = inwin / float(w) - (s == t)
        cnt = np.minimum(t + 1.0, float(w))
        band_first[:, g, :] = inwin / cnt - (s == t)
        for r in range(16):
            srel = r - 16
            row = (((np.arange(128) - srel) >= 0) & ((np.arange(128) - srel) <= w - 1)) / float(w)
            for slot in range(8):
                bandh[slot * 16 + r, slot, g, :] = row
    return band, bandh, band_first


SCALE = float(128 ** -0.5)


def build_sb(QT=tuple(range(16))):
    nc = _get_nc()
    qT_d = dram_in(nc, "qT", [128, 4, 2048], BF16)
    kT_d = dram_in(nc, "kT", [2, 4, 64, 8192], BF16)
    v_d = dram_in(nc, "v", [2, 4, 64, 8192], BF16)
    mask_d = dram_in(nc, "sbmask", [128, 512], F32)
    U_d = dram_in(nc, "U", [128, 128], BF16)
    ones_d = dram_in(nc, "ones", [128, 128], BF16)
    o_sb = dram_out(nc, "o_sb", [128, 16, 512], F32)
    with _phase(nc) as kb:
        S = lambda n, s, d: sb(kb, n, s, d)
        D = lambda fn, r, w: kb.op("dve", fn, reads=r, writes=w)
        A = lambda fn, r, w: kb.op("act", fn, reads=r, writes=w)
        kT = S("kT", [128, 4, 8192], BF16); v = S("v", [128, 64, 512], BF16)
        for h in range(4):
            for jj in range(4):
                for q in range(2):
                    kb.dma("sp", kT[q * 64:(q + 1) * 64, h, :].rearrange("p (i j s) -> p i j s", j=4, s=128)[:, :, jj, :],
                           kT_d[q, jj].rearrange("p (h i s) -> p h i s", h=4, s=128)[:, h, :, :], writes=[kT.r(("h", h))])
        for jj in range(4):
            for q in range(2):
                kb.dma("sp", v[q * 64:(q + 1) * 64].rearrange("p (i j) c -> p i j c", j=4)[:, :, jj, :], v_d[q, jj].rearrange("p (i c) -> p i c", c=512), writes=[v.r(("g", jj))])
        KT = [kT.r(("h", h)) for h in range(4)]; VV = [v.r(("g", g)) for g in range(8)]
        mask = S("mask", [128, 512], F32); U = S("U", [128, 128], BF16); ones = S("ones", [128, 128], BF16)
        for t, d in ((mask, mask_d), (U, U_d), (ones, ones_d)):
            kb.dma("sp", t[:], d, writes=[t.r()])
        B = [ps(kb, f"B{i}", [128, 512], F32) for i in range(8)]
        qt = S("qt", [128, 4, 128], BF16)
        NBUF = 5
        eb = [S(f"eb{i}", [128, 512], F32) for i in range(NBUF)]
        spb = [S(f"spb{i}", [128, 512], F32) for i in range(NBUF)]
        Lbb = [S(f"Lb{i}", [128, 512], BF16) for i in range(NBUF)]
        tb = [S(f"tb{i}", [128, 512], F32) for i in range(NBUF)]
        wbb = [S(f"wb{i}", [128, 512], BF16) for i in range(NBUF)]
        Csb = S("Csb", [128, 4, 128], F32)
        osb = S("osb", [128, 512], F32)
        Zs = (B[0], B[1], B[2]); As = (B[3], B[4], B[5], B[6]); Ob = B[7]
        qts = [qt, S("qt2", [128, 4, 128], BF16)]
        qss = [S("qs0", [128, 4, 128], BF16), S("qs1", [128, 4, 128], BF16)]

        def make_it(i, m, h, ctr, qt_, qs_):
            Z = Zs[ctr % 3]; Aa = As[ctr % 4]
            e_ = eb[ctr % NBUF]; sp_ = spb[ctr % NBUF]; L_ = Lbb[ctr % NBUF]; t_ = tb[ctr % NBUF]; w_ = wbb[ctr % NBUF]
            diag = (m == i)
            first = diag and h == 0
            last = (m == 0 and h == 3)
            hc = slice(h * 128, (h + 1) * 128)

            def S1():
                if first:
                    kb.dma("sp", qt_[:], qT_d[:, :, i * 128:(i + 1) * 128], writes=[qt_.r()])
                    D(lambda e: e.tensor_scalar(out=qs_[:], in0=qt_[:], scalar1=SCALE, scalar2=None, op0=ALU.mult), [qt_.r()], [qs_.r()])
                kb.group("pe", [(lambda e, x=x: e.matmul(Z[:, x * 128:(x + 1) * 128], kT[:, h, m * 512 + x * 128: m * 512 + (x + 1) * 128], qt_[:, h, :], start=True, stop=True)) for x in range(4)],
                         reads=[KT[h], qt_.r()], writes=[Z.r()])
                A(lambda e: e.activation(out=e_[:], in_=Z[:], func=AF.Exp, scale=-SCALE), [Z.r()], [e_.r()])

            def S2():
                A(lambda e: e.activation(out=sp_[:], in_=e_[:], func=AF.Ln, bias=1.0), [e_.r()], [sp_.r()])
                D(lambda e: e.scalar_tensor_tensor(out=L_[:], in0=Z[:], scalar=-SCALE, in1=sp_[:], op0=ALU.mult, op1=ALU.subtract), [Z.r(), sp_.r()], [L_.r()])
                if diag:
                    D(lambda e: e.tensor_tensor(out=L_[:], in0=L_[:], in1=mask[:], op=ALU.mult), [L_.r(), mask.r()], [L_.r()])

            def S3():
                fns = []
                for x in range(4):
                    fns.append(lambda e, x=x: e.matmul(Aa[:, x * 128:(x + 1) * 128], U[:], L_[:, x * 128:(x + 1) * 128], start=True, stop=False))
                    for x2 in range(x + 1, 4):
                        fns.append(lambda e, x=x, x2=x2: e.matmul(Aa[:, x * 128:(x + 1) * 128], ones[:], L_[:, x2 * 128:(x2 + 1) * 128], start=False, stop=False))
                    fns.append(lambda e, x=x: e.matmul(Aa[:, x * 128:(x + 1) * 128], kT[:, h, m * 512 + x * 128: m * 512 + (x + 1) * 128], qs_[:, h, :], start=False, stop=(x == 3)))
                kb.group("pe", fns, reads=[U.r(), ones.r(), L_.r(), KT[h], qs_.r()], writes=[Aa.r()])
                if not diag:
                    D(lambda e: e.tensor_tensor(out=t_[:].rearrange("p (x q) -> p x q", x=4), in0=Aa[:].rearrange("p (x q) -> p x q", x=4), in1=Csb[:, h, :].unsqueeze(1).to_broadcast([128, 4, 128]), op=ALU.add),
                      [Aa.r(), Csb.r(("h", h))], [t_.r()])

            def S4():
                if diag:
                    A(lambda e: e.activation(out=w_[:], in_=Aa[:], func=AF.Exp), [Aa.r()], [w_.r()])
                else:
                    A(lambda e: e.activation(out=w_[:], in_=t_[:], func=AF.Exp), [t_.r()], [w_.r()])
                if diag:
                    D(lambda e: e.tensor_tensor(out=w_[:], in0=w_[:], in1=mask[:], op=ALU.mult), [w_.r(), mask.r()], [w_.r()])

            def S5():
                if m > 0:
                    kb.group("pe", [(lambda e, x=x: e.matmul(Aa[:, 0:128], ones[:], L_[:, x * 128:(x + 1) * 128], start=(x == 0), stop=(x == 3))) for x in range(4)],
                             reads=[ones.r(), L_.r()], writes=[Aa.r()])
                    if diag:
                        D(lambda e: e.tensor_copy(Csb[:, h, :], Aa[:, 0:128]), [Aa.r()], [Csb.r(("h", h))])
                    else:
                        D(lambda e: e.tensor_tensor(out=Csb[:, h, :], in0=Aa[:, 0:128], in1=Csb[:, h, :], op=ALU.add), [Aa.r(), Csb.r(("h", h))], [Csb.r(("h", h))])
                kb.group("pe", [(lambda e, x=x: e.matmul(Ob[:, hc], w_[:, x * 128:(x + 1) * 128], v[:, m * 4 + x, h * 128:(h + 1) * 128], start=(first and x == 0), stop=(last and x == 3))) for x in range(4)],
                         reads=[w_.r()] + VV, writes=[Ob.r(("h", h))] + ([Ob.r(("h", hh)) for hh in range(4)] if first else []))
                if last:
                    A(lambda e: e.copy(osb[:], Ob[:]), [Ob.r(("h", hh)) for hh in range(4)], [osb.r()])
                    kb.dma("sp", o_sb[:, i, :], osb[:], reads=[osb.r()], is_output=True)
            return (S1, S2, S3, S4, S5)

        its = []
        ctr = 0
        for qi, i in enumerate(QT):
            for m in range(i, -1, -1):
                for h in range(4):
                    its.append(make_it(i, m, h, ctr, qts[qi % 2], qss[qi % 2]))
                    ctr += 1
        emit_skewed(its, 5)
        kb.finish()
        print("SB instructions:", kb.nins, {k: len(v) for k, v in kb.q.items()})
    return nc


def sb_mask(j):
    s = np.arange(128)[:, None]
    col = np.arange(512)[None, :]
    jj = col // 128; t = col % 128
    vis = (jj < j) | ((jj == j) & (s < t))
    return vis.astype(np.float32)


def sb_consts():
    jx = np.arange(128)[:, None]; sx = np.arange(128)[None, :]
    U = (jx >= sx).astype(np.float32).astype(ml_dtypes.bfloat16)
    ones = np.ones((128, 128), np.float32).astype(ml_dtypes.bfloat16)
    return U, ones


NT = 16
GT = 2


def build_tail():
    nc = _get_nc()
    xs = dram_in(nc, "xs", [NT * 128, 1024], F32)
    mixa_d = dram_in(nc, "mixa", [128, NT, 512], F32)
    mixb_d = dram_in(nc, "mixb", [128, NT, 512], F32)
    modrow_d = dram_in(nc, "modrow", [1, 6144], F32)
    fnorm_d = dram_in(nc, "fnorm", [128, 8], F32)
    w_out = dram_in(nc, "w_out", [1024, 1024], F32)
    w1 = dram_in(nc, "w1", [1024, 5632], F32)
    w2 = dram_in(nc, "w2", [2816, 1024], F32)
    identb_d = dram_in(nc, "identb", [128, 128], BF16)
    identf_d = dram_in(nc, "identf", [128, 128], F32)
    o_x = dram_out(nc, "o_x", [NT * 128, 1024], F32)
    with _phase(nc) as kb:
        S = lambda n, s, d: sb(kb, n, s, d)
        D = lambda fn, r, w: kb.op("dve", fn, reads=r, writes=w)
        identb = S("identb", [128, 128], BF16); fnorm = S("fnorm", [128, 8], F32)
        mod48 = S("mod48", [48, 128], F32); identf = S("identf", [128, 128], F32)
        kb.dma("sp", identb[:], identb_d, writes=[identb.r()])
        kb.dma("sp", fnorm[:], fnorm_d, writes=[fnorm.r()])
        kb.dma("sp", mod48[:], modrow_d.rearrange("o (c p) -> (o c) p", p=128), writes=[mod48.r()])
        kb.dma("sp", identf[:], identf_d, writes=[identf.r()])
        woutb = S("woutb", [128, 8, 1024], BF16); w1b = S("w1b", [128, 8, 5632], BF16); w2b = S("w2b", [128, 22, 1024], BF16)
        for k in range(8):
            kb.dma("pool", woutb[:, k, :], w_out[k * 128:(k + 1) * 128, :], writes=[woutb.r(("k", k))])
        for k in range(8):
            kb.dma("pool", w1b[:, k, :], w1[k * 128:(k + 1) * 128, :], writes=[w1b.r(("k", k))])
        for f in range(22):
            kb.dma("pool", w2b[:, f, :], w2[f * 128:(f + 1) * 128, :], writes=[w2b.r(("k", f))])
        WO = [woutb.r(("k", k)) for k in range(8)]; W1 = [w1b.r(("k", k)) for k in range(8)]; W1f = [W1] * 22; W1u = [W1] * 22; W2 = [w2b.r(("k", f)) for f in range(22)]
        pT = ps(kb, "pT", [128, 8, 128], BF16)
        pp = [ps(kb, f"pp{i}", [128, 512], F32) for i in range(2)]
        pg = ps(kb, "pg", [128, 512], F32); pu = ps(kb, "pu", [128, 512], F32)
        modP = S("modP", [128, 48], F32); G1b = S("G1b", [128, 1024], F32); G2b = S("G2b", [128, 1024], F32)
        kb.op("pe", lambda e: e.transpose(pg[:, 0:48], mod48[:], identf[0:48, 0:48]), reads=[mod48.r(), identf.r()], writes=[pg.r()])
        D(lambda e: e.tensor_copy(modP[:], pg[:, 0:48]), [pg.r()], [modP.r()])
        kb.dma("sp", G1b[:], modrow_d[0, 2048:3072].partition_broadcast(128), writes=[G1b.r()])
        kb.dma("sp", G2b[:], modrow_d[0, 5120:6144].partition_broadcast(128), writes=[G2b.r()])
        a2 = S("a2", [128, 8], F32)
        D(lambda e: e.scalar_tensor_tensor(out=a2[:], in0=modP[:, 32:40], scalar=1.0, in1=fnorm[:], op0=ALU.add, op1=ALU.mult), [modP.r(), fnorm.r()], [a2.r()])

        class SH:
            def __getitem__(s, idx):
                p, sl = idx
                return modP[p, slice(sl.start + 24, sl.stop + 24)]
            def r(s, key=None):
                return modP.r(key)
        sh2 = SH()
        xt = [S("xt0", [128, 1024], F32)] * 2
        mixt = [S("mixt0", [128, 8, 128], BF16)] * 2
        x1s = [S(f"x1_{i}", [128, GT, 1024], F32) for i in range(2)]
        xn = S("xn", [128, 1024], BF16); junk = xn
        st = S("st", [128, 1], F32); st2 = S("st2", [128, 1], F32)
        hT4s = [S(f"hT4_{i}", [128, 8, GT * 128], BF16) for i in range(2)]
        sg = S("sg", [128, GT * 128], F32); actT = S("actT", [128, 22, GT * 128], BF16)
        yt = S("yt", [128, 1024], F32)
        mst = yt

        def front(g):
            x1 = x1s[g % 2]; hT4 = hT4s[g % 2]
            for t in range(GT):
                i = g * GT + t
                x_ = xt[i % 2]; m_ = mixt[i % 2]
                kb.dma("sp", x_[:], xs[i * 128:(i + 1) * 128, :], writes=[x_.r()])
                kb.dma("sp", mst[:, 0:512], mixa_d[:, i, :], writes=[yt.r(("c", 0))])
                kb.dma("sp", mst[:, 512:1024], mixb_d[:, i, :], writes=[yt.r(("c", 1))])
                D(lambda e: e.tensor_copy(xn[:], mst[:]), [yt.r(("c", 0)), yt.r(("c", 1))], [xn.r()])
                yield
                kb.group("pe", [(lambda e, k=k: e.transpose(pT[:, k, :], xn[:, k * 128:(k + 1) * 128], identb[:])) for k in range(8)], reads=[xn.r(), identb.r()], writes=[pT.r()])
                kb.op("act", lambda e, m_=m_: e.copy(m_[:], pT[:]), reads=[pT.r()], writes=[m_.r()])
                yield
                for cb in range(2):
                    p_ = pp[cb]
                    kb.group("pe", [(lambda e, k=k, cb=cb, p_=p_, m_=m_: e.matmul(p_[:], m_[:, k, :], woutb[:, k, cb * 512:(cb + 1) * 512], start=(k == 0), stop=(k == 7))) for k in range(8)],
                             reads=[m_.r()] + WO, writes=[p_.r()])
                    D(lambda e, cb=cb, p_=p_: e.tensor_tensor(out=mst[:, cb * 512:(cb + 1) * 512], in0=p_[:], in1=G1b[:, cb * 512:(cb + 1) * 512], op=ALU.mult), [p_.r(), G1b.r()], [yt.r(("c", cb))])
                    yield
                    D(lambda e, cb=cb, x_=x_, t=t: e.tensor_tensor(out=x1[:, t, cb * 512:(cb + 1) * 512], in0=mst[:, cb * 512:(cb + 1) * 512], in1=x_[:, cb * 512:(cb + 1) * 512], op=ALU.add),
                      [yt.r(("c", cb)), x_.r()], [x1.r(("t", t, cb))])
                    yield
                kb.op("act", lambda e, t=t: e.activation(out=junk[:], in_=x1[:, t, :], func=AF.Square, accum_out=st[:, 0:1]),
                      reads=[x1.r(("t", t, 0)), x1.r(("t", t, 1))], writes=[junk.r(), st.r()])
                yield
                kb.op("dve", lambda e: e.tensor_scalar(out=st2[:], in0=st[:], scalar1=1.0 / 1024, scalar2=EPS, op0=ALU.mult, op1=ALU.add), reads=[st.r()], writes=[st2.r()])
                yield
                kb.op("act", lambda e: e.activation(out=st2[:], in_=st2[:], func=AF.Sqrt), reads=[st2.r()], writes=[st2.r()])
                yield
                kb.op("dve", lambda e: e.reciprocal(st[:], st2[:]), reads=[st2.r()], writes=[st.r()])
                yield
                kb.op("act", lambda e, t=t: e.activation(out=xn[:], in_=x1[:, t, :], func=AF.Copy, scale=st[:, 0:1]), reads=[x1.r(("t", t, 0)), x1.r(("t", t, 1)), st.r()], writes=[xn.r()])
                yield
                kb.group("pe", [(lambda e, k=k: e.transpose(pT[:, k, :], xn[:, k * 128:(k + 1) * 128], identb[:])) for k in range(8)], reads=[xn.r(), identb.r()], writes=[pT.r()])
                yield
                for k in range(8):
                    kb.op("act", lambda e, k=k, t=t: e.activation(out=hT4[:, k, t * 128:(t + 1) * 128], in_=pT[:, k, :], func=AF.Identity, scale=a2[:, k:k + 1], bias=modP[:, 24 + k:25 + k]),
                          reads=[pT.r(), a2.r(), modP.r()], writes=[hT4.r(("t", t))])
                    if k % 4 == 3:
                        yield

        def drain(gen, n=None):
            cnt = 0
            for _ in gen:
                cnt += 1
                if n is not None and cnt >= n:
                    return

        NG = NT // GT
        gens = [front(g) for g in range(NG)]
        drain(gens[0])
        for g in range(NG):
            x1 = x1s[g % 2]; hT4 = hT4s[g % 2]
            HT = [hT4.r(("t", t)) for t in range(GT)]
            for f in range(22):
                kb.group("pe", [(lambda e, k=k, f=f: e.matmul(pg[:, 0:GT * 128], w1b[:, k, f * 128:(f + 1) * 128], hT4[:, k, :], start=(k == 0), stop=(k == 7))) for k in range(8)],
                         reads=HT + W1f[f], writes=[pg.r()])
                kb.group("pe", [(lambda e, k=k, f=f: e.matmul(pu[:, 0:GT * 128], w1b[:, k, 2816 + f * 128:2816 + (f + 1) * 128], hT4[:, k, :], start=(k == 0), stop=(k == 7))) for k in range(8)],
                         reads=HT + W1u[f], writes=[pu.r()])
                kb.op("act", lambda e: e.activation(out=sg[:], in_=pg[:, 0:GT * 128], func=AF.Silu), reads=[pg.r()], writes=[sg.r()])
                D(lambda e, f=f: e.tensor_tensor(out=actT[:, f, :], in0=sg[:], in1=pu[:, 0:GT * 128], op=ALU.mult), [sg.r(), pu.r()], [actT.r(("f", f))])
                if g + 1 < NG:
                    drain(gens[g + 1], 2)
            if g + 1 < NG:
                drain(gens[g + 1])
            AT = [actT.r(("f", f)) for f in range(22)]
            for t in range(GT):
                i = g * GT + t
                for cb in range(2):
                    p_ = pp[cb]
                    kb.group("pe", [(lambda e, f=f, cb=cb, p_=p_, t=t: e.matmul(p_[:], actT[:, f, t * 128:(t + 1) * 128], w2b[:, f, cb * 512:(cb + 1) * 512], start=(f == 0), stop=(f == 21))) for f in range(22)],
                             reads=AT + W2, writes=[p_.r()])
                    D(lambda e, cb=cb, p_=p_: e.tensor_tensor(out=yt[:, cb * 512:(cb + 1) * 512], in0=p_[:], in1=G2b[:, cb * 512:(cb + 1) * 512], op=ALU.mult), [p_.r(), G2b.r()], [yt.r(("c", cb))])
                    D(lambda e, cb=cb, t=t: e.tensor_tensor(out=yt[:, cb * 512:(cb + 1) * 512], in0=yt[:, cb * 512:(cb + 1) * 512], in1=x1[:, t, cb * 512:(cb + 1) * 512], op=ALU.add),
                      [yt.r(("c", cb)), x1.r(("t", t, cb))], [yt.r(("c", cb))])
                kb.dma("sp", o_x[i * 128:(i + 1) * 128, :], yt[:], reads=[yt.r(("c", 0)), yt.r(("c", 1))], is_output=True)
        kb.finish()
        print("tail instructions:", kb.nins, {k: len(v) for k, v in kb.q.items()})
    return nc


DBG = False


def build_fused():
    FX.active = True
    FX.nc = bass.Bass("TRN2", target_bir_lowering=False)
    FX.ext = {}
    FX.n_phase = 0
    nc = FX.nc
    es = ExitStack()
    es.__enter__()
    kb = KB(nc, es)
    kb.fused = True
    kb.last_phase = False
    FX.kb = kb
    E = fx_ext
    I = lambda name, shape, dt: nc.dram_tensor(name, list(shape), dt).ap()
    RG = [[0, 1, 2, 3], [4, 5, 6, 7]]

    def allgather(pairs, n_wait=None):
        kb.pes = ExitStack()
        skip = []
        for pi, (src, dst) in enumerate(pairs):
            nq = dst.shape[0]
            rp = src.shape[0] // nq
            for q in range(nq):
                si = src[q * rp:(q + 1) * rp, :]
                do = dst[q].rearrange("j p c -> (j p) c")
                tok = kb.coll(lambda e, si=si, do=do: e.collective_compute("AllGather", ALU.bypass, replica_groups=RG, ins=[si], outs=[do]))
                if n_wait is not None and pi >= n_wait:
                    skip.append(tok[1])
        kb.end_phase(skip=tuple(skip))

    common = dict(identb=E("identb", [128, 128], BF16), identf=E("identf", [128, 128], F32), cT=E("cT", [128, 8], F32))
    xs = E("xs", [2048, 1024], F32)
    i1 = dict(o_mod=I("i_mod0", [1, 6144], F32), o_qT=I("i_qT0", [128, 4, 2048], BF16), o_kT=I("i_kT0", [128, 4, 2048], BF16), o_v=I("i_v0", [128, 16, 512], BF16),
              o_iqT=I("i_iqT", [128, 4, 2048], BF16), o_ikT=I("i_ikT", [128, 2048], BF16), o_iw=I("i_iw", [128, 16, 8], F32), o_oin=I("i_oin", [128, 16, 512], F32),
              o_qdT=I("i_qdT", [128, 2, 2048], BF16), o_kv=I("i_kv", [128, 2, 32, 128], BF16), o_dec=I("i_dec", [128, 2, 32], F32), o_gr=I("i_gr", [128, 16, 512], F32))
    FX.remap = dict(common, xs=xs, pos=E("pos", [128, 16], I32), ada_w=E("ada_w0", [1024, 6144], F32), ada_b=E("ada_b0", [1, 6144], F32),
                    mixn=E("mixn0", [128, 8], F32), w_in=E("ab_w_in", [1024, 3672], F32), gate_up=E("gate_up", [16, 256], F32), gate_b=E("gate_b", [256], F32),
                    q_norm=E("dsa_qn", [128], F32), k_norm=E("dsa_kn", [128], F32), invf=E("invf", [64], F32), tri3=E("tri3", [128, 3, 128], F32),
                    csel=E("csel", [128, 2], F32), amask=E("amask", [128, 128], F32), **i1)
    build_l1()
    G_kT0 = I("g_kT0", [2, 4, 64, 8192], BF16); G_v0 = I("g_v0", [2, 4, 64, 8192], BF16); G_ik = I("g_ik", [1, 4, 128, 2048], BF16)
    G_kv = I("g_kv", [2, 4, 64, 8192], BF16); G_dec = I("g_dec", [1, 4, 128, 64], F32)
    allgather([(i1["o_kv"].rearrange("p a n d -> p (a n d)"), G_kv), (i1["o_dec"].rearrange("p a n -> p (a n)"), G_dec),
               (i1["o_kT"].rearrange("p h t -> p (h t)"), G_kT0), (i1["o_v"].rearrange("p i c -> p (i c)"), G_v0), (i1["o_ikT"], G_ik)], n_wait=2)
    i_gla = I("i_gla", [128, 16, 512], F32)
    FX.remap = dict(common, kv=G_kv, dec=G_dec, qdT=i1["o_qdT"], oin=i1["o_oin"], grs=i1["o_gr"], gsel=E("gsel", [128, 2, 8], F32), gnorm=E("gnorm", [128], F32), o_gla=i_gla)
    build_gla()
    i_dsa = I("i_dsa", [128, 16, 512], F32)
    FX.remap = dict(common, qT=i1["o_qT"], iqT=i1["o_iqT"], iw=i1["o_iw"], kT=G_kT0, v=G_v0, ikT=G_ik, mneg=E("mneg", [128, 512], F32), mpos=E("mpos", [128, 512], F32),
                    pow2=E("pow2", [128, NIT + 1], F32), o_dsa=i_dsa)
    build_dsa()
    i_x1 = I("i_x1", [2048, 1024], F32)
    FX.remap = dict(common, xs=xs, mixa=i_gla, mixb=i_dsa, modrow=i1["o_mod"], fnorm=E("fnorm0", [128, 8], F32), w_out=E("ab_w_out", [1024, 1024], F32),
                    w1=E("w1_0", [1024, 5632], F32), w2=E("w2_0", [2816, 1024], F32), o_x=i_x1)
    build_tail()
    i5 = dict(o_mod=I("i_mod1", [1, 6144], F32), o_qT=I("i_qT1", [128, 4, 2048], BF16), o_kT=I("i_kT1", [128, 4, 2048], BF16), o_v=I("i_v1", [128, 16, 512], BF16),
              o_u=I("i_u", [128, 16, 512], F32), o_uh=I("i_uh", [256, 512], F32))
    FX.remap = dict(common, xs=i_x1, ada_w=E("ada_w1", [1024, 6144], F32), ada_b=E("ada_b1", [1, 6144], F32), mixn=E("mixn1", [128, 8], F32),
                    w_in=E("cd_w_in", [1024, 2048], F32), q_norm=E("sb_qn", [128], F32), k_norm=E("sb_kn", [128], F32), **i5)
    build_l1b()
    G_kT1 = I("g_kT1", [2, 4, 64, 8192], BF16); G_v1 = I("g_v1", [2, 4, 64, 8192], BF16); G_uh = I("g_uh", [1, 4, 256, 512], F32)
    allgather([(i5["o_uh"], G_uh), (i5["o_kT"].rearrange("p h t -> p (h t)"), G_kT1), (i5["o_v"].rearrange("p i c -> p (i c)"), G_v1)], n_wait=1)
    i_pool = I("i_pool", [128, 16, 512], F32)
    FX.remap = dict(common, u=i5["o_u"], guh=G_uh, hsel=E("hsel", [128, 4], F32), band=E("band", [128, 4, 128], F32), band0=E("band0", [128, 4, 128], F32),
                    bandhc=E("bandhc", [128, 2, 16, 4, 128], F32), pool_w=E("pool_w", [4, 128, 128], F32), pool_scale=E("pool_scale", [512], F32), o_pool=i_pool)
    build_pool()
    i_sb = I("i_sb", [128, 16, 512], F32)
    FX.remap = dict(common, qT=i5["o_qT"], kT=G_kT1, v=G_v1, sbmask=E("sbmask", [128, 512], F32), U=E("U", [128, 128], BF16), ones=E("ones", [128, 128], BF16), o_sb=i_sb)
    build_sb()
    if DBG:
        kb.pes = ExitStack()
        for nm, ap in (("d_dsa", i_dsa), ("d_gla", i_gla), ("d_pool", i_pool), ("d_sb", i_sb)):
            o = nc.dram_tensor(nm, [128, 16, 512], F32, kind="ExternalOutput").ap()
            kb.dma("sp", o, ap, is_output=True)
        o = nc.dram_tensor("d_x1", [2048, 1024], F32, kind="ExternalOutput").ap()
        kb.dma("sp", o, i_x1, is_output=True)
        o = nc.dram_tensor("d_mod1", [1, 6144], F32, kind="ExternalOutput").ap()
        kb.dma("sp", o, i5["o_mod"], is_output=True)
        kb.end_phase()
    out = nc.dram_tensor("out", [2048, 1024], F32, kind="ExternalOutput").ap()
    FX.remap = dict(common, xs=i_x1, mixa=i_pool, mixb=i_sb, modrow=i5["o_mod"], fnorm=E("fnorm1", [128, 8], F32), w_out=E("cd_w_out", [1024, 1024], F32),
                    w1=E("w1_1", [1024, 5632], F32), w2=E("w2_1", [2816, 1024], F32), o_x=out)
    kb.last_phase = True
    build_tail()
    es.close()
    FX.active = False
    return nc


def fused_maps(inp):
    identb = np.eye(128, dtype=np.float32).astype(ml_dtypes.bfloat16)
    identf = np.eye(128, dtype=np.float32)
    cs1 = l1_consts()
    pow2 = np.tile((2.0 ** -(np.arange(NIT + 1) + 1)).astype(np.float32)[None, :], (128, 1))
    U, ones = sb_consts()
    L = lambda a: np.ascontiguousarray(a.reshape(8, 128).T)
    maps = []
    for c in range(8):
        b, j = divmod(c, 4)
        mneg, mpos = dsa_masks(j)
        band, band0, bandhc, hsel = pool_consts_core(j)
        maps.append(dict(
            identb=identb, identf=identf, cT=L(inp["c"][b]),
            xs=np.ascontiguousarray(inp["x"][b].reshape(64, 128, 1024)[j::4].reshape(2048, 1024)),
            pos=np.ascontiguousarray(inp["positions"][b].reshape(64, 128)[j::4].T.astype(np.int32)),
            ada_w0=inp["ada_w"][0], ada_b0=inp["ada_b"][0][None, :], ada_w1=inp["ada_w"][1], ada_b1=inp["ada_b"][1][None, :],
            mixn0=L(inp["mix_norm"][0]), mixn1=L(inp["mix_norm"][1]), fnorm0=L(inp["ffn_norm"][0]), fnorm1=L(inp["ffn_norm"][1]),
            ab_w_in=inp["ab_w_in"][0], gate_up=inp["gla_gate_up"][0], gate_b=inp["gla_gate_b"][0], dsa_qn=inp["dsa_q_norm"][0], dsa_kn=inp["dsa_k_norm"][0],
            invf=cs1["invf"], tri3=cs1["tri3"], csel=cs1["csel"], amask=cs1["amask"],
            mneg=mneg, mpos=mpos, pow2=pow2, gsel=gla_sel(j), gnorm=inp["gla_out_norm"][0],
            ab_w_out=inp["ab_w_out"][0], w1_0=inp["ffn_w1"][0], w2_0=inp["ffn_w2"][0],
            cd_w_in=inp["cd_w_in"][0], sb_qn=inp["sb_q_norm"][0], sb_kn=inp["sb_k_norm"][0],
            hsel=hsel, band=band, band0=band0, bandhc=bandhc, pool_w=inp["pool_w"][0], pool_scale=inp["pool_scale"][0],
            sbmask=sb_mask(j), U=U, ones=ones,
            cd_w_out=inp["cd_w_out"][0], w1_1=inp["ffn_w1"][1], w2_1=inp["ffn_w2"][1]))
    return maps


def kernel(**inputs):
    inp = {k: np.asarray(v) for k, v in inputs.items()}
    nc = build_fused()
    res = run_bass_kernel_spmd(nc, fused_maps(inp), core_ids=list(range(8)))
    out = np.zeros((2, 64, 128, 1024), np.float32)
    for c in range(8):
        b, j = divmod(c, 4)
        out[b, j::4] = res.results[c]["out"].reshape(16, 128, 1024)
    kernel.last_results = res.results
    return out.reshape(2, 8192, 1024)
```

```python
import ml_dtypes
import numpy as np
from contextlib import ExitStack
import concourse.bass as bass
import concourse.mybir as mybir
from concourse.bass_utils import run_bass_kernel_spmd

F32 = mybir.dt.float32
BF16 = mybir.dt.bfloat16
I32 = mybir.dt.int32
ALU = mybir.AluOpType
AF = mybir.ActivationFunctionType
AX = mybir.AxisListType

EPOCH = 30000
ND = 32


import types


def bind(fn):
    if getattr(fn, "__closure__", None) is None:
        return fn
    cells = []
    for c in fn.__closure__:
        try:
            cells.append(types.CellType(c.cell_contents))
        except ValueError:
            cells.append(c)
    return types.FunctionType(fn.__code__, fn.__globals__, fn.__name__, fn.__defaults__, tuple(cells))


class Reg:
    __slots__ = ("w", "r", "name")

    def __init__(self, name=""):
        self.w = None
        self.r = []
        self.name = name


class KB:
    def __init__(self, nc, es):
        self.nc = nc
        self.es = es
        self.eng = {"pe": nc.tensor, "act": nc.scalar, "dve": nc.vector, "pool": nc.gpsimd, "sp": nc.sync}
        self.sems = {}
        self.cnt = {k: 0 for k in self.eng}
        self.seen = {k: {} for k in self.eng}
        self.ndma = 0
        self.dsem = [es.enter_context(nc.semaphore(f"d{i}")) for i in range(ND)]
        self.nins = 0
        self.out_toks = []
        self.q = {k: [] for k in self.eng}
        self.rec = None
        self.pes = es
        self.pfx = ""
        self.fused = False
        self.last_phase = True
        self.dma_uses = {}

    def _esem(self, st, epoch):
        key = ("E", st, epoch)
        if key not in self.sems:
            self.sems[key] = self.es.enter_context(self.nc.semaphore(f"e_{st}_{epoch}"))
        return key

    def _semh(self, key):
        if key[0] == "D":
            return self.dsem[key[1]]
        return self.sems[key]

    def _collect(self, st, reads, writes):
        waits = {}

        def need(tok, kind):
            if tok is None:
                return
            tst, key, val = tok
            if tst == st and key[0] == "E":
                if st == "pe":
                    return
                if st in ("act", "dve") and kind != "raw":
                    return
            if self.seen[st].get(key, 0) >= val:
                return
            if waits.get(key, 0) < val:
                waits[key] = val

        for r in reads:
            need(r.w, "raw")
        for w in writes:
            need(w.w, "waw")
            for t in w.r:
                need(t, "war")
        return waits

    def _dowaits(self, st, waits):
        for key, val in waits.items():
            h = self._semh(key)
            self.q[st].append(lambda eng, h=h, val=val: eng.wait_ge(h, val))
            self.seen[st][key] = val
            self.nins += 1

    def _record(self, tok, reads, writes):
        for r in reads:
            if tok[1][0] == "E":
                r.r = [t for t in r.r if not (t[0] == tok[0] and t[1] == tok[1])]
            r.r.append(tok)
        for w in writes:
            w.w = tok
            w.r = []

    def emit_roundrobin(self, chains):
        self.rec = None
        n = max(len(c) for c in chains)
        for k in range(n):
            for c in chains:
                if k < len(c):
                    c[k]()

    def op(self, st, fn, reads=(), writes=()):
        fn = bind(fn)
        if self.rec is not None:
            reads = tuple(reads); writes = tuple(writes); rec = self.rec
            rec.append(lambda: self._norec(rec, self.op, st, fn, reads, writes))
            return None
        waits = self._collect(st, reads, writes)
        self._dowaits(st, waits)
        self.cnt[st] += 1
        c = self.cnt[st]
        epoch, val = divmod(c - 1, EPOCH)
        key = self._esem(st, epoch)
        h = self.sems[key]
        self.q[st].append(lambda eng, fn=fn, h=h: fn(eng).then_inc(h, 1))
        tok = (st, key, val + 1)
        self._record(tok, reads, writes)
        self.nins += 1
        return tok

    def _norec(self, rec, f, *a, **kw):
        saved = self.rec
        self.rec = None
        try:
            return f(*a, **kw)
        finally:
            self.rec = saved

    def group(self, st, fns, reads=(), writes=()):
        if self.rec is not None:
            fns = [bind(f) for f in fns]; reads = tuple(reads); writes = tuple(writes); rec = self.rec
            rec.append(lambda: self._norec(rec, self.group, st, fns, reads, writes))
            return None
        waits = self._collect(st, reads, writes)
        self._dowaits(st, waits)
        fns = [bind(f) for f in fns]
        for fn in fns[:-1]:
            self.q[st].append(fn)
            self.nins += 1
        self.nins += 1
        self.cnt[st] += 1
        c = self.cnt[st]
        epoch, val = divmod(c - 1, EPOCH)
        key = self._esem(st, epoch)
        h = self.sems[key]
        self.q[st].append(lambda eng, fn=fns[-1], h=h: fn(eng).then_inc(h, 1))
        tok = (st, key, val + 1)
        self._record(tok, reads, writes)
        return tok

    def dma(self, st, out, in_, reads=(), writes=(), is_output=False, **kw):
        if self.rec is not None:
            reads = tuple(reads); writes = tuple(writes); rec = self.rec
            rec.append(lambda: self._norec(rec, self.dma, st, out, in_, reads, writes, is_output, **kw))
            return None
        i = self.ndma
        self.ndma += 1
        j = i % ND
        use = i // ND
        key = ("D", j)
        waits = self._collect(st, reads, writes)
        if use > 0 and self.seen[st].get(key, 0) < 16 * use:
            waits[key] = max(waits.get(key, 0), 16 * use)
        self._dowaits(st, waits)
        h = self.dsem[j]
        self.q[st].append(lambda eng, out=out, in_=in_, kw=kw, h=h: eng.dma_start(out=out, in_=in_, **kw).then_inc(h, 16))
        tok = (st, key, 16 * (use + 1))
        self.dma_uses[key] = 16 * (use + 1)
        self._record(tok, reads, writes)
        self.nins += 1
        if is_output:
            self.out_toks.append(tok)
        return tok

    def coll(self, fn, reads=(), writes=()):
        st = "pool"
        fn = bind(fn)
        idx = len([k for k in self.sems if k[0] == "C"])
        key = ("C", idx)
        self.sems[key] = self.es.enter_context(self.nc.semaphore(f"cc{idx}"))
        waits = self._collect(st, reads, writes)
        self._dowaits(st, waits)
        h = self.sems[key]
        self.q[st].append(lambda eng, fn=fn, h=h: fn(eng).then_inc(h, 1))
        tok = (st, key, 1)
        self.dma_uses[key] = 1
        self._record(tok, reads, writes)
        self.nins += 1
        return tok

    def barrier(self, skip=()):
        targets = {k: v for k, v in self.dma_uses.items() if k not in skip}
        for e, c in self.cnt.items():
            if c > 0:
                epoch, val = divmod(c - 1, EPOCH)
                targets[("E", e, epoch)] = val + 1
        for st in self.eng:
            for key, val in targets.items():
                if self.seen[st].get(key, 0) < val:
                    h = self._semh(key)
                    self.q[st].append(lambda eng, h=h, val=val: eng.wait_ge(h, val))
                    self.seen[st][key] = val

    def end_phase(self, skip=()):
        self.barrier(skip)
        self.replay()
        self.q = {k: [] for k in self.eng}
        self.pes.close()

    def finish(self):
        if self.fused and not self.last_phase:
            self.end_phase()
            return
        st = "sp"
        for tok in self.out_toks:
            _, key, val = tok
            if self.seen[st].get(key, 0) < val:
                h = self._semh(key)
                self.q[st].append(lambda eng, h=h, val=val: eng.wait_ge(h, val))
                self.seen[st][key] = val
        self.replay()

    def replay(self):
        q = self.q
        with self.nc.Block() as block:
            @block.sync
            def _(e):
                for f in q["sp"]:
                    f(e)

            @block.tensor
            def _(e):
                for f in q["pe"]:
                    f(e)

            @block.scalar
            def _(e):
                for f in q["act"]:
                    f(e)

            @block.vector
            def _(e):
                for f in q["dve"]:
                    f(e)

            @block.gpsimd
            def _(e):
                for f in q["pool"]:
                    f(e)


class T:
    def __init__(self, t, name=""):
        self.t = t
        self.reg = Reg(name)
        self.sub = {}

    def __getitem__(self, idx):
        return self.t[idx]

    def r(self, key=None):
        if key is None:
            return self.reg
        if key not in self.sub:
            self.sub[key] = Reg()
        return self.sub[key]


def sb(kb, name, shape, dt):
    return T(kb.pes.enter_context(kb.nc.sbuf_tensor("s_" + kb.pfx + name, list(shape), dt)), name)


def ps(kb, name, shape, dt=F32):
    return T(kb.pes.enter_context(kb.nc.psum_tensor("p_" + kb.pfx + name, list(shape), dt)), name)


STAGE = 99.0
NTL = 16

EPS = 1e-6
NT = 16
TWO_PI = float(2 * np.pi)


class FX:
    active = False
    nc = None
    kb = None
    remap = {}
    ext = {}
    n_phase = 0


def fx_ext(name, shape, dt):
    if name not in FX.ext:
        FX.ext[name] = FX.nc.dram_tensor(name, list(shape), dt, kind="ExternalInput").ap()
    return FX.ext[name]


def _get_nc():
    if FX.active:
        return FX.nc
    return bass.Bass("TRN2", target_bir_lowering=False)


class _phase:
    def __init__(self, nc):
        self.nc = nc

    def __enter__(self):
        if FX.active:
            kb = FX.kb
            kb.pes = ExitStack()
            kb.pfx = f"ph{FX.n_phase}_"
            FX.n_phase += 1
            return kb
        self.es = ExitStack()
        self.es.__enter__()
        return KB(self.nc, self.es)

    def __exit__(self, *a):
        if not FX.active:
            self.es.__exit__(*a)
        return False


def dram_in(nc, name, shape, dt):
    if FX.active:
        ap = FX.remap[name]
        assert list(ap.shape) == list(shape), (name, ap.shape, shape)
        return ap
    return nc.dram_tensor(name, list(shape), dt, kind="ExternalInput").ap()


def dram_out(nc, name, shape, dt):
    if FX.active:
        ap = FX.remap[name]
        assert list(ap.shape) == list(shape), (name, ap.shape, shape)
        return ap
    return nc.dram_tensor(name, list(shape), dt, kind="ExternalOutput").ap()


def emit_mod(kb, cT_d, ada_w_d, ada_b_d, modrow, modP, name, pA, pB):
    nc = kb.nc
    cT = sb(kb, name + "cT", [128, 8], F32)
    cond = sb(kb, name + "cond", [128, 8], F32)
    adab = sb(kb, name + "adab", [1, 512], F32)
    one = sb(kb, name + "one", [1, 1], F32)
    wblk = [sb(kb, name + "wblk0", [128, 8, 512], F32)] * 2
    pm = pA; pmp = pB
    kb.dma("sp", cT[:], cT_d, writes=[cT.r()])
    kb.op("dve", lambda e: e.memset(one[:], 1.0), writes=[one.r()])
    kb.op("act", lambda e: e.activation(out=cond[:], in_=cT[:], func=AF.Silu), reads=[cT.r()], writes=[cond.r()])
    wv = ada_w_d.rearrange("(k p) n -> p k n", p=128)
    for cb in range(12):
        w = wblk[cb % 2]
        kb.dma("sp", w[:], wv[:, :, cb * 512:(cb + 1) * 512], writes=[w.r()])
        kb.dma("sp", adab[:], ada_b_d[:, cb * 512:(cb + 1) * 512], writes=[adab.r()])
        kb.group("pe", [(lambda e, k=k, w=w: e.matmul(pm[0:1, :], cond[:, k:k + 1], w[:, k, :], start=(k == 0), stop=(k == 7))) for k in range(8)],
                 reads=[cond.r(), w.r()], writes=[pm.r()])
        kb.op("dve", lambda e, cb=cb: e.tensor_tensor(out=modrow[0:1, cb * 512:(cb + 1) * 512], in0=pm[0:1, :], in1=adab[0:1, :], op=ALU.add),
              reads=[pm.r(), adab.r()], writes=[modrow.r()])
    kb.group("pe", [(lambda e, c=c: e.matmul(pmp[:, c:c + 1], modrow[0:1, c * 128:(c + 1) * 128], one[:], start=True, stop=True)) for c in range(48)],
             reads=[modrow.r(), one.r()], writes=[pmp.r()])
    kb.op("dve", lambda e: e.tensor_copy(modP[:], pmp[:, 0:48]), reads=[pmp.r()], writes=[modP.r()])


def emit_rstd(kb, ss, n, tmp, name=""):
    kb.op("dve", lambda e: e.tensor_scalar(out=tmp[:], in0=ss[:], scalar1=1.0 / n, scalar2=EPS, op0=ALU.mult, op1=ALU.add), reads=[ss.r()], writes=[tmp.r()])
    kb.op("act", lambda e: e.activation(out=tmp[:], in_=tmp[:], func=AF.Sqrt), reads=[tmp.r()], writes=[tmp.r()])
    kb.op("dve", lambda e: e.reciprocal(ss[:], tmp[:]), reads=[tmp.r()], writes=[ss.r()])


def emit_norm_hT(kb, xt, hT, a, sh, ident, junk, xn, pT, st, st2):
    kb.op("act", lambda e: e.activation(out=junk[:], in_=xt[:], func=AF.Square, accum_out=st[:, 0:1]), reads=[xt.r()], writes=[junk.r(), st.r()])
    emit_rstd(kb, st, 1024, st2)
    kb.op("act", lambda e: e.activation(out=xn[:], in_=xt[:], func=AF.Copy, scale=st[:, 0:1]), reads=[xt.r(), st.r()], writes=[xn.r()])
    kb.group("pe", [(lambda e, k=k: e.transpose(pT[:, k, :], xn[:, k * 128:(k + 1) * 128], ident[:])) for k in range(8)],
             reads=[xn.r(), ident.r()], writes=[pT.r()])
    for k in range(8):
        kb.op("act", lambda e, k=k: e.activation(out=hT[:, k, :], in_=pT[:, k, :], func=AF.Identity, scale=a[:, k:k + 1], bias=sh[:, k:k + 1]),
              reads=[pT.r(), a.r(), sh.r()], writes=[hT.r()])


def build_l1():
    nc = _get_nc()
    xs = dram_in(nc, "xs", [NT * 128, 1024], F32)
    pos_d = dram_in(nc, "pos", [128, NT], I32)
    cT_d = dram_in(nc, "cT", [128, 8], F32)
    ada_w = dram_in(nc, "ada_w", [1024, 6144], F32)
    ada_b = dram_in(nc, "ada_b", [1, 6144], F32)
    mixn_d = dram_in(nc, "mixn", [128, 8], F32)
    w_in = dram_in(nc, "w_in", [1024, 3672], F32)
    gate_up = dram_in(nc, "gate_up", [16, 256], F32)
    gate_b = dram_in(nc, "gate_b", [256], F32)
    qn_d = dram_in(nc, "q_norm", [128], F32)
    kn_d = dram_in(nc, "k_norm", [128], F32)
    invf_d = dram_in(nc, "invf", [64], F32)
    identb_d = dram_in(nc, "identb", [128, 128], BF16)
    identf_d = dram_in(nc, "identf", [128, 128], F32)
    tri3_d = dram_in(nc, "tri3", [128, 3, 128], F32)
    csel_d = dram_in(nc, "csel", [128, 2], F32)
    amask_d = dram_in(nc, "amask", [128, 128], F32)

    o_mod = dram_out(nc, "o_mod", [1, 6144], F32)
    o_qT = dram_out(nc, "o_qT", [128, 4, NT * 128], BF16)
    o_kT = dram_out(nc, "o_kT", [128, 4, NT * 128], BF16)
    o_v = dram_out(nc, "o_v", [128, NT, 512], BF16)
    o_iqT = dram_out(nc, "o_iqT", [128, 4, NT * 128], BF16)
    o_ikT = dram_out(nc, "o_ikT", [128, NT * 128], BF16)
    o_iw = dram_out(nc, "o_iw", [128, NT, 8], F32)
    o_oin = dram_out(nc, "o_oin", [128, NT, 512], F32)
    o_qdT = dram_out(nc, "o_qdT", [128, 2, NT * 128], BF16)
    o_kv = dram_out(nc, "o_kv", [128, 2, 2 * NT, 128], BF16)
    o_dec = dram_out(nc, "o_dec", [128, 2, 2 * NT], F32)
    o_gr = dram_out(nc, "o_gr", [128, NT, 512], F32)

    with _phase(nc) as kb:
        S = lambda n, s, d: sb(kb, n, s, d)
        identb = S("identb", [128, 128], BF16); identf = S("identf", [128, 128], F32)
        tri3 = S("tri3", [128, 3, 128], F32); csel = S("csel", [128, 2], F32); amask = S("amask", [128, 128], F32)
        invf = S("invf", [128, 64], F32); qnb = S("qnb", [128, 128], F32); knb = S("knb", [128, 128], F32)
        gbb = S("gbb", [128, 256], F32); gup = S("gup", [16, 256], F32); mixn = S("mixn", [128, 8], F32)
        posi = S("posi", [128, NT], I32)
        for t, d in ((identb, identb_d), (identf, identf_d), (tri3, tri3_d), (csel, csel_d), (amask, amask_d), (gup, gate_up), (mixn, mixn_d), (posi, pos_d)):
            kb.dma("sp", t[:], d, writes=[t.r()])
        for t, d in ((invf, invf_d), (qnb, qn_d), (knb, kn_d), (gbb, gate_b)):
            kb.dma("sp", t[:], d.partition_broadcast(128), writes=[t.r()])
        wb = S("wb", [128, 8, 3672], BF16)
        for k in range(8):
            kb.dma("pool", wb[:, k, :], w_in[k * 128:(k + 1) * 128, :], writes=[wb.r(("k", k))])
        wb_regs = [wb.r(("k", k)) for k in range(8)]
        modrow = S("modrow", [1, 6144], F32); modP = S("modP", [128, 48], F32)
        pT = ps(kb, "pT", [128, 8, 128], BF16)
        pp = [ps(kb, f"pp{i}", [128, 512], F32) for i in range(2)]
        pA = ps(kb, "pA", [128, 512], F32)
        pB = ps(kb, "pB", [128, 512], F32)
        pTb = pT
        pTg = ps(kb, "pTg", [128, 8, 128], BF16)
        pKV = ps(kb, "pKV", [128, 4, 256], F32)
        emit_mod(kb, cT_d, ada_w, ada_b, modrow, modP, "m0", pA, pB)
        kb.dma("sp", o_mod, modrow[:], reads=[modrow.r()], is_output=True)
        a1 = S("a1", [128, 8], F32)
        kb.op("dve", lambda e: e.scalar_tensor_tensor(out=a1[:], in0=modP[:, 8:16], scalar=1.0, in1=mixn[:], op0=ALU.add, op1=ALU.mult),
              reads=[modP.r(), mixn.r()], writes=[a1.r()])
        if STAGE < 2:
            kb.finish(); return nc
        posf = S("posf", [128, NT], F32)
        ang = S("ang", [128, NT * 64], F32); ki = S("ki", [128, NT * 64], I32); kf = S("kf", [128, NT * 64], F32)
        rs = ang; rc = S("rc", [128, NT * 64], F32); m1 = kf
        sinT = S("sinT", [128, NT, 64], F32); cosT = S("cosT", [128, NT, 64], F32)
        kb.op("dve", lambda e: e.tensor_copy(posf[:], posi[:]), reads=[posi.r()], writes=[posf.r()])
        for i in range(NT):
            kb.op("dve", lambda e, i=i: e.tensor_scalar(out=ang[:, i * 64:(i + 1) * 64], in0=invf[:], scalar1=posf[:, i:i + 1], scalar2=None, op0=ALU.mult),
                  reads=[invf.r(), posf.r()], writes=[ang.r()])
        C1 = 6.28125; C2 = TWO_PI - C1
        D = lambda fn, r, w: kb.op("dve", fn, reads=r, writes=w)
        D(lambda e: e.tensor_scalar(out=kf[:], in0=ang[:], scalar1=1.0 / TWO_PI, scalar2=None, op0=ALU.mult), [ang.r()], [kf.r()])
        D(lambda e: e.tensor_copy(ki[:], kf[:]), [kf.r()], [ki.r()])
        D(lambda e: e.tensor_copy(kf[:], ki[:]), [ki.r()], [kf.r()])
        D(lambda e: e.scalar_tensor_tensor(out=rs[:], in0=kf[:], scalar=-C1, in1=ang[:], op0=ALU.mult, op1=ALU.add), [kf.r(), ang.r()], [rs.r()])
        D(lambda e: e.scalar_tensor_tensor(out=rs[:], in0=kf[:], scalar=-C2, in1=rs[:], op0=ALU.mult, op1=ALU.add), [kf.r(), rs.r()], [rs.r()])
        PI = float(np.pi)
        D(lambda e: e.tensor_scalar(out=m1[:], in0=rs[:], scalar1=PI, scalar2=-TWO_PI, op0=ALU.is_gt, op1=ALU.mult), [rs.r()], [m1.r()])
        D(lambda e: e.tensor_tensor(out=rs[:], in0=rs[:], in1=m1[:], op=ALU.add), [rs.r(), m1.r()], [rs.r()])
        D(lambda e: e.tensor_scalar(out=m1[:], in0=rs[:], scalar1=-PI, scalar2=TWO_PI, op0=ALU.is_lt, op1=ALU.mult), [rs.r()], [m1.r()])
        D(lambda e: e.tensor_tensor(out=rs[:], in0=rs[:], in1=m1[:], op=ALU.add), [rs.r(), m1.r()], [rs.r()])
        D(lambda e: e.tensor_scalar(out=rc[:], in0=rs[:], scalar1=PI / 2, scalar2=None, op0=ALU.add), [rs.r()], [rc.r()])
        D(lambda e: e.tensor_scalar(out=m1[:], in0=rc[:], scalar1=PI, scalar2=-TWO_PI, op0=ALU.is_gt, op1=ALU.mult), [rc.r()], [m1.r()])
        D(lambda e: e.tensor_tensor(out=rc[:], in0=rc[:], in1=m1[:], op=ALU.add), [rc.r(), m1.r()], [rc.r()])
        for t in (rs, rc):
            D(lambda e, t=t: e.tensor_scalar(out=t[:], in0=t[:], scalar1=PI, scalar2=-PI, op0=ALU.min, op1=ALU.max), [t.r()], [t.r()])
        kb.op("act", lambda e: e.activation(out=sinT[:].rearrange("p a b -> p (a b)"), in_=rs[:], func=AF.Sin), reads=[rs.r()], writes=[sinT.r()])
        kb.op("act", lambda e: e.activation(out=cosT[:].rearrange("p a b -> p (a b)"), in_=rc[:], func=AF.Sin), reads=[rc.r()], writes=[cosT.r()])

        qT_t = S("qT_t", [128, 4, 128], BF16); kT_t = S("kT_t", [128, 4, 128], BF16)
        iqT_t = S("iqT_t", [128, 4, 128], BF16); ikT_t = S("ikT_t", [128, 128], BF16)
        qdT_t = S("qdT_t", [128, 2, 128], BF16)
        iw_sb = S("iw_sb", [128, NT, 8], F32)
        kv_t = S("kv_t", [128, 2, 2, 128], BF16); dec_sb = S("dec_sb", [128, 2, 2 * NT], F32)

        xt = [S(f"xt{i}", [128, 1024], F32) for i in range(2)]
        junk = S("junk", [128, 1024], BF16); xn = S("xn", [128, 1024], BF16)
        st = S("st", [128, 1], F32); st2 = S("st2", [128, 1], F32)
        hT = [S(f"hT{i}", [128, 8, 128], BF16) for i in range(2)]
        proj = S("proj", [128, 3672], F32)
        sq = S("sq", [128, 512], F32); ssq = S("ssq", [128, 4], F32); ssq2 = S("ssq2", [128, 4], F32)
        qn = S("qn", [128, 4, 128], F32); qr = S("qr", [128, 4, 128], BF16)
        tA = S("tA", [128, 4, 64], F32); tB = S("tB", [128, 4, 64], F32)
        iqr = S("iqr", [128, 8, 64], BF16); iA = S("iA", [128, 8, 32], F32); iB = S("iB", [128, 8, 32], F32)
        ik2 = S("ik2", [128, 128], BF16); kA = S("kA", [128, 32], F32); kB_ = S("kB_", [128, 32], F32)
        vb = S("vb", [128, 512], BF16); gvb = S("gvb", [128, 512], BF16)
        glT = S("glT", [16, 128], F32); pre = S("pre", [128, 256], F32); lg = S("lg", [128, 256], F32)
        bmid = S("bmid", [128, 256], F32); blast = S("blast", [128, 256], F32)
        d1 = S("d1", [128, 256], F32); d3 = S("d3", [128, 256], F32)
        E1 = S("E1", [128, 256], F32); E2 = S("E2", [128, 256], F32); E3 = S("E3", [128, 256], F32); E4 = S("E4", [128, 256], F32)
        qkd = S("qkd", [128, 3, 256], BF16)
        kdec = S("kdec", [128, 256], BF16)
        qkT = S("qkT", [128, 6, 128], BF16)
        attT = S("attT", [128, 4, 128], BF16)
        oin = S("oin", [128, 512], F32); grs = S("grs", [128, 512], F32)
        dect = S("dect", [128, 2, 2], F32)

        def rope(src, dst, nh, half, cos, sin, A, B, cols_per_head):
            sv = src
            t1 = sv[:, :, 0:half]; t2 = sv[:, :, half:2 * half]
            cb_ = cos.unsqueeze(1).to_broadcast([128, nh, half]); sb_ = sin.unsqueeze(1).to_broadcast([128, nh, half])
            return t1, t2, cb_, sb_

        for i in range(NTL if STAGE >= 3 else 0):
            x_ = xt[i % 2]; h_ = hT[i % 2]
            kb.dma("sp", x_[:], xs[i * 128:(i + 1) * 128, :], writes=[x_.r()])
            emit_norm_hT(kb, x_, h_, a1, modP_sh(modP), identb, junk, xn, pT, st, st2)
            for cb in range(8):
                c0 = cb * 512; c1 = min(3672, c0 + 512); p_ = pp[cb % 2]
                kb.group("pe", [(lambda e, k=k, c0=c0, c1=c1, p_=p_, h_=h_: e.matmul(p_[:, 0:c1 - c0], h_[:, k, :], wb[:, k, c0:c1], start=(k == 0), stop=(k == 7))) for k in range(8)],
                         reads=[h_.r()] + wb_regs, writes=[p_.r()])
                if cb % 2 == 0:
                    kb.op("act", lambda e, c0=c0, c1=c1, p_=p_: e.copy(proj[:, c0:c1], p_[:, 0:c1 - c0]), reads=[p_.r()], writes=[proj.r(("c", cb))])
                else:
                    kb.op("dve", lambda e, c0=c0, c1=c1, p_=p_: e.tensor_copy(proj[:, c0:c1], p_[:, 0:c1 - c0]), reads=[p_.r()], writes=[proj.r(("c", cb))])
            PR = [proj.r(("c", cb)) for cb in range(8)]
            cos_i = cosT[:, i, :]; sin_i = sinT[:, i, :]
            cos32 = cosT[:, i, 0:64:2]; sin32 = sinT[:, i, 0:64:2]

            C1, C2, C3 = [], [], []
            kb.rec = C1
            for (c0, gain, dstT, dstD) in ((1552, qnb, qT_t, o_qT), (2064, knb, kT_t, o_kT)):
                src = proj[:, c0:c0 + 512]
                D(lambda e, src=src: e.tensor_tensor(out=sq[:], in0=src, in1=src, op=ALU.mult), PR, [sq.r()])
                D(lambda e: e.tensor_reduce(out=ssq[:], in_=sq[:].rearrange("p (h d) -> p h d", h=4), axis=AX.X, op=ALU.add), [sq.r()], [ssq.r()])
                emit_rstd(kb, ssq, 128, ssq2)
                s3 = src.rearrange("p (h d) -> p h d", h=4)
                D(lambda e, s3=s3: e.tensor_tensor(out=qn[:], in0=s3, in1=ssq[:].unsqueeze(2).to_broadcast([128, 4, 128]), op=ALU.mult), PR + [ssq.r()], [qn.r()])
                D(lambda e, gain=gain: e.tensor_tensor(out=qn[:], in0=qn[:], in1=gain[:].unsqueeze(1).to_broadcast([128, 4, 128]), op=ALU.mult), [qn.r(), gain.r()], [qn.r()])
                t1 = qn[:, :, 0:64]; t2 = qn[:, :, 64:128]
                cb_ = cos_i.unsqueeze(1).to_broadcast([128, 4, 64]); sb_ = sin_i.unsqueeze(1).to_broadcast([128, 4, 64])
                D(lambda e: e.tensor_tensor(out=tA[:], in0=t1, in1=cb_, op=ALU.mult), [qn.r(), cosT.r()], [tA.r()])
                D(lambda e: e.tensor_tensor(out=tB[:], in0=t2, in1=sb_, op=ALU.mult), [qn.r(), sinT.r()], [tB.r()])
                D(lambda e: e.tensor_tensor(out=qr[:, :, 0:64], in0=tA[:], in1=tB[:], op=ALU.subtract), [tA.r(), tB.r()], [qr.r()])
                D(lambda e: e.tensor_tensor(out=tA[:], in0=t2, in1=cb_, op=ALU.mult), [qn.r(), cosT.r()], [tA.r()])
                D(lambda e: e.tensor_tensor(out=tB[:], in0=t1, in1=sb_, op=ALU.mult), [qn.r(), sinT.r()], [tB.r()])
                D(lambda e: e.tensor_tensor(out=qr[:, :, 64:128], in0=tA[:], in1=tB[:], op=ALU.add), [tA.r(), tB.r()], [qr.r()])
                kb.group("pe", [(lambda e, h=h: e.transpose(pTb[:, h, :], qr[:, h, :], identb[:])) for h in range(4)], reads=[qr.r(), identb.r()], writes=[pTb.r()])
                kb.op("act", lambda e, dstT=dstT: e.copy(dstT[:], pTb[:, 0:4, :]), reads=[pTb.r()], writes=[dstT.r()])
                kb.dma("sp", dstD[:, :, i * 128:(i + 1) * 128], dstT[:], reads=[dstT.r()], is_output=True)
            kb.rec = C2
            D(lambda e: e.tensor_copy(vb[:], proj[:, 2576:3088]), PR, [vb.r()])
            kb.dma("sp", o_v[:, i, :], vb[:], reads=[vb.r()], is_output=True)
            kb.rec = C1
            iq3 = proj[:, 3088:3600].rearrange("p (h d) -> p h d", h=8)
            t1 = iq3[:, :, 0:32]; t2 = iq3[:, :, 32:64]
            cb_ = cos32.unsqueeze(1).to_broadcast([128, 8, 32]); sb_ = sin32.unsqueeze(1).to_broadcast([128, 8, 32])
            D(lambda e: e.tensor_tensor(out=iA[:], in0=t1, in1=cb_, op=ALU.mult), PR + [cosT.r()], [iA.r()])
            D(lambda e: e.tensor_tensor(out=iB[:], in0=t2, in1=sb_, op=ALU.mult), PR + [sinT.r()], [iB.r()])
            D(lambda e: e.tensor_tensor(out=iqr[:, :, 0:32], in0=iA[:], in1=iB[:], op=ALU.subtract), [iA.r(), iB.r()], [iqr.r()])
            D(lambda e: e.tensor_tensor(out=iA[:], in0=t2, in1=cb_, op=ALU.mult), PR + [cosT.r()], [iA.r()])
            D(lambda e: e.tensor_tensor(out=iB[:], in0=t1, in1=sb_, op=ALU.mult), PR + [sinT.r()], [iB.r()])
            D(lambda e: e.tensor_tensor(out=iqr[:, :, 32:64], in0=iA[:], in1=iB[:], op=ALU.add), [iA.r(), iB.r()], [iqr.r()])
            iq2 = iqr[:].rearrange("p h d -> p (h d)")
            kb.group("pe", [(lambda e, g=g: e.transpose(pTb[:, g, :], iq2[:, g * 128:(g + 1) * 128], identb[:])) for g in range(4)], reads=[iqr.r(), identb.r()], writes=[pTb.r()])
            kb.op("act", lambda e: e.copy(iqT_t[:], pTb[:, 0:4, :]), reads=[pTb.r()], writes=[iqT_t.r()])
            kb.dma("sp", o_iqT[:, :, i * 128:(i + 1) * 128], iqT_t[:], reads=[iqT_t.r()], is_output=True)
            kb.rec = C2
            k1 = proj[:, 3600:3632]; k2 = proj[:, 3632:3664]
            D(lambda e: e.tensor_tensor(out=kA[:], in0=k1, in1=cos32, op=ALU.mult), PR + [cosT.r()], [kA.r()])
            D(lambda e: e.tensor_tensor(out=kB_[:], in0=k2, in1=sin32, op=ALU.mult), PR + [sinT.r()], [kB_.r()])
            D(lambda e: e.tensor_tensor(out=ik2[:, 0:32], in0=kA[:], in1=kB_[:], op=ALU.subtract), [kA.r(), kB_.r()], [ik2.r()])
            D(lambda e: e.tensor_tensor(out=kA[:], in0=k2, in1=cos32, op=ALU.mult), PR + [cosT.r()], [kA.r()])
            D(lambda e: e.tensor_tensor(out=kB_[:], in0=k1, in1=sin32, op=ALU.mult), PR + [sinT.r()], [kB_.r()])
            D(lambda e: e.tensor_tensor(out=ik2[:, 32:64], in0=kA[:], in1=kB_[:], op=ALU.add), [kA.r(), kB_.r()], [ik2.r()])
            D(lambda e: e.tensor_copy(ik2[:, 64:128], ik2[:, 0:64]), [ik2.r()], [ik2.r()])
            kb.op("pe", lambda e: e.transpose(pTb[:, 4, :], ik2[:], identb[:]), reads=[ik2.r(), identb.r()], writes=[pTb.r()])
            kb.op("act", lambda e: e.copy(ikT_t[:], pTb[:, 4, :]), reads=[pTb.r()], writes=[ikT_t.r()])
            kb.dma("sp", o_ikT[:, i * 128:(i + 1) * 128], ikT_t[:], reads=[ikT_t.r()], is_output=True)
            D(lambda e: e.tensor_copy(iw_sb[:, i, :], proj[:, 3664:3672]), PR, [iw_sb.r()])
            kb.rec = C3
            kb.op("pe", lambda e: e.transpose(pA[0:16, 0:128], proj[:, 1024:1040], identf[:]), reads=PR + [identf.r()], writes=[pA.r()])
            kb.op("act", lambda e: e.copy(glT[:], pA[0:16, 0:128]), reads=[pA.r()], writes=[glT.r()])
            kb.op("pe", lambda e: e.matmul(pA[:, 0:256], glT[:], gup[:], start=True, stop=True), reads=[glT.r(), gup.r()], writes=[pA.r()])
            D(lambda e: e.tensor_tensor(out=pre[:], in0=pA[:, 0:256], in1=gbb[:], op=ALU.add), [pA.r(), gbb.r()], [pre.r()])
            kb.op("act", lambda e: e.activation(out=lg[:], in_=pre[:], func=AF.Exp, scale=-1.0), reads=[pre.r()], writes=[lg.r()])
            kb.op("act", lambda e: e.activation(out=lg[:], in_=lg[:], func=AF.Ln, bias=1.0), reads=[lg.r()], writes=[lg.r()])
            kb.group("pe", [
                lambda e: e.matmul(pA[:, 0:256], tri3[:, 0, :], lg[:], start=True, stop=True),
                lambda e: e.matmul(pA[:, 256:512], tri3[:, 1, :], lg[:], start=True, stop=True),
                lambda e: e.matmul(pB[:, 0:256], tri3[:, 2, :], lg[:], start=True, stop=True),
                lambda e: e.matmul(pB[:, 256:258], lg[:, 0:128], csel[:], start=True, stop=True),
                lambda e: e.matmul(pB[:, 258:260], lg[:, 128:256], csel[:], start=True, stop=True),
            ], reads=[tri3.r(), lg.r(), csel.r()], writes=[pA.r(), pB.r()])
            kb.op("act", lambda e: e.copy(bmid[:], pA[:, 256:512]), reads=[pA.r()], writes=[bmid.r()])
            kb.op("act", lambda e: e.copy(blast[:], pB[:, 0:256]), reads=[pB.r()], writes=[blast.r()])
            kb.op("act", lambda e: e.activation(out=dec_sb[:, :, 2 * i:2 * i + 2], in_=pB[:, 256:260].rearrange("p (a c) -> p a c", a=2), func=AF.Exp), reads=[pB.r()], writes=[dec_sb.r()])
            D(lambda e: e.tensor_tensor(out=d1[:], in0=pA[:, 0:256], in1=bmid[:], op=ALU.subtract), [pA.r(), bmid.r()], [d1.r()])
            D(lambda e: e.tensor_tensor(out=d3[:], in0=blast[:], in1=pA[:, 0:256], op=ALU.subtract), [pA.r(), blast.r()], [d3.r()])
            kb.op("act", lambda e: e.activation(out=E1[:], in_=d1[:], func=AF.Exp), reads=[d1.r()], writes=[E1.r()])
            kb.op("act", lambda e: e.activation(out=E2[:], in_=d1[:], func=AF.Exp, scale=-1.0), reads=[d1.r()], writes=[E2.r()])
            kb.op("act", lambda e: e.activation(out=E3[:], in_=d3[:], func=AF.Exp), reads=[d3.r()], writes=[E3.r()])
            kb.op("act", lambda e: e.activation(out=E4[:], in_=pA[:, 0:256], func=AF.Exp), reads=[pA.r()], writes=[E4.r()])
            gq = proj[:, 0:256]; gk = proj[:, 256:512]
            D(lambda e: e.scalar_tensor_tensor(out=qkd[:, 0, :], in0=gq, scalar=0.125, in1=E1[:], op0=ALU.mult, op1=ALU.mult), PR + [E1.r()], [qkd.r()])
            D(lambda e: e.tensor_tensor(out=qkd[:, 1, :], in0=gk, in1=E2[:], op=ALU.mult), PR + [E2.r()], [qkd.r()])
            D(lambda e: e.scalar_tensor_tensor(out=qkd[:, 2, :], in0=gq, scalar=0.125, in1=E4[:], op0=ALU.mult, op1=ALU.mult), PR + [E4.r()], [qkd.r()])
            D(lambda e: e.tensor_tensor(out=kdec[:], in0=gk, in1=E3[:], op=ALU.mult), PR + [E3.r()], [kdec.r()])
            kb.group("pe", [(lambda e, w=w, p=p: e.transpose(pTg[:, w * 2 + p, :], qkd[:, w, p * 128:(p + 1) * 128], identb[:])) for w in range(3) for p in range(2)],
                     reads=[qkd.r(), identb.r()], writes=[pTg.r()])
            kb.op("act", lambda e: e.copy(qkT[:], pTg[:, 0:6, :]), reads=[pTg.r()], writes=[qkT.r()])
            D(lambda e: e.tensor_copy(qdT_t[:], qkT[:, 4:6, :]), [qkT.r()], [qdT_t.r()])
            kb.dma("sp", o_qdT[:, :, i * 128:(i + 1) * 128], qdT_t[:], reads=[qdT_t.r()], is_output=True)
            D(lambda e: e.tensor_copy(gvb[:], proj[:, 512:1024]), PR, [gvb.r()])
            def attmm(e, h):
                p, hb = divmod(h, 2); hb *= 64
                dst = pp[0] if hb == 0 else pp[1]
                return e.matmul(dst[:, p * 128:(p + 1) * 128], qkT[hb:hb + 64, 2 + p, :], qkT[hb:hb + 64, 0 + p, :], start=True, stop=True)
            kb.group("pe", [(lambda e, h=h: attmm(e, h)) for h in (0, 2, 1, 3)], reads=[qkT.r()], writes=[pp[0].r(), pp[1].r()])
            for hh in range(2):
                D(lambda e, hh=hh: e.tensor_tensor(out=attT[:, hh:4:2, :], in0=pp[hh][:, 0:256].rearrange("p (h i) -> p h i", h=2), in1=amask[:].unsqueeze(1).to_broadcast([128, 2, 128]), op=ALU.mult),
                  [pp[hh].r(), amask.r()], [attT.r()])
            kb.group("pe", [(lambda e, h=h: e.matmul(pB[:, h * 128:(h + 1) * 128], attT[:, h, :], gvb[:, h * 128:(h + 1) * 128], start=True, stop=True)) for h in range(4)],
                     reads=[attT.r(), gvb.r()], writes=[pB.r()])
            kb.op("act", lambda e: e.copy(oin[:], pB[:]), reads=[pB.r()], writes=[oin.r()])
            kb.dma("sp", o_oin[:, i, :], oin[:], reads=[oin.r()], is_output=True)
            kb.group("pe", [(lambda e, p=p, c=c: e.matmul(pKV[:, c * 2 + p, :], kdec[c * 64:(c + 1) * 64, p * 128:(p + 1) * 128], gvb[c * 64:(c + 1) * 64, p * 256:(p + 1) * 256], start=True, stop=True))
                            for p in range(2) for c in range(2)], reads=[kdec.r(), gvb.r()], writes=[pKV.r()])
            pk = pKV[:].rearrange("q (c p) n -> q p c n", p=2)
            kb.op("act", lambda e: e.copy(kv_t[0:64], pk[0:64, :, :, 0:128]), reads=[pKV.r()], writes=[kv_t.r()])
            D(lambda e: e.tensor_copy(kv_t[64:128], pk[64:128, :, :, 128:256]), [pKV.r()], [kv_t.r()])
            kb.dma("sp", o_kv[:, :, 2 * i:2 * i + 2, :], kv_t[:], reads=[kv_t.r()], is_output=True)
            kb.rec = C2
            kb.op("act", lambda e: e.activation(out=grs[:], in_=proj[:, 1040:1552], func=AF.Silu), reads=PR, writes=[grs.r()])
            kb.dma("sp", o_gr[:, i, :], grs[:], reads=[grs.r()], is_output=True)
            kb.emit_roundrobin([C3, C1, C2])
        for t, d in ((iw_sb, o_iw), (dec_sb, o_dec)):
            kb.dma("sp", d, t[:], reads=[t.r()], is_output=True)
        kb.finish()
        print("L1 instructions:", kb.nins, {k: len(v) for k, v in kb.q.items()})
    return nc


class _ShView:
    pass


def modP_sh(modP):
    class V:
        def __getitem__(s, idx):
            return modP[idx]
        def r(s, key=None):
            return modP.r(key)
    return V()


def l1_consts():
    half = 64
    invf = (10000.0 ** (-np.arange(half, dtype=np.float32) / half)).astype(np.float32)
    j = np.arange(128)[:, None]; i = np.arange(128)[None, :]
    same = (j // 64) == (i // 64)
    tri = (same & (j <= i)).astype(np.float32)
    mmid = (same & ((j % 64) <= 31)).astype(np.float32)
    mlast = same.astype(np.float32)
    tri3 = np.stack([tri, mmid, mlast], axis=1) * (-1.0 / 16.0)
    csel = np.stack([(np.arange(128) < 64), (np.arange(128) >= 64)], axis=1).astype(np.float32) * (-1.0 / 16.0)
    amask = tri.copy()
    return dict(invf=invf, identb=np.eye(128, dtype=np.float32).astype(ml_dtypes.bfloat16), identf=np.eye(128, dtype=np.float32),
                tri3=np.ascontiguousarray(tri3.astype(np.float32)), csel=csel, amask=amask)


NIT = 10
C0 = 11.3137085
SCALE = float(128 ** -0.5)


def emit_skewed(its, nst):
    n = len(its)
    for step in range(n + nst - 1):
        for stg in range(nst):
            k = step - stg
            if 0 <= k < n:
                its[k][stg]()


def build_dsa(QT=tuple(range(16))):
    nc = _get_nc()
    qT_d = dram_in(nc, "qT", [128, 4, 2048], BF16)
    iqT_d = dram_in(nc, "iqT", [128, 4, 2048], BF16)
    iw_d = dram_in(nc, "iw", [128, 16, 8], F32)
    kT_d = dram_in(nc, "kT", [2, 4, 64, 8192], BF16)
    v_d = dram_in(nc, "v", [2, 4, 64, 8192], BF16)
    ikT_d = dram_in(nc, "ikT", [1, 4, 128, 2048], BF16)
    mneg_d = dram_in(nc, "mneg", [128, 512], F32)
    mpos_d = dram_in(nc, "mpos", [128, 512], F32)
    identb_d = dram_in(nc, "identb", [128, 128], BF16)
    identf_d = dram_in(nc, "identf", [128, 128], F32)
    pow2_d = dram_in(nc, "pow2", [128, NIT + 1], F32)
    o_dsa = dram_out(nc, "o_dsa", [128, 16, 512], F32)
    with _phase(nc) as kb:
        S = lambda n, s, d: sb(kb, n, s, d)
        D = lambda fn, r, w: kb.op("dve", fn, reads=r, writes=w)
        A = lambda fn, r, w: kb.op("act", fn, reads=r, writes=w)
        v = S("v", [128, 64, 512], BF16); ikT = S("ikT", [128, 8192], BF16)
        kTb = [S(f"kTb{i}", [128, 512], BF16) for i in range(3)]
        for jj in range(4):
            for q in range(2):
                kb.dma("sp", v[q * 64:(q + 1) * 64].rearrange("p (i j) c -> p i j c", j=4)[:, :, jj, :], v_d[q, jj].rearrange("p (i c) -> p i c", c=512), writes=[v.r(("g", jj))])
        for jj in range(4):
            kb.dma("sp", ikT[:].rearrange("p (i j s) -> p i j s", j=4, s=128)[:, :, jj, :], ikT_d[0, jj].rearrange("p (i s) -> p i s", s=128), writes=[ikT.r()])
        VV = [v.r(("g", g)) for g in range(8)]
        iw = S("iw", [128, 16, 8], F32); mneg = S("mneg", [128, 512], F32); mpos = S("mpos", [128, 512], F32)
        identb = S("identb", [128, 128], BF16); identf = S("identf", [128, 128], F32); pow2 = S("pow2", [128, NIT + 1], F32)
        for t, d in ((iw, iw_d), (mneg, mneg_d), (mpos, mpos_d), (identb, identb_d), (identf, identf_d), (pow2, pow2_d)):
            kb.dma("sp", t[:], d, writes=[t.r()])
        B = [ps(kb, f"B{i}", [128, 512], F32) for i in range(4)] + [None, None] + [ps(kb, f"B{i}", [128, 512], F32) for i in (6, 7)]
        X2 = ps(kb, "X2", [128, 2, 512], F32)
        qts = [S(f"qt{i}", [128, 4, 128], BF16) for i in range(2)]; iqts = [S(f"iqt{i}", [128, 4, 128], BF16) for i in range(2)]
        diagw = S("diagw", [128, 8, 128], BF16)
        Rt2 = [S(f"Rt2_{i}", [128, 2, 512], BF16) for i in range(2)]
        scores = [S(f"score{i}", [128, 8192], F32) for i in range(2)]
        masks = [S(f"maskb{i}", [128, 8192], BF16) for i in range(2)]
        tmp = S("tmp", [128, 512], F32)
        sts = [S(f"st{i}", [128, 8], F32) for i in range(2)]
        Hh = S("Hh", [128, NIT + 1], F32)
        lo_t = S("lo_t", [128, 1], F32); mid_t = S("mid_t", [128, 1], F32); cnt_t = S("cnt_t", [128, 1], F32); g_t = S("g_t", [128, 1], F32)
        Eb = [S(f"Eb{i}", [128, 512], BF16) for i in range(2)]
        Pb = [S(f"Pb{i}", [128, 512], BF16) for i in range(2)]
        PT = [S(f"PT{i}", [128, 4, 128], BF16) for i in range(2)]
        rs = S("rs", [128, 4, 16], F32); rsum = S("rsum", [128, 4], F32); rinv = S("rinv", [128, 4], F32)
        osb = S("osb", [128, 512], F32)
        Sbanks = (B[0], B[1], B[7]); Ob = B[2]; pTv = B[3][:].bitcast(BF16); SC = B[6]
        cstate = [0]

        def phaseI(i, par):
            iqt = iqts[par]; score = scores[par]; st = sts[par]
            kb.dma("sp", iqt[:], iqT_d[:, :, i * 128:(i + 1) * 128], writes=[iqt.r()])
            for h in range(8):
                A(lambda e, h=h: e.activation(out=diagw[:, h, :], in_=identf[:], func=AF.Copy, scale=iw[:, i, h:h + 1]), [identf.r(), iw.r()], [diagw.r()])
            yield
            for m in range(i + 1):
                ks = slice(m * 512, (m + 1) * 512)
                for p in range(4):
                    kb.group("pe", [lambda e, p=p: e.matmul(X2[:, 0, :], iqt[0:64, p, :], ikT[0:64, ks], start=True, stop=True),
                                    lambda e, p=p: e.matmul(X2[:, 1, :], iqt[64:128, p, :], ikT[64:128, ks], start=True, stop=True)],
                             reads=[iqt.r(), ikT.r()], writes=[X2.r()])
                    R_ = Rt2[p % 2]
                    A(lambda e, R_=R_: e.activation(out=R_[:].rearrange("p a b -> p (a b)"), in_=X2[:].rearrange("p a b -> p (a b)"), func=AF.Relu), [X2.r()], [R_.r(("h", 0)), R_.r(("h", 1))])
                    kb.group("pe", [lambda e, p=p, R_=R_: e.matmul(SC[:], diagw[:, 2 * p, :], R_[:, 0, :], start=(p == 0), stop=False),
                                    lambda e, p=p, R_=R_: e.matmul(SC[:], diagw[:, 2 * p + 1, :], R_[:, 1, :], start=False, stop=(p == 3))],
                             reads=[diagw.r(), R_.r(("h", 0)), R_.r(("h", 1))], writes=[SC.r()])
                    yield
                if m < i:
                    A(lambda e: e.copy(score[:, ks], SC[:]), [SC.r()], [score.r(("m", m))])
                else:
                    D(lambda e: e.tensor_tensor(out=score[:, ks], in0=SC[:], in1=mneg[:], op=ALU.add), [SC.r(), mneg.r()], [score.r(("m", m))])
                    D(lambda e: e.tensor_tensor(out=tmp[:], in0=SC[:], in1=mpos[:], op=ALU.add), [SC.r(), mpos.r()], [tmp.r()])
                    D(lambda e: e.tensor_reduce(out=st[:, 1:2], in_=tmp[:], axis=AX.X, op=ALU.min), [tmp.r()], [st.r()])
                yield

        def phaseII(i, par):
            score = scores[par]; st = sts[par]; maskb = masks[par]
            SCR = [score.r(("m", m)) for m in range(i + 1)]
            W = (i + 1) * 512
            if i > 0:
                D(lambda e: e.tensor_reduce(out=st[:, 0:1], in_=score[:, 0:i * 512], axis=AX.X, op=ALU.min), SCR, [st.r()])
                D(lambda e: e.tensor_tensor(out=st[:, 2:3], in0=st[:, 0:1], in1=st[:, 1:2], op=ALU.min), [st.r()], [st.r()])
            else:
                D(lambda e: e.tensor_copy(st[:, 2:3], st[:, 1:2]), [st.r()], [st.r()])
            yield
            D(lambda e: e.tensor_reduce(out=st[:, 3:4], in_=score[:, 0:W], axis=AX.X, op=ALU.max), SCR, [st.r()])
            D(lambda e: e.tensor_tensor(out=st[:, 4:5], in0=st[:, 3:4], in1=st[:, 2:3], op=ALU.subtract), [st.r()], [st.r()])
            yield
            D(lambda e: e.tensor_scalar(out=Hh[:], in0=pow2[:], scalar1=st[:, 4:5], scalar2=None, op0=ALU.mult), [pow2.r(), st.r()], [Hh.r()])
            D(lambda e: e.tensor_copy(lo_t[:], st[:, 2:3]), [st.r()], [lo_t.r()])
            D(lambda e: e.tensor_tensor(out=mid_t[:], in0=st[:, 2:3], in1=Hh[:, 0:1], op=ALU.add), [st.r(), Hh.r()], [mid_t.r()])
            yield
            for k in range(NIT):
                D(lambda e: e.tensor_scalar(out=maskb[:, 0:W], in0=score[:, 0:W], scalar1=mid_t[:, 0:1], scalar2=None, op0=ALU.is_ge, op1=ALU.add, accum_out=cnt_t[:, 0:1]),
                  SCR + [mid_t.r()], [maskb.r(), cnt_t.r()])
                yield
                D(lambda e, k=k: e.tensor_scalar(out=g_t[:], in0=cnt_t[:], scalar1=255.5, scalar2=Hh[:, k:k + 1], op0=ALU.is_ge, op1=ALU.mult), [cnt_t.r(), Hh.r()], [g_t.r()])
                yield
                D(lambda e, k=k: e.scalar_tensor_tensor(out=mid_t[:], in0=g_t[:], scalar=lo_t[:, 0:1], in1=Hh[:, k + 1:k + 2], op0=ALU.add, op1=ALU.add), [g_t.r(), lo_t.r(), Hh.r()], [mid_t.r()])
                D(lambda e: e.tensor_tensor(out=lo_t[:], in0=lo_t[:], in1=g_t[:], op=ALU.add), [lo_t.r(), g_t.r()], [lo_t.r()])
                yield
            D(lambda e: e.tensor_scalar(out=maskb[:, 0:W], in0=score[:, 0:W], scalar1=lo_t[:, 0:1], scalar2=-30000.0, op0=ALU.is_lt, op1=ALU.mult), SCR + [lo_t.r()], [maskb.r()])
            yield

        def phaseIII(i, par):
            qt = qts[par]; maskb = masks[par]
            kb.dma("sp", qt[:], qT_d[:, :, i * 128:(i + 1) * 128], writes=[qt.r()])

            def make_it3(m, h, c3):
                ks = slice(m * 512, (m + 1) * 512)
                Sb = Sbanks[c3 % 3]; E_ = Eb[c3 % 2]; P_ = Pb[c3 % 2]; PT_ = PT[c3 % 2]; kt_ = kTb[c3 % 3]
                first = (m == 0 and h == 0)
                hc = slice(h * 128, (h + 1) * 128)

                def S1():
                    for q in range(2):
                        kb.dma("sp", kt_[q * 64:(q + 1) * 64].rearrange("p (j s) -> p j s", j=4), kT_d[q].rearrange("j p (h i s) -> p j h i s", h=4, s=128)[:, :, h, m, :], writes=[kt_.r()])
                    kb.group("pe", [lambda e: e.matmul(Sb[:], qt[:, h, :], kt_[:], start=True, stop=False),
                                    lambda e: e.matmul(Sb[:], identb[:], maskb[:, ks], start=False, stop=True)],
                             reads=[qt.r(), kt_.r(), identb.r(), maskb.r()], writes=[Sb.r()])
                    A(lambda e: e.activation(out=P_[:], in_=Sb[:], func=AF.Exp, scale=SCALE, bias=-C0, accum_out=rs[:, h, m:m + 1]), [Sb.r()], [P_.r(), rs.r(("c", h, m))])

                def S2():
                    kb.group("pe", [(lambda e, x=x: e.transpose(pTv[:, x * 128:(x + 1) * 128], P_[:, x * 128:(x + 1) * 128], identb[:])) for x in range(4)],
                             reads=[P_.r(), identb.r()], writes=[B[3].r()])
                    if c3 % 2 == 0:
                        A(lambda e: e.copy(PT_[:].rearrange("p a b -> p (a b)"), pTv[:, 0:512]), [B[3].r()], [PT_.r()])
                    else:
                        D(lambda e: e.tensor_copy(PT_[:].rearrange("p a b -> p (a b)"), pTv[:, 0:512]), [B[3].r()], [PT_.r()])

                def S3():
                    kb.group("pe", [(lambda e, x=x: e.matmul(Ob[:, hc], PT_[:, x, :], v[:, m * 4 + x, h * 128:(h + 1) * 128], start=(first and x == 0), stop=(m == i and h == 3 and x == 3))) for x in range(4)],
                             reads=[PT_.r()] + VV, writes=[Ob.r(("h", h))] + ([Ob.r(("h", hh)) for hh in range(4)] if first else []))
                return (S1, S2, S3)

            its3 = []
            for m in range(i + 1):
                for h in range(4):
                    its3.append(make_it3(m, h, cstate[0]))
                    cstate[0] += 1
            n = len(its3)
            for step in range(n + 2):
                for stg in range(3):
                    k = step - stg
                    if 0 <= k < n:
                        its3[k][stg]()
                yield
            D(lambda e: e.tensor_reduce(out=rsum[:], in_=rs[:, :, 0:i + 1], axis=AX.X, op=ALU.add), [rs.r(("c", hh, mm)) for hh in range(4) for mm in range(i + 1)], [rsum.r()])
            D(lambda e: e.reciprocal(rinv[:], rsum[:]), [rsum.r()], [rinv.r()])
            D(lambda e: e.tensor_tensor(out=osb[:].rearrange("p (h d) -> p h d", h=4), in0=Ob[:].rearrange("p (h d) -> p h d", h=4), in1=rinv[:].unsqueeze(2).to_broadcast([128, 4, 128]), op=ALU.mult),
              [Ob.r(("h", hh)) for hh in range(4)] + [rinv.r()], [osb.r()])
            kb.dma("sp", o_dsa[:, i, :], osb[:], reads=[osb.r()], is_output=True)
            yield

        def run_interleaved(gens):
            lists = []
            for g in gens:
                lists.append(g)
            active = [[g, w, 0.0] for g, w in lists]
            total = max(w for _, w in lists)
            for stepi in range(total):
                for a in active:
                    g, w, acc = a
                    a[2] += w / total
                    while a[2] >= 1.0:
                        a[2] -= 1.0
                        try:
                            next(g)
                        except StopIteration:
                            a[2] = -1e9
            for a in active:
                for _ in a[0]:
                    pass

        def est_I(i):
            return 1 + (i + 1) * 5

        def est_II(i):
            return 3 + NIT * 3 + 1

        def est_III(i):
            return 4 * (i + 1) + 3

        QL = list(QT)
        for _ in phaseI(QL[0], 0):
            pass
        nq = len(QL)
        for t in range(nq + 1):
            gens = []
            if t < nq:
                gens.append((phaseII(QL[t], t % 2), est_II(QL[t])))
            if t >= 1:
                gens.append((phaseIII(QL[t - 1], (t - 1) % 2), est_III(QL[t - 1])))
            if t + 1 < nq:
                gens.append((phaseI(QL[t + 1], (t + 1) % 2), est_I(QL[t + 1])))
            run_interleaved(gens)
        kb.finish()
        print("DSA instructions:", kb.nins, {k: len(v) for k, v in kb.q.items()})
    return nc


def dsa_masks(j):
    q = np.arange(128)[:, None]
    col = np.arange(512)[None, :]
    jj = col // 128; s = col % 128
    vis = (jj < j) | ((jj == j) & (s <= q))
    mneg = np.where(vis, 0.0, -1e30).astype(np.float32)
    mpos = np.where(vis, 0.0, 1e30).astype(np.float32)
    return mneg, mpos


def gather_global(per_core, axis_tok_tiles):
    st = np.stack(per_core, axis=axis_tok_tiles + 1)
    sh = list(st.shape)
    sh[axis_tok_tiles:axis_tok_tiles + 2] = [64]
    return st.reshape(sh)


def build_gla(NB=16):
    nc = _get_nc()
    kv_d = dram_in(nc, "kv", [2, 4, 64, 8192], BF16)
    dec_d = dram_in(nc, "dec", [1, 4, 128, 64], F32)
    qd_d = dram_in(nc, "qdT", [128, 2, 2048], BF16)
    oin_d = dram_in(nc, "oin", [128, 16, 512], F32)
    grs_d = dram_in(nc, "grs", [128, 16, 512], F32)
    gsel_d = dram_in(nc, "gsel", [128, 2, 8], F32)
    gn_d = dram_in(nc, "gnorm", [128], F32)
    o_gla = dram_out(nc, "o_gla", [128, 16, 512], F32)
    with _phase(nc) as kb:
        S_ = lambda n, s, d: sb(kb, n, s, d)
        D = lambda fn, r, w: kb.op("dve", fn, reads=r, writes=w)
        A = lambda fn, r, w: kb.op("act", fn, reads=r, writes=w)
        dec = S_("dec", [128, 2, 128], F32); gsel = S_("gsel", [128, 2, 8], F32); gnb = S_("gnb", [128, 128], F32)
        for jj in range(4):
            kb.dma("sp", dec[:].rearrange("p a (i j c) -> p a i j c", j=4, c=2)[:, :, :, jj, :], dec_d[0, jj].rearrange("p (a i c) -> p a i c", a=2, c=2), writes=[dec.r()])
        kb.dma("sp", gsel[:], gsel_d, writes=[gsel.r()])
        kb.dma("sp", gnb[:], gn_d.partition_broadcast(128), writes=[gnb.r()])
        St = S_("St", [128, 2, 128], F32); Ssels = [S_(f"Ssel{i}", [128, 2, 256], F32) for i in range(2)]; Sselb = S_("Sselb", [128, 2, 2, 128], BF16)
        kvb = [S_(f"kvb{i}", [128, 4, 2, 2, 128], BF16) for i in range(2)]
        qd = S_("qd", [128, 2, 128], BF16); oin = S_("oin", [128, 512], F32); grs = S_("grs", [128, 512], F32)
        og = S_("og", [128, 512], F32); sq = S_("sq", [128, 512], F32); ssq = S_("ssq", [128, 4], F32); ssq2 = S_("ssq2", [128, 4], F32)
        BA = ps(kb, "BA", [128, 512], F32); BB = ps(kb, "BB", [128, 512], F32)
        D(lambda e: e.memset(St[:], 0.0), [], [St.r(("p", 0)), St.r(("p", 1))])
        Sflat = St[:].rearrange("p a b -> p (a b)")

        def scan(i):
            Ssel = Ssels[i % 2]
            D(lambda e: e.memset(Ssel[:], 0.0), [], [Ssel.r(("c", 0)), Ssel.r(("c", 1))])
            kvb_ = kvb[i % 2]
            for jj in range(4):
                for q in range(2):
                    kb.dma("sp", kvb_[q * 64:(q + 1) * 64, jj], kv_d[q, jj].rearrange("p (a n d) -> p a n d", a=2, d=128)[:, :, 2 * i:2 * i + 2, :], writes=[kvb_.r()])
            for k in range(8):
                n = 8 * i + k
                for c in range(2):
                    D(lambda e, c=c, k=k: e.scalar_tensor_tensor(out=Ssel[:, c, :], in0=Sflat, scalar=gsel[:, c, k:k + 1], in1=Ssel[:, c, :], op0=ALU.mult, op1=ALU.add),
                      [St.r(("p", 0)), St.r(("p", 1)), gsel.r(), Ssel.r(("c", c))], [Ssel.r(("c", c))])
                for p in range(2):
                    D(lambda e, p=p, n=n, k=k: e.scalar_tensor_tensor(out=St[:, p, :], in0=St[:, p, :], scalar=dec[:, p, n:n + 1], in1=kvb_[:, k // 2, p, k % 2, :], op0=ALU.mult, op1=ALU.add),
                      [St.r(("p", p)), dec.r(), kvb_.r()], [St.r(("p", p))])
                yield

        def epilogue(i):
            Ssel = Ssels[i % 2]
            A(lambda e: e.copy(Sselb[:].rearrange("p c a b -> p (c a b)"), Ssel[:].rearrange("p c x -> p (c x)")), [Ssel.r(("c", 0)), Ssel.r(("c", 1))], [Sselb.r()])
            kb.dma("sp", qd[:], qd_d[:, :, i * 128:(i + 1) * 128], writes=[qd.r()])
            kb.dma("sp", oin[:], oin_d[:, i, :], writes=[oin.r()])
            kb.dma("sp", grs[:], grs_d[:, i, :], writes=[grs.r()])
            yield
            fns = []
            for c in range(2):
                for p in range(2):
                    for half in range(2):
                        bank = BA if half == 0 else BB
                        fns.append(lambda e, c=c, p=p, half=half, bank=bank: e.matmul(bank[:, (c * 2 + p) * 128:(c * 2 + p + 1) * 128], qd[half * 64:(half + 1) * 64, p, :], Sselb[half * 64:(half + 1) * 64, c, p, :], start=True, stop=True))
            kb.group("pe", fns, reads=[qd.r(), Sselb.r()], writes=[BA.r(), BB.r()])
            yield
            for h in range(4):
                p, half = divmod(h, 2)
                bank = BA if half == 0 else BB
                for c in range(2):
                    rows = slice(c * 64, (c + 1) * 64)
                    D(lambda e, h=h, c=c, p=p, bank=bank, rows=rows: e.tensor_tensor(out=og[rows, h * 128:(h + 1) * 128], in0=bank[rows, (c * 2 + p) * 128:(c * 2 + p + 1) * 128], in1=oin[rows, h * 128:(h + 1) * 128], op=ALU.add),
                      [bank.r(), oin.r()], [og.r(("h", h))])
                yield
            OG = [og.r(("h", h)) for h in range(4)]
            D(lambda e: e.tensor_tensor(out=sq[:], in0=og[:], in1=og[:], op=ALU.mult), OG, [sq.r()])
            yield
            D(lambda e: e.tensor_reduce(out=ssq[:], in_=sq[:].rearrange("p (h d) -> p h d", h=4), axis=AX.X, op=ALU.add), [sq.r()], [ssq.r()])
            yield
            kb.op("dve", lambda e: e.tensor_scalar(out=ssq2[:], in0=ssq[:], scalar1=1.0 / 128, scalar2=EPS, op0=ALU.mult, op1=ALU.add), reads=[ssq.r()], writes=[ssq2.r()])
            yield
            kb.op("act", lambda e: e.activation(out=ssq2[:], in_=ssq2[:], func=AF.Sqrt), reads=[ssq2.r()], writes=[ssq2.r()])
            yield
            kb.op("dve", lambda e: e.reciprocal(ssq[:], ssq2[:]), reads=[ssq2.r()], writes=[ssq.r()])
            yield
            og3 = og[:].rearrange("p (h d) -> p h d", h=4)
            D(lambda e: e.tensor_tensor(out=og3, in0=og3, in1=ssq[:].unsqueeze(2).to_broadcast([128, 4, 128]), op=ALU.mult), OG + [ssq.r()], OG)
            yield
            D(lambda e: e.tensor_tensor(out=og3, in0=og3, in1=gnb[:].unsqueeze(1).to_broadcast([128, 4, 128]), op=ALU.mult), OG + [gnb.r()], OG)
            yield
            D(lambda e: e.tensor_tensor(out=og[:], in0=og[:], in1=grs[:], op=ALU.mult), OG + [grs.r()], OG)
            kb.dma("sp", o_gla[:, i, :], og[:], reads=OG, is_output=True)
            yield

        for _ in scan(0):
            pass
        for i in range(NB):
            ep = epilogue(i)
            if i + 1 < NB:
                for _ in scan(i + 1):
                    for _n in range(2):
                        try:
                            next(ep)
                        except StopIteration:
                            break
            for _ in ep:
                pass
        kb.finish()
        print("GLA instructions:", kb.nins, {k: len(v) for k, v in kb.q.items()})
    return nc


def gla_sel(j):
    g = np.zeros((128, 2, 8), np.float32)
    for c in range(2):
        g[:, c, 2 * j + c] = 1.0
    return g


NT = 16
POOL_W = (2, 4, 8, 16)


def build_l1b():
    nc = _get_nc()
    xs = dram_in(nc, "xs", [NT * 128, 1024], F32)
    cT_d = dram_in(nc, "cT", [128, 8], F32)
    ada_w = dram_in(nc, "ada_w", [1024, 6144], F32)
    ada_b = dram_in(nc, "ada_b", [1, 6144], F32)
    mixn_d = dram_in(nc, "mixn", [128, 8], F32)
    w_in = dram_in(nc, "w_in", [1024, 2048], F32)
    qn_d = dram_in(nc, "q_norm", [128], F32)
    kn_d = dram_in(nc, "k_norm", [128], F32)
    identb_d = dram_in(nc, "identb", [128, 128], BF16)
    o_mod = dram_out(nc, "o_mod", [1, 6144], F32)
    o_qT = dram_out(nc, "o_qT", [128, 4, NT * 128], BF16)
    o_kT = dram_out(nc, "o_kT", [128, 4, NT * 128], BF16)
    o_v = dram_out(nc, "o_v", [128, NT, 512], BF16)
    o_u = dram_out(nc, "o_u", [128, NT, 512], F32)
    o_uh = dram_out(nc, "o_uh", [256, 512], F32)
    with _phase(nc) as kb:
        S = lambda n, s, d: sb(kb, n, s, d)
        D = lambda fn, r, w: kb.op("dve", fn, reads=r, writes=w)
        A = lambda fn, r, w: kb.op("act", fn, reads=r, writes=w)
        identb = S("identb", [128, 128], BF16); mixn = S("mixn", [128, 8], F32)
        qnb = S("qnb", [128, 128], F32); knb = S("knb", [128, 128], F32)
        for t, d in ((identb, identb_d), (mixn, mixn_d)):
            kb.dma("sp", t[:], d, writes=[t.r()])
        for t, d in ((qnb, qn_d), (knb, kn_d)):
            kb.dma("sp", t[:], d.partition_broadcast(128), writes=[t.r()])
        wb = S("wb", [128, 8, 2048], BF16)
        for k in range(8):
            kb.dma("pool", wb[:, k, :], w_in[k * 128:(k + 1) * 128, :], writes=[wb.r(("k", k))])
        WB = [wb.r(("k", k)) for k in range(8)]
        modrow = S("modrow", [1, 6144], F32); modP = S("modP", [128, 48], F32)
        pT = ps(kb, "pT", [128, 8, 128], BF16)
        pp = [ps(kb, f"pp{i}", [128, 512], F32) for i in range(2)]
        pA = ps(kb, "pA", [128, 512], F32); pB = ps(kb, "pB", [128, 512], F32)
        emit_mod(kb, cT_d, ada_w, ada_b, modrow, modP, "m1", pA, pB)
        kb.dma("sp", o_mod, modrow[:], reads=[modrow.r()], is_output=True)
        a1 = S("a1", [128, 8], F32)
        D(lambda e: e.scalar_tensor_tensor(out=a1[:], in0=modP[:, 8:16], scalar=1.0, in1=mixn[:], op0=ALU.add, op1=ALU.mult), [modP.r(), mixn.r()], [a1.r()])
        xt = [S(f"xt{i}", [128, 1024], F32) for i in range(2)]
        junk = S("junk", [128, 1024], BF16); xn = S("xn", [128, 1024], BF16)
        st = S("st", [128, 1], F32); st2 = S("st2", [128, 1], F32)
        hT = [S(f"hT{i}", [128, 8, 128], BF16) for i in range(2)]
        proj = S("proj", [128, 2048], F32)
        sq = S("sq", [128, 512], F32); ssq = S("ssq", [128, 4], F32); ssq2 = S("ssq2", [128, 4], F32)
        qn = S("qn", [128, 4, 128], F32); qr = S("qr", [128, 4, 128], BF16)
        sqb = S("sqb", [128, 512], F32); ssqb = S("ssqb", [128, 4], F32); ssq2b = S("ssq2b", [128, 4], F32)
        qnb2 = S("qnb2", [128, 4, 128], F32); qrb = S("qrb", [128, 4, 128], BF16)
        TMP = [(sq, ssq, ssq2, qn, qr), (sqb, ssqb, ssq2b, qnb2, qrb)]
        qT_t = S("qT_t", [128, 4, 128], BF16); kT_t = S("kT_t", [128, 4, 128], BF16)
        vb = S("vb", [128, 512], BF16)
        for i in range(NT):
            x_ = xt[i % 2]; h_ = hT[i % 2]
            kb.dma("sp", x_[:], xs[i * 128:(i + 1) * 128, :], writes=[x_.r()])
            emit_norm_hT(kb, x_, h_, a1, modP_sh(modP), identb, junk, xn, pT, st, st2)
            for cb in range(4):
                p_ = pp[cb % 2]
                kb.group("pe", [(lambda e, k=k, cb=cb, p_=p_, h_=h_: e.matmul(p_[:], h_[:, k, :], wb[:, k, cb * 512:(cb + 1) * 512], start=(k == 0), stop=(k == 7))) for k in range(8)],
                         reads=[h_.r()] + WB, writes=[p_.r()])
                if cb % 2 == 0:
                    A(lambda e, cb=cb, p_=p_: e.copy(proj[:, cb * 512:(cb + 1) * 512], p_[:]), [p_.r()], [proj.r(("c", cb))])
                else:
                    D(lambda e, cb=cb, p_=p_: e.tensor_copy(proj[:, cb * 512:(cb + 1) * 512], p_[:]), [p_.r()], [proj.r(("c", cb))])
            PR = [proj.r(("c", cb)) for cb in range(4)]
            CH = [[], [], []]
            for ci, (c0, gain, dstT, dstD) in enumerate(((512, qnb, qT_t, o_qT), (1024, knb, kT_t, o_kT))):
                kb.rec = CH[ci]
                sq_, ssq_, ssq2_, qn_, qr_ = TMP[ci]
                src = proj[:, c0:c0 + 512]
                D(lambda e, src=src, sq_=sq_: e.tensor_tensor(out=sq_[:], in0=src, in1=src, op=ALU.mult), PR, [sq_.r()])
                D(lambda e, sq_=sq_, ssq_=ssq_: e.tensor_reduce(out=ssq_[:], in_=sq_[:].rearrange("p (h d) -> p h d", h=4), axis=AX.X, op=ALU.add), [sq_.r()], [ssq_.r()])
                emit_rstd(kb, ssq_, 128, ssq2_)
                s3 = src.rearrange("p (h d) -> p h d", h=4)
                D(lambda e, s3=s3, qn_=qn_, ssq_=ssq_: e.tensor_tensor(out=qn_[:], in0=s3, in1=ssq_[:].unsqueeze(2).to_broadcast([128, 4, 128]), op=ALU.mult), PR + [ssq_.r()], [qn_.r()])
                D(lambda e, gain=gain, qn_=qn_, qr_=qr_: e.tensor_tensor(out=qr_[:], in0=qn_[:], in1=gain[:].unsqueeze(1).to_broadcast([128, 4, 128]), op=ALU.mult), [qn_.r(), gain.r()], [qr_.r()])
                kb.group("pe", [(lambda e, h=h, qr_=qr_, ci=ci: e.transpose(pT[:, 4 * ci + h, :], qr_[:, h, :], identb[:])) for h in range(4)], reads=[qr_.r(), identb.r()], writes=[pT.r()])
                A(lambda e, dstT=dstT, ci=ci: e.copy(dstT[:], pT[:, 4 * ci:4 * ci + 4, :]), [pT.r()], [dstT.r()])
                kb.dma("sp", dstD[:, :, i * 128:(i + 1) * 128], dstT[:], reads=[dstT.r()], is_output=True)
            kb.rec = CH[2]
            D(lambda e: e.tensor_copy(vb[:], proj[:, 1536:2048]), PR, [vb.r()])
            kb.dma("sp", o_v[:, i, :], vb[:], reads=[vb.r()], is_output=True)
            kb.dma("sp", o_u[:, i, :], proj[:, 0:512], reads=PR, is_output=True)
            kb.dma("sp", o_uh[i * 16:(i + 1) * 16, :], proj[112:128, 0:512], reads=PR, is_output=True)
            kb.emit_roundrobin(CH)
        kb.finish()
        print("L1b instructions:", kb.nins, {k: len(v) for k, v in kb.q.items()})
    return nc


def build_pool():
    nc = _get_nc()
    u_d = dram_in(nc, "u", [128, NT, 512], F32)
    guh_d = dram_in(nc, "guh", [1, 4, 256, 512], F32)
    hsel_d = dram_in(nc, "hsel", [128, 4], F32)
    band_d = dram_in(nc, "band", [128, 4, 128], F32)
    band0_d = dram_in(nc, "band0", [128, 4, 128], F32)
    bandhc_d = dram_in(nc, "bandhc", [128, 2, NT, 4, 128], F32)
    pw_d = dram_in(nc, "pool_w", [4, 128, 128], F32)
    psc_d = dram_in(nc, "pool_scale", [512], F32)
    o_pool = dram_out(nc, "o_pool", [128, NT, 512], F32)
    with _phase(nc) as kb:
        S = lambda n, s, d: sb(kb, n, s, d)
        D = lambda fn, r, w: kb.op("dve", fn, reads=r, writes=w)
        A = lambda fn, r, w: kb.op("act", fn, reads=r, writes=w)
        hsel = S("hsel", [128, 4], F32); band = S("band", [128, 4, 128], F32); band0 = S("band0", [128, 4, 128], F32)
        pscb = S("pscb", [128, 512], F32); pw = S("pw", [128, 4, 128], BF16)
        for t, d in ((hsel, hsel_d), (band, band_d), (band0, band0_d)):
            kb.dma("sp", t[:], d, writes=[t.r()])
        kb.dma("sp", pscb[:], psc_d.partition_broadcast(128), writes=[pscb.r()])
        kb.dma("pool", pw[:], pw_d.rearrange("g c d -> c g d"), writes=[pw.r()])
        uhc = S("uhc", [128, 4, 2, 512], F32); uhs = S("uhs", [128, 2, 512], F32)
        for jj in range(4):
            for hh in range(2):
                kb.dma("sp", uhc[:, jj, hh, :], guh_d[0, jj, hh * 128:(hh + 1) * 128, :], writes=[uhc.r()])
        uc = uhc[:].rearrange("p j h c -> p j (h c)"); us = uhs[:].rearrange("p h c -> p (h c)")
        D(lambda e: e.tensor_scalar(out=us, in0=uc[:, 0, :], scalar1=hsel[:, 0:1], scalar2=None, op0=ALU.mult), [uhc.r(), hsel.r()], [uhs.r()])
        for jj in range(1, 4):
            D(lambda e, jj=jj: e.scalar_tensor_tensor(out=us, in0=uc[:, jj, :], scalar=hsel[:, jj:jj + 1], in1=us, op0=ALU.mult, op1=ALU.add), [uhc.r(), hsel.r(), uhs.r()], [uhs.r()])
        pA = ps(kb, "pA", [128, 512], F32); pB = ps(kb, "pB", [128, 512], F32)
        ut = [S(f"ut{i}", [128, 512], F32) for i in range(2)]
        bh = [S(f"bh{i}", [128, 2, 4, 128], F32) for i in range(2)]
        plT = S("plT", [128, 4, 128], BF16); opl = S("opl", [128, 512], F32)
        for i in range(NT):
            u_ = ut[i % 2]; bh_ = bh[i % 2]
            kb.dma("sp", u_[:], u_d[:, i, :], writes=[u_.r()])
            kb.dma("sp", bh_[:], bandhc_d[:, :, i, :, :], writes=[bh_.r()])
            fns = []
            for g in range(4):
                bo = band0[:, g, :] if i == 0 else band[:, g, :]
                fns.append(lambda e, g=g, bo=bo, u_=u_: e.matmul(pA[:, g * 128:(g + 1) * 128], u_[:, g * 128:(g + 1) * 128], bo, start=True, stop=False))
                fns.append(lambda e, g=g, bh_=bh_: e.matmul(pA[:, g * 128:(g + 1) * 128], uhs[:, 0, g * 128:(g + 1) * 128], bh_[:, 0, g, :], start=False, stop=False))
                fns.append(lambda e, g=g, bh_=bh_: e.matmul(pA[:, g * 128:(g + 1) * 128], uhs[:, 1, g * 128:(g + 1) * 128], bh_[:, 1, g, :], start=False, stop=True))
            kb.group("pe", fns, reads=[u_.r(), bh_.r(), uhs.r(), band.r(), band0.r()], writes=[pA.r()])
            A(lambda e: e.copy(plT[:].rearrange("p g t -> p (g t)"), pA[:]), [pA.r()], [plT.r()])
            kb.group("pe", [(lambda e, g=g: e.matmul(pB[:, g * 128:(g + 1) * 128], plT[:, g, :], pw[:, g, :], start=True, stop=True)) for g in range(4)], reads=[plT.r(), pw.r()], writes=[pB.r()])
            D(lambda e: e.tensor_tensor(out=opl[:], in0=pB[:], in1=pscb[:], op=ALU.mult), [pB.r(), pscb.r()], [opl.r()])
            kb.dma("sp", o_pool[:, i, :], opl[:], reads=[opl.r()], is_output=True)
        kb.finish()
    return nc


def pool_consts_core(j):
    s_ = np.arange(128)[:, None]; t_ = np.arange(128)[None, :]
    band = np.zeros((128, 4, 128), np.float32); band_first = np.zeros((128, 4, 128), np.float32)
    bandhc = np.zeros((128, 2, 16, 4, 128), np.float32)
    for g, w in enumerate(POOL_W):
        inwin = ((t_ - s_) >= 0) & ((t_ - s_) <= w - 1)
        band[:, g, :] = inwin / float(w) - (s_ == t_)
        cnt = np.minimum(t_ + 1.0, float(w))
        band_first[:, g, :] = inwin / cnt - (s_ == t_)
        for i in range(16):
            isrc = i if j > 0 else i - 1
            if isrc < 0:
                continue
            half, slot = divmod(isrc, 8)
            for r in range(16):
                srel = r - 16
                row = (((np.arange(128) - srel) >= 0) & ((np.arange(128) - srel) <= w - 1)) / float(w)
                bandhc[slot * 16 + r, half, i, g, :] = row
    hsel = np.zeros((128, 4), np.float32)
    hsel[:, (j - 1) % 4] = 1.0
    return band, (band_first if j == 0 else band), bandhc, hsel


def pool_consts():
    s = np.arange(128)[:, None]; t = np.arange(128)[None, :]
    band = np.zeros((128, 4, 128), np.float32); band_first = np.zeros((128, 4, 128), np.float32)
    bandh = np.zeros((128, 8, 4, 128), np.float32)
    for g, w in enumerate(POOL_W):
        inwin = ((t - s) >= 0) & ((t - s) <= w - 1)
        band[:, g, :] = inwin / float(w) - (s == t)
        cnt = np.minimum(t + 1.0, float(w))
        band_first[:, g, :] = inwin / cnt - (s == t)
        for r in range(16):
            srel = r - 16
            row = (((np.arange(128) - srel) >= 0) & ((np.arange(128) - srel) <= w - 1)) / float(w)
            for slot in range(8):
                bandh[slot * 16 + r, slot, g, :] = row
    return band, bandh, band_first


SCALE = float(128 ** -0.5)


def build_sb(QT=tuple(range(16))):
    nc = _get_nc()
    qT_d = dram_in(nc, "qT", [128, 4, 2048], BF16)
    kT_d = dram_in(nc, "kT", [2, 4, 64, 8192], BF16)
    v_d = dram_in(nc, "v", [2, 4, 64, 8192], BF16)
    mask_d = dram_in(nc, "sbmask", [128, 512], F32)
    U_d = dram_in(nc, "U", [128, 128], BF16)
    ones_d = dram_in(nc, "ones", [128, 128], BF16)
    o_sb = dram_out(nc, "o_sb", [128, 16, 512], F32)
    with _phase(nc) as kb:
        S = lambda n, s, d: sb(kb, n, s, d)
        D = lambda fn, r, w: kb.op("dve", fn, reads=r, writes=w)
        A = lambda fn, r, w: kb.op("act", fn, reads=r, writes=w)
        kT = S("kT", [128, 4, 8192], BF16); v = S("v", [128, 64, 512], BF16)
        for h in range(4):
            for jj in range(4):
                for q in range(2):
                    kb.dma("sp", kT[q * 64:(q + 1) * 64, h, :].rearrange("p (i j s) -> p i j s", j=4, s=128)[:, :, jj, :],
                           kT_d[q, jj].rearrange("p (h i s) -> p h i s", h=4, s=128)[:, h, :, :], writes=[kT.r(("h", h))])
        for jj in range(4):
            for q in range(2):
                kb.dma("sp", v[q * 64:(q + 1) * 64].rearrange("p (i j) c -> p i j c", j=4)[:, :, jj, :], v_d[q, jj].rearrange("p (i c) -> p i c", c=512), writes=[v.r(("g", jj))])
        KT = [kT.r(("h", h)) for h in range(4)]; VV = [v.r(("g", g)) for g in range(8)]
        mask = S("mask", [128, 512], F32); U = S("U", [128, 128], BF16); ones = S("ones", [128, 128], BF16)
        for t, d in ((mask, mask_d), (U, U_d), (ones, ones_d)):
            kb.dma("sp", t[:], d, writes=[t.r()])
        B = [ps(kb, f"B{i}", [128, 512], F32) for i in range(8)]
        qt = S("qt", [128, 4, 128], BF16)
        NBUF = 5
        eb = [S(f"eb{i}", [128, 512], F32) for i in range(NBUF)]
        spb = [S(f"spb{i}", [128, 512], F32) for i in range(NBUF)]
        Lbb = [S(f"Lb{i}", [128, 512], BF16) for i in range(NBUF)]
        tb = [S(f"tb{i}", [128, 512], F32) for i in range(NBUF)]
        wbb = [S(f"wb{i}", [128, 512], BF16) for i in range(NBUF)]
        Csb = S("Csb", [128, 4, 128], F32)
        osb = S("osb", [128, 512], F32)
        Zs = (B[0], B[1], B[2]); As = (B[3], B[4], B[5], B[6]); Ob = B[7]
        qts = [qt, S("qt2", [128, 4, 128], BF16)]
        qss = [S("qs0", [128, 4, 128], BF16), S("qs1", [128, 4, 128], BF16)]

        def make_it(i, m, h, ctr, qt_, qs_):
            Z = Zs[ctr % 3]; Aa = As[ctr % 4]
            e_ = eb[ctr % NBUF]; sp_ = spb[ctr % NBUF]; L_ = Lbb[ctr % NBUF]; t_ = tb[ctr % NBUF]; w_ = wbb[ctr % NBUF]
            diag = (m == i)
            first = diag and h == 0
            last = (m == 0 and h == 3)
            hc = slice(h * 128, (h + 1) * 128)

            def S1():
                if first:
                    kb.dma("sp", qt_[:], qT_d[:, :, i * 128:(i + 1) * 128], writes=[qt_.r()])
                    D(lambda e: e.tensor_scalar(out=qs_[:], in0=qt_[:], scalar1=SCALE, scalar2=None, op0=ALU.mult), [qt_.r()], [qs_.r()])
                kb.group("pe", [(lambda e, x=x: e.matmul(Z[:, x * 128:(x + 1) * 128], kT[:, h, m * 512 + x * 128: m * 512 + (x + 1) * 128], qt_[:, h, :], start=True, stop=True)) for x in range(4)],
                         reads=[KT[h], qt_.r()], writes=[Z.r()])
                A(lambda e: e.activation(out=e_[:], in_=Z[:], func=AF.Exp, scale=-SCALE), [Z.r()], [e_.r()])

            def S2():
                A(lambda e: e.activation(out=sp_[:], in_=e_[:], func=AF.Ln, bias=1.0), [e_.r()], [sp_.r()])
                D(lambda e: e.scalar_tensor_tensor(out=L_[:], in0=Z[:], scalar=-SCALE, in1=sp_[:], op0=ALU.mult, op1=ALU.subtract), [Z.r(), sp_.r()], [L_.r()])
                if diag:
                    D(lambda e: e.tensor_tensor(out=L_[:], in0=L_[:], in1=mask[:], op=ALU.mult), [L_.r(), mask.r()], [L_.r()])

            def S3():
                fns = []
                for x in range(4):
                    fns.append(lambda e, x=x: e.matmul(Aa[:, x * 128:(x + 1) * 128], U[:], L_[:, x * 128:(x + 1) * 128], start=True, stop=False))
                    for x2 in range(x + 1, 4):
                        fns.append(lambda e, x=x, x2=x2: e.matmul(Aa[:, x * 128:(x + 1) * 128], ones[:], L_[:, x2 * 128:(x2 + 1) * 128], start=False, stop=False))
                    fns.append(lambda e, x=x: e.matmul(Aa[:, x * 128:(x + 1) * 128], kT[:, h, m * 512 + x * 128: m * 512 + (x + 1) * 128], qs_[:, h, :], start=False, stop=(x == 3)))
                kb.group("pe", fns, reads=[U.r(), ones.r(), L_.r(), KT[h], qs_.r()], writes=[Aa.r()])
                if not diag:
                    D(lambda e: e.tensor_tensor(out=t_[:].rearrange("p (x q) -> p x q", x=4), in0=Aa[:].rearrange("p (x q) -> p x q", x=4), in1=Csb[:, h, :].unsqueeze(1).to_broadcast([128, 4, 128]), op=ALU.add),
                      [Aa.r(), Csb.r(("h", h))], [t_.r()])

            def S4():
                if diag:
                    A(lambda e: e.activation(out=w_[:], in_=Aa[:], func=AF.Exp), [Aa.r()], [w_.r()])
                else:
                    A(lambda e: e.activation(out=w_[:], in_=t_[:], func=AF.Exp), [t_.r()], [w_.r()])
                if diag:
                    D(lambda e: e.tensor_tensor(out=w_[:], in0=w_[:], in1=mask[:], op=ALU.mult), [w_.r(), mask.r()], [w_.r()])

            def S5():
                if m > 0:
                    kb.group("pe", [(lambda e, x=x: e.matmul(Aa[:, 0:128], ones[:], L_[:, x * 128:(x + 1) * 128], start=(x == 0), stop=(x == 3))) for x in range(4)],
                             reads=[ones.r(), L_.r()], writes=[Aa.r()])
                    if diag:
                        D(lambda e: e.tensor_copy(Csb[:, h, :], Aa[:, 0:128]), [Aa.r()], [Csb.r(("h", h))])
                    else:
                        D(lambda e: e.tensor_tensor(out=Csb[:, h, :], in0=Aa[:, 0:128], in1=Csb[:, h, :], op=ALU.add), [Aa.r(), Csb.r(("h", h))], [Csb.r(("h", h))])
                kb.group("pe", [(lambda e, x=x: e.matmul(Ob[:, hc], w_[:, x * 128:(x + 1) * 128], v[:, m * 4 + x, h * 128:(h + 1) * 128], start=(first and x == 0), stop=(last and x == 3))) for x in range(4)],
                         reads=[w_.r()] + VV, writes=[Ob.r(("h", h))] + ([Ob.r(("h", hh)) for hh in range(4)] if first else []))
                if last:
                    A(lambda e: e.copy(osb[:], Ob[:]), [Ob.r(("h", hh)) for hh in range(4)], [osb.r()])
                    kb.dma("sp", o_sb[:, i, :], osb[:], reads=[osb.r()], is_output=True)
            return (S1, S2, S3, S4, S5)

        its = []
        ctr = 0
        for qi, i in enumerate(QT):
            for m in range(i, -1, -1):
                for h in range(4):
                    its.append(make_it(i, m, h, ctr, qts[qi % 2], qss[qi % 2]))
                    ctr += 1
        emit_skewed(its, 5)
        kb.finish()
        print("SB instructions:", kb.nins, {k: len(v) for k, v in kb.q.items()})
    return nc


def sb_mask(j):
    s = np.arange(128)[:, None]
    col = np.arange(512)[None, :]
    jj = col // 128; t = col % 128
    vis = (jj < j) | ((jj == j) & (s < t))
    return vis.astype(np.float32)


def sb_consts():
    jx = np.arange(128)[:, None]; sx = np.arange(128)[None, :]
    U = (jx >= sx).astype(np.float32).astype(ml_dtypes.bfloat16)
    ones = np.ones((128, 128), np.float32).astype(ml_dtypes.bfloat16)
    return U, ones


NT = 16
GT = 2


def build_tail():
    nc = _get_nc()
    xs = dram_in(nc, "xs", [NT * 128, 1024], F32)
    mixa_d = dram_in(nc, "mixa", [128, NT, 512], F32)
    mixb_d = dram_in(nc, "mixb", [128, NT, 512], F32)
    modrow_d = dram_in(nc, "modrow", [1, 6144], F32)
    fnorm_d = dram_in(nc, "fnorm", [128, 8], F32)
    w_out = dram_in(nc, "w_out", [1024, 1024], F32)
    w1 = dram_in(nc, "w1", [1024, 5632], F32)
    w2 = dram_in(nc, "w2", [2816, 1024], F32)
    identb_d = dram_in(nc, "identb", [128, 128], BF16)
    identf_d = dram_in(nc, "identf", [128, 128], F32)
    o_x = dram_out(nc, "o_x", [NT * 128, 1024], F32)
    with _phase(nc) as kb:
        S = lambda n, s, d: sb(kb, n, s, d)
        D = lambda fn, r, w: kb.op("dve", fn, reads=r, writes=w)
        identb = S("identb", [128, 128], BF16); fnorm = S("fnorm", [128, 8], F32)
        mod48 = S("mod48", [48, 128], F32); identf = S("identf", [128, 128], F32)
        kb.dma("sp", identb[:], identb_d, writes=[identb.r()])
        kb.dma("sp", fnorm[:], fnorm_d, writes=[fnorm.r()])
        kb.dma("sp", mod48[:], modrow_d.rearrange("o (c p) -> (o c) p", p=128), writes=[mod48.r()])
        kb.dma("sp", identf[:], identf_d, writes=[identf.r()])
        woutb = S("woutb", [128, 8, 1024], BF16); w1b = S("w1b", [128, 8, 5632], BF16); w2b = S("w2b", [128, 22, 1024], BF16)
        for k in range(8):
            kb.dma("pool", woutb[:, k, :], w_out[k * 128:(k + 1) * 128, :], writes=[woutb.r(("k", k))])
        for k in range(8):
            kb.dma("pool", w1b[:, k, :], w1[k * 128:(k + 1) * 128, :], writes=[w1b.r(("k", k))])
        for f in range(22):
            kb.dma("pool", w2b[:, f, :], w2[f * 128:(f + 1) * 128, :], writes=[w2b.r(("k", f))])
        WO = [woutb.r(("k", k)) for k in range(8)]; W1 = [w1b.r(("k", k)) for k in range(8)]; W1f = [W1] * 22; W1u = [W1] * 22; W2 = [w2b.r(("k", f)) for f in range(22)]
        pT = ps(kb, "pT", [128, 8, 128], BF16)
        pp = [ps(kb, f"pp{i}", [128, 512], F32) for i in range(2)]
        pg = ps(kb, "pg", [128, 512], F32); pu = ps(kb, "pu", [128, 512], F32)
        modP = S("modP", [128, 48], F32); G1b = S("G1b", [128, 1024], F32); G2b = S("G2b", [128, 1024], F32)
        kb.op("pe", lambda e: e.transpose(pg[:, 0:48], mod48[:], identf[0:48, 0:48]), reads=[mod48.r(), identf.r()], writes=[pg.r()])
        D(lambda e: e.tensor_copy(modP[:], pg[:, 0:48]), [pg.r()], [modP.r()])
        kb.dma("sp", G1b[:], modrow_d[0, 2048:3072].partition_broadcast(128), writes=[G1b.r()])
        kb.dma("sp", G2b[:], modrow_d[0, 5120:6144].partition_broadcast(128), writes=[G2b.r()])
        a2 = S("a2", [128, 8], F32)
        D(lambda e: e.scalar_tensor_tensor(out=a2[:], in0=modP[:, 32:40], scalar=1.0, in1=fnorm[:], op0=ALU.add, op1=ALU.mult), [modP.r(), fnorm.r()], [a2.r()])

        class SH:
            def __getitem__(s, idx):
                p, sl = idx
                return modP[p, slice(sl.start + 24, sl.stop + 24)]
            def r(s, key=None):
                return modP.r(key)
        sh2 = SH()
        xt = [S("xt0", [128, 1024], F32)] * 2
        mixt = [S("mixt0", [128, 8, 128], BF16)] * 2
        x1s = [S(f"x1_{i}", [128, GT, 1024], F32) for i in range(2)]
        xn = S("xn", [128, 1024], BF16); junk = xn
        st = S("st", [128, 1], F32); st2 = S("st2", [128, 1], F32)
        hT4s = [S(f"hT4_{i}", [128, 8, GT * 128], BF16) for i in range(2)]
        sg = S("sg", [128, GT * 128], F32); actT = S("actT", [128, 22, GT * 128], BF16)
        yt = S("yt", [128, 1024], F32)
        mst = yt

        def front(g):
            x1 = x1s[g % 2]; hT4 = hT4s[g % 2]
            for t in range(GT):
                i = g * GT + t
                x_ = xt[i % 2]; m_ = mixt[i % 2]
                kb.dma("sp", x_[:], xs[i * 128:(i + 1) * 128, :], writes=[x_.r()])
                kb.dma("sp", mst[:, 0:512], mixa_d[:, i, :], writes=[yt.r(("c", 0))])
                kb.dma("sp", mst[:, 512:1024], mixb_d[:, i, :], writes=[yt.r(("c", 1))])
                D(lambda e: e.tensor_copy(xn[:], mst[:]), [yt.r(("c", 0)), yt.r(("c", 1))], [xn.r()])
                yield
                kb.group("pe", [(lambda e, k=k: e.transpose(pT[:, k, :], xn[:, k * 128:(k + 1) * 128], identb[:])) for k in range(8)], reads=[xn.r(), identb.r()], writes=[pT.r()])
                kb.op("act", lambda e, m_=m_: e.copy(m_[:], pT[:]), reads=[pT.r()], writes=[m_.r()])
                yield
                for cb in range(2):
                    p_ = pp[cb]
                    kb.group("pe", [(lambda e, k=k, cb=cb, p_=p_, m_=m_: e.matmul(p_[:], m_[:, k, :], woutb[:, k, cb * 512:(cb + 1) * 512], start=(k == 0), stop=(k == 7))) for k in range(8)],
                             reads=[m_.r()] + WO, writes=[p_.r()])
                    D(lambda e, cb=cb, p_=p_: e.tensor_tensor(out=mst[:, cb * 512:(cb + 1) * 512], in0=p_[:], in1=G1b[:, cb * 512:(cb + 1) * 512], op=ALU.mult), [p_.r(), G1b.r()], [yt.r(("c", cb))])
                    yield
                    D(lambda e, cb=cb, x_=x_, t=t: e.tensor_tensor(out=x1[:, t, cb * 512:(cb + 1) * 512], in0=mst[:, cb * 512:(cb + 1) * 512], in1=x_[:, cb * 512:(cb + 1) * 512], op=ALU.add),
                      [yt.r(("c", cb)), x_.r()], [x1.r(("t", t, cb))])
                    yield
                kb.op("act", lambda e, t=t: e.activation(out=junk[:], in_=x1[:, t, :], func=AF.Square, accum_out=st[:, 0:1]),
                      reads=[x1.r(("t", t, 0)), x1.r(("t", t, 1))], writes=[junk.r(), st.r()])
                yield
                kb.op("dve", lambda e: e.tensor_scalar(out=st2[:], in0=st[:], scalar1=1.0 / 1024, scalar2=EPS, op0=ALU.mult, op1=ALU.add), reads=[st.r()], writes=[st2.r()])
                yield
                kb.op("act", lambda e: e.activation(out=st2[:], in_=st2[:], func=AF.Sqrt), reads=[st2.r()], writes=[st2.r()])
                yield
                kb.op("dve", lambda e: e.reciprocal(st[:], st2[:]), reads=[st2.r()], writes=[st.r()])
                yield
                kb.op("act", lambda e, t=t: e.activation(out=xn[:], in_=x1[:, t, :], func=AF.Copy, scale=st[:, 0:1]), reads=[x1.r(("t", t, 0)), x1.r(("t", t, 1)), st.r()], writes=[xn.r()])
                yield
                kb.group("pe", [(lambda e, k=k: e.transpose(pT[:, k, :], xn[:, k * 128:(k + 1) * 128], identb[:])) for k in range(8)], reads=[xn.r(), identb.r()], writes=[pT.r()])
                yield
                for k in range(8):
                    kb.op("act", lambda e, k=k, t=t: e.activation(out=hT4[:, k, t * 128:(t + 1) * 128], in_=pT[:, k, :], func=AF.Identity, scale=a2[:, k:k + 1], bias=modP[:, 24 + k:25 + k]),
                          reads=[pT.r(), a2.r(), modP.r()], writes=[hT4.r(("t", t))])
                    if k % 4 == 3:
                        yield

        def drain(gen, n=None):
            cnt = 0
            for _ in gen:
                cnt += 1
                if n is not None and cnt >= n:
                    return

        NG = NT // GT
        gens = [front(g) for g in range(NG)]
        drain(gens[0])
        for g in range(NG):
            x1 = x1s[g % 2]; hT4 = hT4s[g % 2]
            HT = [hT4.r(("t", t)) for t in range(GT)]
            for f in range(22):
                kb.group("pe", [(lambda e, k=k, f=f: e.matmul(pg[:, 0:GT * 128], w1b[:, k, f * 128:(f + 1) * 128], hT4[:, k, :], start=(k == 0), stop=(k == 7))) for k in range(8)],
                         reads=HT + W1f[f], writes=[pg.r()])
                kb.group("pe", [(lambda e, k=k, f=f: e.matmul(pu[:, 0:GT * 128], w1b[:, k, 2816 + f * 128:2816 + (f + 1) * 128], hT4[:, k, :], start=(k == 0), stop=(k == 7))) for k in range(8)],
                         reads=HT + W1u[f], writes=[pu.r()])
                kb.op("act", lambda e: e.activation(out=sg[:], in_=pg[:, 0:GT * 128], func=AF.Silu), reads=[pg.r()], writes=[sg.r()])
                D(lambda e, f=f: e.tensor_tensor(out=actT[:, f, :], in0=sg[:], in1=pu[:, 0:GT * 128], op=ALU.mult), [sg.r(), pu.r()], [actT.r(("f", f))])
                if g + 1 < NG:
                    drain(gens[g + 1], 2)
            if g + 1 < NG:
                drain(gens[g + 1])
            AT = [actT.r(("f", f)) for f in range(22)]
            for t in range(GT):
                i = g * GT + t
                for cb in range(2):
                    p_ = pp[cb]
                    kb.group("pe", [(lambda e, f=f, cb=cb, p_=p_, t=t: e.matmul(p_[:], actT[:, f, t * 128:(t + 1) * 128], w2b[:, f, cb * 512:(cb + 1) * 512], start=(f == 0), stop=(f == 21))) for f in range(22)],
                             reads=AT + W2, writes=[p_.r()])
                    D(lambda e, cb=cb, p_=p_: e.tensor_tensor(out=yt[:, cb * 512:(cb + 1) * 512], in0=p_[:], in1=G2b[:, cb * 512:(cb + 1) * 512], op=ALU.mult), [p_.r(), G2b.r()], [yt.r(("c", cb))])
                    D(lambda e, cb=cb, t=t: e.tensor_tensor(out=yt[:, cb * 512:(cb + 1) * 512], in0=yt[:, cb * 512:(cb + 1) * 512], in1=x1[:, t, cb * 512:(cb + 1) * 512], op=ALU.add),
                      [yt.r(("c", cb)), x1.r(("t", t, cb))], [yt.r(("c", cb))])
                kb.dma("sp", o_x[i * 128:(i + 1) * 128, :], yt[:], reads=[yt.r(("c", 0)), yt.r(("c", 1))], is_output=True)
        kb.finish()
        print("tail instructions:", kb.nins, {k: len(v) for k, v in kb.q.items()})
    return nc


DBG = False


def build_fused():
    FX.active = True
    FX.nc = bass.Bass("TRN2", target_bir_lowering=False)
    FX.ext = {}
    FX.n_phase = 0
    nc = FX.nc
    es = ExitStack()
    es.__enter__()
    kb = KB(nc, es)
    kb.fused = True
    kb.last_phase = False
    FX.kb = kb
    E = fx_ext
    I = lambda name, shape, dt: nc.dram_tensor(name, list(shape), dt).ap()
    RG = [[0, 1, 2, 3], [4, 5, 6, 7]]

    def allgather(pairs, n_wait=None):
        kb.pes = ExitStack()
        skip = []
        for pi, (src, dst) in enumerate(pairs):
            nq = dst.shape[0]
            rp = src.shape[0] // nq
            for q in range(nq):
                si = src[q * rp:(q + 1) * rp, :]
                do = dst[q].rearrange("j p c -> (j p) c")
                tok = kb.coll(lambda e, si=si, do=do: e.collective_compute("AllGather", ALU.bypass, replica_groups=RG, ins=[si], outs=[do]))
                if n_wait is not None and pi >= n_wait:
                    skip.append(tok[1])
        kb.end_phase(skip=tuple(skip))

    common = dict(identb=E("identb", [128, 128], BF16), identf=E("identf", [128, 128], F32), cT=E("cT", [128, 8], F32))
    xs = E("xs", [2048, 1024], F32)
    i1 = dict(o_mod=I("i_mod0", [1, 6144], F32), o_qT=I("i_qT0", [128, 4, 2048], BF16), o_kT=I("i_kT0", [128, 4, 2048], BF16), o_v=I("i_v0", [128, 16, 512], BF16),
              o_iqT=I("i_iqT", [128, 4, 2048], BF16), o_ikT=I("i_ikT", [128, 2048], BF16), o_iw=I("i_iw", [128, 16, 8], F32), o_oin=I("i_oin", [128, 16, 512], F32),
              o_qdT=I("i_qdT", [128, 2, 2048], BF16), o_kv=I("i_kv", [128, 2, 32, 128], BF16), o_dec=I("i_dec", [128, 2, 32], F32), o_gr=I("i_gr", [128, 16, 512], F32))
    FX.remap = dict(common, xs=xs, pos=E("pos", [128, 16], I32), ada_w=E("ada_w0", [1024, 6144], F32), ada_b=E("ada_b0", [1, 6144], F32),
                    mixn=E("mixn0", [128, 8], F32), w_in=E("ab_w_in", [1024, 3672], F32), gate_up=E("gate_up", [16, 256], F32), gate_b=E("gate_b", [256], F32),
                    q_norm=E("dsa_qn", [128], F32), k_norm=E("dsa_kn", [128], F32), invf=E("invf", [64], F32), tri3=E("tri3", [128, 3, 128], F32),
                    csel=E("csel", [128, 2], F32), amask=E("amask", [128, 128], F32), **i1)
    build_l1()
    G_kT0 = I("g_kT0", [2, 4, 64, 8192], BF16); G_v0 = I("g_v0", [2, 4, 64, 8192], BF16); G_ik = I("g_ik", [1, 4, 128, 2048], BF16)
    G_kv = I("g_kv", [2, 4, 64, 8192], BF16); G_dec = I("g_dec", [1, 4, 128, 64], F32)
    allgather([(i1["o_kv"].rearrange("p a n d -> p (a n d)"), G_kv), (i1["o_dec"].rearrange("p a n -> p (a n)"), G_dec),
               (i1["o_kT"].rearrange("p h t -> p (h t)"), G_kT0), (i1["o_v"].rearrange("p i c -> p (i c)"), G_v0), (i1["o_ikT"], G_ik)], n_wait=2)
    i_gla = I("i_gla", [128, 16, 512], F32)
    FX.remap = dict(common, kv=G_kv, dec=G_dec, qdT=i1["o_qdT"], oin=i1["o_oin"], grs=i1["o_gr"], gsel=E("gsel", [128, 2, 8], F32), gnorm=E("gnorm", [128], F32), o_gla=i_gla)
    build_gla()
    i_dsa = I("i_dsa", [128, 16, 512], F32)
    FX.remap = dict(common, qT=i1["o_qT"], iqT=i1["o_iqT"], iw=i1["o_iw"], kT=G_kT0, v=G_v0, ikT=G_ik, mneg=E("mneg", [128, 512], F32), mpos=E("mpos", [128, 512], F32),
                    pow2=E("pow2", [128, NIT + 1], F32), o_dsa=i_dsa)
    build_dsa()
    i_x1 = I("i_x1", [2048, 1024], F32)
    FX.remap = dict(common, xs=xs, mixa=i_gla, mixb=i_dsa, modrow=i1["o_mod"], fnorm=E("fnorm0", [128, 8], F32), w_out=E("ab_w_out", [1024, 1024], F32),
                    w1=E("w1_0", [1024, 5632], F32), w2=E("w2_0", [2816, 1024], F32), o_x=i_x1)
    build_tail()
    i5 = dict(o_mod=I("i_mod1", [1, 6144], F32), o_qT=I("i_qT1", [128, 4, 2048], BF16), o_kT=I("i_kT1", [128, 4, 2048], BF16), o_v=I("i_v1", [128, 16, 512], BF16),
              o_u=I("i_u", [128, 16, 512], F32), o_uh=I("i_uh", [256, 512], F32))
    FX.remap = dict(common, xs=i_x1, ada_w=E("ada_w1", [1024, 6144], F32), ada_b=E("ada_b1", [1, 6144], F32), mixn=E("mixn1", [128, 8], F32),
                    w_in=E("cd_w_in", [1024, 2048], F32), q_norm=E("sb_qn", [128], F32), k_norm=E("sb_kn", [128], F32), **i5)
    build_l1b()
    G_kT1 = I("g_kT1", [2, 4, 64, 8192], BF16); G_v1 = I("g_v1", [2, 4, 64, 8192], BF16); G_uh = I("g_uh", [1, 4, 256, 512], F32)
    allgather([(i5["o_uh"], G_uh), (i5["o_kT"].rearrange("p h t -> p (h t)"), G_kT1), (i5["o_v"].rearrange("p i c -> p (i c)"), G_v1)], n_wait=1)
    i_pool = I("i_pool", [128, 16, 512], F32)
    FX.remap = dict(common, u=i5["o_u"], guh=G_uh, hsel=E("hsel", [128, 4], F32), band=E("band", [128, 4, 128], F32), band0=E("band0", [128, 4, 128], F32),
                    bandhc=E("bandhc", [128, 2, 16, 4, 128], F32), pool_w=E("pool_w", [4, 128, 128], F32), pool_scale=E("pool_scale", [512], F32), o_pool=i_pool)
    build_pool()
    i_sb = I("i_sb", [128, 16, 512], F32)
    FX.remap = dict(common, qT=i5["o_qT"], kT=G_kT1, v=G_v1, sbmask=E("sbmask", [128, 512], F32), U=E("U", [128, 128], BF16), ones=E("ones", [128, 128], BF16), o_sb=i_sb)
    build_sb()
    if DBG:
        kb.pes = ExitStack()
        for nm, ap in (("d_dsa", i_dsa), ("d_gla", i_gla), ("d_pool", i_pool), ("d_sb", i_sb)):
            o = nc.dram_tensor(nm, [128, 16, 512], F32, kind="ExternalOutput").ap()
            kb.dma("sp", o, ap, is_output=True)
        o = nc.dram_tensor("d_x1", [2048, 1024], F32, kind="ExternalOutput").ap()
        kb.dma("sp", o, i_x1, is_output=True)
        o = nc.dram_tensor("d_mod1", [1, 6144], F32, kind="ExternalOutput").ap()
        kb.dma("sp", o, i5["o_mod"], is_output=True)
        kb.end_phase()
    out = nc.dram_tensor("out", [2048, 1024], F32, kind="ExternalOutput").ap()
    FX.remap = dict(common, xs=i_x1, mixa=i_pool, mixb=i_sb, modrow=i5["o_mod"], fnorm=E("fnorm1", [128, 8], F32), w_out=E("cd_w_out", [1024, 1024], F32),
                    w1=E("w1_1", [1024, 5632], F32), w2=E("w2_1", [2816, 1024], F32), o_x=out)
    kb.last_phase = True
    build_tail()
    es.close()
    FX.active = False
    return nc


def fused_maps(inp):
    identb = np.eye(128, dtype=np.float32).astype(ml_dtypes.bfloat16)
    identf = np.eye(128, dtype=np.float32)
    cs1 = l1_consts()
    pow2 = np.tile((2.0 ** -(np.arange(NIT + 1) + 1)).astype(np.float32)[None, :], (128, 1))
    U, ones = sb_consts()
    L = lambda a: np.ascontiguousarray(a.reshape(8, 128).T)
    maps = []
    for c in range(8):
        b, j = divmod(c, 4)
        mneg, mpos = dsa_masks(j)
        band, band0, bandhc, hsel = pool_consts_core(j)
        maps.append(dict(
            identb=identb, identf=identf, cT=L(inp["c"][b]),
            xs=np.ascontiguousarray(inp["x"][b].reshape(64, 128, 1024)[j::4].reshape(2048, 1024)),
            pos=np.ascontiguousarray(inp["positions"][b].reshape(64, 128)[j::4].T.astype(np.int32)),
            ada_w0=inp["ada_w"][0], ada_b0=inp["ada_b"][0][None, :], ada_w1=inp["ada_w"][1], ada_b1=inp["ada_b"][1][None, :],
            mixn0=L(inp["mix_norm"][0]), mixn1=L(inp["mix_norm"][1]), fnorm0=L(inp["ffn_norm"][0]), fnorm1=L(inp["ffn_norm"][1]),
            ab_w_in=inp["ab_w_in"][0], gate_up=inp["gla_gate_up"][0], gate_b=inp["gla_gate_b"][0], dsa_qn=inp["dsa_q_norm"][0], dsa_kn=inp["dsa_k_norm"][0],
            invf=cs1["invf"], tri3=cs1["tri3"], csel=cs1["csel"], amask=cs1["amask"],
            mneg=mneg, mpos=mpos, pow2=pow2, gsel=gla_sel(j), gnorm=inp["gla_out_norm"][0],
            ab_w_out=inp["ab_w_out"][0], w1_0=inp["ffn_w1"][0], w2_0=inp["ffn_w2"][0],
            cd_w_in=inp["cd_w_in"][0], sb_qn=inp["sb_q_norm"][0], sb_kn=inp["sb_k_norm"][0],
            hsel=hsel, band=band, band0=band0, bandhc=bandhc, pool_w=inp["pool_w"][0], pool_scale=inp["pool_scale"][0],
            sbmask=sb_mask(j), U=U, ones=ones,
            cd_w_out=inp["cd_w_out"][0], w1_1=inp["ffn_w1"][1], w2_1=inp["ffn_w2"][1]))
    return maps


def kernel(**inputs):
    inp = {k: np.asarray(v) for k, v in inputs.items()}
    nc = build_fused()
    res = run_bass_kernel_spmd(nc, fused_maps(inp), core_ids=list(range(8)))
    out = np.zeros((2, 64, 128, 1024), np.float32)
    for c in range(8):
        b, j = divmod(c, 4)
        out[b, j::4] = res.results[c]["out"].reshape(16, 128, 1024)
    kernel.last_results = res.results
    return out.reshape(2, 8192, 1024)
```

```python
import ml_dtypes
import numpy as np
from contextlib import ExitStack
import concourse.bass as bass
import concourse.mybir as mybir
from concourse.bass_utils import run_bass_kernel_spmd

F32 = mybir.dt.float32
BF16 = mybir.dt.bfloat16
I32 = mybir.dt.int32
ALU = mybir.AluOpType
AF = mybir.ActivationFunctionType
AX = mybir.AxisListType

EPOCH = 30000
ND = 32


import types


def bind(fn):
    if getattr(fn, "__closure__", None) is None:
        return fn
    cells = []
    for c in fn.__closure__:
        try:
            cells.append(types.CellType(c.cell_contents))
        except ValueError:
            cells.append(c)
    return types.FunctionType(fn.__code__, fn.__globals__, fn.__name__, fn.__defaults__, tuple(cells))


class Reg:
    __slots__ = ("w", "r", "name")

    def __init__(self, name=""):
        self.w = None
        self.r = []
        self.name = name


class KB:
    def __init__(self, nc, es):
        self.nc = nc
        self.es = es
        self.eng = {"pe": nc.tensor, "act": nc.scalar, "dve": nc.vector, "pool": nc.gpsimd, "sp": nc.sync}
        self.sems = {}
        self.cnt = {k: 0 for k in self.eng}
        self.seen = {k: {} for k in self.eng}
        self.ndma = 0
        self.dsem = [es.enter_context(nc.semaphore(f"d{i}")) for i in range(ND)]
        self.nins = 0
        self.out_toks = []
        self.q = {k: [] for k in self.eng}
        self.rec = None
        self.pes = es
        self.pfx = ""
        self.fused = False
        self.last_phase = True
        self.dma_uses = {}

    def _esem(self, st, epoch):
        key = ("E", st, epoch)
        if key not in self.sems:
            self.sems[key] = self.es.enter_context(self.nc.semaphore(f"e_{st}_{epoch}"))
        return key

    def _semh(self, key):
        if key[0] == "D":
            return self.dsem[key[1]]
        return self.sems[key]

    def _collect(self, st, reads, writes):
        waits = {}

        def need(tok, kind):
            if tok is None:
                return
            tst, key, val = tok
            if tst == st and key[0] == "E":
                if st == "pe":
                    return
                if st in ("act", "dve") and kind != "raw":
                    return
            if self.seen[st].get(key, 0) >= val:
                return
            if waits.get(key, 0) < val:
                waits[key] = val

        for r in reads:
            need(r.w, "raw")
        for w in writes:
            need(w.w, "waw")
            for t in w.r:
                need(t, "war")
        return waits

    def _dowaits(self, st, waits):
        for key, val in waits.items():
            h = self._semh(key)
            self.q[st].append(lambda eng, h=h, val=val: eng.wait_ge(h, val))
            self.seen[st][key] = val
            self.nins += 1

    def _record(self, tok, reads, writes):
        for r in reads:
            if tok[1][0] == "E":
                r.r = [t for t in r.r if not (t[0] == tok[0] and t[1] == tok[1])]
            r.r.append(tok)
        for w in writes:
            w.w = tok
            w.r = []

    def emit_roundrobin(self, chains):
        self.rec = None
        n = max(len(c) for c in chains)
        for k in range(n):
            for c in chains:
                if k < len(c):
                    c[k]()

    def op(self, st, fn, reads=(), writes=()):
        fn = bind(fn)
        if self.rec is not None:
            reads = tuple(reads); writes = tuple(writes); rec = self.rec
            rec.append(lambda: self._norec(rec, self.op, st, fn, reads, writes))
            return None
        waits = self._collect(st, reads, writes)
        self._dowaits(st, waits)
        self.cnt[st] += 1
        c = self.cnt[st]
        epoch, val = divmod(c - 1, EPOCH)
        key = self._esem(st, epoch)
        h = self.sems[key]
        self.q[st].append(lambda eng, fn=fn, h=h: fn(eng).then_inc(h, 1))
        tok = (st, key, val + 1)
        self._record(tok, reads, writes)
        self.nins += 1
        return tok

    def _norec(self, rec, f, *a, **kw):
        saved = self.rec
        self.rec = None
        try:
            return f(*a, **kw)
        finally:
            self.rec = saved

    def group(self, st, fns, reads=(), writes=()):
        if self.rec is not None:
            fns = [bind(f) for f in fns]; reads = tuple(reads); writes = tuple(writes); rec = self.rec
            rec.append(lambda: self._norec(rec, self.group, st, fns, reads, writes))
            return None
        waits = self._collect(st, reads, writes)
        self._dowaits(st, waits)
        fns = [bind(f) for f in fns]
        for fn in fns[:-1]:
            self.q[st].append(fn)
            self.nins += 1
        self.nins += 1
        self.cnt[st] += 1
        c = self.cnt[st]
        epoch, val = divmod(c - 1, EPOCH)
        key = self._esem(st, epoch)
        h = self.sems[key]
        self.q[st].append(lambda eng, fn=fns[-1], h=h: fn(eng).then_inc(h, 1))
        tok = (st, key, val + 1)
        self._record(tok, reads, writes)
        return tok

    def dma(self, st, out, in_, reads=(), writes=(), is_output=False, **kw):
        if self.rec is not None:
            reads = tuple(reads); writes = tuple(writes); rec = self.rec
            rec.append(lambda: self._norec(rec, self.dma, st, out, in_, reads, writes, is_output, **kw))
            return None
        i = self.ndma
        self.ndma += 1
        j = i % ND
        use = i // ND
        key = ("D", j)
        waits = self._collect(st, reads, writes)
        if use > 0 and self.seen[st].get(key, 0) < 16 * use:
            waits[key] = max(waits.get(key, 0), 16 * use)
        self._dowaits(st, waits)
        h = self.dsem[j]
        self.q[st].append(lambda eng, out=out, in_=in_, kw=kw, h=h: eng.dma_start(out=out, in_=in_, **kw).then_inc(h, 16))
        tok = (st, key, 16 * (use + 1))
        self.dma_uses[key] = 16 * (use + 1)
        self._record(tok, reads, writes)
        self.nins += 1
        if is_output:
            self.out_toks.append(tok)
        return tok

    def coll(self, fn, reads=(), writes=()):
        st = "pool"
        fn = bind(fn)
        idx = len([k for k in self.sems if k[0] == "C"])
        key = ("C", idx)
        self.sems[key] = self.es.enter_context(self.nc.semaphore(f"cc{idx}"))
        waits = self._collect(st, reads, writes)
        self._dowaits(st, waits)
        h = self.sems[key]
        self.q[st].append(lambda eng, fn=fn, h=h: fn(eng).then_inc(h, 1))
        tok = (st, key, 1)
        self.dma_uses[key] = 1
        self._record(tok, reads, writes)
        self.nins += 1
        return tok

    def barrier(self, skip=()):
        targets = {k: v for k, v in self.dma_uses.items() if k not in skip}
        for e, c in self.cnt.items():
            if c > 0:
                epoch, val = divmod(c - 1, EPOCH)
                targets[("E", e, epoch)] = val + 1
        for st in self.eng:
            for key, val in targets.items():
                if self.seen[st].get(key, 0) < val:
                    h = self._semh(key)
                    self.q[st].append(lambda eng, h=h, val=val: eng.wait_ge(h, val))
                    self.seen[st][key] = val

    def end_phase(self, skip=()):
        self.barrier(skip)
        self.replay()
        self.q = {k: [] for k in self.eng}
        self.pes.close()

    def finish(self):
        if self.fused and not self.last_phase:
            self.end_phase()
            return
        st = "sp"
        for tok in self.out_toks:
            _, key, val = tok
            if self.seen[st].get(key, 0) < val:
                h = self._semh(key)
                self.q[st].append(lambda eng, h=h, val=val: eng.wait_ge(h, val))
                self.seen[st][key] = val
        self.replay()

    def replay(self):
        q = self.q
        with self.nc.Block() as block:
            @block.sync
            def _(e):
                for f in q["sp"]:
                    f(e)

            @block.tensor
            def _(e):
                for f in q["pe"]:
                    f(e)

            @block.scalar
            def _(e):
                for f in q["act"]:
                    f(e)

            @block.vector
            def _(e):
                for f in q["dve"]:
                    f(e)

            @block.gpsimd
            def _(e):
                for f in q["pool"]:
                    f(e)


class T:
    def __init__(self, t, name=""):
        self.t = t
        self.reg = Reg(name)
        self.sub = {}

    def __getitem__(self, idx):
        return self.t[idx]

    def r(self, key=None):
        if key is None:
            return self.reg
        if key not in self.sub:
            self.sub[key] = Reg()
        return self.sub[key]


def sb(kb, name, shape, dt):
    return T(kb.pes.enter_context(kb.nc.sbuf_tensor("s_" + kb.pfx + name, list(shape), dt)), name)


def ps(kb, name, shape, dt=F32):
    return T(kb.pes.enter_context(kb.nc.psum_tensor("p_" + kb.pfx + name, list(shape), dt)), name)


STAGE = 99.0
NTL = 16

EPS = 1e-6
NT = 16
TWO_PI = float(2 * np.pi)


class FX:
    active = False
    nc = None
    kb = None
    remap = {}
    ext = {}
    n_phase = 0


def fx_ext(name, shape, dt):
    if name not in FX.ext:
        FX.ext[name] = FX.nc.dram_tensor(name, list(shape), dt, kind="ExternalInput").ap()
    return FX.ext[name]


def _get_nc():
    if FX.active:
        return FX.nc
    return bass.Bass("TRN2", target_bir_lowering=False)


class _phase:
    def __init__(self, nc):
        self.nc = nc

    def __enter__(self):
        if FX.active:
            kb = FX.kb
            kb.pes = ExitStack()
            kb.pfx = f"ph{FX.n_phase}_"
            FX.n_phase += 1
            return kb
        self.es = ExitStack()
        self.es.__enter__()
        return KB(self.nc, self.es)

    def __exit__(self, *a):
        if not FX.active:
            self.es.__exit__(*a)
        return False


def dram_in(nc, name, shape, dt):
    if FX.active:
        ap = FX.remap[name]
        assert list(ap.shape) == list(shape), (name, ap.shape, shape)
        return ap
    return nc.dram_tensor(name, list(shape), dt, kind="ExternalInput").ap()


def dram_out(nc, name, shape, dt):
    if FX.active:
        ap = FX.remap[name]
        assert list(ap.shape) == list(shape), (name, ap.shape, shape)
        return ap
    return nc.dram_tensor(name, list(shape), dt, kind="ExternalOutput").ap()


def emit_mod(kb, cT_d, ada_w_d, ada_b_d, modrow, modP, name, pA, pB):
    nc = kb.nc
    cT = sb(kb, name + "cT", [128, 8], F32)
    cond = sb(kb, name + "cond", [128, 8], F32)
    one = sb(kb, name + "one", [1, 1], F32)
    wblk = [sb(kb, name + f"wblk{i}", [128, 8, 256], F32) for i in range(2)]
    pms = (pA, pB); pmp = pB
    kb.dma("sp", cT[:], cT_d, writes=[cT.r()])
    kb.dma("sp", modrow[:], ada_b_d, writes=[modrow.r()])
    kb.op("dve", lambda e: e.memset(one[:], 1.0), writes=[one.r()])
    kb.op("act", lambda e: e.activation(out=cond[:], in_=cT[:], func=AF.Silu), reads=[cT.r()], writes=[cond.r()])
    wv = ada_w_d.rearrange("(k p) n -> p k n", p=128)
    for cb in range(24):
        w = wblk[cb % 2]; pm = pms[cb % 2]
        kb.dma("sp", w[:], wv[:, :, cb * 256:(cb + 1) * 256], writes=[w.r()])
        kb.group("pe", [(lambda e, k=k, w=w, pm=pm: e.matmul(pm[0:1, 0:256], cond[:, k:k + 1], w[:, k, :], start=(k == 0), stop=(k == 7))) for k in range(8)],
                 reads=[cond.r(), w.r()], writes=[pm.r()])
        kb.op("dve", lambda e, cb=cb, pm=pm: e.tensor_tensor(out=modrow[0:1, cb * 256:(cb + 1) * 256], in0=pm[0:1, 0:256], in1=modrow[0:1, cb * 256:(cb + 1) * 256], op=ALU.add),
              reads=[pm.r(), modrow.r()], writes=[modrow.r()])
    kb.group("pe", [(lambda e, c=c: e.matmul(pmp[:, c:c + 1], modrow[0:1, c * 128:(c + 1) * 128], one[:], start=True, stop=True)) for c in range(48)],
             reads=[modrow.r(), one.r()], writes=[pmp.r()])
    kb.op("dve", lambda e: e.tensor_copy(modP[:], pmp[:, 0:48]), reads=[pmp.r()], writes=[modP.r()])


def emit_rstd(kb, ss, n, tmp, name=""):
    kb.op("dve", lambda e: e.tensor_scalar(out=tmp[:], in0=ss[:], scalar1=1.0 / n, scalar2=EPS, op0=ALU.mult, op1=ALU.add), reads=[ss.r()], writes=[tmp.r()])
    kb.op("act", lambda e: e.activation(out=tmp[:], in_=tmp[:], func=AF.Sqrt), reads=[tmp.r()], writes=[tmp.r()])
    kb.op("dve", lambda e: e.reciprocal(ss[:], tmp[:]), reads=[tmp.r()], writes=[ss.r()])


def emit_norm_hT(kb, xt, hT, a, sh, ident, junk, xn, pT, st, st2):
    kb.op("act", lambda e: e.activation(out=junk[:], in_=xt[:], func=AF.Square, accum_out=st[:, 0:1]), reads=[xt.r()], writes=[junk.r(), st.r()])
    emit_rstd(kb, st, 1024, st2)
    kb.op("act", lambda e: e.activation(out=xn[:], in_=xt[:], func=AF.Copy, scale=st[:, 0:1]), reads=[xt.r(), st.r()], writes=[xn.r()])
    kb.group("pe", [(lambda e, k=k: e.transpose(pT[:, k, :], xn[:, k * 128:(k + 1) * 128], ident[:])) for k in range(8)],
             reads=[xn.r(), ident.r()], writes=[pT.r()])
    for k in range(8):
        kb.op("act", lambda e, k=k: e.activation(out=hT[:, k, :], in_=pT[:, k, :], func=AF.Identity, scale=a[:, k:k + 1], bias=sh[:, k:k + 1]),
              reads=[pT.r(), a.r(), sh.r()], writes=[hT.r()])


def build_l1():
    nc = _get_nc()
    xs = dram_in(nc, "xs", [NT * 128, 1024], F32)
    pos_d = dram_in(nc, "pos", [128, NT], I32)
    cT_d = dram_in(nc, "cT", [128, 8], F32)
    ada_w = dram_in(nc, "ada_w", [1024, 6144], F32)
    ada_b = dram_in(nc, "ada_b", [1, 6144], F32)
    mixn_d = dram_in(nc, "mixn", [128, 8], F32)
    w_in = dram_in(nc, "w_in", [1024, 3672], F32)
    gate_up = dram_in(nc, "gate_up", [16, 256], F32)
    gate_b = dram_in(nc, "gate_b", [256], F32)
    qn_d = dram_in(nc, "q_norm", [128], F32)
    kn_d = dram_in(nc, "k_norm", [128], F32)
    invf_d = dram_in(nc, "invf", [64], F32)
    identb_d = dram_in(nc, "identb", [128, 128], BF16)
    identf_d = dram_in(nc, "identf", [128, 128], F32)
    tri3_d = dram_in(nc, "tri3", [128, 3, 128], F32)
    csel_d = dram_in(nc, "csel", [128, 2], F32)
    amask_d = dram_in(nc, "amask", [128, 128], F32)

    o_mod = dram_out(nc, "o_mod", [1, 6144], F32)
    o_qT = dram_out(nc, "o_qT", [128, 4, NT * 128], BF16)
    o_kT = dram_out(nc, "o_kT", [128, 4, NT * 128], BF16)
    o_v = dram_out(nc, "o_v", [128, NT, 512], BF16)
    o_iqT = dram_out(nc, "o_iqT", [128, 4, NT * 128], BF16)
    o_ikT = dram_out(nc, "o_ikT", [128, NT * 128], BF16)
    o_iw = dram_out(nc, "o_iw", [128, NT, 8], F32)
    o_oin = dram_out(nc, "o_oin", [128, NT, 512], F32)
    o_qdT = dram_out(nc, "o_qdT", [128, 2, NT * 128], BF16)
    o_kv = dram_out(nc, "o_kv", [128, 2, 2 * NT, 128], BF16)
    o_dec = dram_out(nc, "o_dec", [128, 2, 2 * NT], F32)
    o_gr = dram_out(nc, "o_gr", [128, NT, 512], F32)

    with _phase(nc) as kb:
        S = lambda n, s, d: sb(kb, n, s, d)
        identb = S("identb", [128, 128], BF16); identf = S("identf", [128, 128], F32)
        tri3 = S("tri3", [128, 3, 128], F32); csel = S("csel", [128, 2], F32); amask = S("amask", [128, 128], F32)
        invf = S("invf", [128, 64], F32); qnb = S("qnb", [128, 128], F32); knb = S("knb", [128, 128], F32)
        gbb = S("gbb", [128, 256], F32); gup = S("gup", [16, 256], F32); mixn = S("mixn", [128, 8], F32)
        posi = S("posi", [128, NT], I32)
        for t, d in ((identb, identb_d), (identf, identf_d), (tri3, tri3_d), (csel, csel_d), (amask, amask_d), (gup, gate_up), (mixn, mixn_d), (posi, pos_d)):
            kb.dma("sp", t[:], d, writes=[t.r()])
        for t, d in ((invf, invf_d), (qnb, qn_d), (knb, kn_d), (gbb, gate_b)):
            kb.dma("sp", t[:], d.partition_broadcast(128), writes=[t.r()])
        wb = S("wb", [128, 8, 3672], BF16)
        for k in range(8):
            kb.dma("pool", wb[:, k, :], w_in[k * 128:(k + 1) * 128, :], writes=[wb.r(("k", k))])
        wb_regs = [wb.r(("k", k)) for k in range(8)]
        modrow = S("modrow", [1, 6144], F32); modP = S("modP", [128, 48], F32)
        pT = ps(kb, "pT", [128, 8, 128], BF16)
        pp = [ps(kb, f"pp{i}", [128, 512], F32) for i in range(2)]
        pA = ps(kb, "pA", [128, 512], F32)
        pB = ps(kb, "pB", [128, 512], F32)
        pTb = pT
        pTg = ps(kb, "pTg", [128, 8, 128], BF16)
        pKV = ps(kb, "pKV", [128, 4, 256], F32)
        emit_mod(kb, cT_d, ada_w, ada_b, modrow, modP, "m0", pA, pB)
        kb.dma("sp", o_mod, modrow[:], reads=[modrow.r()], is_output=True)
        a1 = S("a1", [128, 8], F32)
        kb.op("dve", lambda e: e.scalar_tensor_tensor(out=a1[:], in0=modP[:, 8:16], scalar=1.0, in1=mixn[:], op0=ALU.add, op1=ALU.mult),
              reads=[modP.r(), mixn.r()], writes=[a1.r()])
        if STAGE < 2:
            kb.finish(); return nc
        posf = S("posf", [128, NT], F32)
        ang = S("ang", [128, NT * 64], F32); ki = S("ki", [128, NT * 64], I32); kf = S("kf", [128, NT * 64], F32)
        rs = ang; rc = S("rc", [128, NT * 64], F32); m1 = kf
        sinT = S("sinT", [128, NT, 64], F32); cosT = S("cosT", [128, NT, 64], F32)
        kb.op("dve", lambda e: e.tensor_copy(posf[:], posi[:]), reads=[posi.r()], writes=[posf.r()])
        for i in range(NT):
            kb.op("dve", lambda e, i=i: e.tensor_scalar(out=ang[:, i * 64:(i + 1) * 64], in0=invf[:], scalar1=posf[:, i:i + 1], scalar2=None, op0=ALU.mult),
                  reads=[invf.r(), posf.r()], writes=[ang.r()])
        C1 = 6.28125; C2 = TWO_PI - C1
        D = lambda fn, r, w: kb.op("dve", fn, reads=r, writes=w)
        D(lambda e: e.tensor_scalar(out=kf[:], in0=ang[:], scalar1=1.0 / TWO_PI, scalar2=None, op0=ALU.mult), [ang.r()], [kf.r()])
        D(lambda e: e.tensor_copy(ki[:], kf[:]), [kf.r()], [ki.r()])
        D(lambda e: e.tensor_copy(kf[:], ki[:]), [ki.r()], [kf.r()])
        D(lambda e: e.scalar_tensor_tensor(out=rs[:], in0=kf[:], scalar=-C1, in1=ang[:], op0=ALU.mult, op1=ALU.add), [kf.r(), ang.r()], [rs.r()])
        D(lambda e: e.scalar_tensor_tensor(out=rs[:], in0=kf[:], scalar=-C2, in1=rs[:], op0=ALU.mult, op1=ALU.add), [kf.r(), rs.r()], [rs.r()])
        PI = float(np.pi)
        D(lambda e: e.tensor_scalar(out=m1[:], in0=rs[:], scalar1=PI, scalar2=-TWO_PI, op0=ALU.is_gt, op1=ALU.mult), [rs.r()], [m1.r()])
        D(lambda e: e.tensor_tensor(out=rs[:], in0=rs[:], in1=m1[:], op=ALU.add), [rs.r(), m1.r()], [rs.r()])
        D(lambda e: e.tensor_scalar(out=m1[:], in0=rs[:], scalar1=-PI, scalar2=TWO_PI, op0=ALU.is_lt, op1=ALU.mult), [rs.r()], [m1.r()])
        D(lambda e: e.tensor_tensor(out=rs[:], in0=rs[:], in1=m1[:], op=ALU.add), [rs.r(), m1.r()], [rs.r()])
        D(lambda e: e.tensor_scalar(out=rc[:], in0=rs[:], scalar1=PI / 2, scalar2=None, op0=ALU.add), [rs.r()], [rc.r()])
        D(lambda e: e.tensor_scalar(out=m1[:], in0=rc[:], scalar1=PI, scalar2=-TWO_PI, op0=ALU.is_gt, op1=ALU.mult), [rc.r()], [m1.r()])
        D(lambda e: e.tensor_tensor(out=rc[:], in0=rc[:], in1=m1[:], op=ALU.add), [rc.r(), m1.r()], [rc.r()])
        for t in (rs, rc):
            D(lambda e, t=t: e.tensor_scalar(out=t[:], in0=t[:], scalar1=PI, scalar2=-PI, op0=ALU.min, op1=ALU.max), [t.r()], [t.r()])
        kb.op("act", lambda e: e.activation(out=sinT[:].rearrange("p a b -> p (a b)"), in_=rs[:], func=AF.Sin), reads=[rs.r()], writes=[sinT.r()])
        kb.op("act", lambda e: e.activation(out=cosT[:].rearrange("p a b -> p (a b)"), in_=rc[:], func=AF.Sin), reads=[rc.r()], writes=[cosT.r()])

        qT_t = S("qT_t", [128, 4, 128], BF16); kT_t = S("kT_t", [128, 4, 128], BF16)
        iqT_t = S("iqT_t", [128, 4, 128], BF16); ikT_t = S("ikT_t", [128, 128], BF16)
        qdT_t = S("qdT_t", [128, 2, 128], BF16)
        iw_sb = S("iw_sb", [128, NT, 8], F32)
        kv_t = S("kv_t", [128, 2, 2, 128], BF16); dec_sb = S("dec_sb", [128, 2, 2 * NT], F32)

        xt = [S(f"xt{i}", [128, 1024], F32) for i in range(2)]
        junk = S("junk", [128, 1024], BF16); xn = S("xn", [128, 1024], BF16)
        st = S("st", [128, 1], F32); st2 = S("st2", [128, 1], F32)
        hT = [S(f"hT{i}", [128, 8, 128], BF16) for i in range(2)]
        proj = S("proj", [128, 3672], F32)
        sq = S("sq", [128, 512], F32); ssq = S("ssq", [128, 4], F32); ssq2 = S("ssq2", [128, 4], F32)
        qn = S("qn", [128, 4, 128], F32); qr = S("qr", [128, 4, 128], BF16)
        tA = S("tA", [128, 4, 64], F32); tB = S("tB", [128, 4, 64], F32)
        iqr = S("iqr", [128, 8, 64], BF16); iA = S("iA", [128, 8, 32], F32); iB = S("iB", [128, 8, 32], F32)
        ik2 = S("ik2", [128, 128], BF16); kA = S("kA", [128, 32], F32); kB_ = S("kB_", [128, 32], F32)
        vb = S("vb", [128, 512], BF16); gvb = S("gvb", [128, 512], BF16)
        glT = S("glT", [16, 128], F32); pre = S("pre", [128, 256], F32); lg = S("lg", [128, 256], F32)
        bmid = S("bmid", [128, 256], F32); blast = S("blast", [128, 256], F32)
        d1 = S("d1", [128, 256], F32); d3 = S("d3", [128, 256], F32)
        E1 = S("E1", [128, 256], F32); E2 = S("E2", [128, 256], F32); E3 = S("E3", [128, 256], F32); E4 = S("E4", [128, 256], F32)
        qkd = S("qkd", [128, 3, 256], BF16)
        kdec = S("kdec", [128, 256], BF16)
        qkT = S("qkT", [128, 6, 128], BF16)
        attT = S("attT", [128, 4, 128], BF16)
        oin = S("oin", [128, 512], F32); grs = S("grs", [128, 512], F32)
        dect = S("dect", [128, 2, 2], F32)

        def rope(src, dst, nh, half, cos, sin, A, B, cols_per_head):
            sv = src
            t1 = sv[:, :, 0:half]; t2 = sv[:, :, half:2 * half]
            cb_ = cos.unsqueeze(1).to_broadcast([128, nh, half]); sb_ = sin.unsqueeze(1).to_broadcast([128, nh, half])
            return t1, t2, cb_, sb_

        for i in range(NTL if STAGE >= 3 else 0):
            x_ = xt[i % 2]; h_ = hT[i % 2]
            kb.dma("sp", x_[:], xs[i * 128:(i + 1) * 128, :], writes=[x_.r()])
            emit_norm_hT(kb, x_, h_, a1, modP_sh(modP), identb, junk, xn, pT, st, st2)
            for cb in range(8):
                c0 = cb * 512; c1 = min(3672, c0 + 512); p_ = pp[cb % 2]
                kb.group("pe", [(lambda e, k=k, c0=c0, c1=c1, p_=p_, h_=h_: e.matmul(p_[:, 0:c1 - c0], h_[:, k, :], wb[:, k, c0:c1], start=(k == 0), stop=(k == 7))) for k in range(8)],
                         reads=[h_.r()] + wb_regs, writes=[p_.r()])
                if cb % 2 == 0:
                    kb.op("act", lambda e, c0=c0, c1=c1, p_=p_: e.copy(proj[:, c0:c1], p_[:, 0:c1 - c0]), reads=[p_.r()], writes=[proj.r(("c", cb))])
                else:
                    kb.op("dve", lambda e, c0=c0, c1=c1, p_=p_: e.tensor_copy(proj[:, c0:c1], p_[:, 0:c1 - c0]), reads=[p_.r()], writes=[proj.r(("c", cb))])
            PR = [proj.r(("c", cb)) for cb in range(8)]
            cos_i = cosT[:, i, :]; sin_i = sinT[:, i, :]
            cos32 = cosT[:, i, 0:64:2]; sin32 = sinT[:, i, 0:64:2]

            C1, C2, C3 = [], [], []
            kb.rec = C1
            for (c0, gain, dstT, dstD) in ((1552, qnb, qT_t, o_qT), (2064, knb, kT_t, o_kT)):
                src = proj[:, c0:c0 + 512]
                D(lambda e, src=src: e.tensor_tensor(out=sq[:], in0=src, in1=src, op=ALU.mult), PR, [sq.r()])
                D(lambda e: e.tensor_reduce(out=ssq[:], in_=sq[:].rearrange("p (h d) -> p h d", h=4), axis=AX.X, op=ALU.add), [sq.r()], [ssq.r()])
                emit_rstd(kb, ssq, 128, ssq2)
                s3 = src.rearrange("p (h d) -> p h d", h=4)
                D(lambda e, s3=s3: e.tensor_tensor(out=qn[:], in0=s3, in1=ssq[:].unsqueeze(2).to_broadcast([128, 4, 128]), op=ALU.mult), PR + [ssq.r()], [qn.r()])
                D(lambda e, gain=gain: e.tensor_tensor(out=qn[:], in0=qn[:], in1=gain[:].unsqueeze(1).to_broadcast([128, 4, 128]), op=ALU.mult), [qn.r(), gain.r()], [qn.r()])
                t1 = qn[:, :, 0:64]; t2 = qn[:, :, 64:128]
                cb_ = cos_i.unsqueeze(1).to_broadcast([128, 4, 64]); sb_ = sin_i.unsqueeze(1).to_broadcast([128, 4, 64])
                D(lambda e: e.tensor_tensor(out=tA[:], in0=t1, in1=cb_, op=ALU.mult), [qn.r(), cosT.r()], [tA.r()])
                D(lambda e: e.tensor_tensor(out=tB[:], in0=t2, in1=sb_, op=ALU.mult), [qn.r(), sinT.r()], [tB.r()])
                D(lambda e: e.tensor_tensor(out=qr[:, :, 0:64], in0=tA[:], in1=tB[:], op=ALU.subtract), [tA.r(), tB.r()], [qr.r()])
                D(lambda e: e.tensor_tensor(out=tA[:], in0=t2, in1=cb_, op=ALU.mult), [qn.r(), cosT.r()], [tA.r()])
                D(lambda e: e.tensor_tensor(out=tB[:], in0=t1, in1=sb_, op=ALU.mult), [qn.r(), sinT.r()], [tB.r()])
                D(lambda e: e.tensor_tensor(out=qr[:, :, 64:128], in0=tA[:], in1=tB[:], op=ALU.add), [tA.r(), tB.r()], [qr.r()])
                kb.group("pe", [(lambda e, h=h: e.transpose(pTb[:, h, :], qr[:, h, :], identb[:])) for h in range(4)], reads=[qr.r(), identb.r()], writes=[pTb.r()])
                kb.op("act", lambda e, dstT=dstT: e.copy(dstT[:], pTb[:, 0:4, :]), reads=[pTb.r()], writes=[dstT.r()])
                kb.dma("sp", dstD[:, :, i * 128:(i + 1) * 128], dstT[:], reads=[dstT.r()], is_output=True)
            kb.rec = C2
            D(lambda e: e.tensor_copy(vb[:], proj[:, 2576:3088]), PR, [vb.r()])
            kb.dma("sp", o_v[:, i, :], vb[:], reads=[vb.r()], is_output=True)
            kb.rec = C1
            iq3 = proj[:, 3088:3600].rearrange("p (h d) -> p h d", h=8)
            t1 = iq3[:, :, 0:32]; t2 = iq3[:, :, 32:64]
            cb_ = cos32.unsqueeze(1).to_broadcast([128, 8, 32]); sb_ = sin32.unsqueeze(1).to_broadcast([128, 8, 32])
            D(lambda e: e.tensor_tensor(out=iA[:], in0=t1, in1=cb_, op=ALU.mult), PR + [cosT.r()], [iA.r()])
            D(lambda e: e.tensor_tensor(out=iB[:], in0=t2, in1=sb_, op=ALU.mult), PR + [sinT.r()], [iB.r()])
            D(lambda e: e.tensor_tensor(out=iqr[:, :, 0:32], in0=iA[:], in1=iB[:], op=ALU.subtract), [iA.r(), iB.r()], [iqr.r()])
            D(lambda e: e.tensor_tensor(out=iA[:], in0=t2, in1=cb_, op=ALU.mult), PR + [cosT.r()], [iA.r()])
            D(lambda e: e.tensor_tensor(out=iB[:], in0=t1, in1=sb_, op=ALU.mult), PR + [sinT.r()], [iB.r()])
            D(lambda e: e.tensor_tensor(out=iqr[:, :, 32:64], in0=iA[:], in1=iB[:], op=ALU.add), [iA.r(), iB.r()], [iqr.r()])
            iq2 = iqr[:].rearrange("p h d -> p (h d)")
            kb.group("pe", [(lambda e, g=g: e.transpose(pTb[:, g, :], iq2[:, g * 128:(g + 1) * 128], identb[:])) for g in range(4)], reads=[iqr.r(), identb.r()], writes=[pTb.r()])
            kb.op("act", lambda e: e.copy(iqT_t[:], pTb[:, 0:4, :]), reads=[pTb.r()], writes=[iqT_t.r()])
            kb.dma("sp", o_iqT[:, :, i * 128:(i + 1) * 128], iqT_t[:], reads=[iqT_t.r()], is_output=True)
            kb.rec = C2
            k1 = proj[:, 3600:3632]; k2 = proj[:, 3632:3664]
            D(lambda e: e.tensor_tensor(out=kA[:], in0=k1, in1=cos32, op=ALU.mult), PR + [cosT.r()], [kA.r()])
            D(lambda e: e.tensor_tensor(out=kB_[:], in0=k2, in1=sin32, op=ALU.mult), PR + [sinT.r()], [kB_.r()])
            D(lambda e: e.tensor_tensor(out=ik2[:, 0:32], in0=kA[:], in1=kB_[:], op=ALU.subtract), [kA.r(), kB_.r()], [ik2.r()])
            D(lambda e: e.tensor_tensor(out=kA[:], in0=k2, in1=cos32, op=ALU.mult), PR + [cosT.r()], [kA.r()])
            D(lambda e: e.tensor_tensor(out=kB_[:], in0=k1, in1=sin32, op=ALU.mult), PR + [sinT.r()], [kB_.r()])
            D(lambda e: e.tensor_tensor(out=ik2[:, 32:64], in0=kA[:], in1=kB_[:], op=ALU.add), [kA.r(), kB_.r()], [ik2.r()])
            D(lambda e: e.tensor_copy(ik2[:, 64:128], ik2[:, 0:64]), [ik2.r()], [ik2.r()])
            kb.op("pe", lambda e: e.transpose(pTb[:, 4, :], ik2[:], identb[:]), reads=[ik2.r(), identb.r()], writes=[pTb.r()])
            kb.op("act", lambda e: e.copy(ikT_t[:], pTb[:, 4, :]), reads=[pTb.r()], writes=[ikT_t.r()])
            kb.dma("sp", o_ikT[:, i * 128:(i + 1) * 128], ikT_t[:], reads=[ikT_t.r()], is_output=True)
            D(lambda e: e.tensor_copy(iw_sb[:, i, :], proj[:, 3664:3672]), PR, [iw_sb.r()])
            kb.rec = C3
            kb.op("pe", lambda e: e.transpose(pA[0:16, 0:128], proj[:, 1024:1040], identf[:]), reads=PR + [identf.r()], writes=[pA.r()])
            kb.op("act", lambda e: e.copy(glT[:], pA[0:16, 0:128]), reads=[pA.r()], writes=[glT.r()])
            kb.op("pe", lambda e: e.matmul(pA[:, 0:256], glT[:], gup[:], start=True, stop=True), reads=[glT.r(), gup.r()], writes=[pA.r()])
            D(lambda e: e.tensor_tensor(out=pre[:], in0=pA[:, 0:256], in1=gbb[:], op=ALU.add), [pA.r(), gbb.r()], [pre.r()])
            kb.op("act", lambda e: e.activation(out=lg[:], in_=pre[:], func=AF.Exp, scale=-1.0), reads=[pre.r()], writes=[lg.r()])
            kb.op("act", lambda e: e.activation(out=lg[:], in_=lg[:], func=AF.Ln, bias=1.0), reads=[lg.r()], writes=[lg.r()])
            kb.group("pe", [
                lambda e: e.matmul(pA[:, 0:256], tri3[:, 0, :], lg[:], start=True, stop=True),
                lambda e: e.matmul(pA[:, 256:512], tri3[:, 1, :], lg[:], start=True, stop=True),
                lambda e: e.matmul(pB[:, 0:256], tri3[:, 2, :], lg[:], start=True, stop=True),
                lambda e: e.matmul(pB[:, 256:258], lg[:, 0:128], csel[:], start=True, stop=True),
                lambda e: e.matmul(pB[:, 258:260], lg[:, 128:256], csel[:], start=True, stop=True),
            ], reads=[tri3.r(), lg.r(), csel.r()], writes=[pA.r(), pB.r()])
            kb.op("act", lambda e: e.copy(bmid[:], pA[:, 256:512]), reads=[pA.r()], writes=[bmid.r()])
            kb.op("act", lambda e: e.copy(blast[:], pB[:, 0:256]), reads=[pB.r()], writes=[blast.r()])
            kb.op("act", lambda e: e.activation(out=dec_sb[:, :, 2 * i:2 * i + 2], in_=pB[:, 256:260].rearrange("p (a c) -> p a c", a=2), func=AF.Exp), reads=[pB.r()], writes=[dec_sb.r()])
            D(lambda e: e.tensor_tensor(out=d1[:], in0=pA[:, 0:256], in1=bmid[:], op=ALU.subtract), [pA.r(), bmid.r()], [d1.r()])
            D(lambda e: e.tensor_tensor(out=d3[:], in0=blast[:], in1=pA[:, 0:256], op=ALU.subtract), [pA.r(), blast.r()], [d3.r()])
            kb.op("act", lambda e: e.activation(out=E1[:], in_=d1[:], func=AF.Exp), reads=[d1.r()], writes=[E1.r()])
            kb.op("act", lambda e: e.activation(out=E2[:], in_=d1[:], func=AF.Exp, scale=-1.0), reads=[d1.r()], writes=[E2.r()])
            kb.op("act", lambda e: e.activation(out=E3[:], in_=d3[:], func=AF.Exp), reads=[d3.r()], writes=[E3.r()])
            kb.op("act", lambda e: e.activation(out=E4[:], in_=pA[:, 0:256], func=AF.Exp), reads=[pA.r()], writes=[E4.r()])
            gq = proj[:, 0:256]; gk = proj[:, 256:512]
            D(lambda e: e.scalar_tensor_tensor(out=qkd[:, 0, :], in0=gq, scalar=0.125, in1=E1[:], op0=ALU.mult, op1=ALU.mult), PR + [E1.r()], [qkd.r()])
            D(lambda e: e.tensor_tensor(out=qkd[:, 1, :], in0=gk, in1=E2[:], op=ALU.mult), PR + [E2.r()], [qkd.r()])
            D(lambda e: e.scalar_tensor_tensor(out=qkd[:, 2, :], in0=gq, scalar=0.125, in1=E4[:], op0=ALU.mult, op1=ALU.mult), PR + [E4.r()], [qkd.r()])
            D(lambda e: e.tensor_tensor(out=kdec[:], in0=gk, in1=E3[:], op=ALU.mult), PR + [E3.r()], [kdec.r()])
            kb.group("pe", [(lambda e, w=w, p=p: e.transpose(pTg[:, w * 2 + p, :], qkd[:, w, p * 128:(p + 1) * 128], identb[:])) for w in range(3) for p in range(2)],
                     reads=[qkd.r(), identb.r()], writes=[pTg.r()])
            kb.op("act", lambda e: e.copy(qkT[:], pTg[:, 0:6, :]), reads=[pTg.r()], writes=[qkT.r()])
            D(lambda e: e.tensor_copy(qdT_t[:], qkT[:, 4:6, :]), [qkT.r()], [qdT_t.r()])
            kb.dma("sp", o_qdT[:, :, i * 128:(i + 1) * 128], qdT_t[:], reads=[qdT_t.r()], is_output=True)
            D(lambda e: e.tensor_copy(gvb[:], proj[:, 512:1024]), PR, [gvb.r()])
            def attmm(e, h):
                p, hb = divmod(h, 2); hb *= 64
                dst = pp[0] if hb == 0 else pp[1]
                return e.matmul(dst[:, p * 128:(p + 1) * 128], qkT[hb:hb + 64, 2 + p, :], qkT[hb:hb + 64, 0 + p, :], start=True, stop=True)
            kb.group("pe", [(lambda e, h=h: attmm(e, h)) for h in (0, 2, 1, 3)], reads=[qkT.r()], writes=[pp[0].r(), pp[1].r()])
            for hh in range(2):
                D(lambda e, hh=hh: e.tensor_tensor(out=attT[:, hh:4:2, :], in0=pp[hh][:, 0:256].rearrange("p (h i) -> p h i", h=2), in1=amask[:].unsqueeze(1).to_broadcast([128, 2, 128]), op=ALU.mult),
                  [pp[hh].r(), amask.r()], [attT.r()])
            kb.group("pe", [(lambda e, h=h: e.matmul(pB[:, h * 128:(h + 1) * 128], attT[:, h, :], gvb[:, h * 128:(h + 1) * 128], start=True, stop=True)) for h in range(4)],
                     reads=[attT.r(), gvb.r()], writes=[pB.r()])
            kb.op("act", lambda e: e.copy(oin[:], pB[:]), reads=[pB.r()], writes=[oin.r()])
            kb.dma("sp", o_oin[:, i, :], oin[:], reads=[oin.r()], is_output=True)
            kb.group("pe", [(lambda e, p=p, c=c: e.matmul(pKV[:, c * 2 + p, :], kdec[c * 64:(c + 1) * 64, p * 128:(p + 1) * 128], gvb[c * 64:(c + 1) * 64, p * 256:(p + 1) * 256], start=True, stop=True))
                            for p in range(2) for c in range(2)], reads=[kdec.r(), gvb.r()], writes=[pKV.r()])
            pk = pKV[:].rearrange("q (c p) n -> q p c n", p=2)
            kb.op("act", lambda e: e.copy(kv_t[0:64], pk[0:64, :, :, 0:128]), reads=[pKV.r()], writes=[kv_t.r()])
            D(lambda e: e.tensor_copy(kv_t[64:128], pk[64:128, :, :, 128:256]), [pKV.r()], [kv_t.r()])
            kb.dma("sp", o_kv[:, :, 2 * i:2 * i + 2, :], kv_t[:], reads=[kv_t.r()], is_output=True)
            kb.rec = C2
            kb.op("act", lambda e: e.activation(out=grs[:], in_=proj[:, 1040:1552], func=AF.Silu), reads=PR, writes=[grs.r()])
            kb.dma("sp", o_gr[:, i, :], grs[:], reads=[grs.r()], is_output=True)
            kb.emit_roundrobin([C3, C1, C2])
        for t, d in ((iw_sb, o_iw), (dec_sb, o_dec)):
            kb.dma("sp", d, t[:], reads=[t.r()], is_output=True)
        kb.finish()
        print("L1 instructions:", kb.nins, {k: len(v) for k, v in kb.q.items()})
    return nc


class _ShView:
    pass


def modP_sh(modP):
    class V:
        def __getitem__(s, idx):
            return modP[idx]
        def r(s, key=None):
            return modP.r(key)
    return V()


def l1_consts():
    half = 64
    invf = (10000.0 ** (-np.arange(half, dtype=np.float32) / half)).astype(np.float32)
    j = np.arange(128)[:, None]; i = np.arange(128)[None, :]
    same = (j // 64) == (i // 64)
    tri = (same & (j <= i)).astype(np.float32)
    mmid = (same & ((j % 64) <= 31)).astype(np.float32)
    mlast = same.astype(np.float32)
    tri3 = np.stack([tri, mmid, mlast], axis=1) * (-1.0 / 16.0)
    csel = np.stack([(np.arange(128) < 64), (np.arange(128) >= 64)], axis=1).astype(np.float32) * (-1.0 / 16.0)
    amask = tri.copy()
    return dict(invf=invf, identb=np.eye(128, dtype=np.float32).astype(ml_dtypes.bfloat16), identf=np.eye(128, dtype=np.float32),
                tri3=np.ascontiguousarray(tri3.astype(np.float32)), csel=csel, amask=amask)


NIT = 10
C0 = 11.3137085
SCALE = float(128 ** -0.5)


def emit_skewed(its, nst):
    n = len(its)
    for step in range(n + nst - 1):
        for stg in range(nst):
            k = step - stg
            if 0 <= k < n:
                its[k][stg]()


def build_dsa(QT=tuple(range(16))):
    nc = _get_nc()
    qT_d = dram_in(nc, "qT", [128, 4, 2048], BF16)
    iqT_d = dram_in(nc, "iqT", [128, 4, 2048], BF16)
    iw_d = dram_in(nc, "iw", [128, 16, 8], F32)
    kT_d = dram_in(nc, "kT", [2, 4, 64, 8192], BF16)
    v_d = dram_in(nc, "v", [2, 4, 64, 8192], BF16)
    ikT_d = dram_in(nc, "ikT", [1, 4, 128, 2048], BF16)
    mneg_d = dram_in(nc, "mneg", [128, 512], F32)
    mpos_d = dram_in(nc, "mpos", [128, 512], F32)
    identb_d = dram_in(nc, "identb", [128, 128], BF16)
    identf_d = dram_in(nc, "identf", [128, 128], F32)
    pow2_d = dram_in(nc, "pow2", [128, NIT + 1], F32)
    o_dsa = dram_out(nc, "o_dsa", [128, 16, 512], F32)
    with _phase(nc) as kb:
        S = lambda n, s, d: sb(kb, n, s, d)
        D = lambda fn, r, w: kb.op("dve", fn, reads=r, writes=w)
        A = lambda fn, r, w: kb.op("act", fn, reads=r, writes=w)
        v = S("v", [128, 64, 512], BF16); ikT = S("ikT", [128, 8192], BF16)
        kTb = [S(f"kTb{i}", [128, 512], BF16) for i in range(3)]
        for jj in range(4):
            for q in range(2):
                kb.dma("sp", v[q * 64:(q + 1) * 64].rearrange("p (i j) c -> p i j c", j=4)[:, :, jj, :], v_d[q, jj].rearrange("p (i c) -> p i c", c=512), writes=[v.r(("g", jj))])
        for jj in range(4):
            kb.dma("sp", ikT[:].rearrange("p (i j s) -> p i j s", j=4, s=128)[:, :, jj, :], ikT_d[0, jj].rearrange("p (i s) -> p i s", s=128), writes=[ikT.r()])
        VV = [v.r(("g", g)) for g in range(8)]
        iw = S("iw", [128, 16, 8], F32); mneg = S("mneg", [128, 512], F32); mpos = S("mpos", [128, 512], F32)
        identb = S("identb", [128, 128], BF16); identf = S("identf", [128, 128], F32); pow2 = S("pow2", [128, NIT + 1], F32)
        for t, d in ((iw, iw_d), (mneg, mneg_d), (mpos, mpos_d), (identb, identb_d), (identf, identf_d), (pow2, pow2_d)):
            kb.dma("sp", t[:], d, writes=[t.r()])
        B = [ps(kb, f"B{i}", [128, 512], F32) for i in range(4)] + [None, None] + [ps(kb, f"B{i}", [128, 512], F32) for i in (6, 7)]
        X2 = ps(kb, "X2", [128, 2, 512], F32)
        qts = [S(f"qt{i}", [128, 4, 128], BF16) for i in range(2)]; iqts = [S(f"iqt{i}", [128, 4, 128], BF16) for i in range(2)]
        diagw = S("diagw", [128, 8, 128], BF16)
        Rt2 = [S(f"Rt2_{i}", [128, 2, 512], BF16) for i in range(2)]
        scores = [S(f"score{i}", [128, 8192], F32) for i in range(2)]
        masks = [S(f"maskb{i}", [128, 8192], BF16) for i in range(2)]
        tmp = S("tmp", [128, 512], F32)
        sts = [S(f"st{i}", [128, 8], F32) for i in range(2)]
        Hh = S("Hh", [128, NIT + 1], F32)
        lo_t = S("lo_t", [128, 1], F32); mid_t = S("mid_t", [128, 1], F32); cnt_t = S("cnt_t", [128, 1], F32); g_t = S("g_t", [128, 1], F32)
        Eb = [S(f"Eb{i}", [128, 512], BF16) for i in range(2)]
        Pb = [S(f"Pb{i}", [128, 512], BF16) for i in range(2)]
        PT = [S(f"PT{i}", [128, 4, 128], BF16) for i in range(2)]
        rs = S("rs", [128, 4, 16], F32); rsum = S("rsum", [128, 4], F32); rinv = S("rinv", [128, 4], F32)
        osb = S("osb", [128, 512], F32)
        Sbanks = (B[0], B[1], B[7]); Ob = B[2]; pTv = B[3][:].bitcast(BF16); SC = B[6]
        cstate = [0]

        def phaseI(i, par):
            iqt = iqts[par]; score = scores[par]; st = sts[par]
            kb.dma("sp", iqt[:], iqT_d[:, :, i * 128:(i + 1) * 128], writes=[iqt.r()])
            for h in range(8):
                A(lambda e, h=h: e.activation(out=diagw[:, h, :], in_=identf[:], func=AF.Copy, scale=iw[:, i, h:h + 1]), [identf.r(), iw.r()], [diagw.r()])
            yield
            for m in range(i + 1):
                ks = slice(m * 512, (m + 1) * 512)
                for p in range(4):
                    kb.group("pe", [lambda e, p=p: e.matmul(X2[:, 0, :], iqt[0:64, p, :], ikT[0:64, ks], start=True, stop=True),
                                    lambda e, p=p: e.matmul(X2[:, 1, :], iqt[64:128, p, :], ikT[64:128, ks], start=True, stop=True)],
                             reads=[iqt.r(), ikT.r()], writes=[X2.r()])
                    R_ = Rt2[p % 2]
                    A(lambda e, R_=R_: e.activation(out=R_[:].rearrange("p a b -> p (a b)"), in_=X2[:].rearrange("p a b -> p (a b)"), func=AF.Relu), [X2.r()], [R_.r(("h", 0)), R_.r(("h", 1))])
                    kb.group("pe", [lambda e, p=p, R_=R_: e.matmul(SC[:], diagw[:, 2 * p, :], R_[:, 0, :], start=(p == 0), stop=False),
                                    lambda e, p=p, R_=R_: e.matmul(SC[:], diagw[:, 2 * p + 1, :], R_[:, 1, :], start=False, stop=(p == 3))],
                             reads=[diagw.r(), R_.r(("h", 0)), R_.r(("h", 1))], writes=[SC.r()])
                    yield
                if m < i:
                    A(lambda e: e.copy(score[:, ks], SC[:]), [SC.r()], [score.r(("m", m))])
                else:
                    D(lambda e: e.tensor_tensor(out=score[:, ks], in0=SC[:], in1=mneg[:], op=ALU.add), [SC.r(), mneg.r()], [score.r(("m", m))])
                    D(lambda e: e.tensor_tensor(out=tmp[:], in0=SC[:], in1=mpos[:], op=ALU.add), [SC.r(), mpos.r()], [tmp.r()])
                    D(lambda e: e.tensor_reduce(out=st[:, 1:2], in_=tmp[:], axis=AX.X, op=ALU.min), [tmp.r()], [st.r()])
                yield

        def phaseII(i, par):
            score = scores[par]; st = sts[par]; maskb = masks[par]
            SCR = [score.r(("m", m)) for m in range(i + 1)]
            W = (i + 1) * 512
            if i > 0:
                D(lambda e: e.tensor_reduce(out=st[:, 0:1], in_=score[:, 0:i * 512], axis=AX.X, op=ALU.min), SCR, [st.r()])
                D(lambda e: e.tensor_tensor(out=st[:, 2:3], in0=st[:, 0:1], in1=st[:, 1:2], op=ALU.min), [st.r()], [st.r()])
            else:
                D(lambda e: e.tensor_copy(st[:, 2:3], st[:, 1:2]), [st.r()], [st.r()])
            yield
            D(lambda e: e.tensor_reduce(out=st[:, 3:4], in_=score[:, 0:W], axis=AX.X, op=ALU.max), SCR, [st.r()])
            D(lambda e: e.tensor_tensor(out=st[:, 4:5], in0=st[:, 3:4], in1=st[:, 2:3], op=ALU.subtract), [st.r()], [st.r()])
            yield
            D(lambda e: e.tensor_scalar(out=Hh[:], in0=pow2[:], scalar1=st[:, 4:5], scalar2=None, op0=ALU.mult), [pow2.r(), st.r()], [Hh.r()])
            D(lambda e: e.tensor_copy(lo_t[:], st[:, 2:3]), [st.r()], [lo_t.r()])
            D(lambda e: e.tensor_tensor(out=mid_t[:], in0=st[:, 2:3], in1=Hh[:, 0:1], op=ALU.add), [st.r(), Hh.r()], [mid_t.r()])
            yield
            for k in range(NIT):
                D(lambda e: e.tensor_scalar(out=maskb[:, 0:W], in0=score[:, 0:W], scalar1=mid_t[:, 0:1], scalar2=None, op0=ALU.is_ge, op1=ALU.add, accum_out=cnt_t[:, 0:1]),
                  SCR + [mid_t.r()], [maskb.r(), cnt_t.r()])
                yield
                D(lambda e, k=k: e.tensor_scalar(out=g_t[:], in0=cnt_t[:], scalar1=255.5, scalar2=Hh[:, k:k + 1], op0=ALU.is_ge, op1=ALU.mult), [cnt_t.r(), Hh.r()], [g_t.r()])
                yield
                D(lambda e, k=k: e.scalar_tensor_tensor(out=mid_t[:], in0=g_t[:], scalar=lo_t[:, 0:1], in1=Hh[:, k + 1:k + 2], op0=ALU.add, op1=ALU.add), [g_t.r(), lo_t.r(), Hh.r()], [mid_t.r()])
                D(lambda e: e.tensor_tensor(out=lo_t[:], in0=lo_t[:], in1=g_t[:], op=ALU.add), [lo_t.r(), g_t.r()], [lo_t.r()])
                yield
            D(lambda e: e.tensor_scalar(out=maskb[:, 0:W], in0=score[:, 0:W], scalar1=lo_t[:, 0:1], scalar2=-30000.0, op0=ALU.is_lt, op1=ALU.mult), SCR + [lo_t.r()], [maskb.r()])
            yield

        def phaseIII(i, par):
            qt = qts[par]; maskb = masks[par]
            kb.dma("sp", qt[:], qT_d[:, :, i * 128:(i + 1) * 128], writes=[qt.r()])

            def make_it3(m, h, c3):
                ks = slice(m * 512, (m + 1) * 512)
                Sb = Sbanks[c3 % 3]; E_ = Eb[c3 % 2]; P_ = Pb[c3 % 2]; PT_ = PT[c3 % 2]; kt_ = kTb[c3 % 3]
                first = (m == 0 and h == 0)
                hc = slice(h * 128, (h + 1) * 128)

                def S1():
                    for q in range(2):
                        kb.dma("sp", kt_[q * 64:(q + 1) * 64].rearrange("p (j s) -> p j s", j=4), kT_d[q].rearrange("j p (h i s) -> p j h i s", h=4, s=128)[:, :, h, m, :], writes=[kt_.r()])
                    kb.group("pe", [lambda e: e.matmul(Sb[:], qt[:, h, :], kt_[:], start=True, stop=False),
                                    lambda e: e.matmul(Sb[:], identb[:], maskb[:, ks], start=False, stop=True)],
                             reads=[qt.r(), kt_.r(), identb.r(), maskb.r()], writes=[Sb.r()])
                    A(lambda e: e.activation(out=P_[:], in_=Sb[:], func=AF.Exp, scale=SCALE, bias=-C0, accum_out=rs[:, h, m:m + 1]), [Sb.r()], [P_.r(), rs.r(("c", h, m))])

                def S2():
                    kb.group("pe", [(lambda e, x=x: e.transpose(pTv[:, x * 128:(x + 1) * 128], P_[:, x * 128:(x + 1) * 128], identb[:])) for x in range(4)],
                             reads=[P_.r(), identb.r()], writes=[B[3].r()])
                    if c3 % 2 == 0:
                        A(lambda e: e.copy(PT_[:].rearrange("p a b -> p (a b)"), pTv[:, 0:512]), [B[3].r()], [PT_.r()])
                    else:
                        D(lambda e: e.tensor_copy(PT_[:].rearrange("p a b -> p (a b)"), pTv[:, 0:512]), [B[3].r()], [PT_.r()])

                def S3():
                    kb.group("pe", [(lambda e, x=x: e.matmul(Ob[:, hc], PT_[:, x, :], v[:, m * 4 + x, h * 128:(h + 1) * 128], start=(first and x == 0), stop=(m == i and h == 3 and x == 3))) for x in range(4)],
                             reads=[PT_.r()] + VV, writes=[Ob.r(("h", h))] + ([Ob.r(("h", hh)) for hh in range(4)] if first else []))
                return (S1, S2, S3)

            its3 = []
            for m in range(i + 1):
                for h in range(4):
                    its3.append(make_it3(m, h, cstate[0]))
                    cstate[0] += 1
            n = len(its3)
            for step in range(n + 2):
                for stg in range(3):
                    k = step - stg
                    if 0 <= k < n:
                        its3[k][stg]()
                yield
            D(lambda e: e.tensor_reduce(out=rsum[:], in_=rs[:, :, 0:i + 1], axis=AX.X, op=ALU.add), [rs.r(("c", hh, mm)) for hh in range(4) for mm in range(i + 1)], [rsum.r()])
            D(lambda e: e.reciprocal(rinv[:], rsum[:]), [rsum.r()], [rinv.r()])
            D(lambda e: e.tensor_tensor(out=osb[:].rearrange("p (h d) -> p h d", h=4), in0=Ob[:].rearrange("p (h d) -> p h d", h=4), in1=rinv[:].unsqueeze(2).to_broadcast([128, 4, 128]), op=ALU.mult),
              [Ob.r(("h", hh)) for hh in range(4)] + [rinv.r()], [osb.r()])
            kb.dma("sp", o_dsa[:, i, :], osb[:], reads=[osb.r()], is_output=True)
            yield

        def run_interleaved(gens):
            lists = []
            for g in gens:
                lists.append(g)
            active = [[g, w, 0.0] for g, w in lists]
            total = max(w for _, w in lists)
            for stepi in range(total):
                for a in active:
                    g, w, acc = a
                    a[2] += w / total
                    while a[2] >= 1.0:
                        a[2] -= 1.0
                        try:
                            next(g)
                        except StopIteration:
                            a[2] = -1e9
            for a in active:
                for _ in a[0]:
                    pass

        def est_I(i):
            return 1 + (i + 1) * 5

        def est_II(i):
            return 3 + NIT * 3 + 1

        def est_III(i):
            return 4 * (i + 1) + 3

        QL = list(QT)
        for _ in phaseI(QL[0], 0):
            pass
        nq = len(QL)
        for t in range(nq + 1):
            gens = []
            if t < nq:
                gens.append((phaseII(QL[t], t % 2), est_II(QL[t])))
            if t >= 1:
                gens.append((phaseIII(QL[t - 1], (t - 1) % 2), est_III(QL[t - 1])))
            if t + 1 < nq:
                gens.append((phaseI(QL[t + 1], (t + 1) % 2), est_I(QL[t + 1])))
            run_interleaved(gens)
        kb.finish()
        print("DSA instructions:", kb.nins, {k: len(v) for k, v in kb.q.items()})
    return nc


def dsa_masks(j):
    q = np.arange(128)[:, None]
    col = np.arange(512)[None, :]
    jj = col // 128; s = col % 128
    vis = (jj < j) | ((jj == j) & (s <= q))
    mneg = np.where(vis, 0.0, -1e30).astype(np.float32)
    mpos = np.where(vis, 0.0, 1e30).astype(np.float32)
    return mneg, mpos


def gather_global(per_core, axis_tok_tiles):
    st = np.stack(per_core, axis=axis_tok_tiles + 1)
    sh = list(st.shape)
    sh[axis_tok_tiles:axis_tok_tiles + 2] = [64]
    return st.reshape(sh)


def build_gla(NB=16):
    nc = _get_nc()
    kv_d = dram_in(nc, "kv", [2, 4, 64, 8192], BF16)
    dec_d = dram_in(nc, "dec", [1, 4, 128, 64], F32)
    qd_d = dram_in(nc, "qdT", [128, 2, 2048], BF16)
    oin_d = dram_in(nc, "oin", [128, 16, 512], F32)
    grs_d = dram_in(nc, "grs", [128, 16, 512], F32)
    gsel_d = dram_in(nc, "gsel", [128, 2, 8], F32)
    gn_d = dram_in(nc, "gnorm", [128], F32)
    o_gla = dram_out(nc, "o_gla", [128, 16, 512], F32)
    with _phase(nc) as kb:
        S_ = lambda n, s, d: sb(kb, n, s, d)
        D = lambda fn, r, w: kb.op("dve", fn, reads=r, writes=w)
        A = lambda fn, r, w: kb.op("act", fn, reads=r, writes=w)
        dec = S_("dec", [128, 2, 128], F32); gsel = S_("gsel", [128, 2, 8], F32); gnb = S_("gnb", [128, 128], F32)
        for jj in range(4):
            kb.dma("sp", dec[:].rearrange("p a (i j c) -> p a i j c", j=4, c=2)[:, :, :, jj, :], dec_d[0, jj].rearrange("p (a i c) -> p a i c", a=2, c=2), writes=[dec.r()])
        kb.dma("sp", gsel[:], gsel_d, writes=[gsel.r()])
        kb.dma("sp", gnb[:], gn_d.partition_broadcast(128), writes=[gnb.r()])
        St = S_("St", [128, 2, 128], F32); Ssels = [S_(f"Ssel{i}", [128, 2, 256], F32) for i in range(2)]; Sselb = S_("Sselb", [128, 2, 2, 128], BF16)
        kvb = [S_(f"kvb{i}", [128, 4, 2, 2, 128], BF16) for i in range(2)]
        qd = S_("qd", [128, 2, 128], BF16); oin = S_("oin", [128, 512], F32); grs = S_("grs", [128, 512], F32)
        og = S_("og", [128, 512], F32); sq = S_("sq", [128, 512], F32); ssq = S_("ssq", [128, 4], F32); ssq2 = S_("ssq2", [128, 4], F32)
        BA = ps(kb, "BA", [128, 512], F32); BB = ps(kb, "BB", [128, 512], F32)
        D(lambda e: e.memset(St[:], 0.0), [], [St.r(("p", 0)), St.r(("p", 1))])
        Sflat = St[:].rearrange("p a b -> p (a b)")

        def scan(i):
            Ssel = Ssels[i % 2]
            D(lambda e: e.memset(Ssel[:], 0.0), [], [Ssel.r(("c", 0)), Ssel.r(("c", 1))])
            kvb_ = kvb[i % 2]
            for jj in range(4):
                for q in range(2):
                    kb.dma("sp", kvb_[q * 64:(q + 1) * 64, jj], kv_d[q, jj].rearrange("p (a n d) -> p a n d", a=2, d=128)[:, :, 2 * i:2 * i + 2, :], writes=[kvb_.r()])
            for k in range(8):
                n = 8 * i + k
                for c in range(2):
                    D(lambda e, c=c, k=k: e.scalar_tensor_tensor(out=Ssel[:, c, :], in0=Sflat, scalar=gsel[:, c, k:k + 1], in1=Ssel[:, c, :], op0=ALU.mult, op1=ALU.add),
                      [St.r(("p", 0)), St.r(("p", 1)), gsel.r(), Ssel.r(("c", c))], [Ssel.r(("c", c))])
                for p in range(2):
                    D(lambda e, p=p, n=n, k=k: e.scalar_tensor_tensor(out=St[:, p, :], in0=St[:, p, :], scalar=dec[:, p, n:n + 1], in1=kvb_[:, k // 2, p, k % 2, :], op0=ALU.mult, op1=ALU.add),
                      [St.r(("p", p)), dec.r(), kvb_.r()], [St.r(("p", p))])
                yield

        def epilogue(i):
            Ssel = Ssels[i % 2]
            A(lambda e: e.copy(Sselb[:].rearrange("p c a b -> p (c a b)"), Ssel[:].rearrange("p c x -> p (c x)")), [Ssel.r(("c", 0)), Ssel.r(("c", 1))], [Sselb.r()])
            kb.dma("sp", qd[:], qd_d[:, :, i * 128:(i + 1) * 128], writes=[qd.r()])
            kb.dma("sp", oin[:], oin_d[:, i, :], writes=[oin.r()])
            kb.dma("sp", grs[:], grs_d[:, i, :], writes=[grs.r()])
            yield
            fns = []
            for c in range(2):
                for p in range(2):
                    for half in range(2):
                        bank = BA if half == 0 else BB
                        fns.append(lambda e, c=c, p=p, half=half, bank=bank: e.matmul(bank[:, (c * 2 + p) * 128:(c * 2 + p + 1) * 128], qd[half * 64:(half + 1) * 64, p, :], Sselb[half * 64:(half + 1) * 64, c, p, :], start=True, stop=True))
            kb.group("pe", fns, reads=[qd.r(), Sselb.r()], writes=[BA.r(), BB.r()])
            yield
            for h in range(4):
                p, half = divmod(h, 2)
                bank = BA if half == 0 else BB
                for c in range(2):
                    rows = slice(c * 64, (c + 1) * 64)
                    D(lambda e, h=h, c=c, p=p, bank=bank, rows=rows: e.tensor_tensor(out=og[rows, h * 128:(h + 1) * 128], in0=bank[rows, (c * 2 + p) * 128:(c * 2 + p + 1) * 128], in1=oin[rows, h * 128:(h + 1) * 128], op=ALU.add),
                      [bank.r(), oin.r()], [og.r(("h", h))])
                yield
            OG = [og.r(("h", h)) for h in range(4)]
            D(lambda e: e.tensor_tensor(out=sq[:], in0=og[:], in1=og[:], op=ALU.mult), OG, [sq.r()])
            yield
            D(lambda e: e.tensor_reduce(out=ssq[:], in_=sq[:].rearrange("p (h d) -> p h d", h=4), axis=AX.X, op=ALU.add), [sq.r()], [ssq.r()])
            yield
            kb.op("dve", lambda e: e.tensor_scalar(out=ssq2[:], in0=ssq[:], scalar1=1.0 / 128, scalar2=EPS, op0=ALU.mult, op1=ALU.add), reads=[ssq.r()], writes=[ssq2.r()])
            yield
            kb.op("act", lambda e: e.activation(out=ssq2[:], in_=ssq2[:], func=AF.Sqrt), reads=[ssq2.r()], writes=[ssq2.r()])
            yield
            kb.op("dve", lambda e: e.reciprocal(ssq[:], ssq2[:]), reads=[ssq2.r()], writes=[ssq.r()])
            yield
            og3 = og[:].rearrange("p (h d) -> p h d", h=4)
            D(lambda e: e.tensor_tensor(out=og3, in0=og3, in1=ssq[:].unsqueeze(2).to_broadcast([128, 4, 128]), op=ALU.mult), OG + [ssq.r()], OG)
            yield
            D(lambda e: e.tensor_tensor(out=og3, in0=og3, in1=gnb[:].unsqueeze(1).to_broadcast([128, 4, 128]), op=ALU.mult), OG + [gnb.r()], OG)
            yield
            D(lambda e: e.tensor_tensor(out=og[:], in0=og[:], in1=grs[:], op=ALU.mult), OG + [grs.r()], OG)
            kb.dma("sp", o_gla[:, i, :], og[:], reads=OG, is_output=True)
            yield

        for _ in scan(0):
            pass
        for i in range(NB):
            ep = epilogue(i)
            if i + 1 < NB:
                for _ in scan(i + 1):
                    for _n in range(2):
                        try:
                            next(ep)
                        except StopIteration:
                            break
            for _ in ep:
                pass
        kb.finish()
        print("GLA instructions:", kb.nins, {k: len(v) for k, v in kb.q.items()})
    return nc


def gla_sel(j):
    g = np.zeros((128, 2, 8), np.float32)
    for c in range(2):
        g[:, c, 2 * j + c] = 1.0
    return g


NT = 16
POOL_W = (2, 4, 8, 16)


def build_l1b():
    nc = _get_nc()
    xs = dram_in(nc, "xs", [NT * 128, 1024], F32)
    cT_d = dram_in(nc, "cT", [128, 8], F32)
    ada_w = dram_in(nc, "ada_w", [1024, 6144], F32)
    ada_b = dram_in(nc, "ada_b", [1, 6144], F32)
    mixn_d = dram_in(nc, "mixn", [128, 8], F32)
    w_in = dram_in(nc, "w_in", [1024, 2048], F32)
    qn_d = dram_in(nc, "q_norm", [128], F32)
    kn_d = dram_in(nc, "k_norm", [128], F32)
    identb_d = dram_in(nc, "identb", [128, 128], BF16)
    o_mod = dram_out(nc, "o_mod", [1, 6144], F32)
    o_qT = dram_out(nc, "o_qT", [128, 4, NT * 128], BF16)
    o_kT = dram_out(nc, "o_kT", [128, 4, NT * 128], BF16)
    o_v = dram_out(nc, "o_v", [128, NT, 512], BF16)
    o_u = dram_out(nc, "o_u", [128, NT, 512], F32)
    o_uh = dram_out(nc, "o_uh", [256, 512], F32)
    with _phase(nc) as kb:
        S = lambda n, s, d: sb(kb, n, s, d)
        D = lambda fn, r, w: kb.op("dve", fn, reads=r, writes=w)
        A = lambda fn, r, w: kb.op("act", fn, reads=r, writes=w)
        identb = S("identb", [128, 128], BF16); mixn = S("mixn", [128, 8], F32)
        qnb = S("qnb", [128, 128], F32); knb = S("knb", [128, 128], F32)
        for t, d in ((identb, identb_d), (mixn, mixn_d)):
            kb.dma("sp", t[:], d, writes=[t.r()])
        for t, d in ((qnb, qn_d), (knb, kn_d)):
            kb.dma("sp", t[:], d.partition_broadcast(128), writes=[t.r()])
        wb = S("wb", [128, 8, 2048], BF16)
        for k in range(8):
            kb.dma("pool", wb[:, k, :], w_in[k * 128:(k + 1) * 128, :], writes=[wb.r(("k", k))])
        WB = [wb.r(("k", k)) for k in range(8)]
        modrow = S("modrow", [1, 6144], F32); modP = S("modP", [128, 48], F32)
        pT = ps(kb, "pT", [128, 8, 128], BF16)
        pp = [ps(kb, f"pp{i}", [128, 512], F32) for i in range(2)]
        pA = ps(kb, "pA", [128, 512], F32); pB = ps(kb, "pB", [128, 512], F32)
        emit_mod(kb, cT_d, ada_w, ada_b, modrow, modP, "m1", pA, pB)
        kb.dma("sp", o_mod, modrow[:], reads=[modrow.r()], is_output=True)
        a1 = S("a1", [128, 8], F32)
        D(lambda e: e.scalar_tensor_tensor(out=a1[:], in0=modP[:, 8:16], scalar=1.0, in1=mixn[:], op0=ALU.add, op1=ALU.mult), [modP.r(), mixn.r()], [a1.r()])
        xt = [S(f"xt{i}", [128, 1024], F32) for i in range(2)]
        junk = S("junk", [128, 1024], BF16); xn = S("xn", [128, 1024], BF16)
        st = S("st", [128, 1], F32); st2 = S("st2", [128, 1], F32)
        hT = [S(f"hT{i}", [128, 8, 128], BF16) for i in range(2)]
        proj = S("proj", [128, 2048], F32)
        sq = S("sq", [128, 512], F32); ssq = S("ssq", [128, 4], F32); ssq2 = S("ssq2", [128, 4], F32)
        qn = S("qn", [128, 4, 128], F32); qr = S("qr", [128, 4, 128], BF16)
        sqb = S("sqb", [128, 512], F32); ssqb = S("ssqb", [128, 4], F32); ssq2b = S("ssq2b", [128, 4], F32)
        qnb2 = S("qnb2", [128, 4, 128], F32); qrb = S("qrb", [128, 4, 128], BF16)
        TMP = [(sq, ssq, ssq2, qn, qr), (sqb, ssqb, ssq2b, qnb2, qrb)]
        qT_t = S("qT_t", [128, 4, 128], BF16); kT_t = S("kT_t", [128, 4, 128], BF16)
        vb = S("vb", [128, 512], BF16)
        for i in range(NT):
            x_ = xt[i % 2]; h_ = hT[i % 2]
            kb.dma("sp", x_[:], xs[i * 128:(i + 1) * 128, :], writes=[x_.r()])
            emit_norm_hT(kb, x_, h_, a1, modP_sh(modP), identb, junk, xn, pT, st, st2)
            for cb in range(4):
                p_ = pp[cb % 2]
                kb.group("pe", [(lambda e, k=k, cb=cb, p_=p_, h_=h_: e.matmul(p_[:], h_[:, k, :], wb[:, k, cb * 512:(cb + 1) * 512], start=(k == 0), stop=(k == 7))) for k in range(8)],
                         reads=[h_.r()] + WB, writes=[p_.r()])
                if cb % 2 == 0:
                    A(lambda e, cb=cb, p_=p_: e.copy(proj[:, cb * 512:(cb + 1) * 512], p_[:]), [p_.r()], [proj.r(("c", cb))])
                else:
                    D(lambda e, cb=cb, p_=p_: e.tensor_copy(proj[:, cb * 512:(cb + 1) * 512], p_[:]), [p_.r()], [proj.r(("c", cb))])
            PR = [proj.r(("c", cb)) for cb in range(4)]
            CH = [[], [], []]
            for ci, (c0, gain, dstT, dstD) in enumerate(((512, qnb, qT_t, o_qT), (1024, knb, kT_t, o_kT))):
                kb.rec = CH[ci]
                sq_, ssq_, ssq2_, qn_, qr_ = TMP[ci]
                src = proj[:, c0:c0 + 512]
                D(lambda e, src=src, sq_=sq_: e.tensor_tensor(out=sq_[:], in0=src, in1=src, op=ALU.mult), PR, [sq_.r()])
                D(lambda e, sq_=sq_, ssq_=ssq_: e.tensor_reduce(out=ssq_[:], in_=sq_[:].rearrange("p (h d) -> p h d", h=4), axis=AX.X, op=ALU.add), [sq_.r()], [ssq_.r()])
                emit_rstd(kb, ssq_, 128, ssq2_)
                s3 = src.rearrange("p (h d) -> p h d", h=4)
                D(lambda e, s3=s3, qn_=qn_, ssq_=ssq_: e.tensor_tensor(out=qn_[:], in0=s3, in1=ssq_[:].unsqueeze(2).to_broadcast([128, 4, 128]), op=ALU.mult), PR + [ssq_.r()], [qn_.r()])
                D(lambda e, gain=gain, qn_=qn_, qr_=qr_: e.tensor_tensor(out=qr_[:], in0=qn_[:], in1=gain[:].unsqueeze(1).to_broadcast([128, 4, 128]), op=ALU.mult), [qn_.r(), gain.r()], [qr_.r()])
                kb.group("pe", [(lambda e, h=h, qr_=qr_, ci=ci: e.transpose(pT[:, 4 * ci + h, :], qr_[:, h, :], identb[:])) for h in range(4)], reads=[qr_.r(), identb.r()], writes=[pT.r()])
                A(lambda e, dstT=dstT, ci=ci: e.copy(dstT[:], pT[:, 4 * ci:4 * ci + 4, :]), [pT.r()], [dstT.r()])
                kb.dma("sp", dstD[:, :, i * 128:(i + 1) * 128], dstT[:], reads=[dstT.r()], is_output=True)
            kb.rec = CH[2]
            D(lambda e: e.tensor_copy(vb[:], proj[:, 1536:2048]), PR, [vb.r()])
            kb.dma("sp", o_v[:, i, :], vb[:], reads=[vb.r()], is_output=True)
            kb.dma("sp", o_u[:, i, :], proj[:, 0:512], reads=PR, is_output=True)
            kb.dma("sp", o_uh[i * 16:(i + 1) * 16, :], proj[112:128, 0:512], reads=PR, is_output=True)
            kb.emit_roundrobin(CH)
        kb.finish()
        print("L1b instructions:", kb.nins, {k: len(v) for k, v in kb.q.items()})
    return nc


def build_pool():
    nc = _get_nc()
    u_d = dram_in(nc, "u", [128, NT, 512], F32)
    guh_d = dram_in(nc, "guh", [1, 4, 256, 512], F32)
    hsel_d = dram_in(nc, "hsel", [128, 4], F32)
    band_d = dram_in(nc, "band", [128, 4, 128], F32)
    band0_d = dram_in(nc, "band0", [128, 4, 128], F32)
    bandhc_d = dram_in(nc, "bandhc", [128, 2, NT, 4, 128], F32)
    pw_d = dram_in(nc, "pool_w", [4, 128, 128], F32)
    psc_d = dram_in(nc, "pool_scale", [512], F32)
    o_pool = dram_out(nc, "o_pool", [128, NT, 512], F32)
    with _phase(nc) as kb:
        S = lambda n, s, d: sb(kb, n, s, d)
        D = lambda fn, r, w: kb.op("dve", fn, reads=r, writes=w)
        A = lambda fn, r, w: kb.op("act", fn, reads=r, writes=w)
        hsel = S("hsel", [128, 4], F32); band = S("band", [128, 4, 128], F32); band0 = S("band0", [128, 4, 128], F32)
        pscb = S("pscb", [128, 512], F32); pw = S("pw", [128, 4, 128], BF16)
        for t, d in ((hsel, hsel_d), (band, band_d), (band0, band0_d)):
            kb.dma("sp", t[:], d, writes=[t.r()])
        kb.dma("sp", pscb[:], psc_d.partition_broadcast(128), writes=[pscb.r()])
        kb.dma("pool", pw[:], pw_d.rearrange("g c d -> c g d"), writes=[pw.r()])
        uhc = S("uhc", [128, 4, 2, 512], F32); uhs = S("uhs", [128, 2, 512], F32)
        for jj in range(4):
            for hh in range(2):
                kb.dma("sp", uhc[:, jj, hh, :], guh_d[0, jj, hh * 128:(hh + 1) * 128, :], writes=[uhc.r()])
        uc = uhc[:].rearrange("p j h c -> p j (h c)"); us = uhs[:].rearrange("p h c -> p (h c)")
        D(lambda e: e.tensor_scalar(out=us, in0=uc[:, 0, :], scalar1=hsel[:, 0:1], scalar2=None, op0=ALU.mult), [uhc.r(), hsel.r()], [uhs.r()])
        for jj in range(1, 4):
            D(lambda e, jj=jj: e.scalar_tensor_tensor(out=us, in0=uc[:, jj, :], scalar=hsel[:, jj:jj + 1], in1=us, op0=ALU.mult, op1=ALU.add), [uhc.r(), hsel.r(), uhs.r()], [uhs.r()])
        pA = ps(kb, "pA", [128, 512], F32); pB = ps(kb, "pB", [128, 512], F32)
        ut = [S(f"ut{i}", [128, 512], F32) for i in range(2)]
        bh = [S(f"bh{i}", [128, 2, 4, 128], F32) for i in range(2)]
        plT = S("plT", [128, 4, 128], BF16); opl = S("opl", [128, 512], F32)
        for i in range(NT):
            u_ = ut[i % 2]; bh_ = bh[i % 2]
            kb.dma("sp", u_[:], u_d[:, i, :], writes=[u_.r()])
            kb.dma("sp", bh_[:], bandhc_d[:, :, i, :, :], writes=[bh_.r()])
            fns = []
            for g in range(4):
                bo = band0[:, g, :] if i == 0 else band[:, g, :]
                fns.append(lambda e, g=g, bo=bo, u_=u_: e.matmul(pA[:, g * 128:(g + 1) * 128], u_[:, g * 128:(g + 1) * 128], bo, start=True, stop=False))
                fns.append(lambda e, g=g, bh_=bh_: e.matmul(pA[:, g * 128:(g + 1) * 128], uhs[:, 0, g * 128:(g + 1) * 128], bh_[:, 0, g, :], start=False, stop=False))
                fns.append(lambda e, g=g, bh_=bh_: e.matmul(pA[:, g * 128:(g + 1) * 128], uhs[:, 1, g * 128:(g + 1) * 128], bh_[:, 1, g, :], start=False, stop=True))
            kb.group("pe", fns, reads=[u_.r(), bh_.r(), uhs.r(), band.r(), band0.r()], writes=[pA.r()])
            A(lambda e: e.copy(plT[:].rearrange("p g t -> p (g t)"), pA[:]), [pA.r()], [plT.r()])
            kb.group("pe", [(lambda e, g=g: e.matmul(pB[:, g * 128:(g + 1) * 128], plT[:, g, :], pw[:, g, :], start=True, stop=True)) for g in range(4)], reads=[plT.r(), pw.r()], writes=[pB.r()])
            D(lambda e: e.tensor_tensor(out=opl[:], in0=pB[:], in1=pscb[:], op=ALU.mult), [pB.r(), pscb.r()], [opl.r()])
            kb.dma("sp", o_pool[:, i, :], opl[:], reads=[opl.r()], is_output=True)
        kb.finish()
    return nc


def pool_consts_core(j):
    s_ = np.arange(128)[:, None]; t_ = np.arange(128)[None, :]
    band = np.zeros((128, 4, 128), np.float32); band_first = np.zeros((128, 4, 128), np.float32)
    bandhc = np.zeros((128, 2, 16, 4, 128), np.float32)
    for g, w in enumerate(POOL_W):
        inwin = ((t_ - s_) >= 0) & ((t_ - s_) <= w - 1)
        band[:, g, :] = inwin / float(w) - (s_ == t_)
        cnt = np.minimum(t_ + 1.0, float(w))
        band_first[:, g, :] = inwin / cnt - (s_ == t_)
        for i in range(16):
            isrc = i if j > 0 else i - 1
            if isrc < 0:
                continue
            half, slot = divmod(isrc, 8)
            for r in range(16):
                srel = r - 16
                row = (((np.arange(128) - srel) >= 0) & ((np.arange(128) - srel) <= w - 1)) / float(w)
                bandhc[slot * 16 + r, half, i, g, :] = row
    hsel = np.zeros((128, 4), np.float32)
    hsel[:, (j - 1) % 4] = 1.0
    return band, (band_first if j == 0 else band), bandhc, hsel


def pool_consts():
    s = np.arange(128)[:, None]; t = np.arange(128)[None, :]
    band = np.zeros((128, 4, 128), np.float32); band_first = np.zeros((128, 4, 128), np.float32)
    bandh = np.zeros((128, 8, 4, 128), np.float32)
    for g, w in enumerate(POOL_W):
        inwin = ((t - s) >= 0) & ((t - s) <= w - 1)
        band[:, g, :] = inwin / float(w) - (s == t)
        cnt = np.minimum(t + 1.0, float(w))
        band_first[:, g, :] = inwin / cnt - (s == t)
        for r in range(16):
            srel = r - 16
            row = (((np.arange(128) - srel) >= 0) & ((np.arange(128) - srel) <= w - 1)) / float(w)
            for slot in range(8):
                bandh[slot * 16 + r, slot, g, :] = row
    return band, bandh, band_first


SCALE = float(128 ** -0.5)


def build_sb(QT=tuple(range(16))):
    nc = _get_nc()
    qT_d = dram_in(nc, "qT", [128, 4, 2048], BF16)
    kT_d = dram_in(nc, "kT", [2, 4, 64, 8192], BF16)
    v_d = dram_in(nc, "v", [2, 4, 64, 8192], BF16)
    mask_d = dram_in(nc, "sbmask", [128, 512], F32)
    U_d = dram_in(nc, "U", [128, 128], BF16)
    ones_d = dram_in(nc, "ones", [128, 128], BF16)
    o_sb = dram_out(nc, "o_sb", [128, 16, 512], F32)
    with _phase(nc) as kb:
        S = lambda n, s, d: sb(kb, n, s, d)
        D = lambda fn, r, w: kb.op("dve", fn, reads=r, writes=w)
        A = lambda fn, r, w: kb.op("act", fn, reads=r, writes=w)
        kT = S("kT", [128, 4, 8192], BF16); v = S("v", [128, 64, 512], BF16)
        for h in range(4):
            for jj in range(4):
                for q in range(2):
                    kb.dma("sp", kT[q * 64:(q + 1) * 64, h, :].rearrange("p (i j s) -> p i j s", j=4, s=128)[:, :, jj, :],
                           kT_d[q, jj].rearrange("p (h i s) -> p h i s", h=4, s=128)[:, h, :, :], writes=[kT.r(("h", h))])
        for jj in range(4):
            for q in range(2):
                kb.dma("sp", v[q * 64:(q + 1) * 64].rearrange("p (i j) c -> p i j c", j=4)[:, :, jj, :], v_d[q, jj].rearrange("p (i c) -> p i c", c=512), writes=[v.r(("g", jj))])
        KT = [kT.r(("h", h)) for h in range(4)]; VV = [v.r(("g", g)) for g in range(8)]
        mask = S("mask", [128, 512], F32); U = S("U", [128, 128], BF16); ones = S("ones", [128, 128], BF16)
        for t, d in ((mask, mask_d), (U, U_d), (ones, ones_d)):
            kb.dma("sp", t[:], d, writes=[t.r()])
        B = [ps(kb, f"B{i}", [128, 512], F32) for i in range(8)]
        qt = S("qt", [128, 4, 128], BF16)
        NBUF = 5
        eb = [S(f"eb{i}", [128, 512], F32) for i in range(NBUF)]
        spb = [S(f"spb{i}", [128, 512], F32) for i in range(NBUF)]
        Lbb = [S(f"Lb{i}", [128, 512], BF16) for i in range(NBUF)]
        tb = [S(f"tb{i}", [128, 512], F32) for i in range(NBUF)]
        wbb = [S(f"wb{i}", [128, 512], BF16) for i in range(NBUF)]
        Csb = S("Csb", [128, 4, 128], F32)
        osb = S("osb", [128, 512], F32)
        Zs = (B[0], B[1], B[2]); As = (B[3], B[4], B[5], B[6]); Ob = B[7]
        qts = [qt, S("qt2", [128, 4, 128], BF16)]
        qss = [S("qs0", [128, 4, 128], BF16), S("qs1", [128, 4, 128], BF16)]

        def make_it(i, m, h, ctr, qt_, qs_):
            Z = Zs[ctr % 3]; Aa = As[ctr % 4]
            e_ = eb[ctr % NBUF]; sp_ = spb[ctr % NBUF]; L_ = Lbb[ctr % NBUF]; t_ = tb[ctr % NBUF]; w_ = wbb[ctr % NBUF]
            diag = (m == i)
            first = diag and h == 0
            last = (m == 0 and h == 3)
            hc = slice(h * 128, (h + 1) * 128)

            def S1():
                if first:
                    kb.dma("sp", qt_[:], qT_d[:, :, i * 128:(i + 1) * 128], writes=[qt_.r()])
                    D(lambda e: e.tensor_scalar(out=qs_[:], in0=qt_[:], scalar1=SCALE, scalar2=None, op0=ALU.mult), [qt_.r()], [qs_.r()])
                kb.group("pe", [(lambda e, x=x: e.matmul(Z[:, x * 128:(x + 1) * 128], kT[:, h, m * 512 + x * 128: m * 512 + (x + 1) * 128], qt_[:, h, :], start=True, stop=True)) for x in range(4)],
                         reads=[KT[h], qt_.r()], writes=[Z.r()])
                A(lambda e: e.activation(out=e_[:], in_=Z[:], func=AF.Exp, scale=-SCALE), [Z.r()], [e_.r()])

            def S2():
                A(lambda e: e.activation(out=sp_[:], in_=e_[:], func=AF.Ln, bias=1.0), [e_.r()], [sp_.r()])
                D(lambda e: e.scalar_tensor_tensor(out=L_[:], in0=Z[:], scalar=-SCALE, in1=sp_[:], op0=ALU.mult, op1=ALU.subtract), [Z.r(), sp_.r()], [L_.r()])
                if diag:
                    D(lambda e: e.tensor_tensor(out=L_[:], in0=L_[:], in1=mask[:], op=ALU.mult), [L_.r(), mask.r()], [L_.r()])

            def S3():
                fns = []
                for x in range(4):
                    fns.append(lambda e, x=x: e.matmul(Aa[:, x * 128:(x + 1) * 128], U[:], L_[:, x * 128:(x + 1) * 128], start=True, stop=False))
                    for x2 in range(x + 1, 4):
                        fns.append(lambda e, x=x, x2=x2: e.matmul(Aa[:, x * 128:(x + 1) * 128], ones[:], L_[:, x2 * 128:(x2 + 1) * 128], start=False, stop=False))
                    fns.append(lambda e, x=x: e.matmul(Aa[:, x * 128:(x + 1) * 128], kT[:, h, m * 512 + x * 128: m * 512 + (x + 1) * 128], qs_[:, h, :], start=False, stop=(x == 3)))
                kb.group("pe", fns, reads=[U.r(), ones.r(), L_.r(), KT[h], qs_.r()], writes=[Aa.r()])
                if not diag:
                    D(lambda e: e.tensor_tensor(out=t_[:].rearrange("p (x q) -> p x q", x=4), in0=Aa[:].rearrange("p (x q) -> p x q", x=4), in1=Csb[:, h, :].unsqueeze(1).to_broadcast([128, 4, 128]), op=ALU.add),
                      [Aa.r(), Csb.r(("h", h))], [t_.r()])

            def S4():
                if diag:
                    A(lambda e: e.activation(out=w_[:], in_=Aa[:], func=AF.Exp), [Aa.r()], [w_.r()])
                else:
                    A(lambda e: e.activation(out=w_[:], in_=t_[:], func=AF.Exp), [t_.r()], [w_.r()])
                if diag:
                    D(lambda e: e.tensor_tensor(out=w_[:], in0=w_[:], in1=mask[:], op=ALU.mult), [w_.r(), mask.r()], [w_.r()])

            def S5():
                if m > 0:
                    kb.group("pe", [(lambda e, x=x: e.matmul(Aa[:, 0:128], ones[:], L_[:, x * 128:(x + 1) * 128], start=(x == 0), stop=(x == 3))) for x in range(4)],
                             reads=[ones.r(), L_.r()], writes=[Aa.r()])
                    if diag:
                        D(lambda e: e.tensor_copy(Csb[:, h, :], Aa[:, 0:128]), [Aa.r()], [Csb.r(("h", h))])
                    else:
                        D(lambda e: e.tensor_tensor(out=Csb[:, h, :], in0=Aa[:, 0:128], in1=Csb[:, h, :], op=ALU.add), [Aa.r(), Csb.r(("h", h))], [Csb.r(("h", h))])
                kb.group("pe", [(lambda e, x=x: e.matmul(Ob[:, hc], w_[:, x * 128:(x + 1) * 128], v[:, m * 4 + x, h * 128:(h + 1) * 128], start=(first and x == 0), stop=(last and x == 3))) for x in range(4)],
                         reads=[w_.r()] + VV, writes=[Ob.r(("h", h))] + ([Ob.r(("h", hh)) for hh in range(4)] if first else []))
                if last:
                    A(lambda e: e.copy(osb[:], Ob[:]), [Ob.r(("h", hh)) for hh in range(4)], [osb.r()])
                    kb.dma("sp", o_sb[:, i, :], osb[:], reads=[osb.r()], is_output=True)
            return (S1, S2, S3, S4, S5)

        its = []
        ctr = 0
        for qi, i in enumerate(QT):
            for m in range(i, -1, -1):
                for h in range(4):
                    its.append(make_it(i, m, h, ctr, qts[qi % 2], qss[qi % 2]))
                    ctr += 1
        emit_skewed(its, 5)
        kb.finish()
        print("SB instructions:", kb.nins, {k: len(v) for k, v in kb.q.items()})
    return nc


def sb_mask(j):
    s = np.arange(128)[:, None]
    col = np.arange(512)[None, :]
    jj = col // 128; t = col % 128
    vis = (jj < j) | ((jj == j) & (s < t))
    return vis.astype(np.float32)


def sb_consts():
    jx = np.arange(128)[:, None]; sx = np.arange(128)[None, :]
    U = (jx >= sx).astype(np.float32).astype(ml_dtypes.bfloat16)
    ones = np.ones((128, 128), np.float32).astype(ml_dtypes.bfloat16)
    return U, ones


NT = 16
GT = 2


def build_tail():
    nc = _get_nc()
    xs = dram_in(nc, "xs", [NT * 128, 1024], F32)
    mixa_d = dram_in(nc, "mixa", [128, NT, 512], F32)
    mixb_d = dram_in(nc, "mixb", [128, NT, 512], F32)
    modrow_d = dram_in(nc, "modrow", [1, 6144], F32)
    fnorm_d = dram_in(nc, "fnorm", [128, 8], F32)
    w_out = dram_in(nc, "w_out", [1024, 1024], F32)
    w1 = dram_in(nc, "w1", [1024, 5632], F32)
    w2 = dram_in(nc, "w2", [2816, 1024], F32)
    identb_d = dram_in(nc, "identb", [128, 128], BF16)
    identf_d = dram_in(nc, "identf", [128, 128], F32)
    o_x = dram_out(nc, "o_x", [NT * 128, 1024], F32)
    with _phase(nc) as kb:
        S = lambda n, s, d: sb(kb, n, s, d)
        D = lambda fn, r, w: kb.op("dve", fn, reads=r, writes=w)
        identb = S("identb", [128, 128], BF16); fnorm = S("fnorm", [128, 8], F32)
        mod48 = S("mod48", [48, 128], F32); identf = S("identf", [128, 128], F32)
        kb.dma("sp", identb[:], identb_d, writes=[identb.r()])
        kb.dma("sp", fnorm[:], fnorm_d, writes=[fnorm.r()])
        kb.dma("sp", mod48[:], modrow_d.rearrange("o (c p) -> (o c) p", p=128), writes=[mod48.r()])
        kb.dma("sp", identf[:], identf_d, writes=[identf.r()])
        woutb = S("woutb", [128, 8, 1024], BF16); w1b = S("w1b", [128, 8, 5632], BF16); w2b = S("w2b", [128, 22, 1024], BF16)
        for k in range(8):
            kb.dma("pool", woutb[:, k, :], w_out[k * 128:(k + 1) * 128, :], writes=[woutb.r(("k", k))])
        for k in range(8):
            kb.dma("pool", w1b[:, k, :], w1[k * 128:(k + 1) * 128, :], writes=[w1b.r(("k", k))])
        for f in range(22):
            kb.dma("pool", w2b[:, f, :], w2[f * 128:(f + 1) * 128, :], writes=[w2b.r(("k", f))])
        WO = [woutb.r(("k", k)) for k in range(8)]; W1 = [w1b.r(("k", k)) for k in range(8)]; W1f = [W1] * 22; W1u = [W1] * 22; W2 = [w2b.r(("k", f)) for f in range(22)]
        pT = ps(kb, "pT", [128, 8, 128], BF16)
        pp = [ps(kb, f"pp{i}", [128, 512], F32) for i in range(2)]
        pg = ps(kb, "pg", [128, 512], F32); pu = ps(kb, "pu", [128, 512], F32)
        modP = S("modP", [128, 48], F32); G1b = S("G1b", [128, 1024], F32); G2b = S("G2b", [128, 1024], F32)
        kb.op("pe", lambda e: e.transpose(pg[:, 0:48], mod48[:], identf[0:48, 0:48]), reads=[mod48.r(), identf.r()], writes=[pg.r()])
        D(lambda e: e.tensor_copy(modP[:], pg[:, 0:48]), [pg.r()], [modP.r()])
        kb.dma("sp", G1b[:], modrow_d[0, 2048:3072].partition_broadcast(128), writes=[G1b.r()])
        kb.dma("sp", G2b[:], modrow_d[0, 5120:6144].partition_broadcast(128), writes=[G2b.r()])
        a2 = S("a2", [128, 8], F32)
        D(lambda e: e.scalar_tensor_tensor(out=a2[:], in0=modP[:, 32:40], scalar=1.0, in1=fnorm[:], op0=ALU.add, op1=ALU.mult), [modP.r(), fnorm.r()], [a2.r()])

        class SH:
            def __getitem__(s, idx):
                p, sl = idx
                return modP[p, slice(sl.start + 24, sl.stop + 24)]
            def r(s, key=None):
                return modP.r(key)
        sh2 = SH()
        xt = [S("xt0", [128, 1024], F32)] * 2
        mixt = [S("mixt0", [128, 8, 128], BF16)] * 2
        x1s = [S(f"x1_{i}", [128, GT, 1024], F32) for i in range(2)]
        xn = S("xn", [128, 1024], BF16); junk = xn
        st = S("st", [128, 1], F32); st2 = S("st2", [128, 1], F32)
        hT4s = [S(f"hT4_{i}", [128, 8, GT * 128], BF16) for i in range(2)]
        sg = S("sg", [128, GT * 128], F32); actT = S("actT", [128, 22, GT * 128], BF16)
        yt = S("yt", [128, 1024], F32)
        mst = yt

        def front(g):
            x1 = x1s[g % 2]; hT4 = hT4s[g % 2]
            for t in range(GT):
                i = g * GT + t
                x_ = xt[i % 2]; m_ = mixt[i % 2]
                kb.dma("sp", x_[:], xs[i * 128:(i + 1) * 128, :], writes=[x_.r()])
                kb.dma("sp", mst[:, 0:512], mixa_d[:, i, :], writes=[yt.r(("c", 0))])
                kb.dma("sp", mst[:, 512:1024], mixb_d[:, i, :], writes=[yt.r(("c", 1))])
                D(lambda e: e.tensor_copy(xn[:], mst[:]), [yt.r(("c", 0)), yt.r(("c", 1))], [xn.r()])
                yield
                kb.group("pe", [(lambda e, k=k: e.transpose(pT[:, k, :], xn[:, k * 128:(k + 1) * 128], identb[:])) for k in range(8)], reads=[xn.r(), identb.r()], writes=[pT.r()])
                kb.op("act", lambda e, m_=m_: e.copy(m_[:], pT[:]), reads=[pT.r()], writes=[m_.r()])
                yield
                for cb in range(2):
                    p_ = pp[cb]
                    kb.group("pe", [(lambda e, k=k, cb=cb, p_=p_, m_=m_: e.matmul(p_[:], m_[:, k, :], woutb[:, k, cb * 512:(cb + 1) * 512], start=(k == 0), stop=(k == 7))) for k in range(8)],
                             reads=[m_.r()] + WO, writes=[p_.r()])
                    D(lambda e, cb=cb, p_=p_: e.tensor_tensor(out=mst[:, cb * 512:(cb + 1) * 512], in0=p_[:], in1=G1b[:, cb * 512:(cb + 1) * 512], op=ALU.mult), [p_.r(), G1b.r()], [yt.r(("c", cb))])
                    yield
                    D(lambda e, cb=cb, x_=x_, t=t: e.tensor_tensor(out=x1[:, t, cb * 512:(cb + 1) * 512], in0=mst[:, cb * 512:(cb + 1) * 512], in1=x_[:, cb * 512:(cb + 1) * 512], op=ALU.add),
                      [yt.r(("c", cb)), x_.r()], [x1.r(("t", t, cb))])
                    yield
                kb.op("act", lambda e, t=t: e.activation(out=junk[:], in_=x1[:, t, :], func=AF.Square, accum_out=st[:, 0:1]),
                      reads=[x1.r(("t", t, 0)), x1.r(("t", t, 1))], writes=[junk.r(), st.r()])
                yield
                kb.op("dve", lambda e: e.tensor_scalar(out=st2[:], in0=st[:], scalar1=1.0 / 1024, scalar2=EPS, op0=ALU.mult, op1=ALU.add), reads=[st.r()], writes=[st2.r()])
                yield
                kb.op("act", lambda e: e.activation(out=st2[:], in_=st2[:], func=AF.Sqrt), reads=[st2.r()], writes=[st2.r()])
                yield
                kb.op("dve", lambda e: e.reciprocal(st[:], st2[:]), reads=[st2.r()], writes=[st.r()])
                yield
                kb.op("act", lambda e, t=t: e.activation(out=xn[:], in_=x1[:, t, :], func=AF.Copy, scale=st[:, 0:1]), reads=[x1.r(("t", t, 0)), x1.r(("t", t, 1)), st.r()], writes=[xn.r()])
                yield
                kb.group("pe", [(lambda e, k=k: e.transpose(pT[:, k, :], xn[:, k * 128:(k + 1) * 128], identb[:])) for k in range(8)], reads=[xn.r(), identb.r()], writes=[pT.r()])
                yield
                for k in range(8):
                    kb.op("act", lambda e, k=k, t=t: e.activation(out=hT4[:, k, t * 128:(t + 1) * 128], in_=pT[:, k, :], func=AF.Identity, scale=a2[:, k:k + 1], bias=modP[:, 24 + k:25 + k]),
                          reads=[pT.r(), a2.r(), modP.r()], writes=[hT4.r(("t", t))])
                    if k % 4 == 3:
                        yield

        def drain(gen, n=None):
            cnt = 0
            for _ in gen:
                cnt += 1
                if n is not None and cnt >= n:
                    return

        NG = NT // GT
        gens = [front(g) for g in range(NG)]
        drain(gens[0])
        for g in range(NG):
            x1 = x1s[g % 2]; hT4 = hT4s[g % 2]
            HT = [hT4.r(("t", t)) for t in range(GT)]
            for f in range(22):
                kb.group("pe", [(lambda e, k=k, f=f: e.matmul(pg[:, 0:GT * 128], w1b[:, k, f * 128:(f + 1) * 128], hT4[:, k, :], start=(k == 0), stop=(k == 7))) for k in range(8)],
                         reads=HT + W1f[f], writes=[pg.r()])
                kb.group("pe", [(lambda e, k=k, f=f: e.matmul(pu[:, 0:GT * 128], w1b[:, k, 2816 + f * 128:2816 + (f + 1) * 128], hT4[:, k, :], start=(k == 0), stop=(k == 7))) for k in range(8)],
                         reads=HT + W1u[f], writes=[pu.r()])
                kb.op("act", lambda e: e.activation(out=sg[:], in_=pg[:, 0:GT * 128], func=AF.Silu), reads=[pg.r()], writes=[sg.r()])
                D(lambda e, f=f: e.tensor_tensor(out=actT[:, f, :], in0=sg[:], in1=pu[:, 0:GT * 128], op=ALU.mult), [sg.r(), pu.r()], [actT.r(("f", f))])
                if g + 1 < NG:
                    drain(gens[g + 1], 2)
            if g + 1 < NG:
                drain(gens[g + 1])
            AT = [actT.r(("f", f)) for f in range(22)]
            for t in range(GT):
                i = g * GT + t
                for cb in range(2):
                    p_ = pp[cb]
                    kb.group("pe", [(lambda e, f=f, cb=cb, p_=p_, t=t: e.matmul(p_[:], actT[:, f, t * 128:(t + 1) * 128], w2b[:, f, cb * 512:(cb + 1) * 512], start=(f == 0), stop=(f == 21))) for f in range(22)],
                             reads=AT + W2, writes=[p_.r()])
                    D(lambda e, cb=cb, p_=p_: e.tensor_tensor(out=yt[:, cb * 512:(cb + 1) * 512], in0=p_[:], in1=G2b[:, cb * 512:(cb + 1) * 512], op=ALU.mult), [p_.r(), G2b.r()], [yt.r(("c", cb))])
                    D(lambda e, cb=cb, t=t: e.tensor_tensor(out=yt[:, cb * 512:(cb + 1) * 512], in0=yt[:, cb * 512:(cb + 1) * 512], in1=x1[:, t, cb * 512:(cb + 1) * 512], op=ALU.add),
                      [yt.r(("c", cb)), x1.r(("t", t, cb))], [yt.r(("c", cb))])
                kb.dma("sp", o_x[i * 128:(i + 1) * 128, :], yt[:], reads=[yt.r(("c", 0)), yt.r(("c", 1))], is_output=True)
        kb.finish()
        print("tail instructions:", kb.nins, {k: len(v) for k, v in kb.q.items()})
    return nc


DBG = False


def build_fused():
    FX.active = True
    FX.nc = bass.Bass("TRN2", target_bir_lowering=False)
    FX.ext = {}
    FX.n_phase = 0
    nc = FX.nc
    es = ExitStack()
    es.__enter__()
    kb = KB(nc, es)
    kb.fused = True
    kb.last_phase = False
    FX.kb = kb
    E = fx_ext
    I = lambda name, shape, dt: nc.dram_tensor(name, list(shape), dt).ap()
    RG = [[0, 1, 2, 3], [4, 5, 6, 7]]

    def allgather(pairs, n_wait=None):
        kb.pes = ExitStack()
        skip = []
        for pi, (src, dst) in enumerate(pairs):
            nq = dst.shape[0]
            rp = src.shape[0] // nq
            for q in range(nq):
                si = src[q * rp:(q + 1) * rp, :]
                do = dst[q].rearrange("j p c -> (j p) c")
                tok = kb.coll(lambda e, si=si, do=do: e.collective_compute("AllGather", ALU.bypass, replica_groups=RG, ins=[si], outs=[do]))
                if n_wait is not None and pi >= n_wait:
                    skip.append(tok[1])
        kb.end_phase(skip=tuple(skip))

    common = dict(identb=E("identb", [128, 128], BF16), identf=E("identf", [128, 128], F32), cT=E("cT", [128, 8], F32))
    xs = E("xs", [2048, 1024], F32)
    i1 = dict(o_mod=I("i_mod0", [1, 6144], F32), o_qT=I("i_qT0", [128, 4, 2048], BF16), o_kT=I("i_kT0", [128, 4, 2048], BF16), o_v=I("i_v0", [128, 16, 512], BF16),
              o_iqT=I("i_iqT", [128, 4, 2048], BF16), o_ikT=I("i_ikT", [128, 2048], BF16), o_iw=I("i_iw", [128, 16, 8], F32), o_oin=I("i_oin", [128, 16, 512], F32),
              o_qdT=I("i_qdT", [128, 2, 2048], BF16), o_kv=I("i_kv", [128, 2, 32, 128], BF16), o_dec=I("i_dec", [128, 2, 32], F32), o_gr=I("i_gr", [128, 16, 512], F32))
    FX.remap = dict(common, xs=xs, pos=E("pos", [128, 16], I32), ada_w=E("ada_w0", [1024, 6144], F32), ada_b=E("ada_b0", [1, 6144], F32),
                    mixn=E("mixn0", [128, 8], F32), w_in=E("ab_w_in", [1024, 3672], F32), gate_up=E("gate_up", [16, 256], F32), gate_b=E("gate_b", [256], F32),
                    q_norm=E("dsa_qn", [128], F32), k_norm=E("dsa_kn", [128], F32), invf=E("invf", [64], F32), tri3=E("tri3", [128, 3, 128], F32),
                    csel=E("csel", [128, 2], F32), amask=E("amask", [128, 128], F32), **i1)
    build_l1()
    G_kT0 = I("g_kT0", [2, 4, 64, 8192], BF16); G_v0 = I("g_v0", [2, 4, 64, 8192], BF16); G_ik = I("g_ik", [1, 4, 128, 2048], BF16)
    G_kv = I("g_kv", [2, 4, 64, 8192], BF16); G_dec = I("g_dec", [1, 4, 128, 64], F32)
    allgather([(i1["o_kv"].rearrange("p a n d -> p (a n d)"), G_kv), (i1["o_dec"].rearrange("p a n -> p (a n)"), G_dec),
               (i1["o_kT"].rearrange("p h t -> p (h t)"), G_kT0), (i1["o_v"].rearrange("p i c -> p (i c)"), G_v0), (i1["o_ikT"], G_ik)], n_wait=2)
    i_gla = I("i_gla", [128, 16, 512], F32)
    FX.remap = dict(common, kv=G_kv, dec=G_dec, qdT=i1["o_qdT"], oin=i1["o_oin"], grs=i1["o_gr"], gsel=E("gsel", [128, 2, 8], F32), gnorm=E("gnorm", [128], F32), o_gla=i_gla)
    build_gla()
    i_dsa = I("i_dsa", [128, 16, 512], F32)
    FX.remap = dict(common, qT=i1["o_qT"], iqT=i1["o_iqT"], iw=i1["o_iw"], kT=G_kT0, v=G_v0, ikT=G_ik, mneg=E("mneg", [128, 512], F32), mpos=E("mpos", [128, 512], F32),
                    pow2=E("pow2", [128, NIT + 1], F32), o_dsa=i_dsa)
    build_dsa()
    i_x1 = I("i_x1", [2048, 1024], F32)
    FX.remap = dict(common, xs=xs, mixa=i_gla, mixb=i_dsa, modrow=i1["o_mod"], fnorm=E("fnorm0", [128, 8], F32), w_out=E("ab_w_out", [1024, 1024], F32),
                    w1=E("w1_0", [1024, 5632], F32), w2=E("w2_0", [2816, 1024], F32), o_x=i_x1)
    build_tail()
    i5 = dict(o_mod=I("i_mod1", [1, 6144], F32), o_qT=I("i_qT1", [128, 4, 2048], BF16), o_kT=I("i_kT1", [128, 4, 2048], BF16), o_v=I("i_v1", [128, 16, 512], BF16),
              o_u=I("i_u", [128, 16, 512], F32), o_uh=I("i_uh", [256, 512], F32))
    FX.remap = dict(common, xs=i_x1, ada_w=E("ada_w1", [1024, 6144], F32), ada_b=E("ada_b1", [1, 6144], F32), mixn=E("mixn1", [128, 8], F32),
                    w_in=E("cd_w_in", [1024, 2048], F32), q_norm=E("sb_qn", [128], F32), k_norm=E("sb_kn", [128], F32), **i5)
    build_l1b()
    G_kT1 = I("g_kT1", [2, 4, 64, 8192], BF16); G_v1 = I("g_v1", [2, 4, 64, 8192], BF16); G_uh = I("g_uh", [1, 4, 256, 512], F32)
    allgather([(i5["o_uh"], G_uh), (i5["o_kT"].rearrange("p h t -> p (h t)"), G_kT1), (i5["o_v"].rearrange("p i c -> p (i c)"), G_v1)], n_wait=1)
    i_pool = I("i_pool", [128, 16, 512], F32)
    FX.remap = dict(common, u=i5["o_u"], guh=G_uh, hsel=E("hsel", [128, 4], F32), band=E("band", [128, 4, 128], F32), band0=E("band0", [128, 4, 128], F32),
                    bandhc=E("bandhc", [128, 2, 16, 4, 128], F32), pool_w=E("pool_w", [4, 128, 128], F32), pool_scale=E("pool_scale", [512], F32), o_pool=i_pool)
    build_pool()
    i_sb = I("i_sb", [128, 16, 512], F32)
    FX.remap = dict(common, qT=i5["o_qT"], kT=G_kT1, v=G_v1, sbmask=E("sbmask", [128, 512], F32), U=E("U", [128, 128], BF16), ones=E("ones", [128, 128], BF16), o_sb=i_sb)
    build_sb()
    if DBG:
        kb.pes = ExitStack()
        for nm, ap in (("d_dsa", i_dsa), ("d_gla", i_gla), ("d_pool", i_pool), ("d_sb", i_sb)):
            o = nc.dram_tensor(nm, [128, 16, 512], F32, kind="ExternalOutput").ap()
            kb.dma("sp", o, ap, is_output=True)
        o = nc.dram_tensor("d_x1", [2048, 1024], F32, kind="ExternalOutput").ap()
        kb.dma("sp", o, i_x1, is_output=True)
        o = nc.dram_tensor("d_mod1", [1, 6144], F32, kind="ExternalOutput").ap()
        kb.dma("sp", o, i5["o_mod"], is_output=True)
        kb.end_phase()
    out = nc.dram_tensor("out", [2048, 1024], F32, kind="ExternalOutput").ap()
    FX.remap = dict(common, xs=i_x1, mixa=i_pool, mixb=i_sb, modrow=i5["o_mod"], fnorm=E("fnorm1", [128, 8], F32), w_out=E("cd_w_out", [1024, 1024], F32),
                    w1=E("w1_1", [1024, 5632], F32), w2=E("w2_1", [2816, 1024], F32), o_x=out)
    kb.last_phase = True
    build_tail()
    es.close()
    FX.active = False
    return nc


def fused_maps(inp):
    identb = np.eye(128, dtype=np.float32).astype(ml_dtypes.bfloat16)
    identf = np.eye(128, dtype=np.float32)
    cs1 = l1_consts()
    pow2 = np.tile((2.0 ** -(np.arange(NIT + 1) + 1)).astype(np.float32)[None, :], (128, 1))
    U, ones = sb_consts()
    L = lambda a: np.ascontiguousarray(a.reshape(8, 128).T)
    maps = []
    for c in range(8):
        b, j = divmod(c, 4)
        mneg, mpos = dsa_masks(j)
        band, band0, bandhc, hsel = pool_consts_core(j)
        maps.append(dict(
            identb=identb, identf=identf, cT=L(inp["c"][b]),
            xs=np.ascontiguousarray(inp["x"][b].reshape(64, 128, 1024)[j::4].reshape(2048, 1024)),
            pos=np.ascontiguousarray(inp["positions"][b].reshape(64, 128)[j::4].T.astype(np.int32)),
            ada_w0=inp["ada_w"][0], ada_b0=inp["ada_b"][0][None, :], ada_w1=inp["ada_w"][1], ada_b1=inp["ada_b"][1][None, :],
            mixn0=L(inp["mix_norm"][0]), mixn1=L(inp["mix_norm"][1]), fnorm0=L(inp["ffn_norm"][0]), fnorm1=L(inp["ffn_norm"][1]),
            ab_w_in=inp["ab_w_in"][0], gate_up=inp["gla_gate_up"][0], gate_b=inp["gla_gate_b"][0], dsa_qn=inp["dsa_q_norm"][0], dsa_kn=inp["dsa_k_norm"][0],
            invf=cs1["invf"], tri3=cs1["tri3"], csel=cs1["csel"], amask=cs1["amask"],
            mneg=mneg, mpos=mpos, pow2=pow2, gsel=gla_sel(j), gnorm=inp["gla_out_norm"][0],
            ab_w_out=inp["ab_w_out"][0], w1_0=inp["ffn_w1"][0], w2_0=inp["ffn_w2"][0],
            cd_w_in=inp["cd_w_in"][0], sb_qn=inp["sb_q_norm"][0], sb_kn=inp["sb_k_norm"][0],
            hsel=hsel, band=band, band0=band0, bandhc=bandhc, pool_w=inp["pool_w"][0], pool_scale=inp["pool_scale"][0],
            sbmask=sb_mask(j), U=U, ones=ones,
            cd_w_out=inp["cd_w_out"][0], w1_1=inp["ffn_w1"][1], w2_1=inp["ffn_w2"][1]))
    return maps


def kernel(**inputs):
    inp = {k: np.asarray(v) for k, v in inputs.items()}
    nc = build_fused()
    res = run_bass_kernel_spmd(nc, fused_maps(inp), core_ids=list(range(8)))
    out = np.zeros((2, 64, 128, 1024), np.float32)
    for c in range(8):
        b, j = divmod(c, 4)
        out[b, j::4] = res.results[c]["out"].reshape(16, 128, 1024)
    kernel.last_results = res.results
    return out.reshape(2, 8192, 1024)
```

```python
import ml_dtypes
import numpy as np
from contextlib import ExitStack
import concourse.bass as bass
import concourse.mybir as mybir
from concourse.bass_utils import run_bass_kernel_spmd

F32 = mybir.dt.float32
BF16 = mybir.dt.bfloat16
I32 = mybir.dt.int32
ALU = mybir.AluOpType
AF = mybir.ActivationFunctionType
AX = mybir.AxisListType

EPOCH = 30000
ND = 32


import types


def bind(fn):
    if getattr(fn, "__closure__", None) is None:
        return fn
    cells = []
    for c in fn.__closure__:
        try:
            cells.append(types.CellType(c.cell_contents))
        except ValueError:
            cells.append(c)
    return types.FunctionType(fn.__code__, fn.__globals__, fn.__name__, fn.__defaults__, tuple(cells))


class Reg:
    __slots__ = ("w", "r", "name")

    def __init__(self, name=""):
        self.w = None
        self.r = []
        self.name = name


class KB:
    def __init__(self, nc, es):
        self.nc = nc
        self.es = es
        self.eng = {"pe": nc.tensor, "act": nc.scalar, "dve": nc.vector, "pool": nc.gpsimd, "sp": nc.sync}
        self.sems = {}
        self.cnt = {k: 0 for k in self.eng}
        self.seen = {k: {} for k in self.eng}
        self.NDQ = {"sp": 24, "pool": 8}
        self.ndma = {"sp": 0, "pool": 0}
        self.dsem = {q: [es.enter_context(nc.semaphore(f"d{q}{i}")) for i in range(n)] for q, n in self.NDQ.items()}
        self.nins = 0
        self.out_toks = []
        self.q = {k: [] for k in self.eng}
        self.rec = None
        self.pes = es
        self.pfx = ""
        self.fused = False
        self.last_phase = True
        self.dma_uses = {}

    def _esem(self, st, epoch):
        key = ("E", st, epoch)
        if key not in self.sems:
            self.sems[key] = self.es.enter_context(self.nc.semaphore(f"e_{st}_{epoch}"))
        return key

    def _semh(self, key):
        if key[0] == "D":
            return self.dsem[key[1]][key[2]]
        return self.sems[key]

    def _collect(self, st, reads, writes):
        waits = {}

        def need(tok, kind):
            if tok is None:
                return
            tst, key, val = tok
            if tst == st and key[0] == "E":
                if st == "pe":
                    return
                if st in ("act", "dve") and kind != "raw":
                    return
            if self.seen[st].get(key, 0) >= val:
                return
            if waits.get(key, 0) < val:
                waits[key] = val

        for r in reads:
            need(r.w, "raw")
        for w in writes:
            need(w.w, "waw")
            for t in w.r:
                need(t, "war")
        return waits

    def _dowaits(self, st, waits):
        for key, val in waits.items():
            h = self._semh(key)
            self.q[st].append(lambda eng, h=h, val=val: eng.wait_ge(h, val))
            self.seen[st][key] = val
            self.nins += 1

    def _record(self, tok, reads, writes):
        for r in reads:
            if tok[1][0] == "E":
                r.r = [t for t in r.r if not (t[0] == tok[0] and t[1] == tok[1])]
            r.r.append(tok)
        for w in writes:
            w.w = tok
            w.r = []

    def emit_roundrobin(self, chains):
        self.rec = None
        n = max(len(c) for c in chains)
        for k in range(n):
            for c in chains:
                if k < len(c):
                    c[k]()

    def op(self, st, fn, reads=(), writes=()):
        fn = bind(fn)
        if self.rec is not None:
            reads = tuple(reads); writes = tuple(writes); rec = self.rec
            rec.append(lambda: self._norec(rec, self.op, st, fn, reads, writes))
            return None
        waits = self._collect(st, reads, writes)
        self._dowaits(st, waits)
        self.cnt[st] += 1
        c = self.cnt[st]
        epoch, val = divmod(c - 1, EPOCH)
        key = self._esem(st, epoch)
        h = self.sems[key]
        self.q[st].append(lambda eng, fn=fn, h=h: fn(eng).then_inc(h, 1))
        tok = (st, key, val + 1)
        self._record(tok, reads, writes)
        self.nins += 1
        return tok

    def _norec(self, rec, f, *a, **kw):
        saved = self.rec
        self.rec = None
        try:
            return f(*a, **kw)
        finally:
            self.rec = saved

    def group(self, st, fns, reads=(), writes=()):
        if self.rec is not None:
            fns = [bind(f) for f in fns]; reads = tuple(reads); writes = tuple(writes); rec = self.rec
            rec.append(lambda: self._norec(rec, self.group, st, fns, reads, writes))
            return None
        waits = self._collect(st, reads, writes)
        self._dowaits(st, waits)
        fns = [bind(f) for f in fns]
        for fn in fns[:-1]:
            self.q[st].append(fn)
            self.nins += 1
        self.nins += 1
        self.cnt[st] += 1
        c = self.cnt[st]
        epoch, val = divmod(c - 1, EPOCH)
        key = self._esem(st, epoch)
        h = self.sems[key]
        self.q[st].append(lambda eng, fn=fns[-1], h=h: fn(eng).then_inc(h, 1))
        tok = (st, key, val + 1)
        self._record(tok, reads, writes)
        return tok

    def dma(self, st, out, in_, reads=(), writes=(), is_output=False, **kw):
        if self.rec is not None:
            reads = tuple(reads); writes = tuple(writes); rec = self.rec
            rec.append(lambda: self._norec(rec, self.dma, st, out, in_, reads, writes, is_output, **kw))
            return None
        i = self.ndma[st]
        self.ndma[st] += 1
        nd = self.NDQ[st]
        j = i % nd
        use = i // nd
        key = ("D", st, j)
        waits = self._collect(st, reads, writes)
        if use > 0 and self.seen[st].get(key, 0) < 16 * use:
            waits[key] = max(waits.get(key, 0), 16 * use)
        self._dowaits(st, waits)
        h = self.dsem[st][j]
        self.q[st].append(lambda eng, out=out, in_=in_, kw=kw, h=h: eng.dma_start(out=out, in_=in_, **kw).then_inc(h, 16))
        tok = (st, key, 16 * (use + 1))
        self.dma_uses[key] = 16 * (use + 1)
        self._record(tok, reads, writes)
        self.nins += 1
        if is_output:
            self.out_toks.append(tok)
        return tok

    def coll(self, fn, reads=(), writes=()):
        st = "pool"
        fn = bind(fn)
        idx = len([k for k in self.sems if k[0] == "C"])
        key = ("C", idx)
        self.sems[key] = self.es.enter_context(self.nc.semaphore(f"cc{idx}"))
        waits = self._collect(st, reads, writes)
        self._dowaits(st, waits)
        h = self.sems[key]
        self.q[st].append(lambda eng, fn=fn, h=h: fn(eng).then_inc(h, 1))
        tok = (st, key, 1)
        self.dma_uses[key] = 1
        self._record(tok, reads, writes)
        self.nins += 1
        return tok

    def barrier(self, skip=()):
        targets = {k: v for k, v in self.dma_uses.items() if k not in skip}
        for e, c in self.cnt.items():
            if c > 0:
                epoch, val = divmod(c - 1, EPOCH)
                targets[("E", e, epoch)] = val + 1
        for st in self.eng:
            for key, val in targets.items():
                if self.seen[st].get(key, 0) < val:
                    h = self._semh(key)
                    self.q[st].append(lambda eng, h=h, val=val: eng.wait_ge(h, val))
                    self.seen[st][key] = val

    def end_phase(self, skip=()):
        self.barrier(skip)
        self.replay()
        self.q = {k: [] for k in self.eng}
        self.pes.close()

    def finish(self):
        if self.fused and not self.last_phase:
            self.end_phase()
            return
        st = "sp"
        for tok in self.out_toks:
            _, key, val = tok
            if self.seen[st].get(key, 0) < val:
                h = self._semh(key)
                self.q[st].append(lambda eng, h=h, val=val: eng.wait_ge(h, val))
                self.seen[st][key] = val
        self.replay()

    def replay(self):
        q = self.q
        with self.nc.Block() as block:
            @block.sync
            def _(e):
                for f in q["sp"]:
                    f(e)

            @block.tensor
            def _(e):
                for f in q["pe"]:
                    f(e)

            @block.scalar
            def _(e):
                for f in q["act"]:
                    f(e)

            @block.vector
            def _(e):
                for f in q["dve"]:
                    f(e)

            @block.gpsimd
            def _(e):
                for f in q["pool"]:
                    f(e)


class T:
    def __init__(self, t, name=""):
        self.t = t
        self.reg = Reg(name)
        self.sub = {}

    def __getitem__(self, idx):
        return self.t[idx]

    def r(self, key=None):
        if key is None:
            return self.reg
        if key not in self.sub:
            self.sub[key] = Reg()
        return self.sub[key]


def sb(kb, name, shape, dt):
    return T(kb.pes.enter_context(kb.nc.sbuf_tensor("s_" + kb.pfx + name, list(shape), dt)), name)


def ps(kb, name, shape, dt=F32):
    return T(kb.pes.enter_context(kb.nc.psum_tensor("p_" + kb.pfx + name, list(shape), dt)), name)


STAGE = 99.0
NTL = 16

EPS = 1e-6
NT = 16
TWO_PI = float(2 * np.pi)


class FX:
    active = False
    nc = None
    kb = None
    remap = {}
    ext = {}
    n_phase = 0


def fx_ext(name, shape, dt):
    if name not in FX.ext:
        FX.ext[name] = FX.nc.dram_tensor(name, list(shape), dt, kind="ExternalInput").ap()
    return FX.ext[name]


def _get_nc():
    if FX.active:
        return FX.nc
    return bass.Bass("TRN2", target_bir_lowering=False)


class _phase:
    def __init__(self, nc):
        self.nc = nc

    def __enter__(self):
        if FX.active:
            kb = FX.kb
            kb.pes = ExitStack()
            kb.pfx = f"ph{FX.n_phase}_"
            FX.n_phase += 1
            return kb
        self.es = ExitStack()
        self.es.__enter__()
        return KB(self.nc, self.es)

    def __exit__(self, *a):
        if not FX.active:
            self.es.__exit__(*a)
        return False


def dram_in(nc, name, shape, dt):
    if FX.active:
        ap = FX.remap[name]
        assert list(ap.shape) == list(shape), (name, ap.shape, shape)
        return ap
    return nc.dram_tensor(name, list(shape), dt, kind="ExternalInput").ap()


def dram_out(nc, name, shape, dt):
    if FX.active:
        ap = FX.remap[name]
        assert list(ap.shape) == list(shape), (name, ap.shape, shape)
        return ap
    return nc.dram_tensor(name, list(shape), dt, kind="ExternalOutput").ap()


def emit_mod(kb, cT_d, ada_w_d, ada_b_d, modrow, modP, name, pA, pB):
    nc = kb.nc
    cT = sb(kb, name + "cT", [128, 8], F32)
    cond = sb(kb, name + "cond", [128, 8], F32)
    one = sb(kb, name + "one", [1, 1], F32)
    wblk = [sb(kb, name + f"wblk{i}", [128, 8, 256], F32) for i in range(2)]
    pms = (pA, pB); pmp = pB
    kb.dma("sp", cT[:], cT_d, writes=[cT.r()])
    kb.dma("sp", modrow[:], ada_b_d, writes=[modrow.r()])
    kb.op("dve", lambda e: e.memset(one[:], 1.0), writes=[one.r()])
    kb.op("act", lambda e: e.activation(out=cond[:], in_=cT[:], func=AF.Silu), reads=[cT.r()], writes=[cond.r()])
    wv = ada_w_d.rearrange("(k p) n -> p k n", p=128)
    for cb in range(24):
        w = wblk[cb % 2]; pm = pms[cb % 2]
        kb.dma("sp", w[:], wv[:, :, cb * 256:(cb + 1) * 256], writes=[w.r()])
        kb.group("pe", [(lambda e, k=k, w=w, pm=pm: e.matmul(pm[0:1, 0:256], cond[:, k:k + 1], w[:, k, :], start=(k == 0), stop=(k == 7))) for k in range(8)],
                 reads=[cond.r(), w.r()], writes=[pm.r()])
        kb.op("dve", lambda e, cb=cb, pm=pm: e.tensor_tensor(out=modrow[0:1, cb * 256:(cb + 1) * 256], in0=pm[0:1, 0:256], in1=modrow[0:1, cb * 256:(cb + 1) * 256], op=ALU.add),
              reads=[pm.r(), modrow.r()], writes=[modrow.r()])
    kb.group("pe", [(lambda e, c=c: e.matmul(pmp[:, c:c + 1], modrow[0:1, c * 128:(c + 1) * 128], one[:], start=True, stop=True)) for c in range(48)],
             reads=[modrow.r(), one.r()], writes=[pmp.r()])
    kb.op("dve", lambda e: e.tensor_copy(modP[:], pmp[:, 0:48]), reads=[pmp.r()], writes=[modP.r()])


def emit_rstd(kb, ss, n, tmp, name=""):
    kb.op("dve", lambda e: e.tensor_scalar(out=tmp[:], in0=ss[:], scalar1=1.0 / n, scalar2=EPS, op0=ALU.mult, op1=ALU.add), reads=[ss.r()], writes=[tmp.r()])
    kb.op("act", lambda e: e.activation(out=tmp[:], in_=tmp[:], func=AF.Sqrt), reads=[tmp.r()], writes=[tmp.r()])
    kb.op("dve", lambda e: e.reciprocal(ss[:], tmp[:]), reads=[tmp.r()], writes=[ss.r()])


def emit_norm_hT(kb, xt, hT, a, sh, ident, junk, xn, pT, st, st2):
    kb.op("act", lambda e: e.activation(out=junk[:], in_=xt[:], func=AF.Square, accum_out=st[:, 0:1]), reads=[xt.r()], writes=[junk.r(), st.r()])
    emit_rstd(kb, st, 1024, st2)
    kb.op("act", lambda e: e.activation(out=xn[:], in_=xt[:], func=AF.Copy, scale=st[:, 0:1]), reads=[xt.r(), st.r()], writes=[xn.r()])
    kb.group("pe", [(lambda e, k=k: e.transpose(pT[:, k, :], xn[:, k * 128:(k + 1) * 128], ident[:])) for k in range(8)],
             reads=[xn.r(), ident.r()], writes=[pT.r()])
    for k in range(8):
        kb.op("act", lambda e, k=k: e.activation(out=hT[:, k, :], in_=pT[:, k, :], func=AF.Identity, scale=a[:, k:k + 1], bias=sh[:, k:k + 1]),
              reads=[pT.r(), a.r(), sh.r()], writes=[hT.r()])


def build_l1():
    nc = _get_nc()
    xs = dram_in(nc, "xs", [NT * 128, 1024], F32)
    pos_d = dram_in(nc, "pos", [128, NT], I32)
    cT_d = dram_in(nc, "cT", [128, 8], F32)
    ada_w = dram_in(nc, "ada_w", [1024, 6144], F32)
    ada_b = dram_in(nc, "ada_b", [1, 6144], F32)
    mixn_d = dram_in(nc, "mixn", [128, 8], F32)
    w_in = dram_in(nc, "w_in", [1024, 3672], F32)
    gate_up = dram_in(nc, "gate_up", [16, 256], F32)
    gate_b = dram_in(nc, "gate_b", [256], F32)
    qn_d = dram_in(nc, "q_norm", [128], F32)
    kn_d = dram_in(nc, "k_norm", [128], F32)
    invf_d = dram_in(nc, "invf", [64], F32)
    identb_d = dram_in(nc, "identb", [128, 128], BF16)
    identf_d = dram_in(nc, "identf", [128, 128], F32)
    tri3_d = dram_in(nc, "tri3", [128, 3, 128], F32)
    csel_d = dram_in(nc, "csel", [128, 2], F32)
    amask_d = dram_in(nc, "amask", [128, 128], F32)

    o_mod = dram_out(nc, "o_mod", [1, 6144], F32)
    o_qT = dram_out(nc, "o_qT", [128, 4, NT * 128], BF16)
    o_kT = dram_out(nc, "o_kT", [128, 4, NT * 128], BF16)
    o_v = dram_out(nc, "o_v", [128, NT, 512], BF16)
    o_iqT = dram_out(nc, "o_iqT", [128, 4, NT * 128], BF16)
    o_ikT = dram_out(nc, "o_ikT", [128, NT * 128], BF16)
    o_iw = dram_out(nc, "o_iw", [128, NT, 8], F32)
    o_oin = dram_out(nc, "o_oin", [128, NT, 512], F32)
    o_qdT = dram_out(nc, "o_qdT", [128, 2, NT * 128], BF16)
    o_kv = dram_out(nc, "o_kv", [128, 2, 2 * NT, 128], BF16)
    o_dec = dram_out(nc, "o_dec", [128, 2, 2 * NT], F32)
    o_gr = dram_out(nc, "o_gr", [128, NT, 512], F32)

    with _phase(nc) as kb:
        S = lambda n, s, d: sb(kb, n, s, d)
        identb = S("identb", [128, 128], BF16); identf = S("identf", [128, 128], F32)
        tri3 = S("tri3", [128, 3, 128], F32); csel = S("csel", [128, 2], F32); amask = S("amask", [128, 128], F32)
        invf = S("invf", [128, 64], F32); qnb = S("qnb", [128, 128], F32); knb = S("knb", [128, 128], F32)
        gbb = S("gbb", [128, 256], F32); gup = S("gup", [16, 256], F32); mixn = S("mixn", [128, 8], F32)
        posi = S("posi", [128, NT], I32)
        for t, d in ((identb, identb_d), (identf, identf_d), (tri3, tri3_d), (csel, csel_d), (amask, amask_d), (gup, gate_up), (mixn, mixn_d), (posi, pos_d)):
            kb.dma("sp", t[:], d, writes=[t.r()])
        for t, d in ((invf, invf_d), (qnb, qn_d), (knb, kn_d), (gbb, gate_b)):
            kb.dma("sp", t[:], d.partition_broadcast(128), writes=[t.r()])
        wb = S("wb", [128, 8, 3672], BF16)
        for k in range(8):
            kb.dma("pool", wb[:, k, :], w_in[k * 128:(k + 1) * 128, :], writes=[wb.r(("k", k))])
        wb_regs = [wb.r(("k", k)) for k in range(8)]
        modrow = S("modrow", [1, 6144], F32); modP = S("modP", [128, 48], F32)
        pT = ps(kb, "pT", [128, 8, 128], BF16)
        pp = [ps(kb, f"pp{i}", [128, 512], F32) for i in range(2)]
        pA = ps(kb, "pA", [128, 512], F32)
        pB = ps(kb, "pB", [128, 512], F32)
        pTb = pT
        pTg = ps(kb, "pTg", [128, 8, 128], BF16)
        pKV = ps(kb, "pKV", [128, 4, 256], F32)
        emit_mod(kb, cT_d, ada_w, ada_b, modrow, modP, "m0", pA, pB)
        kb.dma("sp", o_mod, modrow[:], reads=[modrow.r()], is_output=True)
        a1 = S("a1", [128, 8], F32)
        kb.op("dve", lambda e: e.scalar_tensor_tensor(out=a1[:], in0=modP[:, 8:16], scalar=1.0, in1=mixn[:], op0=ALU.add, op1=ALU.mult),
              reads=[modP.r(), mixn.r()], writes=[a1.r()])
        if STAGE < 2:
            kb.finish(); return nc
        posf = S("posf", [128, NT], F32)
        ang = S("ang", [128, NT * 64], F32); ki = S("ki", [128, NT * 64], I32); kf = S("kf", [128, NT * 64], F32)
        rs = ang; rc = S("rc", [128, NT * 64], F32); m1 = kf
        sinT = S("sinT", [128, NT, 64], F32); cosT = S("cosT", [128, NT, 64], F32)
        kb.op("dve", lambda e: e.tensor_copy(posf[:], posi[:]), reads=[posi.r()], writes=[posf.r()])
        for i in range(NT):
            kb.op("dve", lambda e, i=i: e.tensor_scalar(out=ang[:, i * 64:(i + 1) * 64], in0=invf[:], scalar1=posf[:, i:i + 1], scalar2=None, op0=ALU.mult),
                  reads=[invf.r(), posf.r()], writes=[ang.r()])
        C1 = 6.28125; C2 = TWO_PI - C1
        D = lambda fn, r, w: kb.op("dve", fn, reads=r, writes=w)
        D(lambda e: e.tensor_scalar(out=kf[:], in0=ang[:], scalar1=1.0 / TWO_PI, scalar2=None, op0=ALU.mult), [ang.r()], [kf.r()])
        D(lambda e: e.tensor_copy(ki[:], kf[:]), [kf.r()], [ki.r()])
        D(lambda e: e.tensor_copy(kf[:], ki[:]), [ki.r()], [kf.r()])
        D(lambda e: e.scalar_tensor_tensor(out=rs[:], in0=kf[:], scalar=-C1, in1=ang[:], op0=ALU.mult, op1=ALU.add), [kf.r(), ang.r()], [rs.r()])
        D(lambda e: e.scalar_tensor_tensor(out=rs[:], in0=kf[:], scalar=-C2, in1=rs[:], op0=ALU.mult, op1=ALU.add), [kf.r(), rs.r()], [rs.r()])
        PI = float(np.pi)
        D(lambda e: e.tensor_scalar(out=m1[:], in0=rs[:], scalar1=PI, scalar2=-TWO_PI, op0=ALU.is_gt, op1=ALU.mult), [rs.r()], [m1.r()])
        D(lambda e: e.tensor_tensor(out=rs[:], in0=rs[:], in1=m1[:], op=ALU.add), [rs.r(), m1.r()], [rs.r()])
        D(lambda e: e.tensor_scalar(out=m1[:], in0=rs[:], scalar1=-PI, scalar2=TWO_PI, op0=ALU.is_lt, op1=ALU.mult), [rs.r()], [m1.r()])
        D(lambda e: e.tensor_tensor(out=rs[:], in0=rs[:], in1=m1[:], op=ALU.add), [rs.r(), m1.r()], [rs.r()])
        D(lambda e: e.tensor_scalar(out=rc[:], in0=rs[:], scalar1=PI / 2, scalar2=None, op0=ALU.add), [rs.r()], [rc.r()])
        D(lambda e: e.tensor_scalar(out=m1[:], in0=rc[:], scalar1=PI, scalar2=-TWO_PI, op0=ALU.is_gt, op1=ALU.mult), [rc.r()], [m1.r()])
        D(lambda e: e.tensor_tensor(out=rc[:], in0=rc[:], in1=m1[:], op=ALU.add), [rc.r(), m1.r()], [rc.r()])
        for t in (rs, rc):
            D(lambda e, t=t: e.tensor_scalar(out=t[:], in0=t[:], scalar1=PI, scalar2=-PI, op0=ALU.min, op1=ALU.max), [t.r()], [t.r()])
        kb.op("act", lambda e: e.activation(out=sinT[:].rearrange("p a b -> p (a b)"), in_=rs[:], func=AF.Sin), reads=[rs.r()], writes=[sinT.r()])
        kb.op("act", lambda e: e.activation(out=cosT[:].rearrange("p a b -> p (a b)"), in_=rc[:], func=AF.Sin), reads=[rc.r()], writes=[cosT.r()])

        qT_t = S("qT_t", [128, 4, 128], BF16); kT_t = S("kT_t", [128, 4, 128], BF16)
        iqT_t = S("iqT_t", [128, 4, 128], BF16); ikT_t = S("ikT_t", [128, 128], BF16)
        qdT_t = S("qdT_t", [128, 2, 128], BF16)
        iw_sb = S("iw_sb", [128, NT, 8], F32)
        kv_t = S("kv_t", [128, 2, 2, 128], BF16); dec_sb = S("dec_sb", [128, 2, 2 * NT], F32)

        xt = [S(f"xt{i}", [128, 1024], F32) for i in range(2)]
        junk = S("junk", [128, 1024], BF16); xn = S("xn", [128, 1024], BF16)
        st = S("st", [128, 1], F32); st2 = S("st2", [128, 1], F32)
        hT = [S(f"hT{i}", [128, 8, 128], BF16) for i in range(2)]
        proj = S("proj", [128, 3672], F32)
        sq = S("sq", [128, 512], F32); ssq = S("ssq", [128, 4], F32); ssq2 = S("ssq2", [128, 4], F32)
        qn = S("qn", [128, 4, 128], F32); qr = S("qr", [128, 4, 128], BF16)
        tA = S("tA", [128, 4, 64], F32); tB = S("tB", [128, 4, 64], F32)
        iqr = S("iqr", [128, 8, 64], BF16); iA = S("iA", [128, 8, 32], F32); iB = S("iB", [128, 8, 32], F32)
        ik2 = S("ik2", [128, 128], BF16); kA = S("kA", [128, 32], F32); kB_ = S("kB_", [128, 32], F32)
        vb = S("vb", [128, 512], BF16); gvb = S("gvb", [128, 512], BF16)
        glT = S("glT", [16, 128], F32); pre = S("pre", [128, 256], F32); lg = S("lg", [128, 256], F32)
        bmid = S("bmid", [128, 256], F32); blast = S("blast", [128, 256], F32)
        d1 = S("d1", [128, 256], F32); d3 = S("d3", [128, 256], F32)
        E1 = S("E1", [128, 256], F32); E2 = S("E2", [128, 256], F32); E3 = S("E3", [128, 256], F32); E4 = S("E4", [128, 256], F32)
        qkd = S("qkd", [128, 3, 256], BF16)
        kdec = S("kdec", [128, 256], BF16)
        qkT = S("qkT", [128, 6, 128], BF16)
        attT = S("attT", [128, 4, 128], BF16)
        oin = S("oin", [128, 512], F32); grs = S("grs", [128, 512], F32)
        dect = S("dect", [128, 2, 2], F32)

        def rope(src, dst, nh, half, cos, sin, A, B, cols_per_head):
            sv = src
            t1 = sv[:, :, 0:half]; t2 = sv[:, :, half:2 * half]
            cb_ = cos.unsqueeze(1).to_broadcast([128, nh, half]); sb_ = sin.unsqueeze(1).to_broadcast([128, nh, half])
            return t1, t2, cb_, sb_

        for i in range(NTL if STAGE >= 3 else 0):
            x_ = xt[i % 2]; h_ = hT[i % 2]
            kb.dma("sp", x_[:], xs[i * 128:(i + 1) * 128, :], writes=[x_.r()])
            emit_norm_hT(kb, x_, h_, a1, modP_sh(modP), identb, junk, xn, pT, st, st2)
            for cb in range(8):
                c0 = cb * 512; c1 = min(3672, c0 + 512); p_ = pp[cb % 2]
                kb.group("pe", [(lambda e, k=k, c0=c0, c1=c1, p_=p_, h_=h_: e.matmul(p_[:, 0:c1 - c0], h_[:, k, :], wb[:, k, c0:c1], start=(k == 0), stop=(k == 7))) for k in range(8)],
                         reads=[h_.r()] + wb_regs, writes=[p_.r()])
                if cb % 2 == 0:
                    kb.op("act", lambda e, c0=c0, c1=c1, p_=p_: e.copy(proj[:, c0:c1], p_[:, 0:c1 - c0]), reads=[p_.r()], writes=[proj.r(("c", cb))])
                else:
                    kb.op("dve", lambda e, c0=c0, c1=c1, p_=p_: e.tensor_copy(proj[:, c0:c1], p_[:, 0:c1 - c0]), reads=[p_.r()], writes=[proj.r(("c", cb))])
            PR = [proj.r(("c", cb)) for cb in range(8)]
            cos_i = cosT[:, i, :]; sin_i = sinT[:, i, :]
            cos32 = cosT[:, i, 0:64:2]; sin32 = sinT[:, i, 0:64:2]

            C1, C2, C3 = [], [], []
            kb.rec = C1
            for (c0, gain, dstT, dstD) in ((1552, qnb, qT_t, o_qT), (2064, knb, kT_t, o_kT)):
                src = proj[:, c0:c0 + 512]
                D(lambda e, src=src: e.tensor_tensor(out=sq[:], in0=src, in1=src, op=ALU.mult), PR, [sq.r()])
                D(lambda e: e.tensor_reduce(out=ssq[:], in_=sq[:].rearrange("p (h d) -> p h d", h=4), axis=AX.X, op=ALU.add), [sq.r()], [ssq.r()])
                emit_rstd(kb, ssq, 128, ssq2)
                s3 = src.rearrange("p (h d) -> p h d", h=4)
                D(lambda e, s3=s3: e.tensor_tensor(out=qn[:], in0=s3, in1=ssq[:].unsqueeze(2).to_broadcast([128, 4, 128]), op=ALU.mult), PR + [ssq.r()], [qn.r()])
                D(lambda e, gain=gain: e.tensor_tensor(out=qn[:], in0=qn[:], in1=gain[:].unsqueeze(1).to_broadcast([128, 4, 128]), op=ALU.mult), [qn.r(), gain.r()], [qn.r()])
                t1 = qn[:, :, 0:64]; t2 = qn[:, :, 64:128]
                cb_ = cos_i.unsqueeze(1).to_broadcast([128, 4, 64]); sb_ = sin_i.unsqueeze(1).to_broadcast([128, 4, 64])
                D(lambda e: e.tensor_tensor(out=tA[:], in0=t1, in1=cb_, op=ALU.mult), [qn.r(), cosT.r()], [tA.r()])
                D(lambda e: e.tensor_tensor(out=tB[:], in0=t2, in1=sb_, op=ALU.mult), [qn.r(), sinT.r()], [tB.r()])
                D(lambda e: e.tensor_tensor(out=qr[:, :, 0:64], in0=tA[:], in1=tB[:], op=ALU.subtract), [tA.r(), tB.r()], [qr.r()])
                D(lambda e: e.tensor_tensor(out=tA[:], in0=t2, in1=cb_, op=ALU.mult), [qn.r(), cosT.r()], [tA.r()])
                D(lambda e: e.tensor_tensor(out=tB[:], in0=t1, in1=sb_, op=ALU.mult), [qn.r(), sinT.r()], [tB.r()])
                D(lambda e: e.tensor_tensor(out=qr[:, :, 64:128], in0=tA[:], in1=tB[:], op=ALU.add), [tA.r(), tB.r()], [qr.r()])
                kb.group("pe", [(lambda e, h=h: e.transpose(pTb[:, h, :], qr[:, h, :], identb[:])) for h in range(4)], reads=[qr.r(), identb.r()], writes=[pTb.r()])
                kb.op("act", lambda e, dstT=dstT: e.copy(dstT[:], pTb[:, 0:4, :]), reads=[pTb.r()], writes=[dstT.r()])
                kb.dma("sp", dstD[:, :, i * 128:(i + 1) * 128], dstT[:], reads=[dstT.r()], is_output=True)
            kb.rec = C2
            D(lambda e: e.tensor_copy(vb[:], proj[:, 2576:3088]), PR, [vb.r()])
            kb.dma("sp", o_v[:, i, :], vb[:], reads=[vb.r()], is_output=True)
            kb.rec = C1
            iq3 = proj[:, 3088:3600].rearrange("p (h d) -> p h d", h=8)
            t1 = iq3[:, :, 0:32]; t2 = iq3[:, :, 32:64]
            cb_ = cos32.unsqueeze(1).to_broadcast([128, 8, 32]); sb_ = sin32.unsqueeze(1).to_broadcast([128, 8, 32])
            D(lambda e: e.tensor_tensor(out=iA[:], in0=t1, in1=cb_, op=ALU.mult), PR + [cosT.r()], [iA.r()])
            D(lambda e: e.tensor_tensor(out=iB[:], in0=t2, in1=sb_, op=ALU.mult), PR + [sinT.r()], [iB.r()])
            D(lambda e: e.tensor_tensor(out=iqr[:, :, 0:32], in0=iA[:], in1=iB[:], op=ALU.subtract), [iA.r(), iB.r()], [iqr.r()])
            D(lambda e: e.tensor_tensor(out=iA[:], in0=t2, in1=cb_, op=ALU.mult), PR + [cosT.r()], [iA.r()])
            D(lambda e: e.tensor_tensor(out=iB[:], in0=t1, in1=sb_, op=ALU.mult), PR + [sinT.r()], [iB.r()])
            D(lambda e: e.tensor_tensor(out=iqr[:, :, 32:64], in0=iA[:], in1=iB[:], op=ALU.add), [iA.r(), iB.r()], [iqr.r()])
            iq2 = iqr[:].rearrange("p h d -> p (h d)")
            kb.group("pe", [(lambda e, g=g: e.transpose(pTb[:, g, :], iq2[:, g * 128:(g + 1) * 128], identb[:])) for g in range(4)], reads=[iqr.r(), identb.r()], writes=[pTb.r()])
            kb.op("act", lambda e: e.copy(iqT_t[:], pTb[:, 0:4, :]), reads=[pTb.r()], writes=[iqT_t.r()])
            kb.dma("sp", o_iqT[:, :, i * 128:(i + 1) * 128], iqT_t[:], reads=[iqT_t.r()], is_output=True)
            kb.rec = C2
            k1 = proj[:, 3600:3632]; k2 = proj[:, 3632:3664]
            D(lambda e: e.tensor_tensor(out=kA[:], in0=k1, in1=cos32, op=ALU.mult), PR + [cosT.r()], [kA.r()])
            D(lambda e: e.tensor_tensor(out=kB_[:], in0=k2, in1=sin32, op=ALU.mult), PR + [sinT.r()], [kB_.r()])
            D(lambda e: e.tensor_tensor(out=ik2[:, 0:32], in0=kA[:], in1=kB_[:], op=ALU.subtract), [kA.r(), kB_.r()], [ik2.r()])
            D(lambda e: e.tensor_tensor(out=kA[:], in0=k2, in1=cos32, op=ALU.mult), PR + [cosT.r()], [kA.r()])
            D(lambda e: e.tensor_tensor(out=kB_[:], in0=k1, in1=sin32, op=ALU.mult), PR + [sinT.r()], [kB_.r()])
            D(lambda e: e.tensor_tensor(out=ik2[:, 32:64], in0=kA[:], in1=kB_[:], op=ALU.add), [kA.r(), kB_.r()], [ik2.r()])
            D(lambda e: e.tensor_copy(ik2[:, 64:128], ik2[:, 0:64]), [ik2.r()], [ik2.r()])
            kb.op("pe", lambda e: e.transpose(pTb[:, 4, :], ik2[:], identb[:]), reads=[ik2.r(), identb.r()], writes=[pTb.r()])
            kb.op("act", lambda e: e.copy(ikT_t[:], pTb[:, 4, :]), reads=[pTb.r()], writes=[ikT_t.r()])
            kb.dma("sp", o_ikT[:, i * 128:(i + 1) * 128], ikT_t[:], reads=[ikT_t.r()], is_output=True)
            D(lambda e: e.tensor_copy(iw_sb[:, i, :], proj[:, 3664:3672]), PR, [iw_sb.r()])
            kb.rec = C3
            kb.op("pe", lambda e: e.transpose(pA[0:16, 0:128], proj[:, 1024:1040], identf[:]), reads=PR + [identf.r()], writes=[pA.r()])
            kb.op("act", lambda e: e.copy(glT[:], pA[0:16, 0:128]), reads=[pA.r()], writes=[glT.r()])
            kb.op("pe", lambda e: e.matmul(pA[:, 0:256], glT[:], gup[:], start=True, stop=True), reads=[glT.r(), gup.r()], writes=[pA.r()])
            D(lambda e: e.tensor_tensor(out=pre[:], in0=pA[:, 0:256], in1=gbb[:], op=ALU.add), [pA.r(), gbb.r()], [pre.r()])
            kb.op("act", lambda e: e.activation(out=lg[:], in_=pre[:], func=AF.Exp, scale=-1.0), reads=[pre.r()], writes=[lg.r()])
            kb.op("act", lambda e: e.activation(out=lg[:], in_=lg[:], func=AF.Ln, bias=1.0), reads=[lg.r()], writes=[lg.r()])
            kb.group("pe", [
                lambda e: e.matmul(pA[:, 0:256], tri3[:, 0, :], lg[:], start=True, stop=True),
                lambda e: e.matmul(pA[:, 256:512], tri3[:, 1, :], lg[:], start=True, stop=True),
                lambda e: e.matmul(pB[:, 0:256], tri3[:, 2, :], lg[:], start=True, stop=True),
                lambda e: e.matmul(pB[:, 256:258], lg[:, 0:128], csel[:], start=True, stop=True),
                lambda e: e.matmul(pB[:, 258:260], lg[:, 128:256], csel[:], start=True, stop=True),
            ], reads=[tri3.r(), lg.r(), csel.r()], writes=[pA.r(), pB.r()])
            kb.op("act", lambda e: e.copy(bmid[:], pA[:, 256:512]), reads=[pA.r()], writes=[bmid.r()])
            kb.op("act", lambda e: e.copy(blast[:], pB[:, 0:256]), reads=[pB.r()], writes=[blast.r()])
            kb.op("act", lambda e: e.activation(out=dec_sb[:, :, 2 * i:2 * i + 2], in_=pB[:, 256:260].rearrange("p (a c) -> p a c", a=2), func=AF.Exp), reads=[pB.r()], writes=[dec_sb.r()])
            D(lambda e: e.tensor_tensor(out=d1[:], in0=pA[:, 0:256], in1=bmid[:], op=ALU.subtract), [pA.r(), bmid.r()], [d1.r()])
            D(lambda e: e.tensor_tensor(out=d3[:], in0=blast[:], in1=pA[:, 0:256], op=ALU.subtract), [pA.r(), blast.r()], [d3.r()])
            kb.op("act", lambda e: e.activation(out=E1[:], in_=d1[:], func=AF.Exp), reads=[d1.r()], writes=[E1.r()])
            kb.op("act", lambda e: e.activation(out=E2[:], in_=d1[:], func=AF.Exp, scale=-1.0), reads=[d1.r()], writes=[E2.r()])
            kb.op("act", lambda e: e.activation(out=E3[:], in_=d3[:], func=AF.Exp), reads=[d3.r()], writes=[E3.r()])
            kb.op("act", lambda e: e.activation(out=E4[:], in_=pA[:, 0:256], func=AF.Exp), reads=[pA.r()], writes=[E4.r()])
            gq = proj[:, 0:256]; gk = proj[:, 256:512]
            D(lambda e: e.scalar_tensor_tensor(out=qkd[:, 0, :], in0=gq, scalar=0.125, in1=E1[:], op0=ALU.mult, op1=ALU.mult), PR + [E1.r()], [qkd.r()])
            D(lambda e: e.tensor_tensor(out=qkd[:, 1, :], in0=gk, in1=E2[:], op=ALU.mult), PR + [E2.r()], [qkd.r()])
            D(lambda e: e.scalar_tensor_tensor(out=qkd[:, 2, :], in0=gq, scalar=0.125, in1=E4[:], op0=ALU.mult, op1=ALU.mult), PR + [E4.r()], [qkd.r()])
            D(lambda e: e.tensor_tensor(out=kdec[:], in0=gk, in1=E3[:], op=ALU.mult), PR + [E3.r()], [kdec.r()])
            kb.group("pe", [(lambda e, w=w, p=p: e.transpose(pTg[:, w * 2 + p, :], qkd[:, w, p * 128:(p + 1) * 128], identb[:])) for w in range(3) for p in range(2)],
                     reads=[qkd.r(), identb.r()], writes=[pTg.r()])
            kb.op("act", lambda e: e.copy(qkT[:], pTg[:, 0:6, :]), reads=[pTg.r()], writes=[qkT.r()])
            D(lambda e: e.tensor_copy(qdT_t[:], qkT[:, 4:6, :]), [qkT.r()], [qdT_t.r()])
            kb.dma("sp", o_qdT[:, :, i * 128:(i + 1) * 128], qdT_t[:], reads=[qdT_t.r()], is_output=True)
            D(lambda e: e.tensor_copy(gvb[:], proj[:, 512:1024]), PR, [gvb.r()])
            def attmm(e, h):
                p, hb = divmod(h, 2); hb *= 64
                dst = pp[0] if hb == 0 else pp[1]
                return e.matmul(dst[:, p * 128:(p + 1) * 128], qkT[hb:hb + 64, 2 + p, :], qkT[hb:hb + 64, 0 + p, :], start=True, stop=True)
            kb.group("pe", [(lambda e, h=h: attmm(e, h)) for h in (0, 2, 1, 3)], reads=[qkT.r()], writes=[pp[0].r(), pp[1].r()])
            for hh in range(2):
                D(lambda e, hh=hh: e.tensor_tensor(out=attT[:, hh:4:2, :], in0=pp[hh][:, 0:256].rearrange("p (h i) -> p h i", h=2), in1=amask[:].unsqueeze(1).to_broadcast([128, 2, 128]), op=ALU.mult),
                  [pp[hh].r(), amask.r()], [attT.r()])
            kb.group("pe", [(lambda e, h=h: e.matmul(pB[:, h * 128:(h + 1) * 128], attT[:, h, :], gvb[:, h * 128:(h + 1) * 128], start=True, stop=True)) for h in range(4)],
                     reads=[attT.r(), gvb.r()], writes=[pB.r()])
            kb.op("act", lambda e: e.copy(oin[:], pB[:]), reads=[pB.r()], writes=[oin.r()])
            kb.dma("sp", o_oin[:, i, :], oin[:], reads=[oin.r()], is_output=True)
            kb.group("pe", [(lambda e, p=p, c=c: e.matmul(pKV[:, c * 2 + p, :], kdec[c * 64:(c + 1) * 64, p * 128:(p + 1) * 128], gvb[c * 64:(c + 1) * 64, p * 256:(p + 1) * 256], start=True, stop=True))
                            for p in range(2) for c in range(2)], reads=[kdec.r(), gvb.r()], writes=[pKV.r()])
            pk = pKV[:].rearrange("q (c p) n -> q p c n", p=2)
            kb.op("act", lambda e: e.copy(kv_t[0:64], pk[0:64, :, :, 0:128]), reads=[pKV.r()], writes=[kv_t.r()])
            D(lambda e: e.tensor_copy(kv_t[64:128], pk[64:128, :, :, 128:256]), [pKV.r()], [kv_t.r()])
            kb.dma("sp", o_kv[:, :, 2 * i:2 * i + 2, :], kv_t[:], reads=[kv_t.r()], is_output=True)
            kb.rec = C2
            kb.op("act", lambda e: e.activation(out=grs[:], in_=proj[:, 1040:1552], func=AF.Silu), reads=PR, writes=[grs.r()])
            kb.dma("sp", o_gr[:, i, :], grs[:], reads=[grs.r()], is_output=True)
            kb.emit_roundrobin([C3, C1, C2])
        for t, d in ((iw_sb, o_iw), (dec_sb, o_dec)):
            kb.dma("sp", d, t[:], reads=[t.r()], is_output=True)
        kb.finish()
        print("L1 instructions:", kb.nins, {k: len(v) for k, v in kb.q.items()})
    return nc


class _ShView:
    pass


def modP_sh(modP):
    class V:
        def __getitem__(s, idx):
            return modP[idx]
        def r(s, key=None):
            return modP.r(key)
    return V()


def l1_consts():
    half = 64
    invf = (10000.0 ** (-np.arange(half, dtype=np.float32) / half)).astype(np.float32)
    j = np.arange(128)[:, None]; i = np.arange(128)[None, :]
    same = (j // 64) == (i // 64)
    tri = (same & (j <= i)).astype(np.float32)
    mmid = (same & ((j % 64) <= 31)).astype(np.float32)
    mlast = same.astype(np.float32)
    tri3 = np.stack([tri, mmid, mlast], axis=1) * (-1.0 / 16.0)
    csel = np.stack([(np.arange(128) < 64), (np.arange(128) >= 64)], axis=1).astype(np.float32) * (-1.0 / 16.0)
    amask = tri.copy()
    return dict(invf=invf, identb=np.eye(128, dtype=np.float32).astype(ml_dtypes.bfloat16), identf=np.eye(128, dtype=np.float32),
                tri3=np.ascontiguousarray(tri3.astype(np.float32)), csel=csel, amask=amask)


NIT = 10
C0 = 11.3137085
SCALE = float(128 ** -0.5)


def emit_skewed(its, nst):
    n = len(its)
    for step in range(n + nst - 1):
        for stg in range(nst):
            k = step - stg
            if 0 <= k < n:
                its[k][stg]()


def build_dsa(QT=tuple(range(16))):
    nc = _get_nc()
    qT_d = dram_in(nc, "qT", [128, 4, 2048], BF16)
    iqT_d = dram_in(nc, "iqT", [128, 4, 2048], BF16)
    iw_d = dram_in(nc, "iw", [128, 16, 8], F32)
    kT_d = dram_in(nc, "kT", [2, 4, 64, 8192], BF16)
    v_d = dram_in(nc, "v", [2, 4, 64, 8192], BF16)
    ikT_d = dram_in(nc, "ikT", [1, 4, 128, 2048], BF16)
    mneg_d = dram_in(nc, "mneg", [128, 512], F32)
    mpos_d = dram_in(nc, "mpos", [128, 512], F32)
    identb_d = dram_in(nc, "identb", [128, 128], BF16)
    identf_d = dram_in(nc, "identf", [128, 128], F32)
    pow2_d = dram_in(nc, "pow2", [128, NIT + 1], F32)
    o_dsa = dram_out(nc, "o_dsa", [128, 16, 512], F32)
    with _phase(nc) as kb:
        S = lambda n, s, d: sb(kb, n, s, d)
        D = lambda fn, r, w: kb.op("dve", fn, reads=r, writes=w)
        A = lambda fn, r, w: kb.op("act", fn, reads=r, writes=w)
        v = S("v", [128, 64, 512], BF16); ikT = S("ikT", [128, 8192], BF16)
        kTb = [S(f"kTb{i}", [128, 512], BF16) for i in range(3)]
        for jj in range(4):
            for q in range(2):
                kb.dma("sp", v[q * 64:(q + 1) * 64].rearrange("p (i j) c -> p i j c", j=4)[:, :, jj, :], v_d[q, jj].rearrange("p (i c) -> p i c", c=512), writes=[v.r(("g", jj))])
        for jj in range(4):
            kb.dma("sp", ikT[:].rearrange("p (i j s) -> p i j s", j=4, s=128)[:, :, jj, :], ikT_d[0, jj].rearrange("p (i s) -> p i s", s=128), writes=[ikT.r()])
        VV = [v.r(("g", g)) for g in range(8)]
        iw = S("iw", [128, 16, 8], F32); mneg = S("mneg", [128, 512], F32); mpos = S("mpos", [128, 512], F32)
        identb = S("identb", [128, 128], BF16); identf = S("identf", [128, 128], F32); pow2 = S("pow2", [128, NIT + 1], F32)
        for t, d in ((iw, iw_d), (mneg, mneg_d), (mpos, mpos_d), (identb, identb_d), (identf, identf_d), (pow2, pow2_d)):
            kb.dma("sp", t[:], d, writes=[t.r()])
        B = [ps(kb, f"B{i}", [128, 512], F32) for i in range(4)] + [None, None] + [ps(kb, f"B{i}", [128, 512], F32) for i in (6, 7)]
        X2 = ps(kb, "X2", [128, 2, 512], F32)
        qts = [S(f"qt{i}", [128, 4, 128], BF16) for i in range(2)]; iqts = [S(f"iqt{i}", [128, 4, 128], BF16) for i in range(2)]
        diagw = S("diagw", [128, 8, 128], BF16)
        Rt2 = [S(f"Rt2_{i}", [128, 2, 512], BF16) for i in range(2)]
        scores = [S(f"score{i}", [128, 8192], F32) for i in range(2)]
        masks = [S(f"maskb{i}", [128, 8192], BF16) for i in range(2)]
        tmp = S("tmp", [128, 512], F32)
        sts = [S(f"st{i}", [128, 8], F32) for i in range(2)]
        Hh = S("Hh", [128, NIT + 1], F32)
        lo_t = S("lo_t", [128, 1], F32); mid_t = S("mid_t", [128, 1], F32); cnt_t = S("cnt_t", [128, 1], F32); g_t = S("g_t", [128, 1], F32)
        Eb = [S(f"Eb{i}", [128, 512], BF16) for i in range(2)]
        Pb = [S(f"Pb{i}", [128, 512], BF16) for i in range(2)]
        PT = [S(f"PT{i}", [128, 4, 128], BF16) for i in range(2)]
        rs = S("rs", [128, 4, 16], F32); rsum = S("rsum", [128, 4], F32); rinv = S("rinv", [128, 4], F32)
        osb = S("osb", [128, 512], F32)
        Sbanks = (B[0], B[1], B[7]); Ob = B[2]; pTv = B[3][:].bitcast(BF16); SC = B[6]
        cstate = [0]

        def phaseI(i, par):
            iqt = iqts[par]; score = scores[par]; st = sts[par]
            kb.dma("sp", iqt[:], iqT_d[:, :, i * 128:(i + 1) * 128], writes=[iqt.r()])
            for h in range(8):
                A(lambda e, h=h: e.activation(out=diagw[:, h, :], in_=identf[:], func=AF.Copy, scale=iw[:, i, h:h + 1]), [identf.r(), iw.r()], [diagw.r()])
            yield
            for m in range(i + 1):
                ks = slice(m * 512, (m + 1) * 512)
                for p in range(4):
                    kb.group("pe", [lambda e, p=p: e.matmul(X2[:, 0, :], iqt[0:64, p, :], ikT[0:64, ks], start=True, stop=True),
                                    lambda e, p=p: e.matmul(X2[:, 1, :], iqt[64:128, p, :], ikT[64:128, ks], start=True, stop=True)],
                             reads=[iqt.r(), ikT.r()], writes=[X2.r()])
                    R_ = Rt2[p % 2]
                    A(lambda e, R_=R_: e.activation(out=R_[:].rearrange("p a b -> p (a b)"), in_=X2[:].rearrange("p a b -> p (a b)"), func=AF.Relu), [X2.r()], [R_.r(("h", 0)), R_.r(("h", 1))])
                    kb.group("pe", [lambda e, p=p, R_=R_: e.matmul(SC[:], diagw[:, 2 * p, :], R_[:, 0, :], start=(p == 0), stop=False),
                                    lambda e, p=p, R_=R_: e.matmul(SC[:], diagw[:, 2 * p + 1, :], R_[:, 1, :], start=False, stop=(p == 3))],
                             reads=[diagw.r(), R_.r(("h", 0)), R_.r(("h", 1))], writes=[SC.r()])
                    yield
                if m < i:
                    A(lambda e: e.copy(score[:, ks], SC[:]), [SC.r()], [score.r(("m", m))])
                else:
                    D(lambda e: e.tensor_tensor(out=score[:, ks], in0=SC[:], in1=mneg[:], op=ALU.add), [SC.r(), mneg.r()], [score.r(("m", m))])
                    D(lambda e: e.tensor_tensor(out=tmp[:], in0=SC[:], in1=mpos[:], op=ALU.add), [SC.r(), mpos.r()], [tmp.r()])
                    D(lambda e: e.tensor_reduce(out=st[:, 1:2], in_=tmp[:], axis=AX.X, op=ALU.min), [tmp.r()], [st.r()])
                yield

        def phaseII(i, par):
            score = scores[par]; st = sts[par]; maskb = masks[par]
            SCR = [score.r(("m", m)) for m in range(i + 1)]
            W = (i + 1) * 512
            if i > 0:
                D(lambda e: e.tensor_reduce(out=st[:, 0:1], in_=score[:, 0:i * 512], axis=AX.X, op=ALU.min), SCR, [st.r()])
                D(lambda e: e.tensor_tensor(out=st[:, 2:3], in0=st[:, 0:1], in1=st[:, 1:2], op=ALU.min), [st.r()], [st.r()])
            else:
                D(lambda e: e.tensor_copy(st[:, 2:3], st[:, 1:2]), [st.r()], [st.r()])
            yield
            D(lambda e: e.tensor_reduce(out=st[:, 3:4], in_=score[:, 0:W], axis=AX.X, op=ALU.max), SCR, [st.r()])
            D(lambda e: e.tensor_tensor(out=st[:, 4:5], in0=st[:, 3:4], in1=st[:, 2:3], op=ALU.subtract), [st.r()], [st.r()])
            yield
            D(lambda e: e.tensor_scalar(out=Hh[:], in0=pow2[:], scalar1=st[:, 4:5], scalar2=None, op0=ALU.mult), [pow2.r(), st.r()], [Hh.r()])
            D(lambda e: e.tensor_copy(lo_t[:], st[:, 2:3]), [st.r()], [lo_t.r()])
            D(lambda e: e.tensor_tensor(out=mid_t[:], in0=st[:, 2:3], in1=Hh[:, 0:1], op=ALU.add), [st.r(), Hh.r()], [mid_t.r()])
            yield
            for k in range(NIT):
                D(lambda e: e.tensor_scalar(out=maskb[:, 0:W], in0=score[:, 0:W], scalar1=mid_t[:, 0:1], scalar2=None, op0=ALU.is_ge, op1=ALU.add, accum_out=cnt_t[:, 0:1]),
                  SCR + [mid_t.r()], [maskb.r(), cnt_t.r()])
                yield
                D(lambda e, k=k: e.tensor_scalar(out=g_t[:], in0=cnt_t[:], scalar1=255.5, scalar2=Hh[:, k:k + 1], op0=ALU.is_ge, op1=ALU.mult), [cnt_t.r(), Hh.r()], [g_t.r()])
                yield
                D(lambda e, k=k: e.scalar_tensor_tensor(out=mid_t[:], in0=g_t[:], scalar=lo_t[:, 0:1], in1=Hh[:, k + 1:k + 2], op0=ALU.add, op1=ALU.add), [g_t.r(), lo_t.r(), Hh.r()], [mid_t.r()])
                D(lambda e: e.tensor_tensor(out=lo_t[:], in0=lo_t[:], in1=g_t[:], op=ALU.add), [lo_t.r(), g_t.r()], [lo_t.r()])
                yield
            D(lambda e: e.tensor_scalar(out=maskb[:, 0:W], in0=score[:, 0:W], scalar1=lo_t[:, 0:1], scalar2=-30000.0, op0=ALU.is_lt, op1=ALU.mult), SCR + [lo_t.r()], [maskb.r()])
            yield

        def phaseIII(i, par):
            qt = qts[par]; maskb = masks[par]
            kb.dma("sp", qt[:], qT_d[:, :, i * 128:(i + 1) * 128], writes=[qt.r()])

            def make_it3(m, h, c3):
                ks = slice(m * 512, (m + 1) * 512)
                Sb = Sbanks[c3 % 3]; E_ = Eb[c3 % 2]; P_ = Pb[c3 % 2]; PT_ = PT[c3 % 2]; kt_ = kTb[c3 % 3]
                first = (m == 0 and h == 0)
                hc = slice(h * 128, (h + 1) * 128)

                def S1():
                    for q in range(2):
                        kb.dma("sp", kt_[q * 64:(q + 1) * 64].rearrange("p (j s) -> p j s", j=4), kT_d[q].rearrange("j p (h i s) -> p j h i s", h=4, s=128)[:, :, h, m, :], writes=[kt_.r()])
                    kb.group("pe", [lambda e: e.matmul(Sb[:], qt[:, h, :], kt_[:], start=True, stop=False),
                                    lambda e: e.matmul(Sb[:], identb[:], maskb[:, ks], start=False, stop=True)],
                             reads=[qt.r(), kt_.r(), identb.r(), maskb.r()], writes=[Sb.r()])
                    A(lambda e: e.activation(out=P_[:], in_=Sb[:], func=AF.Exp, scale=SCALE, bias=-C0, accum_out=rs[:, h, m:m + 1]), [Sb.r()], [P_.r(), rs.r(("c", h, m))])

                def S2():
                    kb.group("pe", [(lambda e, x=x: e.transpose(pTv[:, x * 128:(x + 1) * 128], P_[:, x * 128:(x + 1) * 128], identb[:])) for x in range(4)],
                             reads=[P_.r(), identb.r()], writes=[B[3].r()])
                    if c3 % 2 == 0:
                        A(lambda e: e.copy(PT_[:].rearrange("p a b -> p (a b)"), pTv[:, 0:512]), [B[3].r()], [PT_.r()])
                    else:
                        D(lambda e: e.tensor_copy(PT_[:].rearrange("p a b -> p (a b)"), pTv[:, 0:512]), [B[3].r()], [PT_.r()])

                def S3():
                    kb.group("pe", [(lambda e, x=x: e.matmul(Ob[:, hc], PT_[:, x, :], v[:, m * 4 + x, h * 128:(h + 1) * 128], start=(first and x == 0), stop=(m == i and h == 3 and x == 3))) for x in range(4)],
                             reads=[PT_.r()] + VV, writes=[Ob.r(("h", h))] + ([Ob.r(("h", hh)) for hh in range(4)] if first else []))
                return (S1, S2, S3)

            its3 = []
            for m in range(i + 1):
                for h in range(4):
                    its3.append(make_it3(m, h, cstate[0]))
                    cstate[0] += 1
            n = len(its3)
            for step in range(n + 2):
                for stg in range(3):
                    k = step - stg
                    if 0 <= k < n:
                        its3[k][stg]()
                yield
            D(lambda e: e.tensor_reduce(out=rsum[:], in_=rs[:, :, 0:i + 1], axis=AX.X, op=ALU.add), [rs.r(("c", hh, mm)) for hh in range(4) for mm in range(i + 1)], [rsum.r()])
            D(lambda e: e.reciprocal(rinv[:], rsum[:]), [rsum.r()], [rinv.r()])
            D(lambda e: e.tensor_tensor(out=osb[:].rearrange("p (h d) -> p h d", h=4), in0=Ob[:].rearrange("p (h d) -> p h d", h=4), in1=rinv[:].unsqueeze(2).to_broadcast([128, 4, 128]), op=ALU.mult),
              [Ob.r(("h", hh)) for hh in range(4)] + [rinv.r()], [osb.r()])
            kb.dma("sp", o_dsa[:, i, :], osb[:], reads=[osb.r()], is_output=True)
            yield

        def run_interleaved(gens):
            lists = []
            for g in gens:
                lists.append(g)
            active = [[g, w, 0.0] for g, w in lists]
            total = max(w for _, w in lists)
            for stepi in range(total):
                for a in active:
                    g, w, acc = a
                    a[2] += w / total
                    while a[2] >= 1.0:
                        a[2] -= 1.0
                        try:
                            next(g)
                        except StopIteration:
                            a[2] = -1e9
            for a in active:
                for _ in a[0]:
                    pass

        def est_I(i):
            return 1 + (i + 1) * 5

        def est_II(i):
            return 3 + NIT * 3 + 1

        def est_III(i):
            return 4 * (i + 1) + 3

        QL = list(QT)
        for _ in phaseI(QL[0], 0):
            pass
        nq = len(QL)
        for t in range(nq + 1):
            gens = []
            if t < nq:
                gens.append((phaseII(QL[t], t % 2), est_II(QL[t])))
            if t >= 1:
                gens.append((phaseIII(QL[t - 1], (t - 1) % 2), est_III(QL[t - 1])))
            if t + 1 < nq:
                gens.append((phaseI(QL[t + 1], (t + 1) % 2), est_I(QL[t + 1])))
            run_interleaved(gens)
        kb.finish()
        print("DSA instructions:", kb.nins, {k: len(v) for k, v in kb.q.items()})
    return nc


def dsa_masks(j):
    q = np.arange(128)[:, None]
    col = np.arange(512)[None, :]
    jj = col // 128; s = col % 128
    vis = (jj < j) | ((jj == j) & (s <= q))
    mneg = np.where(vis, 0.0, -1e30).astype(np.float32)
    mpos = np.where(vis, 0.0, 1e30).astype(np.float32)
    return mneg, mpos


def gather_global(per_core, axis_tok_tiles):
    st = np.stack(per_core, axis=axis_tok_tiles + 1)
    sh = list(st.shape)
    sh[axis_tok_tiles:axis_tok_tiles + 2] = [64]
    return st.reshape(sh)


def build_gla(NB=16):
    nc = _get_nc()
    kv_d = dram_in(nc, "kv", [2, 4, 64, 8192], BF16)
    dec_d = dram_in(nc, "dec", [1, 4, 128, 64], F32)
    qd_d = dram_in(nc, "qdT", [128, 2, 2048], BF16)
    oin_d = dram_in(nc, "oin", [128, 16, 512], F32)
    grs_d = dram_in(nc, "grs", [128, 16, 512], F32)
    gsel_d = dram_in(nc, "gsel", [128, 2, 8], F32)
    gn_d = dram_in(nc, "gnorm", [128], F32)
    o_gla = dram_out(nc, "o_gla", [128, 16, 512], F32)
    with _phase(nc) as kb:
        S_ = lambda n, s, d: sb(kb, n, s, d)
        D = lambda fn, r, w: kb.op("dve", fn, reads=r, writes=w)
        A = lambda fn, r, w: kb.op("act", fn, reads=r, writes=w)
        dec = S_("dec", [128, 2, 128], F32); gsel = S_("gsel", [128, 2, 8], F32); gnb = S_("gnb", [128, 128], F32)
        for jj in range(4):
            kb.dma("sp", dec[:].rearrange("p a (i j c) -> p a i j c", j=4, c=2)[:, :, :, jj, :], dec_d[0, jj].rearrange("p (a i c) -> p a i c", a=2, c=2), writes=[dec.r()])
        kb.dma("sp", gsel[:], gsel_d, writes=[gsel.r()])
        kb.dma("sp", gnb[:], gn_d.partition_broadcast(128), writes=[gnb.r()])
        St = S_("St", [128, 2, 128], F32); Ssels = [S_(f"Ssel{i}", [128, 2, 256], F32) for i in range(2)]; Sselb = S_("Sselb", [128, 2, 2, 128], BF16)
        kvb = [S_(f"kvb{i}", [128, 4, 2, 2, 128], BF16) for i in range(2)]
        qd = S_("qd", [128, 2, 128], BF16); oin = S_("oin", [128, 512], F32); grs = S_("grs", [128, 512], F32)
        og = S_("og", [128, 512], F32); sq = S_("sq", [128, 512], F32); ssq = S_("ssq", [128, 4], F32); ssq2 = S_("ssq2", [128, 4], F32)
        BA = ps(kb, "BA", [128, 512], F32); BB = ps(kb, "BB", [128, 512], F32)
        D(lambda e: e.memset(St[:], 0.0), [], [St.r(("p", 0)), St.r(("p", 1))])
        Sflat = St[:].rearrange("p a b -> p (a b)")

        def scan(i):
            Ssel = Ssels[i % 2]
            D(lambda e: e.memset(Ssel[:], 0.0), [], [Ssel.r(("c", 0)), Ssel.r(("c", 1))])
            kvb_ = kvb[i % 2]
            for jj in range(4):
                for q in range(2):
                    kb.dma("sp", kvb_[q * 64:(q + 1) * 64, jj], kv_d[q, jj].rearrange("p (a n d) -> p a n d", a=2, d=128)[:, :, 2 * i:2 * i + 2, :], writes=[kvb_.r()])
            for k in range(8):
                n = 8 * i + k
                for c in range(2):
                    D(lambda e, c=c, k=k: e.scalar_tensor_tensor(out=Ssel[:, c, :], in0=Sflat, scalar=gsel[:, c, k:k + 1], in1=Ssel[:, c, :], op0=ALU.mult, op1=ALU.add),
                      [St.r(("p", 0)), St.r(("p", 1)), gsel.r(), Ssel.r(("c", c))], [Ssel.r(("c", c))])
                for p in range(2):
                    D(lambda e, p=p, n=n, k=k: e.scalar_tensor_tensor(out=St[:, p, :], in0=St[:, p, :], scalar=dec[:, p, n:n + 1], in1=kvb_[:, k // 2, p, k % 2, :], op0=ALU.mult, op1=ALU.add),
                      [St.r(("p", p)), dec.r(), kvb_.r()], [St.r(("p", p))])
                yield

        def epilogue(i):
            Ssel = Ssels[i % 2]
            A(lambda e: e.copy(Sselb[:].rearrange("p c a b -> p (c a b)"), Ssel[:].rearrange("p c x -> p (c x)")), [Ssel.r(("c", 0)), Ssel.r(("c", 1))], [Sselb.r()])
            kb.dma("sp", qd[:], qd_d[:, :, i * 128:(i + 1) * 128], writes=[qd.r()])
            kb.dma("sp", oin[:], oin_d[:, i, :], writes=[oin.r()])
            kb.dma("sp", grs[:], grs_d[:, i, :], writes=[grs.r()])
            yield
            fns = []
            for c in range(2):
                for p in range(2):
                    for half in range(2):
                        bank = BA if half == 0 else BB
                        fns.append(lambda e, c=c, p=p, half=half, bank=bank: e.matmul(bank[:, (c * 2 + p) * 128:(c * 2 + p + 1) * 128], qd[half * 64:(half + 1) * 64, p, :], Sselb[half * 64:(half + 1) * 64, c, p, :], start=True, stop=True))
            kb.group("pe", fns, reads=[qd.r(), Sselb.r()], writes=[BA.r(), BB.r()])
            yield
            for h in range(4):
                p, half = divmod(h, 2)
                bank = BA if half == 0 else BB
                for c in range(2):
                    rows = slice(c * 64, (c + 1) * 64)
                    D(lambda e, h=h, c=c, p=p, bank=bank, rows=rows: e.tensor_tensor(out=og[rows, h * 128:(h + 1) * 128], in0=bank[rows, (c * 2 + p) * 128:(c * 2 + p + 1) * 128], in1=oin[rows, h * 128:(h + 1) * 128], op=ALU.add),
                      [bank.r(), oin.r()], [og.r(("h", h))])
                yield
            OG = [og.r(("h", h)) for h in range(4)]
            D(lambda e: e.tensor_tensor(out=sq[:], in0=og[:], in1=og[:], op=ALU.mult), OG, [sq.r()])
            yield
            D(lambda e: e.tensor_reduce(out=ssq[:], in_=sq[:].rearrange("p (h d) -> p h d", h=4), axis=AX.X, op=ALU.add), [sq.r()], [ssq.r()])
            yield
            kb.op("dve", lambda e: e.tensor_scalar(out=ssq2[:], in0=ssq[:], scalar1=1.0 / 128, scalar2=EPS, op0=ALU.mult, op1=ALU.add), reads=[ssq.r()], writes=[ssq2.r()])
            yield
            kb.op("act", lambda e: e.activation(out=ssq2[:], in_=ssq2[:], func=AF.Sqrt), reads=[ssq2.r()], writes=[ssq2.r()])
            yield
            kb.op("dve", lambda e: e.reciprocal(ssq[:], ssq2[:]), reads=[ssq2.r()], writes=[ssq.r()])
            yield
            og3 = og[:].rearrange("p (h d) -> p h d", h=4)
            D(lambda e: e.tensor_tensor(out=og3, in0=og3, in1=ssq[:].unsqueeze(2).to_broadcast([128, 4, 128]), op=ALU.mult), OG + [ssq.r()], OG)
            yield
            D(lambda e: e.tensor_tensor(out=og3, in0=og3, in1=gnb[:].unsqueeze(1).to_broadcast([128, 4, 128]), op=ALU.mult), OG + [gnb.r()], OG)
            yield
            D(lambda e: e.tensor_tensor(out=og[:], in0=og[:], in1=grs[:], op=ALU.mult), OG + [grs.r()], OG)
            kb.dma("sp", o_gla[:, i, :], og[:], reads=OG, is_output=True)
            yield

        for _ in scan(0):
            pass
        for i in range(NB):
            ep = epilogue(i)
            if i + 1 < NB:
                for _ in scan(i + 1):
                    for _n in range(2):
                        try:
                            next(ep)
                        except StopIteration:
                            break
            for _ in ep:
                pass
        kb.finish()
        print("GLA instructions:", kb.nins, {k: len(v) for k, v in kb.q.items()})
    return nc


def gla_sel(j):
    g = np.zeros((128, 2, 8), np.float32)
    for c in range(2):
        g[:, c, 2 * j + c] = 1.0
    return g


NT = 16
POOL_W = (2, 4, 8, 16)


def build_l1b():
    nc = _get_nc()
    xs = dram_in(nc, "xs", [NT * 128, 1024], F32)
    cT_d = dram_in(nc, "cT", [128, 8], F32)
    ada_w = dram_in(nc, "ada_w", [1024, 6144], F32)
    ada_b = dram_in(nc, "ada_b", [1, 6144], F32)
    mixn_d = dram_in(nc, "mixn", [128, 8], F32)
    w_in = dram_in(nc, "w_in", [1024, 2048], F32)
    qn_d = dram_in(nc, "q_norm", [128], F32)
    kn_d = dram_in(nc, "k_norm", [128], F32)
    identb_d = dram_in(nc, "identb", [128, 128], BF16)
    o_mod = dram_out(nc, "o_mod", [1, 6144], F32)
    o_qT = dram_out(nc, "o_qT", [128, 4, NT * 128], BF16)
    o_kT = dram_out(nc, "o_kT", [128, 4, NT * 128], BF16)
    o_v = dram_out(nc, "o_v", [128, NT, 512], BF16)
    o_u = dram_out(nc, "o_u", [128, NT, 512], F32)
    o_uh = dram_out(nc, "o_uh", [256, 512], F32)
    with _phase(nc) as kb:
        S = lambda n, s, d: sb(kb, n, s, d)
        D = lambda fn, r, w: kb.op("dve", fn, reads=r, writes=w)
        A = lambda fn, r, w: kb.op("act", fn, reads=r, writes=w)
        identb = S("identb", [128, 128], BF16); mixn = S("mixn", [128, 8], F32)
        qnb = S("qnb", [128, 128], F32); knb = S("knb", [128, 128], F32)
        for t, d in ((identb, identb_d), (mixn, mixn_d)):
            kb.dma("sp", t[:], d, writes=[t.r()])
        for t, d in ((qnb, qn_d), (knb, kn_d)):
            kb.dma("sp", t[:], d.partition_broadcast(128), writes=[t.r()])
        wb = S("wb", [128, 8, 2048], BF16)
        for k in range(8):
            kb.dma("pool", wb[:, k, :], w_in[k * 128:(k + 1) * 128, :], writes=[wb.r(("k", k))])
        WB = [wb.r(("k", k)) for k in range(8)]
        modrow = S("modrow", [1, 6144], F32); modP = S("modP", [128, 48], F32)
        pT = ps(kb, "pT", [128, 8, 128], BF16)
        pp = [ps(kb, f"pp{i}", [128, 512], F32) for i in range(2)]
        pA = ps(kb, "pA", [128, 512], F32); pB = ps(kb, "pB", [128, 512], F32)
        emit_mod(kb, cT_d, ada_w, ada_b, modrow, modP, "m1", pA, pB)
        kb.dma("sp", o_mod, modrow[:], reads=[modrow.r()], is_output=True)
        a1 = S("a1", [128, 8], F32)
        D(lambda e: e.scalar_tensor_tensor(out=a1[:], in0=modP[:, 8:16], scalar=1.0, in1=mixn[:], op0=ALU.add, op1=ALU.mult), [modP.r(), mixn.r()], [a1.r()])
        xt = [S(f"xt{i}", [128, 1024], F32) for i in range(2)]
        junk = S("junk", [128, 1024], BF16); xn = S("xn", [128, 1024], BF16)
        st = S("st", [128, 1], F32); st2 = S("st2", [128, 1], F32)
        hT = [S(f"hT{i}", [128, 8, 128], BF16) for i in range(2)]
        proj = S("proj", [128, 2048], F32)
        sq = S("sq", [128, 512], F32); ssq = S("ssq", [128, 4], F32); ssq2 = S("ssq2", [128, 4], F32)
        qn = S("qn", [128, 4, 128], F32); qr = S("qr", [128, 4, 128], BF16)
        sqb = S("sqb", [128, 512], F32); ssqb = S("ssqb", [128, 4], F32); ssq2b = S("ssq2b", [128, 4], F32)
        qnb2 = S("qnb2", [128, 4, 128], F32); qrb = S("qrb", [128, 4, 128], BF16)
        TMP = [(sq, ssq, ssq2, qn, qr), (sqb, ssqb, ssq2b, qnb2, qrb)]
        qT_t = S("qT_t", [128, 4, 128], BF16); kT_t = S("kT_t", [128, 4, 128], BF16)
        vb = S("vb", [128, 512], BF16)
        for i in range(NT):
            x_ = xt[i % 2]; h_ = hT[i % 2]
            kb.dma("sp", x_[:], xs[i * 128:(i + 1) * 128, :], writes=[x_.r()])
            emit_norm_hT(kb, x_, h_, a1, modP_sh(modP), identb, junk, xn, pT, st, st2)
            for cb in range(4):
                p_ = pp[cb % 2]
                kb.group("pe", [(lambda e, k=k, cb=cb, p_=p_, h_=h_: e.matmul(p_[:], h_[:, k, :], wb[:, k, cb * 512:(cb + 1) * 512], start=(k == 0), stop=(k == 7))) for k in range(8)],
                         reads=[h_.r()] + WB, writes=[p_.r()])
                if cb % 2 == 0:
                    A(lambda e, cb=cb, p_=p_: e.copy(proj[:, cb * 512:(cb + 1) * 512], p_[:]), [p_.r()], [proj.r(("c", cb))])
                else:
                    D(lambda e, cb=cb, p_=p_: e.tensor_copy(proj[:, cb * 512:(cb + 1) * 512], p_[:]), [p_.r()], [proj.r(("c", cb))])
            PR = [proj.r(("c", cb)) for cb in range(4)]
            CH = [[], [], []]
            for ci, (c0, gain, dstT, dstD) in enumerate(((512, qnb, qT_t, o_qT), (1024, knb, kT_t, o_kT))):
                kb.rec = CH[ci]
                sq_, ssq_, ssq2_, qn_, qr_ = TMP[ci]
                src = proj[:, c0:c0 + 512]
                D(lambda e, src=src, sq_=sq_: e.tensor_tensor(out=sq_[:], in0=src, in1=src, op=ALU.mult), PR, [sq_.r()])
                D(lambda e, sq_=sq_, ssq_=ssq_: e.tensor_reduce(out=ssq_[:], in_=sq_[:].rearrange("p (h d) -> p h d", h=4), axis=AX.X, op=ALU.add), [sq_.r()], [ssq_.r()])
                emit_rstd(kb, ssq_, 128, ssq2_)
                s3 = src.rearrange("p (h d) -> p h d", h=4)
                D(lambda e, s3=s3, qn_=qn_, ssq_=ssq_: e.tensor_tensor(out=qn_[:], in0=s3, in1=ssq_[:].unsqueeze(2).to_broadcast([128, 4, 128]), op=ALU.mult), PR + [ssq_.r()], [qn_.r()])
                D(lambda e, gain=gain, qn_=qn_, qr_=qr_: e.tensor_tensor(out=qr_[:], in0=qn_[:], in1=gain[:].unsqueeze(1).to_broadcast([128, 4, 128]), op=ALU.mult), [qn_.r(), gain.r()], [qr_.r()])
                kb.group("pe", [(lambda e, h=h, qr_=qr_, ci=ci: e.transpose(pT[:, 4 * ci + h, :], qr_[:, h, :], identb[:])) for h in range(4)], reads=[qr_.r(), identb.r()], writes=[pT.r()])
                A(lambda e, dstT=dstT, ci=ci: e.copy(dstT[:], pT[:, 4 * ci:4 * ci + 4, :]), [pT.r()], [dstT.r()])
                kb.dma("sp", dstD[:, :, i * 128:(i + 1) * 128], dstT[:], reads=[dstT.r()], is_output=True)
            kb.rec = CH[2]
            D(lambda e: e.tensor_copy(vb[:], proj[:, 1536:2048]), PR, [vb.r()])
            kb.dma("sp", o_v[:, i, :], vb[:], reads=[vb.r()], is_output=True)
            kb.dma("sp", o_u[:, i, :], proj[:, 0:512], reads=PR, is_output=True)
            kb.dma("sp", o_uh[i * 16:(i + 1) * 16, :], proj[112:128, 0:512], reads=PR, is_output=True)
            kb.emit_roundrobin(CH)
        kb.finish()
        print("L1b instructions:", kb.nins, {k: len(v) for k, v in kb.q.items()})
    return nc


def build_pool():
    nc = _get_nc()
    u_d = dram_in(nc, "u", [128, NT, 512], F32)
    guh_d = dram_in(nc, "guh", [1, 4, 256, 512], F32)
    hsel_d = dram_in(nc, "hsel", [128, 4], F32)
    band_d = dram_in(nc, "band", [128, 4, 128], F32)
    band0_d = dram_in(nc, "band0", [128, 4, 128], F32)
    bandhc_d = dram_in(nc, "bandhc", [128, 2, NT, 4, 128], F32)
    pw_d = dram_in(nc, "pool_w", [4, 128, 128], F32)
    psc_d = dram_in(nc, "pool_scale", [512], F32)
    o_pool = dram_out(nc, "o_pool", [128, NT, 512], F32)
    with _phase(nc) as kb:
        S = lambda n, s, d: sb(kb, n, s, d)
        D = lambda fn, r, w: kb.op("dve", fn, reads=r, writes=w)
        A = lambda fn, r, w: kb.op("act", fn, reads=r, writes=w)
        hsel = S("hsel", [128, 4], F32); band = S("band", [128, 4, 128], F32); band0 = S("band0", [128, 4, 128], F32)
        pscb = S("pscb", [128, 512], F32); pw = S("pw", [128, 4, 128], BF16)
        for t, d in ((hsel, hsel_d), (band, band_d), (band0, band0_d)):
            kb.dma("sp", t[:], d, writes=[t.r()])
        kb.dma("sp", pscb[:], psc_d.partition_broadcast(128), writes=[pscb.r()])
        kb.dma("pool", pw[:], pw_d.rearrange("g c d -> c g d"), writes=[pw.r()])
        uhc = S("uhc", [128, 4, 2, 512], F32); uhs = S("uhs", [128, 2, 512], F32)
        for jj in range(4):
            for hh in range(2):
                kb.dma("sp", uhc[:, jj, hh, :], guh_d[0, jj, hh * 128:(hh + 1) * 128, :], writes=[uhc.r()])
        uc = uhc[:].rearrange("p j h c -> p j (h c)"); us = uhs[:].rearrange("p h c -> p (h c)")
        D(lambda e: e.tensor_scalar(out=us, in0=uc[:, 0, :], scalar1=hsel[:, 0:1], scalar2=None, op0=ALU.mult), [uhc.r(), hsel.r()], [uhs.r()])
        for jj in range(1, 4):
            D(lambda e, jj=jj: e.scalar_tensor_tensor(out=us, in0=uc[:, jj, :], scalar=hsel[:, jj:jj + 1], in1=us, op0=ALU.mult, op1=ALU.add), [uhc.r(), hsel.r(), uhs.r()], [uhs.r()])
        pA = ps(kb, "pA", [128, 512], F32); pB = ps(kb, "pB", [128, 512], F32)
        ut = [S(f"ut{i}", [128, 512], F32) for i in range(2)]
        bh = [S(f"bh{i}", [128, 2, 4, 128], F32) for i in range(2)]
        plT = S("plT", [128, 4, 128], BF16); opl = S("opl", [128, 512], F32)
        for i in range(NT):
            u_ = ut[i % 2]; bh_ = bh[i % 2]
            kb.dma("sp", u_[:], u_d[:, i, :], writes=[u_.r()])
            kb.dma("sp", bh_[:], bandhc_d[:, :, i, :, :], writes=[bh_.r()])
            fns = []
            for g in range(4):
                bo = band0[:, g, :] if i == 0 else band[:, g, :]
                fns.append(lambda e, g=g, bo=bo, u_=u_: e.matmul(pA[:, g * 128:(g + 1) * 128], u_[:, g * 128:(g + 1) * 128], bo, start=True, stop=False))
                fns.append(lambda e, g=g, bh_=bh_: e.matmul(pA[:, g * 128:(g + 1) * 128], uhs[:, 0, g * 128:(g + 1) * 128], bh_[:, 0, g, :], start=False, stop=False))
                fns.append(lambda e, g=g, bh_=bh_: e.matmul(pA[:, g * 128:(g + 1) * 128], uhs[:, 1, g * 128:(g + 1) * 128], bh_[:, 1, g, :], start=False, stop=True))
            kb.group("pe", fns, reads=[u_.r(), bh_.r(), uhs.r(), band.r(), band0.r()], writes=[pA.r()])
            A(lambda e: e.copy(plT[:].rearrange("p g t -> p (g t)"), pA[:]), [pA.r()], [plT.r()])
            kb.group("pe", [(lambda e, g=g: e.matmul(pB[:, g * 128:(g + 1) * 128], plT[:, g, :], pw[:, g, :], start=True, stop=True)) for g in range(4)], reads=[plT.r(), pw.r()], writes=[pB.r()])
            D(lambda e: e.tensor_tensor(out=opl[:], in0=pB[:], in1=pscb[:], op=ALU.mult), [pB.r(), pscb.r()], [opl.r()])
            kb.dma("sp", o_pool[:, i, :], opl[:], reads=[opl.r()], is_output=True)
        kb.finish()
    return nc


def pool_consts_core(j):
    s_ = np.arange(128)[:, None]; t_ = np.arange(128)[None, :]
    band = np.zeros((128, 4, 128), np.float32); band_first = np.zeros((128, 4, 128), np.float32)
    bandhc = np.zeros((128, 2, 16, 4, 128), np.float32)
    for g, w in enumerate(POOL_W):
        inwin = ((t_ - s_) >= 0) & ((t_ - s_) <= w - 1)
        band[:, g, :] = inwin / float(w) - (s_ == t_)
        cnt = np.minimum(t_ + 1.0, float(w))
        band_first[:, g, :] = inwin / cnt - (s_ == t_)
        for i in range(16):
            isrc = i if j > 0 else i - 1
            if isrc < 0:
                continue
            half, slot = divmod(isrc, 8)
            for r in range(16):
                srel = r - 16
                row = (((np.arange(128) - srel) >= 0) & ((np.arange(128) - srel) <= w - 1)) / float(w)
                bandhc[slot * 16 + r, half, i, g, :] = row
    hsel = np.zeros((128, 4), np.float32)
    hsel[:, (j - 1) % 4] = 1.0
    return band, (band_first if j == 0 else band), bandhc, hsel


def pool_consts():
    s = np.arange(128)[:, None]; t = np.arange(128)[None, :]
    band = np.zeros((128, 4, 128), np.float32); band_first = np.zeros((128, 4, 128), np.float32)
    bandh = np.zeros((128, 8, 4, 128), np.float32)
    for g, w in enumerate(POOL_W):
        inwin = ((t - s) >= 0) & ((t - s) <= w - 1)
        band[:, g, :] = inwin / float(w) - (s == t)
        cnt = np.minimum(t + 1.0, float(w))
        band_first[:, g, :] = inwin / cnt - (s == t)
        for r in range(16):
            srel = r - 16
            row = (((np.arange(128) - srel) >= 0) & ((np.arange(128) - srel) <= w - 1)) / float(w)
            for slot in range(8):
                bandh[slot * 16 + r, slot, g, :] = row
    return band, bandh, band_first


SCALE = float(128 ** -0.5)


def build_sb(QT=tuple(range(16))):
    nc = _get_nc()
    qT_d = dram_in(nc, "qT", [128, 4, 2048], BF16)
    kT_d = dram_in(nc, "kT", [2, 4, 64, 8192], BF16)
    v_d = dram_in(nc, "v", [2, 4, 64, 8192], BF16)
    mask_d = dram_in(nc, "sbmask", [128, 512], F32)
    U_d = dram_in(nc, "U", [128, 128], BF16)
    ones_d = dram_in(nc, "ones", [128, 128], BF16)
    o_sb = dram_out(nc, "o_sb", [128, 16, 512], F32)
    with _phase(nc) as kb:
        S = lambda n, s, d: sb(kb, n, s, d)
        D = lambda fn, r, w: kb.op("dve", fn, reads=r, writes=w)
        A = lambda fn, r, w: kb.op("act", fn, reads=r, writes=w)
        kT = S("kT", [128, 4, 8192], BF16); v = S("v", [128, 64, 512], BF16)
        for h in range(4):
            for jj in range(4):
                for q in range(2):
                    kb.dma("sp", kT[q * 64:(q + 1) * 64, h, :].rearrange("p (i j s) -> p i j s", j=4, s=128)[:, :, jj, :],
                           kT_d[q, jj].rearrange("p (h i s) -> p h i s", h=4, s=128)[:, h, :, :], writes=[kT.r(("h", h))])
        for jj in range(4):
            for q in range(2):
                kb.dma("sp", v[q * 64:(q + 1) * 64].rearrange("p (i j) c -> p i j c", j=4)[:, :, jj, :], v_d[q, jj].rearrange("p (i c) -> p i c", c=512), writes=[v.r(("g", jj))])
        KT = [kT.r(("h", h)) for h in range(4)]; VV = [v.r(("g", g)) for g in range(8)]
        mask = S("mask", [128, 512], F32); U = S("U", [128, 128], BF16); ones = S("ones", [128, 128], BF16)
        for t, d in ((mask, mask_d), (U, U_d), (ones, ones_d)):
            kb.dma("sp", t[:], d, writes=[t.r()])
        B = [ps(kb, f"B{i}", [128, 512], F32) for i in range(8)]
        qt = S("qt", [128, 4, 128], BF16)
        NBUF = 5
        eb = [S(f"eb{i}", [128, 512], F32) for i in range(NBUF)]
        spb = [S(f"spb{i}", [128, 512], F32) for i in range(NBUF)]
        Lbb = [S(f"Lb{i}", [128, 512], BF16) for i in range(NBUF)]
        tb = [S(f"tb{i}", [128, 512], F32) for i in range(NBUF)]
        wbb = [S(f"wb{i}", [128, 512], BF16) for i in range(NBUF)]
        Csb = S("Csb", [128, 4, 128], F32)
        osb = S("osb", [128, 512], F32)
        Zs = (B[0], B[1], B[2]); As = (B[3], B[4], B[5], B[6]); Ob = B[7]
        qts = [qt, S("qt2", [128, 4, 128], BF16)]
        qss = [S("qs0", [128, 4, 128], BF16), S("qs1", [128, 4, 128], BF16)]

        def make_it(i, m, h, ctr, qt_, qs_):
            Z = Zs[ctr % 3]; Aa = As[ctr % 4]
            e_ = eb[ctr % NBUF]; sp_ = spb[ctr % NBUF]; L_ = Lbb[ctr % NBUF]; t_ = tb[ctr % NBUF]; w_ = wbb[ctr % NBUF]
            diag = (m == i)
            first = diag and h == 0
            last = (m == 0 and h == 3)
            hc = slice(h * 128, (h + 1) * 128)

            def S1():
                if first:
                    kb.dma("sp", qt_[:], qT_d[:, :, i * 128:(i + 1) * 128], writes=[qt_.r()])
                    D(lambda e: e.tensor_scalar(out=qs_[:], in0=qt_[:], scalar1=SCALE, scalar2=None, op0=ALU.mult), [qt_.r()], [qs_.r()])
                kb.group("pe", [(lambda e, x=x: e.matmul(Z[:, x * 128:(x + 1) * 128], kT[:, h, m * 512 + x * 128: m * 512 + (x + 1) * 128], qt_[:, h, :], start=True, stop=True)) for x in range(4)],
                         reads=[KT[h], qt_.r()], writes=[Z.r()])
                A(lambda e: e.activation(out=e_[:], in_=Z[:], func=AF.Exp, scale=-SCALE), [Z.r()], [e_.r()])

            def S2():
                A(lambda e: e.activation(out=sp_[:], in_=e_[:], func=AF.Ln, bias=1.0), [e_.r()], [sp_.r()])
                D(lambda e: e.scalar_tensor_tensor(out=L_[:], in0=Z[:], scalar=-SCALE, in1=sp_[:], op0=ALU.mult, op1=ALU.subtract), [Z.r(), sp_.r()], [L_.r()])
                if diag:
                    D(lambda e: e.tensor_tensor(out=L_[:], in0=L_[:], in1=mask[:], op=ALU.mult), [L_.r(), mask.r()], [L_.r()])

            def S3():
                fns = []
                for x in range(4):
                    fns.append(lambda e, x=x: e.matmul(Aa[:, x * 128:(x + 1) * 128], U[:], L_[:, x * 128:(x + 1) * 128], start=True, stop=False))
                    for x2 in range(x + 1, 4):
                        fns.append(lambda e, x=x, x2=x2: e.matmul(Aa[:, x * 128:(x + 1) * 128], ones[:], L_[:, x2 * 128:(x2 + 1) * 128], start=False, stop=False))
                    fns.append(lambda e, x=x: e.matmul(Aa[:, x * 128:(x + 1) * 128], kT[:, h, m * 512 + x * 128: m * 512 + (x + 1) * 128], qs_[:, h, :], start=False, stop=(x == 3)))
                kb.group("pe", fns, reads=[U.r(), ones.r(), L_.r(), KT[h], qs_.r()], writes=[Aa.r()])
                if not diag:
                    D(lambda e: e.tensor_tensor(out=t_[:].rearrange("p (x q) -> p x q", x=4), in0=Aa[:].rearrange("p (x q) -> p x q", x=4), in1=Csb[:, h, :].unsqueeze(1).to_broadcast([128, 4, 128]), op=ALU.add),
                      [Aa.r(), Csb.r(("h", h))], [t_.r()])

            def S4():
                if diag:
                    A(lambda e: e.activation(out=w_[:], in_=Aa[:], func=AF.Exp), [Aa.r()], [w_.r()])
                else:
                    A(lambda e: e.activation(out=w_[:], in_=t_[:], func=AF.Exp), [t_.r()], [w_.r()])
                if diag:
                    D(lambda e: e.tensor_tensor(out=w_[:], in0=w_[:], in1=mask[:], op=ALU.mult), [w_.r(), mask.r()], [w_.r()])

            def S5():
                if m > 0:
                    kb.group("pe", [(lambda e, x=x: e.matmul(Aa[:, 0:128], ones[:], L_[:, x * 128:(x + 1) * 128], start=(x == 0), stop=(x == 3))) for x in range(4)],
                             reads=[ones.r(), L_.r()], writes=[Aa.r()])
                    if diag:
                        D(lambda e: e.tensor_copy(Csb[:, h, :], Aa[:, 0:128]), [Aa.r()], [Csb.r(("h", h))])
                    else:
                        D(lambda e: e.tensor_tensor(out=Csb[:, h, :], in0=Aa[:, 0:128], in1=Csb[:, h, :], op=ALU.add), [Aa.r(), Csb.r(("h", h))], [Csb.r(("h", h))])
                kb.group("pe", [(lambda e, x=x: e.matmul(Ob[:, hc], w_[:, x * 128:(x + 1) * 128], v[:, m * 4 + x, h * 128:(h + 1) * 128], start=(first and x == 0), stop=(last and x == 3))) for x in range(4)],
                         reads=[w_.r()] + VV, writes=[Ob.r(("h", h))] + ([Ob.r(("h", hh)) for hh in range(4)] if first else []))
                if last:
                    A(lambda e: e.copy(osb[:], Ob[:]), [Ob.r(("h", hh)) for hh in range(4)], [osb.r()])
                    kb.dma("sp", o_sb[:, i, :], osb[:], reads=[osb.r()], is_output=True)
            return (S1, S2, S3, S4, S5)

        its = []
        ctr = 0
        for qi, i in enumerate(QT):
            for m in range(i, -1, -1):
                for h in range(4):
                    its.append(make_it(i, m, h, ctr, qts[qi % 2], qss[qi % 2]))
                    ctr += 1
        emit_skewed(its, 5)
        kb.finish()
        print("SB instructions:", kb.nins, {k: len(v) for k, v in kb.q.items()})
    return nc


def sb_mask(j):
    s = np.arange(128)[:, None]
    col = np.arange(512)[None, :]
    jj = col // 128; t = col % 128
    vis = (jj < j) | ((jj == j) & (s < t))
    return vis.astype(np.float32)


def sb_consts():
    jx = np.arange(128)[:, None]; sx = np.arange(128)[None, :]
    U = (jx >= sx).astype(np.float32).astype(ml_dtypes.bfloat16)
    ones = np.ones((128, 128), np.float32).astype(ml_dtypes.bfloat16)
    return U, ones


NT = 16
GT = 2


def build_tail():
    nc = _get_nc()
    xs = dram_in(nc, "xs", [NT * 128, 1024], F32)
    mixa_d = dram_in(nc, "mixa", [128, NT, 512], F32)
    mixb_d = dram_in(nc, "mixb", [128, NT, 512], F32)
    modrow_d = dram_in(nc, "modrow", [1, 6144], F32)
    fnorm_d = dram_in(nc, "fnorm", [128, 8], F32)
    w_out = dram_in(nc, "w_out", [1024, 1024], F32)
    w1 = dram_in(nc, "w1", [1024, 5632], F32)
    w2 = dram_in(nc, "w2", [2816, 1024], F32)
    identb_d = dram_in(nc, "identb", [128, 128], BF16)
    identf_d = dram_in(nc, "identf", [128, 128], F32)
    o_x = dram_out(nc, "o_x", [NT * 128, 1024], F32)
    with _phase(nc) as kb:
        S = lambda n, s, d: sb(kb, n, s, d)
        D = lambda fn, r, w: kb.op("dve", fn, reads=r, writes=w)
        identb = S("identb", [128, 128], BF16); fnorm = S("fnorm", [128, 8], F32)
        mod48 = S("mod48", [48, 128], F32); identf = S("identf", [128, 128], F32)
        kb.dma("sp", identb[:], identb_d, writes=[identb.r()])
        kb.dma("sp", fnorm[:], fnorm_d, writes=[fnorm.r()])
        kb.dma("sp", mod48[:], modrow_d.rearrange("o (c p) -> (o c) p", p=128), writes=[mod48.r()])
        kb.dma("sp", identf[:], identf_d, writes=[identf.r()])
        woutb = S("woutb", [128, 8, 1024], BF16); w1b = S("w1b", [128, 8, 5632], BF16); w2b = S("w2b", [128, 22, 1024], BF16)
        for k in range(8):
            kb.dma("pool", woutb[:, k, :], w_out[k * 128:(k + 1) * 128, :], writes=[woutb.r(("k", k))])
        for k in range(8):
            kb.dma("pool", w1b[:, k, :], w1[k * 128:(k + 1) * 128, :], writes=[w1b.r(("k", k))])
        for f in range(22):
            kb.dma("pool", w2b[:, f, :], w2[f * 128:(f + 1) * 128, :], writes=[w2b.r(("k", f))])
        WO = [woutb.r(("k", k)) for k in range(8)]; W1 = [w1b.r(("k", k)) for k in range(8)]; W1f = [W1] * 22; W1u = [W1] * 22; W2 = [w2b.r(("k", f)) for f in range(22)]
        pT = ps(kb, "pT", [128, 8, 128], BF16)
        pp = [ps(kb, f"pp{i}", [128, 512], F32) for i in range(2)]
        pg = ps(kb, "pg", [128, 512], F32); pu = ps(kb, "pu", [128, 512], F32)
        modP = S("modP", [128, 48], F32); G1b = S("G1b", [128, 1024], F32); G2b = S("G2b", [128, 1024], F32)
        kb.op("pe", lambda e: e.transpose(pg[:, 0:48], mod48[:], identf[0:48, 0:48]), reads=[mod48.r(), identf.r()], writes=[pg.r()])
        D(lambda e: e.tensor_copy(modP[:], pg[:, 0:48]), [pg.r()], [modP.r()])
        kb.dma("sp", G1b[:], modrow_d[0, 2048:3072].partition_broadcast(128), writes=[G1b.r()])
        kb.dma("sp", G2b[:], modrow_d[0, 5120:6144].partition_broadcast(128), writes=[G2b.r()])
        a2 = S("a2", [128, 8], F32)
        D(lambda e: e.scalar_tensor_tensor(out=a2[:], in0=modP[:, 32:40], scalar=1.0, in1=fnorm[:], op0=ALU.add, op1=ALU.mult), [modP.r(), fnorm.r()], [a2.r()])

        class SH:
            def __getitem__(s, idx):
                p, sl = idx
                return modP[p, slice(sl.start + 24, sl.stop + 24)]
            def r(s, key=None):
                return modP.r(key)
        sh2 = SH()
        xt = [S("xt0", [128, 1024], F32)] * 2
        mixt = [S("mixt0", [128, 8, 128], BF16)] * 2
        x1s = [S(f"x1_{i}", [128, GT, 1024], F32) for i in range(2)]
        xn = S("xn", [128, 1024], BF16); junk = xn
        st = S("st", [128, 1], F32); st2 = S("st2", [128, 1], F32)
        hT4s = [S(f"hT4_{i}", [128, 8, GT * 128], BF16) for i in range(2)]
        sg = S("sg", [128, GT * 128], F32); actT = S("actT", [128, 22, GT * 128], BF16)
        yt = S("yt", [128, 1024], F32)
        mst = yt

        def front(g):
            x1 = x1s[g % 2]; hT4 = hT4s[g % 2]
            for t in range(GT):
                i = g * GT + t
                x_ = xt[i % 2]; m_ = mixt[i % 2]
                kb.dma("sp", x_[:], xs[i * 128:(i + 1) * 128, :], writes=[x_.r()])
                kb.dma("sp", mst[:, 0:512], mixa_d[:, i, :], writes=[yt.r(("c", 0))])
                kb.dma("sp", mst[:, 512:1024], mixb_d[:, i, :], writes=[yt.r(("c", 1))])
                D(lambda e: e.tensor_copy(xn[:], mst[:]), [yt.r(("c", 0)), yt.r(("c", 1))], [xn.r()])
                yield
                kb.group("pe", [(lambda e, k=k: e.transpose(pT[:, k, :], xn[:, k * 128:(k + 1) * 128], identb[:])) for k in range(8)], reads=[xn.r(), identb.r()], writes=[pT.r()])
                kb.op("act", lambda e, m_=m_: e.copy(m_[:], pT[:]), reads=[pT.r()], writes=[m_.r()])
                yield
                for cb in range(2):
                    p_ = pp[cb]
                    kb.group("pe", [(lambda e, k=k, cb=cb, p_=p_, m_=m_: e.matmul(p_[:], m_[:, k, :], woutb[:, k, cb * 512:(cb + 1) * 512], start=(k == 0), stop=(k == 7))) for k in range(8)],
                             reads=[m_.r()] + WO, writes=[p_.r()])
                    D(lambda e, cb=cb, p_=p_: e.tensor_tensor(out=mst[:, cb * 512:(cb + 1) * 512], in0=p_[:], in1=G1b[:, cb * 512:(cb + 1) * 512], op=ALU.mult), [p_.r(), G1b.r()], [yt.r(("c", cb))])
                    yield
                    D(lambda e, cb=cb, x_=x_, t=t: e.tensor_tensor(out=x1[:, t, cb * 512:(cb + 1) * 512], in0=mst[:, cb * 512:(cb + 1) * 512], in1=x_[:, cb * 512:(cb + 1) * 512], op=ALU.add),
                      [yt.r(("c", cb)), x_.r()], [x1.r(("t", t, cb))])
                    yield
                kb.op("act", lambda e, t=t: e.activation(out=junk[:], in_=x1[:, t, :], func=AF.Square, accum_out=st[:, 0:1]),
                      reads=[x1.r(("t", t, 0)), x1.r(("t", t, 1))], writes=[junk.r(), st.r()])
                yield
                kb.op("dve", lambda e: e.tensor_scalar(out=st2[:], in0=st[:], scalar1=1.0 / 1024, scalar2=EPS, op0=ALU.mult, op1=ALU.add), reads=[st.r()], writes=[st2.r()])
                yield
                kb.op("act", lambda e: e.activation(out=st2[:], in_=st2[:], func=AF.Sqrt), reads=[st2.r()], writes=[st2.r()])
                yield
                kb.op("dve", lambda e: e.reciprocal(st[:], st2[:]), reads=[st2.r()], writes=[st.r()])
                yield
                kb.op("act", lambda e, t=t: e.activation(out=xn[:], in_=x1[:, t, :], func=AF.Copy, scale=st[:, 0:1]), reads=[x1.r(("t", t, 0)), x1.r(("t", t, 1)), st.r()], writes=[xn.r()])
                yield
                kb.group("pe", [(lambda e, k=k: e.transpose(pT[:, k, :], xn[:, k * 128:(k + 1) * 128], identb[:])) for k in range(8)], reads=[xn.r(), identb.r()], writes=[pT.r()])
                yield
                for k in range(8):
                    kb.op("act", lambda e, k=k, t=t: e.activation(out=hT4[:, k, t * 128:(t + 1) * 128], in_=pT[:, k, :], func=AF.Identity, scale=a2[:, k:k + 1], bias=modP[:, 24 + k:25 + k]),
                          reads=[pT.r(), a2.r(), modP.r()], writes=[hT4.r(("t", t))])
                    if k % 4 == 3:
                        yield

        def drain(gen, n=None):
            cnt = 0
            for _ in gen:
                cnt += 1
                if n is not None and cnt >= n:
                    return

        NG = NT // GT
        gens = [front(g) for g in range(NG)]
        drain(gens[0])
        for g in range(NG):
            x1 = x1s[g % 2]; hT4 = hT4s[g % 2]
            HT = [hT4.r(("t", t)) for t in range(GT)]
            for f in range(22):
                kb.group("pe", [(lambda e, k=k, f=f: e.matmul(pg[:, 0:GT * 128], w1b[:, k, f * 128:(f + 1) * 128], hT4[:, k, :], start=(k == 0), stop=(k == 7))) for k in range(8)],
                         reads=HT + W1f[f], writes=[pg.r()])
                kb.group("pe", [(lambda e, k=k, f=f: e.matmul(pu[:, 0:GT * 128], w1b[:, k, 2816 + f * 128:2816 + (f + 1) * 128], hT4[:, k, :], start=(k == 0), stop=(k == 7))) for k in range(8)],
                         reads=HT + W1u[f], writes=[pu.r()])
                kb.op("act", lambda e: e.activation(out=sg[:], in_=pg[:, 0:GT * 128], func=AF.Silu), reads=[pg.r()], writes=[sg.r()])
                D(lambda e, f=f: e.tensor_tensor(out=actT[:, f, :], in0=sg[:], in1=pu[:, 0:GT * 128], op=ALU.mult), [sg.r(), pu.r()], [actT.r(("f", f))])
                if g + 1 < NG:
                    drain(gens[g + 1], 2)
            if g + 1 < NG:
                drain(gens[g + 1])
            AT = [actT.r(("f", f)) for f in range(22)]
            for t in range(GT):
                i = g * GT + t
                for cb in range(2):
                    p_ = pp[cb]
                    kb.group("pe", [(lambda e, f=f, cb=cb, p_=p_, t=t: e.matmul(p_[:], actT[:, f, t * 128:(t + 1) * 128], w2b[:, f, cb * 512:(cb + 1) * 512], start=(f == 0), stop=(f == 21))) for f in range(22)],
                             reads=AT + W2, writes=[p_.r()])
                    D(lambda e, cb=cb, p_=p_: e.tensor_tensor(out=yt[:, cb * 512:(cb + 1) * 512], in0=p_[:], in1=G2b[:, cb * 512:(cb + 1) * 512], op=ALU.mult), [p_.r(), G2b.r()], [yt.r(("c", cb))])
                    D(lambda e, cb=cb, t=t: e.tensor_tensor(out=yt[:, cb * 512:(cb + 1) * 512], in0=yt[:, cb * 512:(cb + 1) * 512], in1=x1[:, t, cb * 512:(cb + 1) * 512], op=ALU.add),
                      [yt.r(("c", cb)), x1.r(("t", t, cb))], [yt.r(("c", cb))])
                kb.dma("sp", o_x[i * 128:(i + 1) * 128, :], yt[:], reads=[yt.r(("c", 0)), yt.r(("c", 1))], is_output=True)
        kb.finish()
        print("tail instructions:", kb.nins, {k: len(v) for k, v in kb.q.items()})
    return nc


DBG = False


def build_fused():
    FX.active = True
    FX.nc = bass.Bass("TRN2", target_bir_lowering=False)
    FX.ext = {}
    FX.n_phase = 0
    nc = FX.nc
    es = ExitStack()
    es.__enter__()
    kb = KB(nc, es)
    kb.fused = True
    kb.last_phase = False
    FX.kb = kb
    E = fx_ext
    I = lambda name, shape, dt: nc.dram_tensor(name, list(shape), dt).ap()
    RG = [[0, 1, 2, 3], [4, 5, 6, 7]]

    def allgather(pairs, n_wait=None):
        kb.pes = ExitStack()
        skip = []
        for pi, (src, dst) in enumerate(pairs):
            nq = dst.shape[0]
            rp = src.shape[0] // nq
            for q in range(nq):
                si = src[q * rp:(q + 1) * rp, :]
                do = dst[q].rearrange("j p c -> (j p) c")
                tok = kb.coll(lambda e, si=si, do=do: e.collective_compute("AllGather", ALU.bypass, replica_groups=RG, ins=[si], outs=[do]))
                if n_wait is not None and pi >= n_wait:
                    skip.append(tok[1])
        kb.end_phase(skip=tuple(skip))

    common = dict(identb=E("identb", [128, 128], BF16), identf=E("identf", [128, 128], F32), cT=E("cT", [128, 8], F32))
    xs = E("xs", [2048, 1024], F32)
    i1 = dict(o_mod=I("i_mod0", [1, 6144], F32), o_qT=I("i_qT0", [128, 4, 2048], BF16), o_kT=I("i_kT0", [128, 4, 2048], BF16), o_v=I("i_v0", [128, 16, 512], BF16),
              o_iqT=I("i_iqT", [128, 4, 2048], BF16), o_ikT=I("i_ikT", [128, 2048], BF16), o_iw=I("i_iw", [128, 16, 8], F32), o_oin=I("i_oin", [128, 16, 512], F32),
              o_qdT=I("i_qdT", [128, 2, 2048], BF16), o_kv=I("i_kv", [128, 2, 32, 128], BF16), o_dec=I("i_dec", [128, 2, 32], F32), o_gr=I("i_gr", [128, 16, 512], F32))
    FX.remap = dict(common, xs=xs, pos=E("pos", [128, 16], I32), ada_w=E("ada_w0", [1024, 6144], F32), ada_b=E("ada_b0", [1, 6144], F32),
                    mixn=E("mixn0", [128, 8], F32), w_in=E("ab_w_in", [1024, 3672], F32), gate_up=E("gate_up", [16, 256], F32), gate_b=E("gate_b", [256], F32),
                    q_norm=E("dsa_qn", [128], F32), k_norm=E("dsa_kn", [128], F32), invf=E("invf", [64], F32), tri3=E("tri3", [128, 3, 128], F32),
                    csel=E("csel", [128, 2], F32), amask=E("amask", [128, 128], F32), **i1)
    build_l1()
    G_kT0 = I("g_kT0", [2, 4, 64, 8192], BF16); G_v0 = I("g_v0", [2, 4, 64, 8192], BF16); G_ik = I("g_ik", [1, 4, 128, 2048], BF16)
    G_kv = I("g_kv", [2, 4, 64, 8192], BF16); G_dec = I("g_dec", [1, 4, 128, 64], F32)
    allgather([(i1["o_kv"].rearrange("p a n d -> p (a n d)"), G_kv), (i1["o_dec"].rearrange("p a n -> p (a n)"), G_dec),
               (i1["o_kT"].rearrange("p h t -> p (h t)"), G_kT0), (i1["o_v"].rearrange("p i c -> p (i c)"), G_v0), (i1["o_ikT"], G_ik)], n_wait=2)
    i_gla = I("i_gla", [128, 16, 512], F32)
    FX.remap = dict(common, kv=G_kv, dec=G_dec, qdT=i1["o_qdT"], oin=i1["o_oin"], grs=i1["o_gr"], gsel=E("gsel", [128, 2, 8], F32), gnorm=E("gnorm", [128], F32), o_gla=i_gla)
    build_gla()
    i_dsa = I("i_dsa", [128, 16, 512], F32)
    FX.remap = dict(common, qT=i1["o_qT"], iqT=i1["o_iqT"], iw=i1["o_iw"], kT=G_kT0, v=G_v0, ikT=G_ik, mneg=E("mneg", [128, 512], F32), mpos=E("mpos", [128, 512], F32),
                    pow2=E("pow2", [128, NIT + 1], F32), o_dsa=i_dsa)
    build_dsa()
    i_x1 = I("i_x1", [2048, 1024], F32)
    FX.remap = dict(common, xs=xs, mixa=i_gla, mixb=i_dsa, modrow=i1["o_mod"], fnorm=E("fnorm0", [128, 8], F32), w_out=E("ab_w_out", [1024, 1024], F32),
                    w1=E("w1_0", [1024, 5632], F32), w2=E("w2_0", [2816, 1024], F32), o_x=i_x1)
    build_tail()
    i5 = dict(o_mod=I("i_mod1", [1, 6144], F32), o_qT=I("i_qT1", [128, 4, 2048], BF16), o_kT=I("i_kT1", [128, 4, 2048], BF16), o_v=I("i_v1", [128, 16, 512], BF16),
              o_u=I("i_u", [128, 16, 512], F32), o_uh=I("i_uh", [256, 512], F32))
    FX.remap = dict(common, xs=i_x1, ada_w=E("ada_w1", [1024, 6144], F32), ada_b=E("ada_b1", [1, 6144], F32), mixn=E("mixn1", [128, 8], F32),
                    w_in=E("cd_w_in", [1024, 2048], F32), q_norm=E("sb_qn", [128], F32), k_norm=E("sb_kn", [128], F32), **i5)
    build_l1b()
    G_kT1 = I("g_kT1", [2, 4, 64, 8192], BF16); G_v1 = I("g_v1", [2, 4, 64, 8192], BF16); G_uh = I("g_uh", [1, 4, 256, 512], F32)
    allgather([(i5["o_uh"], G_uh), (i5["o_kT"].rearrange("p h t -> p (h t)"), G_kT1), (i5["o_v"].rearrange("p i c -> p (i c)"), G_v1)], n_wait=1)
    i_pool = I("i_pool", [128, 16, 512], F32)
    FX.remap = dict(common, u=i5["o_u"], guh=G_uh, hsel=E("hsel", [128, 4], F32), band=E("band", [128, 4, 128], F32), band0=E("band0", [128, 4, 128], F32),
                    bandhc=E("bandhc", [128, 2, 16, 4, 128], F32), pool_w=E("pool_w", [4, 128, 128], F32), pool_scale=E("pool_scale", [512], F32), o_pool=i_pool)
    build_pool()
    i_sb = I("i_sb", [128, 16, 512], F32)
    FX.remap = dict(common, qT=i5["o_qT"], kT=G_kT1, v=G_v1, sbmask=E("sbmask", [128, 512], F32), U=E("U", [128, 128], BF16), ones=E("ones", [128, 128], BF16), o_sb=i_sb)
    build_sb()
    if DBG:
        kb.pes = ExitStack()
        for nm, ap in (("d_dsa", i_dsa), ("d_gla", i_gla), ("d_pool", i_pool), ("d_sb", i_sb)):
            o = nc.dram_tensor(nm, [128, 16, 512], F32, kind="ExternalOutput").ap()
            kb.dma("sp", o, ap, is_output=True)
        o = nc.dram_tensor("d_x1", [2048, 1024], F32, kind="ExternalOutput").ap()
        kb.dma("sp", o, i_x1, is_output=True)
        o = nc.dram_tensor("d_mod1", [1, 6144], F32, kind="ExternalOutput").ap()
        kb.dma("sp", o, i5["o_mod"], is_output=True)
        kb.end_phase()
    out = nc.dram_tensor("out", [2048, 1024], F32, kind="ExternalOutput").ap()
    FX.remap = dict(common, xs=i_x1, mixa=i_pool, mixb=i_sb, modrow=i5["o_mod"], fnorm=E("fnorm1", [128, 8], F32), w_out=E("cd_w_out", [1024, 1024], F32),
                    w1=E("w1_1", [1024, 5632], F32), w2=E("w2_1", [2816, 1024], F32), o_x=out)
    kb.last_phase = True
    build_tail()
    es.close()
    FX.active = False
    return nc


def fused_maps(inp):
    identb = np.eye(128, dtype=np.float32).astype(ml_dtypes.bfloat16)
    identf = np.eye(128, dtype=np.float32)
    cs1 = l1_consts()
    pow2 = np.tile((2.0 ** -(np.arange(NIT + 1) + 1)).astype(np.float32)[None, :], (128, 1))
    U, ones = sb_consts()
    L = lambda a: np.ascontiguousarray(a.reshape(8, 128).T)
    maps = []
    for c in range(8):
        b, j = divmod(c, 4)
        mneg, mpos = dsa_masks(j)
        band, band0, bandhc, hsel = pool_consts_core(j)
        maps.append(dict(
            identb=identb, identf=identf, cT=L(inp["c"][b]),
            xs=np.ascontiguousarray(inp["x"][b].reshape(64, 128, 1024)[j::4].reshape(2048, 1024)),
            pos=np.ascontiguousarray(inp["positions"][b].reshape(64, 128)[j::4].T.astype(np.int32)),
            ada_w0=inp["ada_w"][0], ada_b0=inp["ada_b"][0][None, :], ada_w1=inp["ada_w"][1], ada_b1=inp["ada_b"][1][None, :],
            mixn0=L(inp["mix_norm"][0]), mixn1=L(inp["mix_norm"][1]), fnorm0=L(inp["ffn_norm"][0]), fnorm1=L(inp["ffn_norm"][1]),
            ab_w_in=inp["ab_w_in"][0], gate_up=inp["gla_gate_up"][0], gate_b=inp["gla_gate_b"][0], dsa_qn=inp["dsa_q_norm"][0], dsa_kn=inp["dsa_k_norm"][0],
            invf=cs1["invf"], tri3=cs1["tri3"], csel=cs1["csel"], amask=cs1["amask"],
            mneg=mneg, mpos=mpos, pow2=pow2, gsel=gla_sel(j), gnorm=inp["gla_out_norm"][0],
            ab_w_out=inp["ab_w_out"][0], w1_0=inp["ffn_w1"][0], w2_0=inp["ffn_w2"][0],
            cd_w_in=inp["cd_w_in"][0], sb_qn=inp["sb_q_norm"][0], sb_kn=inp["sb_k_norm"][0],
            hsel=hsel, band=band, band0=band0, bandhc=bandhc, pool_w=inp["pool_w"][0], pool_scale=inp["pool_scale"][0],
            sbmask=sb_mask(j), U=U, ones=ones,
            cd_w_out=inp["cd_w_out"][0], w1_1=inp["ffn_w1"][1], w2_1=inp["ffn_w2"][1]))
    return maps


def kernel(**inputs):
    inp = {k: np.asarray(v) for k, v in inputs.items()}
    nc = build_fused()
    res = run_bass_kernel_spmd(nc, fused_maps(inp), core_ids=list(range(8)))
    out = np.zeros((2, 64, 128, 1024), np.float32)
    for c in range(8):
        b, j = divmod(c, 4)
        out[b, j::4] = res.results[c]["out"].reshape(16, 128, 1024)
    kernel.last_results = res.results
    return out.reshape(2, 8192, 1024)
```
